# Optimizing a Trainium2 kernel written in Bass

```python
import jax, jax.numpy as jnp
from jax import lax
import numpy as np

D_MODEL = 1024
BATCH = 8
SEQ = 2048
DEPTH = 1

RWKV_HEADS = 8
RWKV_HEAD_DIM = 64
RWKV_WIDTH = RWKV_HEADS * RWKV_HEAD_DIM
DECAY_LORA = 64
ICLR_LORA = 64
GATE_LORA = 128
GN_EPS = 64e-5
N_DIR = 2
GMLP_GROUPS = 8
GMLP_WIDTH = 512
GMLP_GROUP_DIM = GMLP_WIDTH // GMLP_GROUPS
CHUNK = 128
N_BRANCH = 2
MEM_LEN = 256
XATTN_HEADS = 4
XATTN_HEAD_DIM = D_MODEL // XATTN_HEADS
PEER_HEADS = 8
N_KEYS = 128
N_EXPERTS = N_KEYS * N_KEYS
PEER_QDIM = 256
PEER_HALF = PEER_QDIM // 2
PEER_TOPK = 16
PEER_TOKEN_BLOCK = 128
LN_EPS = 1e-5
ALPHA = (2.0 * DEPTH) ** 0.25
BETA = (8.0 * DEPTH) ** -0.25

RWKV_SPLITS = [int(c) for c in np.cumsum([RWKV_WIDTH, RWKV_WIDTH, RWKV_WIDTH,
                                          N_DIR * DECAY_LORA, N_DIR * ICLR_LORA])]
RWKV_COLS = 3 * RWKV_WIDTH + N_DIR * DECAY_LORA + N_DIR * ICLR_LORA + GATE_LORA
GMLP_COLS = 2 * GMLP_WIDTH
GATE_COLS = N_BRANCH * D_MODEL
IN_COLS = RWKV_COLS + GMLP_COLS + GATE_COLS

kernel_name = "hybrid_rwkv7_gmlp_peer_deepnorm_encoder"


def layer_norm(x, g, b, eps=LN_EPS):
    xf = x.astype(jnp.float32)
    mu = jnp.mean(xf, -1, keepdims=True)
    var = jnp.mean(jnp.square(xf - mu), -1, keepdims=True)
    return ((xf - mu) * lax.rsqrt(var + eps) * g.astype(jnp.float32) + b.astype(jnp.float32)).astype(x.dtype)


def centred_shift(p):
    zero = jnp.zeros_like(p[:, :1])
    prev = jnp.concatenate([zero, p[:, :-1]], axis=1)
    nxt = jnp.concatenate([p[:, 1:], zero], axis=1)
    return 0.5 * (prev + nxt)


def dir_time_major(fwd, bwd):
    t = jnp.stack([fwd, jnp.flip(bwd, axis=1)], axis=0)
    return jnp.moveaxis(t, 2, 0)


def rwkv7_scan(r, w, k, v, a, b):
    n_dir, bsz, nh, n = r.shape[1:]
    s0 = jnp.zeros((n_dir, bsz, nh, n, n), jnp.float32)

    def step(s, inp):
        r_t, w_t, k_t, v_t, a_t, b_t = inp
        sa = jnp.einsum('zbhij,zbhj->zbhi', s, a_t)
        s = s * w_t[..., None, :] + sa[..., :, None] * b_t[..., None, :] + v_t[..., :, None] * k_t[..., None, :]
        y = jnp.einsum('zbhij,zbhj->zbhi', s, r_t)
        return s, y

    _, y = lax.scan(step, s0, (r, w, k, v, a, b))
    return y


def rwkv7_branch(p, mu, w0, w2, a0, a2, g2, k_k, k_a, r_k, gn_g, gn_b):
    bsz, s, _ = p.shape
    f32 = jnp.float32
    p = p + mu * (centred_shift(p) - p)
    r, k, v, wd, ad, gd = jnp.split(p.astype(f32), RWKV_SPLITS, axis=-1)
    wd = wd.reshape(bsz, s, N_DIR, DECAY_LORA)
    ad = ad.reshape(bsz, s, N_DIR, ICLR_LORA)
    w_log = -jax.nn.softplus(-(w0.astype(f32) + jnp.einsum('bszr,zrc->bszc', jnp.tanh(wd), w2.astype(f32)))) - 0.5
    decay = jnp.exp(-jnp.exp(w_log))
    a = jax.nn.sigmoid(a0.astype(f32) + jnp.einsum('bszr,zrc->bszc', ad, a2.astype(f32)))
    g = jnp.einsum('bsr,rc->bsc', jax.nn.sigmoid(gd), g2.astype(f32))
    kk = (k * k_k.astype(f32)).reshape(bsz, s, RWKV_HEADS, RWKV_HEAD_DIM)
    kk = kk / jnp.maximum(jnp.linalg.norm(kk, axis=-1, keepdims=True), 1e-12)
    kk = kk.reshape(bsz, s, RWKV_WIDTH)
    k_dir = k[:, :, None, :] * (1.0 + (a - 1.0) * k_a.astype(f32))
    b_vec = kk[:, :, None, :] * a

    hs = lambda t: t.reshape(bsz, s, RWKV_HEADS, RWKV_HEAD_DIM)
    shared = lambda t: dir_time_major(hs(t), hs(t))
    per_dir = lambda t: dir_time_major(hs(t[:, :, 0]), hs(t[:, :, 1]))
    y = rwkv7_scan(shared(r), per_dir(decay), per_dir(k_dir), shared(v), shared(-kk), per_dir(b_vec))
    y = y[:, 0] + jnp.flip(y[:, 1], axis=0)
    y = jnp.moveaxis(y, 0, 1)
    mu_y = jnp.mean(y, -1, keepdims=True)
    var_y = jnp.mean(jnp.square(y - mu_y), -1, keepdims=True)
    y = ((y - mu_y) * lax.rsqrt(var_y + GN_EPS)).reshape(bsz, s, RWKV_WIDTH) * gn_g.astype(f32) + gn_b.astype(f32)
    rk = jnp.sum((r[:, :, None, :] * k_dir).reshape(bsz, s, N_DIR, RWKV_HEADS, RWKV_HEAD_DIM) * r_k.astype(f32), axis=(2, 4))
    y = y + (rk[..., None] * hs(v)).reshape(bsz, s, RWKV_WIDTH)
    return (y * g).astype(p.dtype)


def gmlp_branch(p, ln_g, ln_b, w_s, b_s):
    bsz, s, _ = p.shape
    u, v = jnp.split(jax.nn.gelu(p, approximate=False), 2, axis=-1)
    v = layer_norm(v, ln_g, ln_b)
    v = v.reshape(bsz, s // CHUNK, CHUNK, GMLP_GROUPS, GMLP_GROUP_DIM)
    sv = jnp.einsum('gpq,bcqgd->bcpgd', w_s, v) + b_s.T[None, None, :, :, None]
    return u * sv.reshape(bsz, s, GMLP_WIDTH)


def hybrid_mixer(h, w_in, rwkv_mu, rwkv_w0, rwkv_w2, rwkv_a0, rwkv_a2, rwkv_g2, rwkv_k_k, rwkv_k_a,
                 rwkv_r_k, rwkv_gn_g, rwkv_gn_b, gmlp_ln_g, gmlp_ln_b, gmlp_w_s, gmlp_b_s, w_branch, w_mix_out):
    bsz, s, d = h.shape
    p = jnp.einsum('bsd,dc->bsc', h, w_in)
    p_a, p_b, p_g = jnp.split(p, [RWKV_COLS, RWKV_COLS + GMLP_COLS], axis=-1)
    y_a = rwkv7_branch(p_a, rwkv_mu, rwkv_w0, rwkv_w2, rwkv_a0, rwkv_a2, rwkv_g2, rwkv_k_k, rwkv_k_a,
                       rwkv_r_k, rwkv_gn_g, rwkv_gn_b)
    y_b = gmlp_branch(p_b, gmlp_ln_g, gmlp_ln_b, gmlp_w_s, gmlp_b_s)
    y = jnp.stack([y_a, y_b], axis=2)
    branch = jnp.einsum('bsnc,ncd->bsnd', y, w_branch)
    gates = jax.nn.sigmoid(p_g.reshape(bsz, s, N_BRANCH, d))
    merged = jnp.sum(gates * branch, axis=2)
    return jnp.einsum('bsd,de->bse', merged, w_mix_out)


def memory_xattn(h, mem_n, w_q, w_kv, w_o):
    bsz, s, d = h.shape
    m = mem_n.shape[1]
    q = jnp.einsum('bsd,de->bse', h, w_q).reshape(bsz, s, XATTN_HEADS, XATTN_HEAD_DIM)
    k, v = jnp.split(jnp.einsum('bmd,de->bme', mem_n, w_kv), 2, axis=-1)
    k = k.reshape(bsz, m, XATTN_HEADS, XATTN_HEAD_DIM)
    v = v.reshape(bsz, m, XATTN_HEADS, XATTN_HEAD_DIM)
    scores = jnp.einsum('bshd,bmhd->bhsm', q, k).astype(jnp.float32) * (XATTN_HEAD_DIM ** -0.5)
    prob = jax.nn.softmax(scores, axis=-1).astype(h.dtype)
    o = jnp.einsum('bhsm,bmhd->bshd', prob, v).reshape(bsz, s, d)
    return jnp.einsum('bsd,de->bse', o, w_o)


def peer(h, w_query, sub_keys, u_table, v_table):
    bsz, s, d = h.shape
    q = jnp.einsum('bsd,dq->bsq', h, w_query).reshape(bsz, s, PEER_HEADS, 2, PEER_HALF)
    sc = jnp.einsum('bshzc,zkc->bshzk', q, sub_keys).astype(jnp.float32)
    top_s, top_i = lax.top_k(sc, PEER_TOPK)
    cand_s = top_s[..., 0, :, None] + top_s[..., 1, None, :]
    cand_i = top_i[..., 0, :, None] * N_KEYS + top_i[..., 1, None, :]
    cand_s = cand_s.reshape(bsz, s, PEER_HEADS, PEER_TOPK * PEER_TOPK)
    cand_i = cand_i.reshape(bsz, s, PEER_HEADS, PEER_TOPK * PEER_TOPK)
    best_s, best_pos = lax.top_k(cand_s, PEER_TOPK)
    expert = jnp.take_along_axis(cand_i, best_pos, axis=-1)
    gate = jax.nn.softmax(best_s, axis=-1).astype(h.dtype)
    n_blk = (bsz * s) // PEER_TOKEN_BLOCK
    xb = h.reshape(n_blk, PEER_TOKEN_BLOCK, d)
    eb = expert.reshape(n_blk, PEER_TOKEN_BLOCK, PEER_HEADS, PEER_TOPK)
    gb = gate.reshape(n_blk, PEER_TOKEN_BLOCK, PEER_HEADS, PEER_TOPK)

    def token_block(args):
        xt, et, gt = args
        u = jnp.take(u_table, et, axis=0)
        act = jax.nn.gelu(jnp.einsum('thkd,td->thk', u, xt), approximate=False)
        vv = jnp.take(v_table, et, axis=0)
        return jnp.einsum('thk,thkd->td', gt * act, vv)

    y = lax.map(token_block, (xb, eb, gb))
    return y.reshape(bsz, s, d)


def setup_inputs(seed: int = 0) -> dict:
    key = jax.random.key(seed)
    keys = list(jax.random.split(key, 64))

    def nk():
        return keys.pop()

    def nrm(shape, scale):
        return jax.random.normal(nk(), shape, jnp.float32) * scale

    def gain(shape):
        return 1.0 + nrm(shape, 0.02)

    L, D = DEPTH, D_MODEL
    col_scale = jnp.ones((IN_COLS,), jnp.float32).at[2 * RWKV_WIDTH:3 * RWKV_WIDTH].set(BETA)
    kv_scale = jnp.ones((2 * D,), jnp.float32).at[D:].set(BETA)
    return {
        "x": nrm((BATCH, SEQ, D), 1.0),
        "mem": nrm((BATCH, MEM_LEN, D), 1.0),
        "ln_emb_g": gain((D,)),
        "ln_emb_b": nrm((D,), 0.02),
        "w_in": nrm((L, D, IN_COLS), D ** -0.5) * col_scale,
        "rwkv_mu": jax.random.uniform(nk(), (L, RWKV_COLS), jnp.float32, 0.0, 1.0),
        "rwkv_w0": jax.random.uniform(nk(), (L, N_DIR, RWKV_WIDTH), jnp.float32, -6.0, -1.0),
        "rwkv_w2": nrm((L, N_DIR, DECAY_LORA, RWKV_WIDTH), 0.1),
        "rwkv_a0": nrm((L, N_DIR, RWKV_WIDTH), 0.1),
        "rwkv_a2": nrm((L, N_DIR, ICLR_LORA, RWKV_WIDTH), 0.1),
        "rwkv_g2": nrm((L, GATE_LORA, RWKV_WIDTH), GATE_LORA ** -0.5),
        "rwkv_k_k": 0.85 + nrm((L, RWKV_WIDTH), 0.02),
        "rwkv_k_a": gain((L, RWKV_WIDTH)),
        "rwkv_r_k": nrm((L, RWKV_HEADS, RWKV_HEAD_DIM), 0.1),
        "rwkv_gn_g": gain((L, RWKV_WIDTH)),
        "rwkv_gn_b": nrm((L, RWKV_WIDTH), 0.02),
        "gmlp_ln_g": gain((L, GMLP_WIDTH)),
        "gmlp_ln_b": nrm((L, GMLP_WIDTH), 0.02),
        "gmlp_w_s": nrm((L, GMLP_GROUPS, CHUNK, CHUNK), CHUNK ** -0.5),
        "gmlp_b_s": gain((L, GMLP_GROUPS, CHUNK)),
        "w_branch": nrm((L, N_BRANCH, RWKV_WIDTH, D), RWKV_WIDTH ** -0.5),
        "w_mix_out": nrm((L, D, D), D ** -0.5) * BETA,
        "ln1_g": gain((L, D)),
        "ln1_b": nrm((L, D), 0.02),
        "mem_ln_g": gain((L, D)),
        "mem_ln_b": nrm((L, D), 0.02),
        "xattn_w_q": nrm((L, D, D), D ** -0.5),
        "xattn_w_kv": nrm((L, D, 2 * D), D ** -0.5) * kv_scale,
        "xattn_w_o": nrm((L, D, D), D ** -0.5) * BETA,
        "ln2_g": gain((L, D)),
        "ln2_b": nrm((L, D), 0.02),
        "peer_w_query": nrm((L, D, PEER_HEADS * PEER_QDIM), D ** -0.5),
        "peer_sub_keys": nrm((L, 2, N_KEYS, PEER_HALF), PEER_HALF ** -0.5),
        "peer_u": nrm((L, N_EXPERTS, D), D ** -0.5),
        "peer_v": nrm((L, N_EXPERTS, D), BETA * PEER_HEADS ** -0.5),
        "ln3_g": gain((L, D)),
        "ln3_b": nrm((L, D), 0.02),
    }


def reference(x, mem, ln_emb_g, ln_emb_b, w_in, rwkv_mu, rwkv_w0, rwkv_w2, rwkv_a0, rwkv_a2, rwkv_g2,
              rwkv_k_k, rwkv_k_a, rwkv_r_k, rwkv_gn_g, rwkv_gn_b, gmlp_ln_g, gmlp_ln_b, gmlp_w_s, gmlp_b_s,
              w_branch, w_mix_out, ln1_g, ln1_b, mem_ln_g, mem_ln_b, xattn_w_q, xattn_w_kv, xattn_w_o,
              ln2_g, ln2_b, peer_w_query, peer_sub_keys, peer_u, peer_v, ln3_g, ln3_b):
    h = layer_norm(x, ln_emb_g, ln_emb_b)
    for l in range(DEPTH):
        mix = hybrid_mixer(h, w_in[l], rwkv_mu[l], rwkv_w0[l], rwkv_w2[l], rwkv_a0[l], rwkv_a2[l], rwkv_g2[l],
                           rwkv_k_k[l], rwkv_k_a[l], rwkv_r_k[l], rwkv_gn_g[l], rwkv_gn_b[l],
                           gmlp_ln_g[l], gmlp_ln_b[l], gmlp_w_s[l], gmlp_b_s[l], w_branch[l], w_mix_out[l])
        h = layer_norm(ALPHA * h + mix, ln1_g[l], ln1_b[l])
        mem_n = layer_norm(mem, mem_ln_g[l], mem_ln_b[l])
        h = layer_norm(ALPHA * h + memory_xattn(h, mem_n, xattn_w_q[l], xattn_w_kv[l], xattn_w_o[l]),
                       ln2_g[l], ln2_b[l])
        h = layer_norm(ALPHA * h + peer(h, peer_w_query[l], peer_sub_keys[l], peer_u[l], peer_v[l]),
                       ln3_g[l], ln3_b[l])
    return h
```

```python
from contextlib import ExitStack

import numpy as np
import concourse.bass as bass
import concourse.mybir as mybir
from concourse.bass_utils import run_bass_kernel_spmd

F32 = mybir.dt.float32
BF16 = mybir.dt.bfloat16
I32 = mybir.dt.int32
U32 = mybir.dt.uint32
AF = mybir.ActivationFunctionType
ALU = mybir.AluOpType
AX = mybir.AxisListType

N_CORES = 8
SEQ = 2048
D = 1024
NT = SEQ // 128


class Res:
    __slots__ = ("name", "w", "r", "excl")

    def __init__(self, name="", excl=False):
        self.name = name
        self.excl = excl
        self.w = None
        self.r = {}


class KB:
    def __init__(self, nc, es):
        self.nc = nc
        self.es = es
        self.eng = {"pe": nc.tensor, "act": nc.scalar, "dve": nc.vector, "pool": nc.gpsimd, "sp": nc.sync}
        self.sem = {}
        self.cnt = {}
        self.seen = {}
        for e in self.eng:
            self.sem[e] = es.enter_context(nc.semaphore("sem_" + e))
            self.cnt[e] = 0
            self.seen[e] = {}
        self.dsem = {}
        for q, n in (("sp", 8), ("pool", 8), ("act", 2)):
            self.dsem[q] = [[es.enter_context(nc.semaphore("dsem_%s%d" % (q, i))), 0] for i in range(n)]
        self.dptr = {q: 0 for q in self.dsem}
        self.n_ins = 0
        self.n_wait = 0
        self._id = 0
        self.defer = None

    def sb(self, shape, dt, name=None):
        self._id += 1
        return self.es.enter_context(self.nc.sbuf_tensor("%s_%d" % (name or "t", self._id), list(shape), dt))

    def ps(self, shape, dt, name=None):
        self._id += 1
        return self.es.enter_context(self.nc.psum_tensor("%s_%d" % (name or "p", self._id), list(shape), dt))

    def _deps(self, reads, writes):
        deps = []
        for r in reads:
            if r.w is not None:
                deps.append(r.w)
        for w in writes:
            if w.w is not None:
                deps.append(w.w)
            deps.extend(w.r.values())
        return deps

    def _wait(self, e, deps):
        eng = self.eng[e]
        seen = self.seen[e]
        best = {}
        for (sem, val) in deps:
            k = id(sem)
            if seen.get(k, 0) >= val:
                continue
            if k not in best or best[k][1] < val:
                best[k] = (sem, val)
        for k, (sem, val) in best.items():
            if e == "pe" and sem is self.sem["pe"]:
                continue
            eng.wait_ge(sem, val)
            self.n_wait += 1
            seen[k] = val

    def _mark(self, tok, reads, writes):
        for r in reads:
            k = id(tok[0])
            r.r[k] = tok
        for w in writes:
            w.w = tok
            w.r = {}

    def mark(self):
        if self.defer is not None:
            self.defer.append(None)

    def op(self, e, fn, reads=(), writes=()):
        if self.defer is not None:
            self.defer.append((e, fn, list(reads), list(writes)))
            return None
        ex = [r for r in reads if r.excl and r not in writes]
        if ex:
            writes = list(writes) + ex
        self._wait(e, self._deps(reads, writes))
        ins = fn(self.eng[e])
        self.cnt[e] += 1
        ins.then_inc(self.sem[e], 1)
        tok = (self.sem[e], self.cnt[e])
        self._mark(tok, reads, writes)
        self.n_ins += 1
        return tok

    def dma(self, q, fn, reads=(), writes=()):
        slots = self.dsem[q]
        slot = slots[self.dptr[q] % len(slots)]
        self.dptr[q] += 1
        deps = self._deps(reads, writes)
        if slot[1] > 0:
            deps.append((slot[0], slot[1]))
        self._wait(q, deps)
        ins = fn(self.eng[q])
        slot[1] += 16
        ins.then_inc(slot[0], 16)
        tok = (slot[0], slot[1])
        self._mark(tok, reads, writes)
        self.n_ins += 1
        return tok

    def wait_all(self, e, ress):
        deps = []
        for r in ress:
            if r.w is not None:
                deps.append(r.w)
            deps.extend(r.r.values())
        self._wait(e, deps)


class Ring:
    def __init__(self, k, n, shape, dt, name="ring"):
        self.items = [(k.sb(shape, dt, name), Res(name)) for _ in range(n)]
        self.i = 0

    def get(self):
        it = self.items[self.i % len(self.items)]
        self.i += 1
        return it


RW = 512
NHP = 4
RWKV_COLS = 1920
GM0 = 1920
GT0 = 2944
C0 = float(np.exp(-0.5))
ALPHA = float(2.0 ** 0.25)
LN_EPS = 1e-5
GN_EPS = 64e-5
CH = 64
BLK = 128
NCH = SEQ // CH
NBLK = SEQ // BLK


def build_program(debug=(), stop=None, scan_steps=None, scan_sub=99):
    nc = bass.Bass("TRN2", target_bir_lowering=False)
    dbg_outs = {}

    def din(name, shape, dt=F32):
        return nc.dram_tensor(name, list(shape), dt, kind="ExternalInput").ap()

    x = din("x", [SEQ, D]); mem = din("mem", [256, D])
    lng = {n: din(n, [1, D]) for n in ("ln_emb_g", "ln_emb_b", "ln1_g", "ln1_b", "ln2_g", "ln2_b", "ln3_g", "ln3_b", "mem_ln_g", "mem_ln_b")}
    w_in = din("w_in", [D, 4992]); mu_d = din("mu", [1, RWKV_COLS])
    w0T_d = din("w0T", [128, 8]); a0T_d = din("a0T", [128, 8])
    w2_d = din("w2", [128, RW]); a2_d = din("a2", [128, RW]); g2_d = din("g2", [128, RW])
    pp_d = din("pp", [128, 20])
    gln_g_d = din("gln_g", [1, 512]); gln_b_d = din("gln_b", [1, 512])
    wsT_d = din("wsT", [128, 8, 128]); bsF_d = din("bsF", [128, 4, 128])
    wbr_d = din("w_branch", [1024, D]); wmix_d = din("w_mix", [D, D])
    wq_d = din("wq", [D, D]); wkv_d = din("wkv", [D, 2 * D]); wo_d = din("wo", [D, D])
    pwq_d = din("pwq", [D, 2048]); skT_d = din("skT", [128, 2, 128])
    pu_d = din("puT", [128, 128, D]); pv_d = din("pvP", [128, 128, D])
    Ub_d = nc.dram_tensor("Ub", [128, 128, D], BF16, kind="Internal").ap()
    Vb_d = nc.dram_tensor("Vb", [128, 128, D], BF16, kind="Internal").ap()
    ident_d = din("c_ident", [128, 128]); onesbd_d = din("c_onesbd", [128, 128])
    maskF_d = din("c_maskF", [128, 512]); maskB_d = din("c_maskB", [128, 512]); rst_d = din("c_rst", [128, BLK])
    iota_d = din("c_iota", [128, 16])
    out_d = nc.dram_tensor("out", [SEQ, D], F32, kind="ExternalOutput").ap()

    es = ExitStack()
    with es:
        k = KB(nc, es)
        RAW = k.sb([128, 40960], F32, "raw")

        def view(off_kb, nbytes, dt, pattern=None, **kw):
            w0 = int(off_kb * 256)
            v = RAW[:, w0:w0 + nbytes // 4]
            if dt != F32:
                v = v.bitcast(dt)
            if pattern:
                v = v.rearrange(pattern, **kw)
            return v

        def barrier():
            toks = [(k.sem[e], k.cnt[e]) for e in k.eng if k.cnt[e] > 0]
            for q in k.dsem:
                for s in k.dsem[q]:
                    if s[1] > 0:
                        toks.append((s[0], s[1]))
            for e in k.eng:
                k._wait(e, toks)

        dbg_res = []

        def dbg(name, ap, res, shape):
            if name not in debug:
                return
            o = nc.dram_tensor("dbg_" + name, list(shape), F32, kind="ExternalOutput").ap()
            dbg_outs[name] = o
            barrier()
            if ap.dtype != F32:
                tmp = k.sb(list(shape), F32, "dbgtmp"); tr = Res()
                k.op("dve", lambda e: e.tensor_copy(tmp[:], ap), reads=res, writes=[tr])
                k.dma("sp", lambda e: e.dma_start(out=o, in_=tmp[:]), reads=[tr])
                dbg_res.append(tr)
            else:
                rr = Res()
                k.dma("sp", lambda e: e.dma_start(out=o, in_=ap), reads=res, writes=[rr])
                dbg_res.append(rr)

        def finish():
            barrier()

        cst = Res("consts")
        ident = k.sb([128, 128], BF16, "ident"); identf = k.sb([128, 128], F32, "identf")
        onesbd = k.sb([128, 128], BF16, "onesbd"); ones64 = k.sb([128, 128], F32, "ones64")
        maskF = k.sb([128, 512], BF16, "maskF"); maskB = k.sb([128, 512], BF16, "maskB")
        rst = k.sb([128, BLK], F32, "rst"); iota16 = k.sb([128, 16], F32, "iota16")
        w0T = k.sb([128, 8], F32, "w0T"); a0T = k.sb([128, 8], F32, "a0T"); pp = k.sb([128, 20], F32, "pp")
        ppx = k.sb([128, 8], F32, "ppx")
        w2b = k.sb([128, RW], BF16, "w2b"); a2b = k.sb([128, RW], BF16, "a2b"); g2b = k.sb([128, RW], BF16, "g2b")
        epsln = k.sb([128, 1], F32, "epsln"); epsgn = k.sb([128, 1], F32, "epsgn")
        for (t, d_) in ((ident, ident_d), (onesbd, onesbd_d), (maskF, maskF_d), (maskB, maskB_d), (w2b, w2_d), (a2b, a2_d), (g2b, g2_d)):
            k.dma("pool", lambda e: e.dma_start(out=t[:], in_=d_), writes=[cst])
        for (t, d_) in ((identf, ident_d), (rst, rst_d), (iota16, iota_d), (w0T, w0T_d), (a0T, a0T_d), (pp, pp_d)):
            k.dma("sp", lambda e: e.dma_start(out=t[:], in_=d_), writes=[cst])
        nw0T = k.sb([128, 8], F32, "nw0T"); na0T = k.sb([128, 8], F32, "na0T"); one1 = k.sb([128, 1], F32, "one1")
        k.op("dve", lambda e: e.memset(one1[:], 1.0), writes=[cst])
        k.op("dve", lambda e: e.memset(epsln[:], LN_EPS), writes=[cst])
        k.op("dve", lambda e: e.memset(epsgn[:], GN_EPS), writes=[cst])
        k.op("dve", lambda e: e.tensor_scalar(ones64[:], identf[:], 0.0, 0.0, ALU.mult, ALU.add), reads=[cst], writes=[cst])
        k.op("dve", lambda e: e.tensor_scalar(ones64[:], onesbd[:], 1.0 / 64.0, None, ALU.mult), reads=[cst], writes=[cst])
        k.op("dve", lambda e: e.tensor_scalar(nw0T[:], w0T[:], -1.0, None, ALU.mult), reads=[cst], writes=[cst])
        k.op("dve", lambda e: e.tensor_scalar(na0T[:], a0T[:], -1.0, None, ALU.mult), reads=[cst], writes=[cst])
        k.op("dve", lambda e: e.tensor_scalar(ppx[:, 0:4], pp[:, 4:8], -1.0, 1.0, ALU.mult, ALU.add), reads=[cst], writes=[cst])

        def act_sigmoid(eng_op, out, in_, nbias, reads, writes):
            eng_op("act", lambda e: e.activation(out=out, in_=in_, func=AF.Exp, bias=nbias, scale=-1.0), reads=reads + [cst], writes=writes)
            eng_op("act", lambda e: e.activation(out=out, in_=out, func=AF.Ln, bias=one1[:, 0:1], scale=1.0), reads=writes + [cst], writes=writes)
            eng_op("act", lambda e: e.activation(out=out, in_=out, func=AF.Exp, scale=-1.0), reads=writes, writes=writes)
        k.op("dve", lambda e: e.tensor_scalar(ppx[:, 4:8], pp[:, 4:8], -2.0, 2.0, ALU.mult, ALU.add), reads=[cst], writes=[cst])
        KK = lambda hp: pp[:, hp:hp + 1]
        KA = lambda hp: pp[:, 4 + hp:5 + hp]
        RK = lambda hp: pp[:, 8 + hp:9 + hp]
        GNG = lambda hp: pp[:, 12 + hp:13 + hp]
        GNB = lambda hp: pp[:, 16 + hp:17 + hp]

        psf = [(k.ps([128, 512], F32, "psf"), Res("psf%d" % i, True)) for i in range(6)]
        psb = [(k.ps([128, 1024], BF16, "psb"), Res("psb%d" % i, True)) for i in range(2)]
        pctr = {"f": 0, "b": 0}

        def PSF():
            it = psf[pctr["f"] % 6]; pctr["f"] += 1
            return it

        def PSB():
            it = psb[pctr["b"] % 2]; pctr["b"] += 1
            return it

        def mm(out, lhsT, rhs, start, stop, reads, writes):
            k.op("pe", lambda e: e.matmul(out, lhsT, rhs, start=start, stop=stop), reads=reads, writes=writes)

        def load_bc(ph, d_ap, n, name):
            t = ph.enter_context(nc.sbuf_tensor(name + "_%d" % k._id, [128, n], F32)); k._id += 1
            r = Res(name)
            k.dma("sp", lambda e: e.dma_start(out=t[:], in_=d_ap.partition_broadcast(128)), writes=[r])
            return t, r

        def phase_sb(ph, shape, dt, name):
            k._id += 1
            return ph.enter_context(nc.sbuf_tensor("%s_%d" % (name, k._id), list(shape), dt))

        class PRing:
            def __init__(self, ph, n, shape, dt, name):
                self.items = [(phase_sb(ph, shape, dt, name), Res(name)) for _ in range(n)]
                self.i = 0

            def get(self):
                it = self.items[self.i % len(self.items)]; self.i += 1
                return it

        def layer_norm(src, src_res, gam, bet, gbres, out, out_res, small, n=1024, eps=None):
            eps = eps if eps is not None else epsln
            nchk = n // 512
            st, sr = small.get()
            for c in range(nchk):
                k.op("dve", lambda e: e.bn_stats(out=st[:, c * 6:(c + 1) * 6], in_=src[:, c * 512:(c + 1) * 512]), reads=src_res, writes=[sr])
            k.op("dve", lambda e: e.bn_aggr(out=st[:, 12:14], in_=st[:, 0:6 * nchk]), reads=[sr], writes=[sr])
            k.op("act", lambda e: e.activation(out=st[:, 14:15], in_=st[:, 13:14], func=AF.Sqrt, bias=eps[:, 0:1], scale=1.0), reads=[sr, cst], writes=[sr])
            k.op("dve", lambda e: e.reciprocal(out=st[:, 14:15], in_=st[:, 14:15]), reads=[sr], writes=[sr])
            k.op("dve", lambda e: e.tensor_scalar(st[:, 15:16], st[:, 12:13], st[:, 14:15], -1.0, ALU.mult, ALU.mult), reads=[sr], writes=[sr])
            k.op("act", lambda e: e.activation(out=out, in_=src, func=AF.Identity, bias=st[:, 15:16], scale=st[:, 14:15]), reads=src_res + [sr], writes=out_res)
            k.op("dve", lambda e: e.tensor_tensor(out=out, in0=out, in1=gam, op=ALU.mult), reads=out_res + gbres, writes=out_res)
            k.op("dve", lambda e: e.tensor_tensor(out=out, in0=out, in1=bet, op=ALU.add), reads=out_res + gbres, writes=out_res)

        def to_fm(hb, hbr, dstT, dst_res, t):
            pt, ptr_ = PSB()
            for c in range(8):
                k.op("pe", lambda e: e.transpose(pt[:, c * 128:(c + 1) * 128], hb[:, c * 128:(c + 1) * 128], ident[:]), reads=[hbr, cst], writes=[ptr_])
            k.op("act", lambda e: e.activation(out=dstT[:, :, t * 128:(t + 1) * 128], in_=pt[:].rearrange("p (c t) -> p c t", t=128), func=AF.Copy), reads=[ptr_], writes=dst_res)

        def phase_h0T(h0T, h0T_res):
            with ExitStack() as ph:
                g_t, g_r = load_bc(ph, lng["ln_emb_g"], D, "g")
                b_t, b_r = load_bc(ph, lng["ln_emb_b"], D, "b")
                xs = PRing(ph, 2, [128, D], F32, "xs")
                hbs = PRing(ph, 2, [128, D], BF16, "hb")
                small = PRing(ph, 4, [128, 16], F32, "lnsm")
                for t in range(NT):
                    xt, xr = xs.get()
                    k.dma("sp", lambda e: e.dma_start(out=xt[:], in_=x[t * 128:(t + 1) * 128, :]), writes=[xr])
                    layer_norm(xt[:], [xr], g_t[:], b_t[:], [g_r, b_r], xt[:], [xr], small)
                    hb, hbr = hbs.get()
                    k.op("act", lambda e: e.activation(out=hb[:], in_=xt[:], func=AF.Copy), reads=[xr], writes=[hbr])
                    if t == 0:
                        dbg("h0_t0", xt[:], [xr], [128, D])
                    to_fm(hb, hbr, h0T, [h0T_res[t]], t)
                barrier()

        h0T = view(64, 32768, BF16, "p (c t) -> p c t", c=8)
        h0T_res = [Res("h0T%d" % t) for t in range(NT)]
        phase_h0T(h0T, h0T_res)
        dbg("h0T", h0T[:, 0, :], h0T_res, [128, SEQ])
        if stop == "A":
            finish()
            return nc, dbg_outs

        shT = view(96, 32768, BF16, "p (c t) -> p c t", c=8); shr = Res("shT")
        rkv = view(0, 12 * SEQ * 2, BF16, "p (c t) -> p c t", c=12)
        wag = view(48, 3 * SEQ * 2, BF16, "p (c t) -> p c t", c=3)
        rkv_res = [[Res("rkv") for _ in range(4)] for _ in range(15)]
        for c in range(8):
            k.op("dve", lambda e: e.tensor_tensor(out=shT[:, c, 1:SEQ - 1], in0=h0T[:, c, 0:SEQ - 2], in1=h0T[:, c, 2:SEQ], op=ALU.add), reads=h0T_res, writes=[shr])
        k.op("dve", lambda e: e.tensor_copy(shT[:, :, 0:1], h0T[:, :, 1:2]), reads=h0T_res, writes=[shr])
        k.op("dve", lambda e: e.tensor_copy(shT[:, :, SEQ - 1:SEQ], h0T[:, :, SEQ - 2:SEQ - 1]), reads=h0T_res, writes=[shr])
        with ExitStack() as ph:
            wsts = PRing(ph, 2, [128, 8, 128], F32, "wst")
            was = PRing(ph, 2, [128, 8, 128], BF16, "wa")
            wbs = PRing(ph, 2, [128, 8, 128], BF16, "wb")
            mus = PRing(ph, 2, [128, 384], F32, "mu")
            for cc in range(15):
                c0 = cc * 128
                wst, wsr = wsts.get()
                k.dma("sp", lambda e: e.dma_start(out=wst[:], in_=w_in[:, c0:c0 + 128].rearrange("(k p) c -> p k c", p=128)), writes=[wsr])
                mt, mr = mus.get()
                k.dma("sp", lambda e: e.dma_start(out=mt[:, 0:128], in_=mu_d[:, c0:c0 + 128].partition_broadcast(128)), writes=[mr])
                k.op("dve", lambda e: e.tensor_scalar(mt[:, 128:256], mt[:, 0:128], -1.0, 1.0, ALU.mult, ALU.add), reads=[mr], writes=[mr])
                k.op("dve", lambda e: e.tensor_scalar(mt[:, 256:384], mt[:, 0:128], 0.5, None, ALU.mult), reads=[mr], writes=[mr])
                wa, war = was.get(); wb, wbr = wbs.get()
                k.op("dve", lambda e: e.tensor_tensor(out=wa[:], in0=wst[:], in1=mt[:, 128:256].unsqueeze(1).to_broadcast([128, 8, 128]), op=ALU.mult), reads=[wsr, mr], writes=[war])
                k.op("pool", lambda e: e.tensor_tensor(out=wb[:], in0=wst[:], in1=mt[:, 256:384].unsqueeze(1).to_broadcast([128, 8, 128]), op=ALU.mult), reads=[wsr, mr], writes=[wbr])
                for tb in range(4):
                    pb, pbr = PSF()
                    ts = slice(tb * 512, (tb + 1) * 512)
                    for kc in range(8):
                        mm(pb[:], wa[:, kc, :], h0T[:, kc, ts], kc == 0, False, [war] + h0T_res[4 * tb:4 * tb + 4], [pbr])
                    for kc in range(8):
                        mm(pb[:], wb[:, kc, :], shT[:, kc, ts], False, kc == 7, [wbr, shr], [pbr])
                    if cc < 12:
                        dest, fn = rkv[:, cc, ts], AF.Copy
                    else:
                        dest, fn = wag[:, cc - 12, ts], (AF.Tanh, AF.Copy, AF.Sigmoid)[cc - 12]
                    k.op("act", lambda e: e.activation(out=dest, in_=pb[:], func=fn), reads=[pbr], writes=[rkv_res[cc][tb]])
            barrier()
        dbg("r0", rkv[:, 0, :], rkv_res[0], [128, SEQ])
        dbg("k0", rkv[:, 4, :], rkv_res[4], [128, SEQ])
        dbg("wd", wag[:, 0, :], rkv_res[12], [128, SEQ])
        if stop == "B":
            finish()
            return nc, dbg_outs


        barrier()
        yaT = rkv[:, 0:4, :]
        yaT_res = [rkv_res[hp] for hp in range(4)]

        class Carver:
            def __init__(self, a_kb, b_kb):
                self.p = a_kb * 1024; self.end = b_kb * 1024

            def alloc(self, shape, dt):
                nb = int(np.prod(shape[1:])) * (2 if dt == BF16 else 4)
                nb = (nb + 31) // 32 * 32
                assert self.p + nb <= self.end, "carver overflow"
                v = RAW[:, self.p // 4:(self.p + nb) // 4]
                self.p += nb
                if dt != F32:
                    v = v.bitcast(dt)
                n = int(np.prod(shape[1:]))
                v = v[:, 0:n]
                if len(shape) == 3:
                    v = v.rearrange("p (a b) -> p a b", a=shape[1])
                return v

        def scan_round(rnd):
            hps = [2 * rnd, 2 * rnd + 1]
            carve = Carver(64, 128)
            carve2 = Carver(144, 160)
            yacc = view(128, 2 * SEQ * 4, F32, "p (c t) -> p c t", c=2)
            yacc_res = [[Res("yacc") for _ in range(NCH)] for _ in range(2)]
            if scan_steps:
                for hl_ in range(2):
                    k.op("pool", lambda e: e.memset(yacc[:, hl_, :], 0.0), writes=yacc_res[hl_])
                    for r_ in yacc_res[hl_]:
                        r_.w = None
            with ExitStack() as ph:
                def T(shape, dt, name):
                    return (phase_sb(ph, shape, dt, name), Res(name))
                tmp = {n: T([128, BLK], F32, n) for n in ("sw", "sa", "cs", "pin", "pex", "ege", "egi", "kr", "rn", "kk", "nkk", "t1", "kd", "bb")}
                ksq = T([128, BLK], BF16, "ksq")
                yb = T([128, BLK], BF16, "yb"); ysqb = T([128, BLK], BF16, "ysqb")
                NBK = NCH // 2
                streams = []
                for z in (0, 1):
                    for hl, hp in enumerate(hps):
                        st = dict(z=z, hp=hp, hl=hl, ui=len(streams))
                        st["A3"] = [dict(AR=carve.alloc([128, 2, 192], BF16), eg=carve.alloc([128, BLK], F32), res=Res("prepA")) for _ in range(3)]
                        st["K2"] = [dict(Kb=carve.alloc([128, 2, 128], BF16), Bb=carve.alloc([128, 2, 128], BF16), Vb=carve.alloc([128, 2, 128], BF16), res=Res("prepK")) for _ in range(2)]
                        for sl in st["A3"]:
                            k.op("pool", lambda e: e.memset(sl["AR"], 0.0), writes=[sl["res"]])
                        for sl in st["K2"]:
                            for nm in ("Kb", "Bb", "Vb"):
                                k.op("pool", lambda e: e.memset(sl[nm], 0.0), writes=[sl["res"]])
                        st["Tm"] = [[(carve2.alloc([128, 512], BF16), Res("Tm")) for _ in range(2)] for _ in range(2)]
                        st["WT"] = T([128, 128], BF16, "WT"); st["UT"] = T([128, 128], BF16, "UT")
                        st["S"] = [T([128, 128], BF16, "S") for _ in range(2)]
                        k.op("pool", lambda e: e.memset(st["S"][0][0][:], 0.0), writes=[st["S"][0][1]])
                        st["si"] = 0
                        streams.append(st)
                KTG = [[(carve.alloc([128, 4, 384], BF16), Res("KTG")) for _ in range(2)] for _ in range(2)]
                MVG = [[(carve.alloc([128, 4, 128], BF16), Res("MVG")) for _ in range(2)] for _ in range(2)]
                IVG = [[(carve.alloc([128, 3, 512], BF16), Res("IVG")) for _ in range(2)] for _ in range(2)]
                chain_res = [Res("chain%d" % i, True) for i in range(2)]
                psbf = [(psb[i][0][:].bitcast(F32), psb[i][1]) for i in range(2)]
                lvl_banks = [[psf[0], psf[1], psf[2]], [psf[3], psbf[0], psbf[1]]]

                def blk_of(st, bi):
                    return bi if st["z"] == 0 else NBK - 1 - bi

                def chunk_of(st, bi, g):
                    b = blk_of(st, bi)
                    return 2 * b + g if st["z"] == 0 else 2 * b + 1 - g

                def prep(st, bi, tmp, ksq, pcol):
                    z, hp = st["z"], st["hp"]
                    b = blk_of(st, bi)
                    sa_ = st["A3"][bi % 3]; sk_ = st["K2"][bi % 2]
                    t0 = b * BLK; ts = slice(t0, t0 + BLK); tb = t0 // 512
                    zs = slice(z * 64, (z + 1) * 64)
                    hc = slice(hp * 128, (hp + 1) * 128)
                    r_ap, k_ap, v_ap = rkv[:, hp, ts], rkv[:, 4 + hp, ts], rkv[:, 8 + hp, ts]
                    rr, kr_, vr_ = [rkv_res[hp][tb]], [rkv_res[4 + hp][tb]], [rkv_res[8 + hp][tb]]
                    pb, pbr = psbf[0][0][:, pcol:pcol + 256], psbf[0][1]
                    mm(pb[:, 0:BLK], w2b[zs, hc], wag[zs, 0, ts], True, True, [cst, rkv_res[12][tb]], [pbr])
                    mm(pb[:, BLK:2 * BLK], a2b[zs, hc], wag[zs, 1, ts], True, True, [cst, rkv_res[13][tb]], [pbr])
                    (sw, swr), (sa, sar), (cs, csr) = tmp["sw"], tmp["sa"], tmp["cs"]
                    (pin, pinr), (pex, pexr), (ege, eger), (egi, egir) = tmp["pin"], tmp["pex"], tmp["ege"], tmp["egi"]
                    act_sigmoid(k.op, sw[:], pb[:, 0:BLK], nw0T[:, z * 4 + hp:z * 4 + hp + 1], [pbr], [swr])
                    act_sigmoid(k.op, sa[:], pb[:, BLK:2 * BLK], na0T[:, z * 4 + hp:z * 4 + hp + 1], [pbr], [sar])
                    k.mark()
                    k.op("dve", lambda e: e.tensor_tensor_scan(out=cs[:], data0=rst[:], data1=sw[:], initial=0.0, op0=ALU.mult, op1=ALU.add), reads=[swr, cst], writes=[csr])
                    v3 = lambda t_: t_.rearrange("p (c t) -> p c t", t=CH)
                    if z == 0:
                        k.op("dve", lambda e: e.tensor_tensor(out=pex[:], in0=cs[:], in1=sw[:], op=ALU.subtract), reads=[csr, swr], writes=[pexr])
                        pin_t, pin_r = cs, csr
                    else:
                        k.op("dve", lambda e: e.tensor_tensor(out=v3(pex[:]), in0=v3(cs[:])[:, :, CH - 1:CH].to_broadcast([128, 2, CH]), in1=v3(cs[:]), op=ALU.subtract), reads=[csr], writes=[pexr])
                        k.op("dve", lambda e: e.tensor_tensor(out=pin[:], in0=pex[:], in1=sw[:], op=ALU.add), reads=[pexr, swr], writes=[pinr])
                        pin_t, pin_r = pin, pinr
                    eg = sa_["eg"]; rA = sa_["res"]; rK = sk_["res"]
                    k.op("act", lambda e: e.activation(out=eg, in_=pin_t[:], func=AF.Exp, scale=-C0), reads=[pin_r], writes=[rA])
                    k.op("act", lambda e: e.activation(out=ege[:], in_=pex[:], func=AF.Exp, scale=-C0), reads=[pexr], writes=[eger])
                    k.op("act", lambda e: e.activation(out=egi[:], in_=pin_t[:], func=AF.Exp, scale=C0), reads=[pin_r], writes=[egir])
                    k.mark()
                    (kr, krr), (rn, rnr), (kk, kkr) = tmp["kr"], tmp["rn"], tmp["kk"]
                    k.op("dve", lambda e: e.tensor_scalar(kr[:], k_ap, KK(hp), None, ALU.mult), reads=kr_ + [cst], writes=[krr])
                    k.op("dve", lambda e: e.tensor_tensor(out=ksq[0][:], in0=kr[:], in1=kr[:], op=ALU.mult), reads=[krr], writes=[ksq[1]])
                    pb2, pb2r = psbf[1][0][:, pcol:pcol + 256], psbf[1][1]
                    mm(pb2[:, 0:BLK], onesbd[:], ksq[0][:], True, True, [cst, ksq[1]], [pb2r])
                    k.op("dve", lambda e: e.tensor_scalar(rn[:], pb2[:, 0:BLK], 1e-24, None, ALU.max), reads=[pb2r], writes=[rnr])
                    k.mark()
                    k.op("act", lambda e: e.activation(out=rn[:], in_=rn[:], func=AF.Ln), reads=[rnr], writes=[rnr])
                    k.op("act", lambda e: e.activation(out=rn[:], in_=rn[:], func=AF.Exp, scale=-0.5), reads=[rnr], writes=[rnr])
                    k.op("dve", lambda e: e.tensor_tensor(out=kk[:], in0=kr[:], in1=rn[:], op=ALU.mult), reads=[krr, rnr], writes=[kkr])
                    nkk, nkkr = tmp["nkk"]
                    k.op("dve", lambda e: e.tensor_scalar(nkk[:], kk[:], -1.0, None, ALU.mult), reads=[kkr], writes=[nkkr])
                    AR, Kb, Bb, Vb = sa_["AR"], sk_["Kb"], sk_["Bb"], sk_["Vb"]
                    k.op("dve", lambda e: e.tensor_tensor(out=AR[:, :, 128:192], in0=v3(r_ap), in1=v3(eg), op=ALU.mult), reads=rr + [rA], writes=[rA])
                    (t1, t1r), (kd, kdr), (bb, bbr) = tmp["t1"], tmp["kd"], tmp["bb"]
                    k.op("dve", lambda e: e.tensor_scalar(t1[:], sa[:], KA(hp), ppx[:, hp:hp + 1], ALU.mult, ALU.add), reads=[sar, cst], writes=[t1r])
                    k.op("dve", lambda e: e.tensor_tensor(out=kd[:], in0=t1[:], in1=k_ap, op=ALU.mult), reads=[t1r] + kr_, writes=[kdr])
                    k.op("dve", lambda e: e.tensor_tensor(out=bb[:], in0=kk[:], in1=sa[:], op=ALU.mult), reads=[kkr, sar], writes=[bbr])
                    k.mark()

                    def half_ops(half):
                        hs = slice(half * 64, (half + 1) * 64); cs_ = slice(half * 64, (half + 1) * 64)
                        eng = "dve" if half == 0 else "pool"
                        k.op(eng, lambda e: e.tensor_tensor(out=AR[hs, :, cs_], in0=v3(nkk[hs, :]), in1=v3(ege[hs, :]), op=ALU.mult), reads=[nkkr, eger, rA], writes=[rA])
                        k.op(eng, lambda e: e.tensor_tensor(out=Kb[hs, :, cs_], in0=v3(kd[hs, :]), in1=v3(egi[hs, :]), op=ALU.mult), reads=[kdr, egir, rK], writes=[rK])
                        k.op(eng, lambda e: e.tensor_tensor(out=Bb[hs, :, cs_], in0=v3(bb[hs, :]), in1=v3(egi[hs, :]), op=ALU.mult), reads=[bbr, egir, rK], writes=[rK])
                        k.op("act", lambda e: e.activation(out=Vb[hs, :, cs_], in_=v3(v_ap)[hs], func=AF.Copy), reads=vr_ + [rK], writes=[rK])
                    half_ops(0)
                    half_ops(1)
                    k.mark()

                def unit_ctx(st, bi, g):
                    bp = bi % 2
                    sa_ = st["A3"][bi % 3]; sk_ = st["K2"][bi % 2]
                    c = chunk_of(st, bi, g); ci = c % 2
                    return dict(st=st, ui=st["ui"], c=c, ci=ci, bp=bp, g=g, rA=sa_["res"], rK=sk_["res"], eg=sa_["eg"],
                                AR=sa_["AR"][:, ci, :], Kb=sk_["Kb"][:, ci, :], Bb=sk_["Bb"][:, ci, :], Vb=sk_["Vb"][:, ci, :],
                                Tm=st["Tm"][bp][g], KT=(KTG[bp][g][0][:, st["ui"], :], KTG[bp][g][1]), Minv=(MVG[bp][g][0][:, st["ui"], :], MVG[bp][g][1]))

                def pre_block(bi):
                    groups = [[unit_ctx(st, bi, g) for st in streams] for g in range(2)]
                    macro = []

                    def t_pe(g):
                        def f():
                            for u in groups[g]:
                                ui = u["ui"]
                                pb, pbr = psf[ui]; rd = [u["rA"], u["rK"]]
                                mm(pb[:, 0:192], u["Kb"], u["AR"], True, True, rd, [pbr])
                                mm(pb[:, 192:384], u["Bb"], u["AR"], True, True, rd, [pbr])
                                mm(pb[:, 384:512], u["AR"][:, 0:128], u["Bb"], True, True, rd, [pbr])
                                pt_, ptr_ = psb[ui // 2]; pt = pt_[:, (ui % 2) * 384:(ui % 2) * 384 + 384]
                                for i, nm in enumerate(("Kb", "Bb", "Vb")):
                                    k.op("pe", lambda e: e.transpose(pt[:, i * 128:(i + 1) * 128], u[nm], ident[:]), reads=rd + [cst], writes=[ptr_])
                        return f

                    def t_ev(g):
                        def f():
                            bp = bi % 2
                            IA, IAr = IVG[g][0]
                            for u in groups[g]:
                                ui = u["ui"]
                                pb, pbr = psf[ui]; Tm, Tmr = u["Tm"]
                                msk = maskF if u["st"]["z"] == 0 else maskB
                                k.op("dve", lambda e: e.tensor_tensor(out=Tm, in0=pb[:], in1=msk[:], op=ALU.mult), reads=[pbr, cst], writes=[Tmr])
                            KTt, KTr = KTG[bp][g]
                            for j in range(2):
                                pt_, ptr_ = psb[j]
                                k.op("act", lambda e: e.activation(out=KTt[:, 2 * j:2 * j + 2, :], in_=pt_[:, 0:768].rearrange("p (u c) -> p u c", u=2), func=AF.Copy), reads=[ptr_], writes=[KTr])
                            for u in groups[g]:
                                ui = u["ui"]; Tm, Tmr = u["Tm"]
                                k.op("dve", lambda e: e.tensor_tensor(out=IA[:, 2, ui * 128:(ui + 1) * 128], in0=Tm[:, 192:320], in1=ident[:], op=ALU.add), reads=[Tmr, cst], writes=[IAr])
                        return f
                    for g in range(2):
                        macro.append([t_pe(g), t_ev(g)])

                    def lvl_pe(l, g):
                        def f():
                            (bP, bPr), (bQ, bQr), (bX, bXr) = lvl_banks[g]
                            for u in groups[g]:
                                ui = u["ui"]; cs_ = slice(ui * 128, (ui + 1) * 128)
                                if l == 1:
                                    Tm, Tmr = u["Tm"]
                                    P, Q, rd = Tm[:, 192:320], Tm[:, 384:512], [Tmr]
                                    mm(bP[:, cs_], Q, P, True, True, rd, [bPr])
                                    mm(bQ[:, cs_], P, Q, True, True, rd, [bQr])
                                else:
                                    src, srcr = IVG[g][l % 2]
                                    P, Q, X = src[:, 0, cs_], src[:, 1, cs_], src[:, 2, cs_]
                                    if l <= 4:
                                        mm(bP[:, cs_], Q, P, True, True, [srcr], [bPr])
                                    if l <= 5:
                                        mm(bQ[:, cs_], P, Q, True, True, [srcr], [bQr])
                                    mm(bX[:, cs_], ident[:], X, True, False, [srcr, cst], [bXr])
                                    mm(bX[:, cs_], Q, X, False, True, [srcr], [bXr])
                        return f

                    def lvl_ev(l, g):
                        def f():
                            (bP, bPr), (bQ, bQr), (bX, bXr) = lvl_banks[g]
                            bp = bi % 2
                            jobs = []
                            if l == 1:
                                dst, dstr = IVG[g][0]
                                jobs = [(dst[:, 0, :], bP, bPr, dstr), (dst[:, 1, :], bQ, bQr, dstr)]
                            elif l <= 5:
                                dst, dstr = IVG[g][(l + 1) % 2]
                                if l <= 4:
                                    jobs.append((dst[:, 0, :], bP, bPr, dstr))
                                jobs.append((dst[:, 1, :], bQ, bQr, dstr))
                                jobs.append((dst[:, 2, :], bX, bXr, dstr))
                            else:
                                mv, mvr = MVG[bp][g]
                                jobs = [(mv[:].rearrange("p u c -> p (u c)"), bX, bXr, mvr)]
                            for i, (o, bk, bkr, dr) in enumerate(jobs):
                                if (i + l + g) % 2 == 0:
                                    k.op("act", lambda e: e.activation(out=o, in_=bk[:, 0:512], func=AF.Copy), reads=[bkr], writes=[dr])
                                else:
                                    k.op("dve", lambda e: e.tensor_copy(o, bk[:, 0:512]), reads=[bkr], writes=[dr])
                        return f
                    for l in range(1, 7):
                        macro.append([lvl_pe(l, 0), lvl_pe(l, 1), lvl_ev(l, 0), lvl_ev(l, 1)])
                    return macro

                def chain_stages(bi, g):
                    ctx = [unit_ctx(st, bi, g) for st in streams]

                    def w_pe():
                        for u in ctx:
                            st = u["st"]; Tm, Tmr = u["Tm"]; KT, KTr = u["KT"]
                            S, Sr = st["S"][st["si"] % 2]
                            ui = u["ui"]
                            pb = psf[4 + ui // 2][0][:, (ui % 2) * 192:(ui % 2) * 192 + 192]; pbr = chain_res[ui // 2]; u["pC"] = (pb, pbr)
                            mm(pb[:, 0:128], Tm[:, 0:128], KT[:, 256:384], True, False, [Tmr, KTr], [pbr])
                            mm(pb[:, 0:128], u["AR"][:, 0:128], S[:], False, True, [u["rA"], Sr], [pbr])

                    def w_ev():
                        for u in ctx:
                            pb, pbr = u["pC"]; WT, WTr = u["st"]["WT"]
                            k.op("act", lambda e: e.activation(out=WT[:], in_=pb[:, 0:128], func=AF.Copy), reads=[pbr], writes=[WTr])

                    def u_pe():
                        for u in ctx:
                            pb, pbr = u["pC"]; WT, WTr = u["st"]["WT"]; Mi, Mir = u["Minv"]
                            mm(pb[:, 0:128], Mi, WT[:], True, True, [Mir, WTr], [pbr])

                    def u_ev():
                        for u in ctx:
                            pb, pbr = u["pC"]; UT, UTr = u["st"]["UT"]
                            k.op("dve", lambda e: e.tensor_copy(UT[:], pb[:, 0:128]), reads=[pbr], writes=[UTr])

                    def ys_pe():
                        for u in ctx:
                            st = u["st"]; Tm, Tmr = u["Tm"]; KT, KTr = u["KT"]; UT, UTr = st["UT"]
                            S, Sr = st["S"][st["si"] % 2]
                            pb, pbr = u["pC"]; rA = u["rA"]
                            mm(pb[:, 128:192], KT[:, 256:384], Tm[:, 128:192], True, False, [KTr, Tmr], [pbr])
                            mm(pb[:, 128:192], S[:], u["AR"][:, 128:192], False, False, [Sr, rA], [pbr])
                            mm(pb[:, 128:192], UT[:], Tm[:, 320:384], False, True, [UTr, Tmr], [pbr])
                            mm(pb[:, 0:128], KT[:, 0:128], KT[:, 256:384], True, False, [KTr], [pbr])
                            mm(pb[:, 0:128], ident[:], S[:], False, False, [cst, Sr], [pbr])
                            mm(pb[:, 0:128], KT[:, 128:256], UT[:], False, True, [KTr, UTr], [pbr])

                    def ys_ev():
                        for u in ctx:
                            st = u["st"]; pb, pbr = u["pC"]; c = u["c"]; ci = u["ci"]
                            Sn, Snr = st["S"][(st["si"] + 1) % 2]
                            eg = u["eg"]
                            col = ci * CH + (CH - 1 if st["z"] == 0 else 0)
                            k.op("act", lambda e: e.activation(out=Sn[:], in_=pb[:, 0:128], func=AF.Identity, scale=eg[:, col:col + 1]), reads=[pbr, u["rA"]], writes=[Snr])
                            st["si"] += 1
                            yr = yacc_res[st["hl"]][c]
                            ydst = yacc[:, st["hl"], c * CH:(c + 1) * CH]
                            if yr.w is None:
                                k.op("dve", lambda e: e.tensor_copy(ydst, pb[:, 128:192]), reads=[pbr], writes=[yr])
                            else:
                                k.op("dve", lambda e: e.tensor_tensor(out=ydst, in0=ydst, in1=pb[:, 128:192], op=ALU.add), reads=[pbr, yr], writes=[yr])
                    return [[w_pe, w_ev], [u_pe, u_ev], [ys_pe, ys_ev]]

                def replay(chunk):
                    for (e_, fn_, rd_, wr_) in chunk:
                        k.op(e_, fn_, rd_, wr_)

                tmpB = {n_: (carve.alloc([128, BLK], F32), Res(n_ + "B")) for n_ in tmp}
                ksqB = (carve.alloc([128, BLK], BF16), Res("ksqB"))

                def record_prep(bi_):
                    per_stream = []
                    for si_, st in enumerate(streams):
                        k.defer = []
                        if si_ % 2 == 0:
                            prep(st, bi_, tmp, ksq, 0)
                        else:
                            prep(st, bi_, tmpB, ksqB, 256)
                        rec = k.defer; k.defer = None
                        chunks = [[]]
                        for it in rec:
                            if it is None:
                                chunks.append([])
                            else:
                                chunks[-1].append(it)
                        per_stream.append(chunks)
                    out = []
                    for p_ in range(0, len(streams), 2):
                        ca, cb = per_stream[p_], per_stream[p_ + 1]
                        for j_ in range(max(len(ca), len(cb))):
                            a_ = ca[j_] if j_ < len(ca) else []
                            b_ = cb[j_] if j_ < len(cb) else []
                            m_ = []
                            for i_ in range(max(len(a_), len(b_))):
                                if i_ < len(a_):
                                    m_.append(a_[i_])
                                if i_ < len(b_):
                                    m_.append(b_[i_])
                            if m_:
                                out.append(m_)
                    return out
                nblk = (scan_steps + 1) // 2 if scan_steps else NBK
                for bi_ in range(min(2, nblk)):
                    for c_ in record_prep(bi_):
                        replay(c_)
                for ms in pre_block(0):
                    for f in ms:
                        f()
                for bi in range(nblk):
                    A = pre_block(bi + 1) if bi + 1 < nblk else []
                    B = [f for pr in (chain_stages(bi, 0) + chain_stages(bi, 1)) for f in pr]
                    C = record_prep(bi + 2) if bi + 2 < nblk else []
                    Af = []
                    for ms in A:
                        flags = [False, True] if len(ms) == 2 else [True, False, False, True]
                        Af += list(zip(ms, flags))
                    nb_per = max(1, (len(B) + max(1, len(Af)) - 1) // max(1, len(Af)))
                    while Af or B or C:
                        safe = True
                        if Af:
                            f, safe = Af.pop(0)
                            f()
                        for _ in range(nb_per if Af else len(B)):
                            if B:
                                B.pop(0)()
                        if C and (safe or not any(op_[0] == "pe" for op_ in C[0])):
                            replay(C.pop(0))
                        if not Af and not B:
                            while C:
                                replay(C.pop(0))
                dbg("yacc%d" % rnd, yacc[:, 0, :], yacc_res[0], [128, SEQ])
                fo = k.op
                fmm = mm
                fctr = [0]

                def FPS():
                    it = psf[fctr[0] % 4]; fctr[0] += 1
                    return it
                def fin_block(hl, hp, b, tmp, ksq, yb, ysqb, bankA, bankB):
                    hc = slice(hp * 128, (hp + 1) * 128)
                    t0 = b * BLK; ts = slice(t0, t0 + BLK); tb = t0 // 512
                    y = yacc[:, hl, ts]; yres = yacc_res[hl][2 * b:2 * b + 2]
                    r_ap, k_ap, v_ap = rkv[:, hp, ts], rkv[:, 4 + hp, ts], rkv[:, 8 + hp, ts]
                    rr, kr_, vr_ = [rkv_res[hp][tb]], [rkv_res[4 + hp][tb]], [rkv_res[8 + hp][tb]]
                    (ysq, ysqr), (mean, meanr), (msq, msqr), (var, varr) = tmp["sw"], tmp["sa"], tmp["cs"], tmp["pin"]
                    (rs, rsr), (yn, ynr), (s0, s0r), (s1, s1r), (bon, bonr) = tmp["pex"], tmp["ege"], tmp["egi"], tmp["kr"], tmp["rn"]
                    fo("dve", lambda e: e.tensor_tensor(out=ysqb[0][:], in0=y, in1=y, op=ALU.mult), reads=yres, writes=[ysqb[1]])
                    fo("act", lambda e: e.activation(out=yb[0][:], in_=y, func=AF.Copy), reads=yres, writes=[yb[1]])
                    pa, par = bankA
                    fmm(pa[:, 0:BLK], onesbd[:], yb[0][:], True, True, [cst, yb[1]], [par])
                    fmm(pa[:, BLK:2 * BLK], onesbd[:], ysqb[0][:], True, True, [cst, ysqb[1]], [par])
                    fo("act", lambda e: e.activation(out=mean[:], in_=pa[:, 0:BLK], func=AF.Copy, scale=1.0 / 64.0), reads=[par], writes=[meanr])
                    fo("dve", lambda e: e.tensor_tensor(out=msq[:], in0=mean[:], in1=mean[:], op=ALU.mult), reads=[meanr], writes=[msqr])
                    fo("dve", lambda e: e.scalar_tensor_tensor(out=var[:], in0=pa[:, BLK:2 * BLK], scalar=1.0 / 64.0, in1=msq[:], op0=ALU.mult, op1=ALU.subtract), reads=[par, msqr], writes=[varr])
                    fo("act", lambda e: e.activation(out=rs[:], in_=var[:], func=AF.Ln, bias=epsgn[:, 0:1], scale=1.0), reads=[varr, cst], writes=[rsr])
                    fo("act", lambda e: e.activation(out=rs[:], in_=rs[:], func=AF.Exp, scale=-0.5), reads=[rsr], writes=[rsr])
                    fo("dve", lambda e: e.tensor_tensor(out=yn[:], in0=y, in1=mean[:], op=ALU.subtract), reads=yres + [meanr], writes=[ynr])
                    fo("dve", lambda e: e.tensor_tensor(out=yn[:], in0=yn[:], in1=rs[:], op=ALU.mult), reads=[ynr, rsr], writes=[ynr])
                    fo("dve", lambda e: e.tensor_scalar(yn[:], yn[:], GNG(hp), GNB(hp), ALU.mult, ALU.add), reads=[ynr, cst], writes=[ynr])
                    pb_, pbr_ = bankA
                    fmm(pb_[:, 0:BLK], a2b[0:64, hc], wag[0:64, 1, ts], True, True, [cst, rkv_res[13][tb]], [pbr_])
                    pb2_, pbr2_ = bankB
                    fmm(pb2_[:, 0:BLK], a2b[64:128, hc], wag[64:128, 1, ts], True, True, [cst, rkv_res[13][tb]], [pbr2_])
                    act_sigmoid(fo, s0[:], pb_[:, 0:BLK], na0T[:, hp:hp + 1], [pbr_], [s0r])
                    act_sigmoid(fo, s1[:], pb2_[:, 0:BLK], na0T[:, 4 + hp:5 + hp], [pbr2_], [s1r])
                    fo("dve", lambda e: e.tensor_tensor(out=s0[:], in0=s0[:], in1=s1[:], op=ALU.add), reads=[s0r, s1r], writes=[s0r])
                    fo("dve", lambda e: e.tensor_scalar(s0[:], s0[:], KA(hp), ppx[:, 4 + hp:5 + hp], ALU.mult, ALU.add), reads=[s0r, cst], writes=[s0r])
                    fo("dve", lambda e: e.tensor_tensor(out=s0[:], in0=s0[:], in1=k_ap, op=ALU.mult), reads=[s0r] + kr_, writes=[s0r])
                    fo("dve", lambda e: e.scalar_tensor_tensor(out=ksq[0][:], in0=r_ap, scalar=RK(hp), in1=s0[:], op0=ALU.mult, op1=ALU.mult), reads=rr + [s0r, cst], writes=[ksq[1]])
                    pc_, pcr_ = bankA
                    fmm(pc_[:, 0:BLK], onesbd[:], ksq[0][:], True, True, [cst, ksq[1]], [pcr_])
                    fmm(pc_[:, BLK:2 * BLK], g2b[:, hc], wag[:, 2, ts], True, True, [cst, rkv_res[14][tb]], [pcr_])
                    fo("dve", lambda e: e.tensor_tensor(out=bon[:], in0=pc_[:, 0:BLK], in1=v_ap, op=ALU.mult), reads=[pcr_] + vr_, writes=[bonr])
                    fo("dve", lambda e: e.tensor_tensor(out=yn[:], in0=yn[:], in1=bon[:], op=ALU.add), reads=[ynr, bonr], writes=[ynr])
                    fo("dve", lambda e: e.tensor_tensor(out=yaT[:, hp, ts], in0=yn[:], in1=pc_[:, BLK:2 * BLK], op=ALU.mult), reads=[ynr, pcr_], writes=[yaT_res[hp][tb]])
                barrier()
                cvf = Carver(64, 128)
                tmp2 = {n_: (cvf.alloc([128, BLK], F32), Res(n_ + "2")) for n_ in tmp}
                ksq2 = (cvf.alloc([128, BLK], BF16), Res("ksq2")); yb2 = (cvf.alloc([128, BLK], BF16), Res("yb2")); ysqb2 = (cvf.alloc([128, BLK], BF16), Res("ysqb2"))
                for b in range(NBLK):
                    recs = []
                    for hl, hp in enumerate(hps):
                        k.defer = []
                        if hl == 0:
                            fin_block(hl, hp, b, tmp, ksq, yb, ysqb, psf[0], psf[1])
                        else:
                            fin_block(hl, hp, b, tmp2, ksq2, yb2, ysqb2, psf[2], psf[3])
                        recs.append([it for it in k.defer if it is not None]); k.defer = None
                    for i_ in range(max(len(r_) for r_ in recs)):
                        for r_ in recs:
                            if i_ < len(r_):
                                k.op(*r_[i_])
                return yacc, yacc_res

        for rnd in range(2):
            yacc, yacc_res = scan_round(rnd)
            if stop == "C0":
                dbg("yaT0", yaT[:, 0, :], yaT_res[0], [128, SEQ])
                finish()
                return nc, dbg_outs
            barrier()
        dbg("yaT0", yaT[:, 0, :], yaT_res[0], [128, SEQ])
        dbg("yaT3", yaT[:, 3, :], yaT_res[3], [128, SEQ])
        if stop == "C":
            finish()
            return nc, dbg_outs

        barrier()
        h0T_res = [Res("h0T%d" % t) for t in range(NT)]
        phase_h0T(h0T, h0T_res)
        ybT = view(16, 4 * SEQ * 2, BF16, "p (c t) -> p c t", c=4)
        ybT_res = [Res("ybT%d" % t) for t in range(NT)]
        with ExitStack() as ph:
            Wu = view(96, 8 * 512 * 2, BF16, "p (k c) -> p k c", k=8); Wv = view(104, 8 * 512 * 2, BF16, "p (k c) -> p k c", k=8)
            wr_ = Res("WuWv")
            k.dma("pool", lambda e: e.dma_start(out=Wu, in_=w_in[:, GM0:GM0 + 512].rearrange("(k p) c -> p k c", p=128)), writes=[wr_])
            k.dma("pool", lambda e: e.dma_start(out=Wv, in_=w_in[:, GM0 + 512:GM0 + 1024].rearrange("(k p) c -> p k c", p=128)), writes=[wr_])
            wsT = phase_sb(ph, [128, 8, 128], BF16, "wsT"); bsF = phase_sb(ph, [128, 4, 128], F32, "bsF")
            k.dma("pool", lambda e: e.dma_start(out=wsT[:], in_=wsT_d), writes=[wr_])
            k.dma("sp", lambda e: e.dma_start(out=bsF[:], in_=bsF_d), writes=[wr_])
            gg, ggr = load_bc(ph, gln_g_d, 512, "gg"); gb, gbr = load_bc(ph, gln_b_d, 512, "gb")
            uTs = [(view(112 + 4 * i, 4 * 512 * 2, BF16, "p (c t) -> p c t", c=4), Res("uT")) for i in range(2)]
            vgs = PRing(ph, 2, [128, 512], F32, "vg")
            vnEs = PRing(ph, 2, [128, 512], BF16, "vnE"); vnOs = PRing(ph, 2, [128, 512], BF16, "vnO")
            svs = PRing(ph, 2, [128, 512], F32, "sv")
            small = PRing(ph, 4, [128, 16], F32, "lnsm")
            for (t_, r_) in vnEs.items + vnOs.items:
                k.op("pool", lambda e: e.memset(t_[:], 0.0), writes=[r_])
            g4 = lambda ap, g: ap.rearrange("p (c g d) -> p c g d", g=2, d=64)[:, :, g, :]
            for tb in range(4):
                ts = slice(tb * 512, (tb + 1) * 512)
                uT, uTr = uTs[tb % 2]
                for cu in range(4):
                    pb, pbr = PSF()
                    for kc in range(8):
                        mm(pb[:], Wu[:, kc, cu * 128:(cu + 1) * 128], h0T[:, kc, ts], kc == 0, kc == 7, [wr_] + h0T_res[4 * tb:4 * tb + 4], [pbr])
                    k.op("act", lambda e: e.activation(out=uT[:, cu, :], in_=pb[:], func=AF.Gelu), reads=[pbr], writes=[uTr])
                for tt in range(4):
                    t = 4 * tb + tt
                    tsl = slice(t * 128, (t + 1) * 128)
                    pb, pbr = PSF()
                    for kc in range(8):
                        mm(pb[:], h0T[:, kc, tsl], Wv[:, kc, :], kc == 0, kc == 7, [wr_, h0T_res[t]], [pbr])
                    vg, vgr = vgs.get()
                    k.op("act", lambda e: e.activation(out=vg[:], in_=pb[:], func=AF.Gelu), reads=[pbr], writes=[vgr])
                    layer_norm(vg[:], [vgr], gg[:], gb[:], [ggr, gbr], vg[:], [vgr], small, n=512)
                    vnE, vnEr = vnEs.get(); vnO, vnOr = vnOs.get()
                    k.op("act", lambda e: e.activation(out=g4(vnE[:], 0), in_=g4(vg[:], 0), func=AF.Copy), reads=[vgr], writes=[vnEr])
                    k.op("pool", lambda e: e.tensor_copy(g4(vnO[:], 1), g4(vg[:], 1)), reads=[vgr], writes=[vnOr])
                    ps, psr = PSF()
                    for c in range(4):
                        cs_ = slice(c * 128, (c + 1) * 128)
                        mm(ps[:, cs_], vnE[:, cs_], wsT[:, 2 * c, :], True, False, [vnEr, wr_], [psr])
                        mm(ps[:, cs_], vnO[:, cs_], wsT[:, 2 * c + 1, :], False, True, [vnOr, wr_], [psr])
                    sv, svr = svs.get()
                    k.op("dve", lambda e: e.tensor_tensor(out=sv[:], in0=ps[:], in1=bsF[:].rearrange("p c t -> p (c t)"), op=ALU.add), reads=[psr, wr_], writes=[svr])
                    k.op("dve", lambda e: e.tensor_tensor(out=ybT[:, :, tsl], in0=sv[:].rearrange("p (c t) -> p c t", c=4), in1=uT[:, :, tt * 128:(tt + 1) * 128], op=ALU.mult), reads=[svr, uTr], writes=[ybT_res[t]])
            barrier()
        dbg("ybT0", ybT[:, 0, :], ybT_res, [128, SEQ])
        if stop == "D":
            finish()
            return nc, dbg_outs

        mgT = view(128, 8 * SEQ * 2, BF16, "p (c t) -> p c t", c=8)
        mg_res = [Res("mg%d" % tb) for tb in range(4)]
        with ExitStack() as ph:
            wbr = view(96, 8 * 1024 * 2, BF16, "p (k d) -> p k d", k=8); wbr_r = Res("wbr")
            for kc in range(8):
                k.dma("pool", lambda e: e.dma_start(out=wbr[:, kc, :], in_=wbr_d[kc * 128:(kc + 1) * 128, :]), writes=[wbr_r])
            wgs = [(view(112 + 2 * i, 8 * 128 * 2, BF16, "p (k c) -> p k c", k=8), Res("wg")) for i in range(4)]
            sigs = PRing(ph, 3, [128, 512], F32, "sig")
            prs = PRing(ph, 2, [128, 512], F32, "pr")
            wi = 0
            for dc in range(8):
                wg2 = []
                for n in range(2):
                    wg, wgr = wgs[wi % 4]; wi += 1
                    c0 = GT0 + n * 1024 + dc * 128
                    k.dma("pool", lambda e: e.dma_start(out=wg, in_=w_in[:, c0:c0 + 128].rearrange("(k p) c -> p k c", p=128)), writes=[wgr])
                    wg2.append((wg, wgr))
                for tb in range(4):
                    ts = slice(tb * 512, (tb + 1) * 512)
                    hr = h0T_res[4 * tb:4 * tb + 4]
                    acc = None
                    for n in range(2):
                        wg, wgr = wg2[n]
                        pg, pgr = PSF()
                        for kc in range(8):
                            mm(pg[:], wg[:, kc, :], h0T[:, kc, ts], kc == 0, kc == 7, [wgr] + hr, [pgr])
                        sg, sgr = sigs.get()
                        k.op("act", lambda e: e.activation(out=sg[:], in_=pg[:], func=AF.Sigmoid), reads=[pgr], writes=[sgr])
                        pbn, pbnr = PSF()
                        yT, yres = (yaT, [yaT_res[c][tb] for c in range(4)]) if n == 0 else (ybT, ybT_res[4 * tb:4 * tb + 4])
                        for c in range(4):
                            mm(pbn[:], wbr[:, n * 4 + c, dc * 128:(dc + 1) * 128], yT[:, c, ts], c == 0, c == 3, [wbr_r] + yres, [pbnr])
                        if n == 0:
                            acc, accr = prs.get()
                            k.op("dve", lambda e: e.tensor_tensor(out=acc[:], in0=sg[:], in1=pbn[:], op=ALU.mult), reads=[sgr, pbnr], writes=[accr])
                        else:
                            k.op("dve", lambda e: e.tensor_tensor(out=sg[:], in0=sg[:], in1=pbn[:], op=ALU.mult), reads=[sgr, pbnr], writes=[sgr])
                            k.op("pool", lambda e: e.tensor_tensor(out=mgT[:, dc, ts], in0=acc[:], in1=sg[:], op=ALU.add), reads=[sgr, accr], writes=[mg_res[tb]])
            barrier()
        dbg("mgT0", mgT[:, 0, :], mg_res, [128, SEQ])
        if stop == "E":
            finish()
            return nc, dbg_outs

        H = view(64, NT * D * 4, F32, "p (t d) -> p t d", t=NT)
        H_res = [Res("H%d" % t) for t in range(NT)]

        def load_w_fm(dst, d_ap, res, q="pool"):
            for kc in range(8):
                k.dma(q, lambda e: e.dma_start(out=dst[:, kc, :], in_=d_ap[kc * 128:(kc + 1) * 128, :]), writes=[res])

        with ExitStack() as ph:
            wmix = view(0, 8 * 1024 * 2, BF16, "p (k d) -> p k d", k=8); wmix_r = Res("wmix")
            load_w_fm(wmix, wmix_d, wmix_r)
            g0, g0r = load_bc(ph, lng["ln_emb_g"], D, "g0"); b0, b0r = load_bc(ph, lng["ln_emb_b"], D, "b0")
            g1, g1r = load_bc(ph, lng["ln1_g"], D, "g1"); b1, b1r = load_bc(ph, lng["ln1_b"], D, "b1")
            xs = PRing(ph, 2, [128, D], F32, "xs")
            small = PRing(ph, 4, [128, 16], F32, "lnsm")
            for t in range(NT):
                tsl = slice(t * 128, (t + 1) * 128)
                xt, xr = xs.get()
                k.dma("sp", lambda e: e.dma_start(out=xt[:], in_=x[tsl, :]), writes=[xr])
                layer_norm(xt[:], [xr], g0[:], b0[:], [g0r, b0r], xt[:], [xr], small)
                for half in range(2):
                    hs = slice(half * 512, (half + 1) * 512)
                    pm, pmr = PSF()
                    for kc in range(8):
                        mm(pm[:], mgT[:, kc, tsl], wmix[:, kc, hs], kc == 0, kc == 7, [mg_res[t // 4], wmix_r], [pmr])
                    k.op("dve", lambda e: e.scalar_tensor_tensor(out=xt[:, hs], in0=xt[:, hs], scalar=ALPHA, in1=pm[:], op0=ALU.mult, op1=ALU.add), reads=[xr, pmr], writes=[xr])
                layer_norm(xt[:], [xr], g1[:], b1[:], [g1r, b1r], H[:, t, :], [H_res[t]], small)
            barrier()
        dbg("h1_t0", H[:, 0, :], [H_res[0]], [128, D])
        dbg("h1_t9", H[:, 9, :], [H_res[9]], [128, D])
        if stop == "F":
            finish()
            return nc, dbg_outs

        with ExitStack() as ph:
            wq = view(0, 8 * 1024 * 2, BF16, "p (k d) -> p k d", k=8); wo = view(16, 8 * 1024 * 2, BF16, "p (k d) -> p k d", k=8)
            wkK = view(32, 8 * 1024 * 2, BF16, "p (k d) -> p k d", k=8); wkV = view(48, 8 * 1024 * 2, BF16, "p (k d) -> p k d", k=8)
            KT = view(128, 8 * 256 * 2, BF16, "p (c m) -> p c m", c=8); Vm = view(132, 2 * 1024 * 2, BF16, "p (m d) -> p m d", m=2)
            memT = view(136, 8 * 256 * 2, BF16, "p (c m) -> p c m", c=8)
            wres = Res("xw"); kvres = Res("kv"); memr = [Res("memT0"), Res("memT1")]
            load_w_fm(wq, wq_d, wres); load_w_fm(wo, wo_d, wres)
            load_w_fm(wkK, wkv_d[:, 0:D], wres); load_w_fm(wkV, wkv_d[:, D:2 * D], wres)
            small = PRing(ph, 4, [128, 16], F32, "lnsm")
            xs = PRing(ph, 2, [128, D], F32, "xs")
            with ExitStack() as ph2:
                gm, gmr = load_bc(ph2, lng["mem_ln_g"], D, "gm"); bm, bmr = load_bc(ph2, lng["mem_ln_b"], D, "bm")
                hbs0 = PRing(ph2, 2, [128, D], BF16, "hb")
                for mt in range(2):
                    xt, xr = xs.get()
                    k.dma("sp", lambda e: e.dma_start(out=xt[:], in_=mem[mt * 128:(mt + 1) * 128, :]), writes=[xr])
                    layer_norm(xt[:], [xr], gm[:], bm[:], [gmr, bmr], xt[:], [xr], small)
                    hb, hbr = hbs0.get()
                    k.op("act", lambda e: e.activation(out=hb[:], in_=xt[:], func=AF.Copy), reads=[xr], writes=[hbr])
                    to_fm(hb, hbr, memT, [memr[mt]], mt)
                barrier()
            g2_, g2r = load_bc(ph, lng["ln2_g"], D, "g2"); b2_, b2r = load_bc(ph, lng["ln2_b"], D, "b2")
            for c in range(8):
                pk, pkr = PSF()
                for kc in range(8):
                    mm(pk[:, 0:256], wkK[:, kc, c * 128:(c + 1) * 128], memT[:, kc, :], kc == 0, kc == 7, [wres] + memr, [pkr])
                k.op("act", lambda e: e.activation(out=KT[:, c, :], in_=pk[:, 0:256], func=AF.Copy), reads=[pkr], writes=[kvres])
            for mt in range(2):
                for half in range(2):
                    hs = slice(half * 512, (half + 1) * 512)
                    pv, pvr = PSF()
                    for kc in range(8):
                        mm(pv[:], memT[:, kc, mt * 128:(mt + 1) * 128], wkV[:, kc, hs], kc == 0, kc == 7, [wres] + memr, [pvr])
                    k.op("act", lambda e: e.activation(out=Vm[:, mt, hs], in_=pv[:], func=AF.Copy), reads=[pvr], writes=[kvres])
            barrier()
            cv = Carver(32, 64)

            class CRing:
                def __init__(self, n, shape, dt, name):
                    self.items = [(cv.alloc(shape, dt), Res(name)) for _ in range(n)]
                    self.i = 0

                def get(self):
                    it = self.items[self.i % len(self.items)]; self.i += 1
                    return it
            hbs = CRing(2, [128, D], BF16, "hb")
            hTs = CRing(2, [128, 8, 128], BF16, "hT")
            qTs = CRing(2, [128, 8, 128], BF16, "qT")
            pexs = CRing(2, [128, 4, 256], BF16, "pex")
            pTs = CRing(2, [128, 8, 128], BF16, "pT")
            oTs = CRing(2, [128, 8, 128], BF16, "oT")
            sm2 = PRing(ph, 2, [128, 16], F32, "sm2")
            SCL = 256.0 ** -0.5
            tab_res = Res("tables")
            stg = [(view(140 + 8 * i, 4 * D * 2, BF16, "p (a f) -> p a f", a=4), Res("stg%d" % i)) for i in range(2)]
            conv_jobs = [(src, dst, ch) for (src, dst) in ((pu_d, Ub_d), (pv_d, Vb_d)) for ch in range(32)]

            def conv_some(n_):
                for _ in range(n_):
                    if not conv_jobs:
                        return
                    src, dst, ch = conv_jobs.pop(0)
                    st_, str_ = stg[ch % 2]
                    k.dma("pool", lambda e: e.dma_start(out=st_, in_=src[ch * 4:(ch + 1) * 4].rearrange("a p f -> p a f")), writes=[str_])
                    k.dma("sp", lambda e: e.dma_start(out=dst[ch * 4:(ch + 1) * 4].rearrange("a p f -> p a f"), in_=st_), reads=[str_], writes=[tab_res])
            for t in range(NT):
                conv_some(4)
                h1 = H[:, t, :]; h1r = H_res[t]
                hb, hbr = hbs.get()
                k.op("act", lambda e: e.activation(out=hb[:], in_=h1, func=AF.Copy), reads=[h1r], writes=[hbr])
                hT, hTr = hTs.get()
                to_fm(hb, hbr, hT, [hTr], 0)
                qT, qTr = qTs.get()
                for g in range(2):
                    pq, pqr = PSF()
                    for cc in range(4):
                        c = g * 4 + cc
                        for kc in range(8):
                            mm(pq[:, cc * 128:(cc + 1) * 128], wq[:, kc, c * 128:(c + 1) * 128], hT[:, kc, :], kc == 0, kc == 7, [wres, hTr], [pqr])
                    k.op("act", lambda e: e.activation(out=qT[:, g * 4:(g + 1) * 4, :], in_=pq[:].rearrange("p (c t) -> p c t", c=4), func=AF.Copy), reads=[pqr], writes=[qTr])
                sm, smr = sm2.get()
                pex, pexr = pexs.get()
                pss = []
                for g in range(2):
                    ps_, psr_ = PSF(); pss.append((ps_, psr_))
                    for hh in range(2):
                        h = g * 2 + hh
                        for j in range(2):
                            mm(ps_[:, hh * 256:(hh + 1) * 256], qT[:, 2 * h + j, :], KT[:, 2 * h + j, :], j == 0, j == 1, [qTr, kvres], [psr_])
                    k.op("dve", lambda e: e.tensor_reduce(out=sm[:, g * 2:(g + 1) * 2], in_=ps_[:].rearrange("p (h m) -> p h m", h=2), axis=AX.X, op=ALU.max), reads=[psr_], writes=[smr])
                k.op("dve", lambda e: e.tensor_scalar(sm[:, 4:8], sm[:, 0:4], -SCL, None, ALU.mult), reads=[smr], writes=[smr])
                for h in range(4):
                    ps_, psr_ = pss[h // 2]
                    k.op("act", lambda e: e.activation(out=pex[:, h, :], in_=ps_[:, (h % 2) * 256:(h % 2 + 1) * 256], func=AF.Exp, bias=sm[:, 4 + h:5 + h], scale=SCL, accum_out=sm[:, 8 + h:9 + h]), reads=[psr_, smr], writes=[pexr, smr])
                k.op("dve", lambda e: e.reciprocal(out=sm[:, 12:16], in_=sm[:, 8:12]), reads=[smr], writes=[smr])
                k.op("dve", lambda e: e.tensor_tensor(out=pex[:], in0=pex[:], in1=sm[:, 12:16].unsqueeze(2).to_broadcast([128, 4, 256]), op=ALU.mult), reads=[pexr, smr], writes=[pexr])
                pt, ptr_ = PSB()
                for h in range(4):
                    for mt in range(2):
                        i = h * 2 + mt
                        k.op("pe", lambda e: e.transpose(pt[:, i * 128:(i + 1) * 128], pex[:, h, mt * 128:(mt + 1) * 128], ident[:]), reads=[pexr, cst], writes=[ptr_])
                pT, pTr = pTs.get()
                k.op("act", lambda e: e.activation(out=pT[:], in_=pt[:].rearrange("p (c t) -> p c t", t=128), func=AF.Copy), reads=[ptr_], writes=[pTr])
                oT, oTr = oTs.get()
                for g in range(2):
                    po, por = PSF()
                    for cc in range(4):
                        c = g * 4 + cc; h = c // 2
                        for mt in range(2):
                            mm(po[:, cc * 128:(cc + 1) * 128], Vm[:, mt, c * 128:(c + 1) * 128], pT[:, h * 2 + mt, :], mt == 0, mt == 1, [kvres, pTr], [por])
                    k.op("dve", lambda e: e.tensor_copy(oT[:, g * 4:(g + 1) * 4, :], po[:].rearrange("p (c t) -> p c t", c=4)), reads=[por], writes=[oTr])
                xt, xr = xs.get()
                for half in range(2):
                    hs = slice(half * 512, (half + 1) * 512)
                    px, pxr = PSF()
                    for c in range(8):
                        mm(px[:], oT[:, c, :], wo[:, c, hs], c == 0, c == 7, [oTr, wres], [pxr])
                    k.op("dve", lambda e: e.scalar_tensor_tensor(out=xt[:, hs], in0=h1[:, hs], scalar=ALPHA, in1=px[:], op0=ALU.mult, op1=ALU.add), reads=[h1r, pxr], writes=[xr])
                layer_norm(xt[:], [xr], g2_[:], b2_[:], [g2r, b2r], H[:, t, :], [H_res[t]], small)
            barrier()
        dbg("h2_t0", H[:, 0, :], [H_res[0]], [128, D])
        dbg("h2_t9", H[:, 9, :], [H_res[9]], [128, D])
        if stop == "H":
            finish()
            return nc, dbg_outs

        with ExitStack() as pho:
            SI1 = phase_sb(pho, [128, NT, 128], F32, "SI1"); SI2 = phase_sb(pho, [128, NT, 128], F32, "SI2"); SG = phase_sb(pho, [128, NT, 128], F32, "SG")
            slot_res = [Res("slot%d" % t) for t in range(NT)]
            with ExitStack() as ph:
                pwq = view(0, 8 * 2048 * 2, BF16, "p (k d) -> p k d", k=8); pw_r = Res("pwq")
                load_w_fm(pwq, pwq_d, pw_r)
                skT = phase_sb(ph, [128, 2, 128], BF16, "skT")
                k.dma("pool", lambda e: e.dma_start(out=skT[:], in_=skT_d), writes=[pw_r])
                sc = view(48, 16 * 128 * 4, F32, "p (c k) -> p c k", c=16); scr = Res("sc")
                cv = Carver(128, 160)

                def CT(shape, dt, name, c=cv):
                    return (c.alloc(shape, dt), Res(name))
                hb, hbr = CT([128, D], BF16, "hb"); hT, hTr = CT([128, 8, 128], BF16, "hT")
                pqT, pqTr = CT([128, 16, 128], BF16, "pqT")
                sc2, sc2r = CT([128, 256], F32, "sc2"); sc2b, sc2br = CT([128, 256], F32, "sc2b")
                ts_c = [Res("ts%d" % c_) for c_ in range(16)]; ti_c = [Res("ti%d" % c_) for c_ in range(16)]
                bs_h = [Res("bs%d" % h_) for h_ in range(8)]; bp_h = [Res("bp%d" % h_) for h_ in range(8)]
                top_s, tsr = CT([128, 256], F32, "top_s"); top_i, tir = CT([128, 256], U32, "top_i"); top_f, tfr = CT([128, 256], F32, "top_f")
                cand, cdr = CT([128, 2048], F32, "cand")
                best_s, bsr = CT([128, 128], F32, "best_s"); best_p, bpr = CT([128, 128], U32, "best_p")
                pf, pfr = CT([128, 128], F32, "pf"); k1f, k1r = CT([128, 128], F32, "k1f"); k2f, k2r = CT([128, 128], F32, "k2f")
                gsum, gsr = CT([128, 16], F32, "gsum")
                eq = cand
                v4 = lambda ap: ap.rearrange("p (h z k) -> p h z k", h=8, z=2)
                v3k = lambda ap: ap.rearrange("p (h k) -> p h k", h=8)
                c4 = lambda ap: ap.rearrange("p (h a b) -> p h a b", h=8, a=16)
                sc_b = [(sc, scr), (view(32, 16 * 128 * 4, F32, "p (c k) -> p c k", c=16), Res("scB"))]
                hb_b = [(hb, hbr), (view(40, D * 2, BF16), Res("hbB"))]
                hT_b = [(hT, hTr), (view(42, 8 * 128 * 2, BF16, "p (c t) -> p c t", c=8), Res("hTB"))]
                pq_b = [(pqT, pqTr), (view(44, 16 * 128 * 2, BF16, "p (c t) -> p c t", c=16), Res("pqTB"))]

                def head(t):
                    hb, hbr = hb_b[t % 2]; hT, hTr = hT_b[t % 2]; pqT, pqTr = pq_b[t % 2]; sc, scr = sc_b[t % 2]
                    h2 = H[:, t, :]; h2r = H_res[t]
                    k.op("act", lambda e: e.activation(out=hb, in_=h2, func=AF.Copy), reads=[h2r], writes=[hbr])
                    to_fm(hb, hbr, hT, [hTr], 0)
                    for g in range(4):
                        pq, pqr = PSF()
                        for cc in range(4):
                            c = g * 4 + cc
                            for kc in range(8):
                                mm(pq[:, cc * 128:(cc + 1) * 128], pwq[:, kc, c * 128:(c + 1) * 128], hT[:, kc, :], kc == 0, kc == 7, [pw_r, hTr], [pqr])
                        k.op("act", lambda e: e.activation(out=pqT[:, g * 4:(g + 1) * 4, :], in_=pq[:].rearrange("p (c t) -> p c t", c=4), func=AF.Copy), reads=[pqr], writes=[pqTr])
                    for g in range(4):
                        ps_, psr_ = PSF()
                        for cc in range(4):
                            c = g * 4 + cc
                            mm(ps_[:, cc * 128:(cc + 1) * 128], pqT[:, c, :], skT[:, c % 2, :], True, True, [pqTr, pw_r], [psr_])
                        k.op("act", lambda e: e.activation(out=sc[:, g * 4:(g + 1) * 4, :], in_=ps_[:].rearrange("p (c k) -> p c k", c=4), func=AF.Copy), reads=[psr_], writes=[scr])

                def tail(t):
                    sc, scr = sc_b[t % 2]
                    def lvl1_ops(c, buf, bufr):
                        lo = slice(c * 16, c * 16 + 8); hi = slice(c * 16 + 8, c * 16 + 16)
                        tr_, ir_ = ts_c[c], ti_c[c]
                        return [
                            lambda: k.op("dve", lambda e: e.max(out=top_s[:, lo], in_=sc[:, c, :]), reads=[scr], writes=[tr_]),
                            lambda: k.op("dve", lambda e: e.max_index(out=top_i[:, lo], in_max=top_s[:, lo], in_values=sc[:, c, :]), reads=[scr, tr_], writes=[ir_]),
                            lambda: k.op("dve", lambda e: e.match_replace(out=buf[:, 0:128], in_to_replace=top_s[:, lo], in_values=sc[:, c, :], imm_value=-1e30), reads=[scr, tr_], writes=[bufr]),
                            lambda: k.op("dve", lambda e: e.max(out=top_s[:, hi], in_=buf[:, 0:128]), reads=[bufr], writes=[tr_]),
                            lambda: k.op("dve", lambda e: e.max_index(out=top_i[:, hi], in_max=top_s[:, hi], in_values=buf[:, 0:128]), reads=[bufr, tr_], writes=[ir_]),
                        ]
                    for c in range(0, 16, 2):
                        oa_ = lvl1_ops(c, sc2, sc2r); ob_ = lvl1_ops(c + 1, sc2b, sc2br)
                        for fa_, fb_ in zip(oa_, ob_):
                            fa_(); fb_()
                    k.op("dve", lambda e: e.tensor_copy(top_f, top_i), reads=ti_c, writes=[tfr])
                    k.op("dve", lambda e: e.tensor_tensor(out=c4(cand), in0=v4(top_s)[:, :, 0, :].unsqueeze(3).to_broadcast([128, 8, 16, 16]),
                                                          in1=v4(top_s)[:, :, 1, :].unsqueeze(2).to_broadcast([128, 8, 16, 16]), op=ALU.add), reads=ts_c, writes=[cdr])
                    candh = cand.rearrange("p (h c) -> p h c", h=8)
                    def lvl2_ops(h, buf, bufr):
                        lo = slice(h * 16, h * 16 + 8); hi = slice(h * 16 + 8, h * 16 + 16)
                        br_, pr_ = bs_h[h], bp_h[h]
                        return [
                            lambda: k.op("dve", lambda e: e.max(out=best_s[:, lo], in_=candh[:, h, :]), reads=[cdr], writes=[br_]),
                            lambda: k.op("dve", lambda e: e.max_index(out=best_p[:, lo], in_max=best_s[:, lo], in_values=candh[:, h, :]), reads=[cdr, br_], writes=[pr_]),
                            lambda: k.op("dve", lambda e: e.match_replace(out=buf, in_to_replace=best_s[:, lo], in_values=candh[:, h, :], imm_value=-1e30), reads=[cdr, br_], writes=[bufr]),
                            lambda: k.op("dve", lambda e: e.max(out=best_s[:, hi], in_=buf), reads=[bufr], writes=[br_]),
                            lambda: k.op("dve", lambda e: e.max_index(out=best_p[:, hi], in_max=best_s[:, hi], in_values=buf), reads=[bufr, br_], writes=[pr_]),
                        ]
                    for h in range(0, 8, 2):
                        oa_ = lvl2_ops(h, sc2, sc2r); ob_ = lvl2_ops(h + 1, sc2b, sc2br)
                        for fa_, fb_ in zip(oa_, ob_):
                            fa_(); fb_()
                    pfu = pf.bitcast(U32)
                    k.op("dve", lambda e: e.tensor_single_scalar(out=pfu, in_=best_p, scalar=4, op=ALU.logical_shift_right), reads=bp_h, writes=[pfr])
                    k.op("dve", lambda e: e.tensor_copy(k1f, pfu), reads=[pfr], writes=[k1r])
                    k.op("dve", lambda e: e.tensor_single_scalar(out=pfu, in_=best_p, scalar=15, op=ALU.bitwise_and), reads=bp_h + [k1r, pfr], writes=[pfr])
                    k.op("dve", lambda e: e.tensor_copy(k2f, pfu), reads=[pfr], writes=[k2r])
                    io4 = iota16[:, :].unsqueeze(1).unsqueeze(1).to_broadcast([128, 8, 16, 16])
                    sr = slot_res[t]
                    for (kf, kr_, z, dst) in ((k1f, k1r, 0, SI1[:, t, :]), (k2f, k2r, 1, SI2[:, t, :])):
                        k.op("dve", lambda e: e.tensor_tensor(out=c4(eq), in0=io4, in1=v3k(kf).unsqueeze(3).to_broadcast([128, 8, 16, 16]), op=ALU.is_equal), reads=[cst, kr_, cdr], writes=[cdr])
                        k.op("dve", lambda e: e.tensor_tensor(out=c4(eq), in0=c4(eq), in1=v4(top_f)[:, :, z, :].unsqueeze(2).to_broadcast([128, 8, 16, 16]), op=ALU.mult), reads=[cdr, tfr], writes=[cdr])
                        k.op("dve", lambda e: e.tensor_reduce(out=dst, in_=eq.rearrange("p (a b) -> p a b", b=16), axis=AX.X, op=ALU.add), reads=[cdr], writes=[sr])
                    gate = SG[:, t, :]
                    k.op("dve", lambda e: e.tensor_tensor(out=v3k(gate), in0=v3k(best_s), in1=v3k(best_s)[:, :, 0:1].to_broadcast([128, 8, 16]), op=ALU.subtract), reads=bs_h, writes=[sr])
                    k.op("act", lambda e: e.activation(out=gate, in_=gate, func=AF.Exp), reads=[sr], writes=[sr])
                    k.op("dve", lambda e: e.tensor_reduce(out=gsum[:, 0:8], in_=v3k(gate), axis=AX.X, op=ALU.add), reads=[sr], writes=[gsr])
                    k.op("dve", lambda e: e.reciprocal(out=gsum[:, 8:16], in_=gsum[:, 0:8]), reads=[gsr], writes=[gsr])
                    k.op("dve", lambda e: e.tensor_tensor(out=v3k(gate), in0=v3k(gate), in1=gsum[:, 8:16].unsqueeze(2).to_broadcast([128, 8, 16]), op=ALU.mult), reads=[sr, gsr], writes=[sr])
                head(0)
                for t in range(NT):
                    if t + 1 < NT:
                        head(t + 1)
                    tail(t)
                barrier()
            dbg("si1", SI1[:, 0, :], [slot_res[0]], [128, 128]); dbg("sg", SG[:, 0, :], [slot_res[0]], [128, 128])
            if stop == "I":
                finish()
                return nc, dbg_outs

            TBK = 256
            with ExitStack() as ph:
                cv = Carver(128, 160)
                NU = 3
                utiles = [(cv.alloc([128, 8, 128], BF16), Res("ut")) for _ in range(NU)]
                vtiles = [(cv.alloc([128, D], BF16), Res("vt")) for _ in range(NU)]
                hTb = cv.alloc([128, 8, TBK], BF16); hTb_res = [Res("hTb0"), Res("hTb1")]
                accs = [(cv.alloc([128, D], F32), Res("acc")) for _ in range(2)]
                hbJ = (cv.alloc([128, D], BF16), Res("hbJ"))
                g3, g3r = load_bc(ph, lng["ln3_g"], D, "g3"); b3, b3r = load_bc(ph, lng["ln3_b"], D, "b3")
                small = PRing(ph, 4, [128, 16], F32, "lnsm")
                iota128 = phase_sb(ph, [128, 128], F32, "iota128"); ior = Res("iota128")
                k.op("pool", lambda e: e.iota(iota128[:], pattern=[[1, 128]], base=0, channel_multiplier=0, allow_small_or_imprecise_dtypes=True), writes=[ior])
                slTs = [(phase_sb(ph, [128, 3, TBK], BF16, "slT"), Res("slT")) for _ in range(2)]
                oh1s = PRing(ph, 8, [128, 128], BF16, "oh1"); oh2s = PRing(ph, 8, [128, 64], BF16, "oh2")

                class VRing:
                    def __init__(self, n, shape, dt, name):
                        self.items = [(cv.alloc(shape, dt), Res(name)) for _ in range(n)]
                        self.i = 0

                    def get(self):
                        it = self.items[self.i % len(self.items)]; self.i += 1
                        return it
                gels = VRing(2, [128, TBK], F32, "gel"); pbs = VRing(2, [128, TBK], BF16, "pb")
                ui = [0]
                NTB = SEQ // TBK
                Gh = [(view(32 * h_, TBK * 64 * 2, BF16, "p (t i) -> p t i", t=TBK), Res("G%d" % h_)) for h_ in range(2)]
                psbf = [(psb[i][0][:].bitcast(F32), psb[i][1]) for i in range(2)]
                gq = [0]

                def prep_block(tb):
                    slT, slTr = slTs[tb % 2]
                    for tt in range(2):
                        pt_, ptr_ = psbf[tt]
                        for a_, arr in enumerate((SI1, SI2, SG)):
                            k.op("pe", lambda e: e.transpose(pt_[:, a_ * 128:(a_ + 1) * 128], arr[:, tb * 2 + tt, :], identf[:]), reads=[slot_res[tb * 2 + tt], cst], writes=[ptr_])
                        k.op("act", lambda e: e.activation(out=slT[:, :, tt * 128:(tt + 1) * 128], in_=pt_[:, 0:384].rearrange("p (a t) -> p a t", a=3), func=AF.Copy), reads=[ptr_], writes=[slTr])

                def g_build_jobs(tb, half):
                    slT, slTr = slTs[tb % 2]
                    Gt, Gtr = Gh[half]
                    jobs = []
                    state = {}
                    pend_pe = []
                    for tk in range(TBK):
                        def job(tk=tk):
                            j = tk % 8
                            if j == 0:
                                state["pg"] = psbf[gq[0] % 2]; gq[0] += 1
                            pg, pgr = state["pg"]
                            o1, o1r = oh1s.get(); o2, o2r = oh2s.get()
                            k.op("dve", lambda e: e.tensor_scalar(o1[:], iota128[:], slT[:, 0, tk:tk + 1], slT[:, 2, tk:tk + 1], ALU.is_equal, ALU.mult), reads=[ior, slTr], writes=[o1r])
                            k.op("dve", lambda e: e.tensor_scalar(o2[:], iota128[:, half * 64:(half + 1) * 64], slT[:, 1, tk:tk + 1], None, ALU.is_equal), reads=[ior, slTr], writes=[o2r])
                            def pe_part():
                                mm(pg[:, j * 64:(j + 1) * 64], o1[:], o2[:], True, True, [o1r, o2r], [pgr])
                                if j == 7:
                                    tq = tk // 8
                                    k.op("act", lambda e: e.activation(out=Gt[:, tq * 8:tq * 8 + 8, :], in_=pg[:].rearrange("p (t i) -> p t i", t=8), func=AF.Copy), reads=[pgr], writes=[Gtr])
                            pend_pe.append(pe_part)
                            while len(pend_pe) > 6:
                                pend_pe.pop(0)()
                        jobs.append(job)

                    def flush():
                        while pend_pe:
                            pend_pe.pop(0)()
                    jobs.append(flush)
                    return jobs

                def issue_act(tb, i2):
                    ut, utr = utiles[ui[0] % NU]; vt, vtr = vtiles[ui[0] % NU]; ui[0] += 1
                    k.dma("sp", lambda e: e.dma_start(out=ut, in_=Ub_d[i2].rearrange("p (k i) -> p k i", k=8)), writes=[utr])
                    k.dma("sp", lambda e: e.dma_start(out=vt, in_=Vb_d[i2]), writes=[vtr])
                    pa, par = psf[4 + i2 % 2]
                    for kc in range(8):
                        mm(pa[:, 0:TBK], ut[:, kc, :], hTb[:, kc, :], kc == 0, kc == 7, [utr] + hTb_res, [par])
                    gel, gelr = gels.get()
                    k.op("act", lambda e: e.activation(out=gel, in_=pa[:, 0:TBK], func=AF.Gelu), reads=[par], writes=[gelr])
                    pb_, pbr_ = pbs.get()
                    Gt, Gtr = Gh[i2 // 64]
                    k.op("dve", lambda e: e.tensor_tensor(out=pb_, in0=gel, in1=Gt[:, :, i2 % 64], op=ALU.mult), reads=[gelr, Gtr], writes=[pbr_])
                    return (pb_, pbr_, vt, vtr)

                def issue_y(i2, st_):
                    pb_, pbr_, vt, vtr = st_
                    for tt in range(2):
                        for half in range(2):
                            py, pyr = psf[tt * 2 + half]
                            mm(py[:], pb_[:, tt * 128:(tt + 1) * 128], vt[:, half * 512:(half + 1) * 512], i2 == 0, i2 == 127, [pbr_, vtr], [pyr])
                prep_block(0)
                for job in g_build_jobs(0, 0):
                    job()
                for tb in range(NTB):
                    t0 = tb * 2
                    for tt in range(2):
                        hb, hbr = hbJ
                        k.op("act", lambda e: e.activation(out=hb, in_=H[:, t0 + tt, :], func=AF.Copy), reads=[H_res[t0 + tt]], writes=[hbr])
                        to_fm(hb, hbr, hTb, [hTb_res[tt]], tt)
                    if tb + 1 < NTB:
                        prep_block(tb + 1)
                    jobs_lo = g_build_jobs(tb, 1)
                    jobs_hi = g_build_jobs(tb + 1, 0) if tb + 1 < NTB else []
                    pend = issue_act(tb, 0)
                    for i2 in range(128):
                        jl = jobs_lo if i2 < 64 else jobs_hi
                        nxt = issue_act(tb, i2 + 1) if i2 + 1 < 128 else None
                        for _ in range(2):
                            if jl:
                                jl.pop(0)()
                        issue_y(i2, pend)
                        pend = nxt
                        for _ in range(2 if i2 % 64 < 62 else 1000):
                            if jl:
                                jl.pop(0)()
                    while jobs_hi:
                        jobs_hi.pop(0)()
                    for tt in range(2):
                        t = t0 + tt
                        acc, accr = accs[tt]
                        for half in range(2):
                            hs = slice(half * 512, (half + 1) * 512)
                            py, pyr = psf[tt * 2 + half]
                            k.op("dve", lambda e: e.scalar_tensor_tensor(out=acc[:, hs], in0=H[:, t, hs], scalar=ALPHA, in1=py[:], op0=ALU.mult, op1=ALU.add), reads=[H_res[t], pyr], writes=[accr])
                        layer_norm(acc, [accr], g3[:], b3[:], [g3r, b3r], acc, [accr], small)
                        k.dma("sp", lambda e: e.dma_start(out=out_d[t * 128:(t + 1) * 128, :], in_=acc), reads=[accr])
                barrier()

        finish()
    return nc, dbg_outs


def _consts():
    c = {}
    c["c_ident"] = np.eye(128, dtype=np.float32)
    ob = np.zeros((128, 128), np.float32); ob[:64, :64] = 1.0; ob[64:, 64:] = 1.0
    c["c_onesbd"] = ob
    s = np.arange(64)
    lt = (s[:, None] < s[None, :]).astype(np.float32)
    le = (s[:, None] <= s[None, :]).astype(np.float32)

    def mk(strict, incl):
        m = np.zeros((128, 512), np.float32)
        bd = np.zeros((128, 128), np.float32); bd[:64, :64] = strict; bd[64:, 64:] = strict
        pl = np.concatenate([incl, incl], axis=0)
        m[:, 0:128] = bd; m[:, 128:192] = pl; m[:, 192:320] = bd; m[:, 320:384] = pl
        bdT = np.zeros((128, 128), np.float32); bdT[:64, :64] = strict.T; bdT[64:, 64:] = strict.T
        m[:, 384:512] = bdT
        return m
    c["c_maskF"] = mk(lt, le)
    c["c_maskB"] = mk(lt.T.copy(), le.T.copy())
    r = np.ones((128, BLK), np.float32); r[:, ::CH] = 0.0
    c["c_rst"] = r
    c["c_iota"] = np.broadcast_to(np.arange(16, dtype=np.float32), (128, 16)).copy()
    return c


def prep_shared(inp):
    f = lambda a: np.ascontiguousarray(np.asarray(a, dtype=np.float32))
    sh = {}
    for n in ("ln_emb_g", "ln_emb_b"):
        sh[n] = f(inp[n]).reshape(1, D)
    for n in ("ln1_g", "ln1_b", "ln2_g", "ln2_b", "ln3_g", "ln3_b", "mem_ln_g", "mem_ln_b"):
        sh[n] = f(inp[n][0]).reshape(1, D)
    sh["w_in"] = f(inp["w_in"][0])
    sh["mu"] = f(inp["rwkv_mu"][0]).reshape(1, RWKV_COLS)
    tr = lambda a: f(np.asarray(a).reshape(-1, 4, 128).transpose(2, 0, 1).reshape(128, -1))
    sh["w0T"] = tr(inp["rwkv_w0"][0]); sh["a0T"] = tr(inp["rwkv_a0"][0])
    sh["w2"] = f(inp["rwkv_w2"][0]).reshape(128, RW); sh["a2"] = f(inp["rwkv_a2"][0]).reshape(128, RW)
    sh["g2"] = f(inp["rwkv_g2"][0])
    cols = [inp["rwkv_k_k"][0], inp["rwkv_k_a"][0], np.asarray(inp["rwkv_r_k"][0]).reshape(-1), inp["rwkv_gn_g"][0], inp["rwkv_gn_b"][0]]
    sh["pp"] = f(np.concatenate([np.asarray(c).reshape(4, 128).T for c in cols], axis=1))
    sh["gln_g"] = f(inp["gmlp_ln_g"][0]).reshape(1, 512); sh["gln_b"] = f(inp["gmlp_ln_b"][0]).reshape(1, 512)
    sh["wsT"] = f(np.asarray(inp["gmlp_w_s"][0]).transpose(2, 0, 1))
    bs = np.repeat(np.asarray(inp["gmlp_b_s"][0]), 64, axis=0)
    sh["bsF"] = f(bs.reshape(4, 128, 128).transpose(1, 0, 2))
    sh["w_branch"] = f(inp["w_branch"][0]).reshape(1024, D)
    sh["w_mix"] = f(inp["w_mix_out"][0])
    sh["wq"] = f(inp["xattn_w_q"][0]); sh["wkv"] = f(inp["xattn_w_kv"][0]); sh["wo"] = f(inp["xattn_w_o"][0])
    sh["pwq"] = f(inp["peer_w_query"][0])
    sh["skT"] = f(np.asarray(inp["peer_sub_keys"][0]).transpose(2, 0, 1))
    sh["puT"] = f(np.asarray(inp["peer_u"][0]).reshape(128, 128, 8, 128).transpose(1, 3, 2, 0).reshape(128, 128, D))
    sh["pvP"] = f(np.asarray(inp["peer_v"][0]).reshape(128, 128, D).transpose(1, 0, 2))
    sh.update(_consts())
    return sh


def make_in_maps(inp, cores):
    sh = prep_shared(inp)
    maps = []
    for b in cores:
        m = dict(sh)
        m["x"] = np.ascontiguousarray(np.asarray(inp["x"][b], dtype=np.float32))
        m["mem"] = np.ascontiguousarray(np.asarray(inp["mem"][b], dtype=np.float32))
        maps.append(m)
    return maps


def kernel(**inputs):
    nc, _ = build_program()
    maps = make_in_maps(inputs, list(range(N_CORES)))
    res = run_bass_kernel_spmd(nc, maps, core_ids=list(range(N_CORES)))
    return np.stack([np.asarray(r["out"], dtype=np.float32) for r in res.results], axis=0)
```

```python
from contextlib import ExitStack

import numpy as np
import concourse.bass as bass
import concourse.mybir as mybir
from concourse.bass_utils import run_bass_kernel_spmd

F32 = mybir.dt.float32
BF16 = mybir.dt.bfloat16
I32 = mybir.dt.int32
U32 = mybir.dt.uint32
AF = mybir.ActivationFunctionType
ALU = mybir.AluOpType
AX = mybir.AxisListType

N_CORES = 8
SEQ = 2048
D = 1024
NT = SEQ // 128


class Res:
    __slots__ = ("name", "w", "r", "excl")

    def __init__(self, name="", excl=False):
        self.name = name
        self.excl = excl
        self.w = None
        self.r = {}


class KB:
    def __init__(self, nc, es):
        self.nc = nc
        self.es = es
        self.eng = {"pe": nc.tensor, "act": nc.scalar, "dve": nc.vector, "pool": nc.gpsimd, "sp": nc.sync}
        self.sem = {}
        self.cnt = {}
        self.seen = {}
        for e in self.eng:
            self.sem[e] = es.enter_context(nc.semaphore("sem_" + e))
            self.cnt[e] = 0
            self.seen[e] = {}
        self.dsem = {}
        for q, n in (("sp", 8), ("pool", 8), ("act", 2)):
            self.dsem[q] = [[es.enter_context(nc.semaphore("dsem_%s%d" % (q, i))), 0] for i in range(n)]
        self.dptr = {q: 0 for q in self.dsem}
        self.n_ins = 0
        self.n_wait = 0
        self._id = 0
        self.defer = None

    def sb(self, shape, dt, name=None):
        self._id += 1
        return self.es.enter_context(self.nc.sbuf_tensor("%s_%d" % (name or "t", self._id), list(shape), dt))

    def ps(self, shape, dt, name=None):
        self._id += 1
        return self.es.enter_context(self.nc.psum_tensor("%s_%d" % (name or "p", self._id), list(shape), dt))

    def _deps(self, reads, writes):
        deps = []
        for r in reads:
            if r.w is not None:
                deps.append(r.w)
        for w in writes:
            if w.w is not None:
                deps.append(w.w)
            deps.extend(w.r.values())
        return deps

    def _wait(self, e, deps):
        eng = self.eng[e]
        seen = self.seen[e]
        best = {}
        for (sem, val) in deps:
            k = id(sem)
            if seen.get(k, 0) >= val:
                continue
            if k not in best or best[k][1] < val:
                best[k] = (sem, val)
        for k, (sem, val) in best.items():
            if e == "pe" and sem is self.sem["pe"]:
                continue
            eng.wait_ge(sem, val)
            self.n_wait += 1
            seen[k] = val

    def _mark(self, tok, reads, writes):
        for r in reads:
            k = id(tok[0])
            r.r[k] = tok
        for w in writes:
            w.w = tok
            w.r = {}

    def mark(self):
        if self.defer is not None:
            self.defer.append(None)

    def op(self, e, fn, reads=(), writes=()):
        if self.defer is not None:
            self.defer.append((e, fn, list(reads), list(writes)))
            return None
        ex = [r for r in reads if r.excl and r not in writes]
        if ex:
            writes = list(writes) + ex
        self._wait(e, self._deps(reads, writes))
        ins = fn(self.eng[e])
        self.cnt[e] += 1
        ins.then_inc(self.sem[e], 1)
        tok = (self.sem[e], self.cnt[e])
        self._mark(tok, reads, writes)
        self.n_ins += 1
        return tok

    def dma(self, q, fn, reads=(), writes=()):
        slots = self.dsem[q]
        slot = slots[self.dptr[q] % len(slots)]
        self.dptr[q] += 1
        deps = self._deps(reads, writes)
        if slot[1] > 0:
            deps.append((slot[0], slot[1]))
        self._wait(q, deps)
        ins = fn(self.eng[q])
        slot[1] += 16
        ins.then_inc(slot[0], 16)
        tok = (slot[0], slot[1])
        self._mark(tok, reads, writes)
        self.n_ins += 1
        return tok

    def wait_all(self, e, ress):
        deps = []
        for r in ress:
            if r.w is not None:
                deps.append(r.w)
            deps.extend(r.r.values())
        self._wait(e, deps)


class Ring:
    def __init__(self, k, n, shape, dt, name="ring"):
        self.items = [(k.sb(shape, dt, name), Res(name)) for _ in range(n)]
        self.i = 0

    def get(self):
        it = self.items[self.i % len(self.items)]
        self.i += 1
        return it


RW = 512
NHP = 4
RWKV_COLS = 1920
GM0 = 1920
GT0 = 2944
C0 = float(np.exp(-0.5))
ALPHA = float(2.0 ** 0.25)
LN_EPS = 1e-5
GN_EPS = 64e-5
CH = 64
BLK = 128
NCH = SEQ // CH
NBLK = SEQ // BLK


def build_program(debug=(), stop=None, scan_steps=None, scan_sub=99):
    nc = bass.Bass("TRN2", target_bir_lowering=False)
    dbg_outs = {}

    def din(name, shape, dt=F32):
        return nc.dram_tensor(name, list(shape), dt, kind="ExternalInput").ap()

    x = din("x", [SEQ, D]); mem = din("mem", [256, D])
    lng = {n: din(n, [1, D]) for n in ("ln_emb_g", "ln_emb_b", "ln1_g", "ln1_b", "ln2_g", "ln2_b", "ln3_g", "ln3_b", "mem_ln_g", "mem_ln_b")}
    w_in = din("w_in", [D, 4992]); mu_d = din("mu", [1, RWKV_COLS])
    w0T_d = din("w0T", [128, 8]); a0T_d = din("a0T", [128, 8])
    w2_d = din("w2", [128, RW]); a2_d = din("a2", [128, RW]); g2_d = din("g2", [128, RW])
    pp_d = din("pp", [128, 20])
    gln_g_d = din("gln_g", [1, 512]); gln_b_d = din("gln_b", [1, 512])
    wsT_d = din("wsT", [128, 8, 128]); bsF_d = din("bsF", [128, 4, 128])
    wbr_d = din("w_branch", [1024, D]); wmix_d = din("w_mix", [D, D])
    wq_d = din("wq", [D, D]); wkv_d = din("wkv", [D, 2 * D]); wo_d = din("wo", [D, D])
    pwq_d = din("pwq", [D, 2048]); skT_d = din("skT", [128, 2, 128])
    pu_d = din("puT", [128, 128, D]); pv_d = din("pvP", [128, 128, D])
    Ub_d = nc.dram_tensor("Ub", [128, 128, D], BF16, kind="Internal").ap()
    Vb_d = nc.dram_tensor("Vb", [128, 128, D], BF16, kind="Internal").ap()
    ident_d = din("c_ident", [128, 128]); onesbd_d = din("c_onesbd", [128, 128])
    maskF_d = din("c_maskF", [128, 512]); maskB_d = din("c_maskB", [128, 512]); rst_d = din("c_rst", [128, BLK])
    iota_d = din("c_iota", [128, 16])
    out_d = nc.dram_tensor("out", [SEQ, D], F32, kind="ExternalOutput").ap()

    es = ExitStack()
    with es:
        k = KB(nc, es)
        RAW = k.sb([128, 40960], F32, "raw")

        def view(off_kb, nbytes, dt, pattern=None, **kw):
            w0 = int(off_kb * 256)
            v = RAW[:, w0:w0 + nbytes // 4]
            if dt != F32:
                v = v.bitcast(dt)
            if pattern:
                v = v.rearrange(pattern, **kw)
            return v

        def barrier():
            toks = [(k.sem[e], k.cnt[e]) for e in k.eng if k.cnt[e] > 0]
            for q in k.dsem:
                for s in k.dsem[q]:
                    if s[1] > 0:
                        toks.append((s[0], s[1]))
            for e in k.eng:
                k._wait(e, toks)

        dbg_res = []

        def dbg(name, ap, res, shape):
            if name not in debug:
                return
            o = nc.dram_tensor("dbg_" + name, list(shape), F32, kind="ExternalOutput").ap()
            dbg_outs[name] = o
            barrier()
            if ap.dtype != F32:
                tmp = k.sb(list(shape), F32, "dbgtmp"); tr = Res()
                k.op("dve", lambda e: e.tensor_copy(tmp[:], ap), reads=res, writes=[tr])
                k.dma("sp", lambda e: e.dma_start(out=o, in_=tmp[:]), reads=[tr])
                dbg_res.append(tr)
            else:
                rr = Res()
                k.dma("sp", lambda e: e.dma_start(out=o, in_=ap), reads=res, writes=[rr])
                dbg_res.append(rr)

        def finish():
            barrier()

        cst = Res("consts")
        ident = k.sb([128, 128], BF16, "ident"); identf = k.sb([128, 128], F32, "identf")
        onesbd = k.sb([128, 128], BF16, "onesbd"); ones64 = k.sb([128, 128], F32, "ones64")
        maskF = k.sb([128, 512], BF16, "maskF"); maskB = k.sb([128, 512], BF16, "maskB")
        rst = k.sb([128, BLK], F32, "rst"); iota16 = k.sb([128, 16], F32, "iota16")
        w0T = k.sb([128, 8], F32, "w0T"); a0T = k.sb([128, 8], F32, "a0T"); pp = k.sb([128, 20], F32, "pp")
        ppx = k.sb([128, 8], F32, "ppx")
        w2b = k.sb([128, RW], BF16, "w2b"); a2b = k.sb([128, RW], BF16, "a2b"); g2b = k.sb([128, RW], BF16, "g2b")
        epsln = k.sb([128, 1], F32, "epsln"); epsgn = k.sb([128, 1], F32, "epsgn")
        for (t, d_) in ((ident, ident_d), (onesbd, onesbd_d), (maskF, maskF_d), (maskB, maskB_d), (w2b, w2_d), (a2b, a2_d), (g2b, g2_d)):
            k.dma("pool", lambda e: e.dma_start(out=t[:], in_=d_), writes=[cst])
        for (t, d_) in ((identf, ident_d), (rst, rst_d), (iota16, iota_d), (w0T, w0T_d), (a0T, a0T_d), (pp, pp_d)):
            k.dma("sp", lambda e: e.dma_start(out=t[:], in_=d_), writes=[cst])
        nw0T = k.sb([128, 8], F32, "nw0T"); na0T = k.sb([128, 8], F32, "na0T"); one1 = k.sb([128, 1], F32, "one1")
        k.op("dve", lambda e: e.memset(one1[:], 1.0), writes=[cst])
        k.op("dve", lambda e: e.memset(epsln[:], LN_EPS), writes=[cst])
        k.op("dve", lambda e: e.memset(epsgn[:], GN_EPS), writes=[cst])
        k.op("dve", lambda e: e.tensor_scalar(ones64[:], identf[:], 0.0, 0.0, ALU.mult, ALU.add), reads=[cst], writes=[cst])
        k.op("dve", lambda e: e.tensor_scalar(ones64[:], onesbd[:], 1.0 / 64.0, None, ALU.mult), reads=[cst], writes=[cst])
        k.op("dve", lambda e: e.tensor_scalar(nw0T[:], w0T[:], -1.0, None, ALU.mult), reads=[cst], writes=[cst])
        k.op("dve", lambda e: e.tensor_scalar(na0T[:], a0T[:], -1.0, None, ALU.mult), reads=[cst], writes=[cst])
        k.op("dve", lambda e: e.tensor_scalar(ppx[:, 0:4], pp[:, 4:8], -1.0, 1.0, ALU.mult, ALU.add), reads=[cst], writes=[cst])

        def act_sigmoid(eng_op, out, in_, nbias, reads, writes):
            eng_op("act", lambda e: e.activation(out=out, in_=in_, func=AF.Exp, bias=nbias, scale=-1.0), reads=reads + [cst], writes=writes)
            eng_op("act", lambda e: e.activation(out=out, in_=out, func=AF.Ln, bias=one1[:, 0:1], scale=1.0), reads=writes + [cst], writes=writes)
            eng_op("act", lambda e: e.activation(out=out, in_=out, func=AF.Exp, scale=-1.0), reads=writes, writes=writes)
        k.op("dve", lambda e: e.tensor_scalar(ppx[:, 4:8], pp[:, 4:8], -2.0, 2.0, ALU.mult, ALU.add), reads=[cst], writes=[cst])
        KK = lambda hp: pp[:, hp:hp + 1]
        KA = lambda hp: pp[:, 4 + hp:5 + hp]
        RK = lambda hp: pp[:, 8 + hp:9 + hp]
        GNG = lambda hp: pp[:, 12 + hp:13 + hp]
        GNB = lambda hp: pp[:, 16 + hp:17 + hp]

        psf = [(k.ps([128, 512], F32, "psf"), Res("psf%d" % i, True)) for i in range(6)]
        psb = [(k.ps([128, 1024], BF16, "psb"), Res("psb%d" % i, True)) for i in range(2)]
        pctr = {"f": 0, "b": 0}

        def PSF():
            it = psf[pctr["f"] % 6]; pctr["f"] += 1
            return it

        def PSB():
            it = psb[pctr["b"] % 2]; pctr["b"] += 1
            return it

        def mm(out, lhsT, rhs, start, stop, reads, writes):
            k.op("pe", lambda e: e.matmul(out, lhsT, rhs, start=start, stop=stop), reads=reads, writes=writes)

        def load_bc(ph, d_ap, n, name):
            t = ph.enter_context(nc.sbuf_tensor(name + "_%d" % k._id, [128, n], F32)); k._id += 1
            r = Res(name)
            k.dma("sp", lambda e: e.dma_start(out=t[:], in_=d_ap.partition_broadcast(128)), writes=[r])
            return t, r

        def phase_sb(ph, shape, dt, name):
            k._id += 1
            return ph.enter_context(nc.sbuf_tensor("%s_%d" % (name, k._id), list(shape), dt))

        class PRing:
            def __init__(self, ph, n, shape, dt, name):
                self.items = [(phase_sb(ph, shape, dt, name), Res(name)) for _ in range(n)]
                self.i = 0

            def get(self):
                it = self.items[self.i % len(self.items)]; self.i += 1
                return it

        def layer_norm(src, src_res, gam, bet, gbres, out, out_res, small, n=1024, eps=None):
            eps = eps if eps is not None else epsln
            nchk = n // 512
            st, sr = small.get()
            for c in range(nchk):
                k.op("dve", lambda e, c=c: e.bn_stats(out=st[:, c * 6:(c + 1) * 6], in_=src[:, c * 512:(c + 1) * 512]), reads=src_res, writes=[sr])
            k.op("dve", lambda e: e.bn_aggr(out=st[:, 12:14], in_=st[:, 0:6 * nchk]), reads=[sr], writes=[sr])
            k.op("act", lambda e: e.activation(out=st[:, 14:15], in_=st[:, 13:14], func=AF.Sqrt, bias=eps[:, 0:1], scale=1.0), reads=[sr, cst], writes=[sr])
            k.op("dve", lambda e: e.reciprocal(out=st[:, 14:15], in_=st[:, 14:15]), reads=[sr], writes=[sr])
            k.op("dve", lambda e: e.tensor_scalar(st[:, 15:16], st[:, 12:13], st[:, 14:15], -1.0, ALU.mult, ALU.mult), reads=[sr], writes=[sr])
            k.op("act", lambda e: e.activation(out=out, in_=src, func=AF.Identity, bias=st[:, 15:16], scale=st[:, 14:15]), reads=src_res + [sr], writes=out_res)
            k.op("dve", lambda e: e.tensor_tensor(out=out, in0=out, in1=gam, op=ALU.mult), reads=out_res + gbres, writes=out_res)
            k.op("dve", lambda e: e.tensor_tensor(out=out, in0=out, in1=bet, op=ALU.add), reads=out_res + gbres, writes=out_res)

        def to_fm(hb, hbr, dstT, dst_res, t):
            pt, ptr_ = PSB()
            for c in range(8):
                k.op("pe", lambda e, c=c: e.transpose(pt[:, c * 128:(c + 1) * 128], hb[:, c * 128:(c + 1) * 128], ident[:]), reads=[hbr, cst], writes=[ptr_])
            k.op("act", lambda e: e.activation(out=dstT[:, :, t * 128:(t + 1) * 128], in_=pt[:].rearrange("p (c t) -> p c t", t=128), func=AF.Copy), reads=[ptr_], writes=dst_res)

        def run_pairs(n, body):
            for t_ in range(0, n, 2):
                recs = []
                for tt_ in (t_, t_ + 1):
                    if tt_ >= n:
                        continue
                    k.defer = []
                    body(tt_)
                    recs.append([x_ for x_ in k.defer if x_ is not None]); k.defer = None
                for i_ in range(max(len(r_) for r_ in recs)):
                    for r_ in recs:
                        if i_ < len(r_):
                            k.op(*r_[i_])

        def phase_h0T(h0T, h0T_res):
            with ExitStack() as ph:
                g_t, g_r = load_bc(ph, lng["ln_emb_g"], D, "g")
                b_t, b_r = load_bc(ph, lng["ln_emb_b"], D, "b")
                xs = PRing(ph, 2, [128, D], F32, "xs")
                hbs = PRing(ph, 2, [128, D], BF16, "hb")
                small = PRing(ph, 4, [128, 16], F32, "lnsm")
                def tile_body(t):
                    xt, xr = xs.get()
                    k.dma("sp", lambda e: e.dma_start(out=xt[:], in_=x[t * 128:(t + 1) * 128, :]), writes=[xr])
                    layer_norm(xt[:], [xr], g_t[:], b_t[:], [g_r, b_r], xt[:], [xr], small)
                    hb, hbr = hbs.get()
                    k.op("act", lambda e: e.activation(out=hb[:], in_=xt[:], func=AF.Copy), reads=[xr], writes=[hbr])
                    to_fm(hb, hbr, h0T, [h0T_res[t]], t)
                run_pairs(NT, tile_body)
                barrier()

        h0T = view(64, 32768, BF16, "p (c t) -> p c t", c=8)
        h0T_res = [Res("h0T%d" % t) for t in range(NT)]
        phase_h0T(h0T, h0T_res)
        dbg("h0T", h0T[:, 0, :], h0T_res, [128, SEQ])
        if stop == "A":
            finish()
            return nc, dbg_outs

        shT = view(96, 32768, BF16, "p (c t) -> p c t", c=8); shr = Res("shT")
        rkv = view(0, 12 * SEQ * 2, BF16, "p (c t) -> p c t", c=12)
        wag = view(48, 3 * SEQ * 2, BF16, "p (c t) -> p c t", c=3)
        rkv_res = [[Res("rkv") for _ in range(4)] for _ in range(15)]
        for c in range(8):
            k.op("dve", lambda e: e.tensor_tensor(out=shT[:, c, 1:SEQ - 1], in0=h0T[:, c, 0:SEQ - 2], in1=h0T[:, c, 2:SEQ], op=ALU.add), reads=h0T_res, writes=[shr])
        k.op("dve", lambda e: e.tensor_copy(shT[:, :, 0:1], h0T[:, :, 1:2]), reads=h0T_res, writes=[shr])
        k.op("dve", lambda e: e.tensor_copy(shT[:, :, SEQ - 1:SEQ], h0T[:, :, SEQ - 2:SEQ - 1]), reads=h0T_res, writes=[shr])
        with ExitStack() as ph:
            wsts = PRing(ph, 2, [128, 8, 128], F32, "wst")
            was = PRing(ph, 2, [128, 8, 128], BF16, "wa")
            wbs = PRing(ph, 2, [128, 8, 128], BF16, "wb")
            mus = PRing(ph, 2, [128, 384], F32, "mu")
            for cc in range(15):
                c0 = cc * 128
                wst, wsr = wsts.get()
                k.dma("sp", lambda e: e.dma_start(out=wst[:], in_=w_in[:, c0:c0 + 128].rearrange("(k p) c -> p k c", p=128)), writes=[wsr])
                mt, mr = mus.get()
                k.dma("sp", lambda e: e.dma_start(out=mt[:, 0:128], in_=mu_d[:, c0:c0 + 128].partition_broadcast(128)), writes=[mr])
                k.op("dve", lambda e: e.tensor_scalar(mt[:, 128:256], mt[:, 0:128], -1.0, 1.0, ALU.mult, ALU.add), reads=[mr], writes=[mr])
                k.op("dve", lambda e: e.tensor_scalar(mt[:, 256:384], mt[:, 0:128], 0.5, None, ALU.mult), reads=[mr], writes=[mr])
                wa, war = was.get(); wb, wbr = wbs.get()
                k.op("dve", lambda e: e.tensor_tensor(out=wa[:], in0=wst[:], in1=mt[:, 128:256].unsqueeze(1).to_broadcast([128, 8, 128]), op=ALU.mult), reads=[wsr, mr], writes=[war])
                k.op("pool", lambda e: e.tensor_tensor(out=wb[:], in0=wst[:], in1=mt[:, 256:384].unsqueeze(1).to_broadcast([128, 8, 128]), op=ALU.mult), reads=[wsr, mr], writes=[wbr])
                for tb in range(4):
                    pb, pbr = PSF()
                    ts = slice(tb * 512, (tb + 1) * 512)
                    for kc in range(8):
                        mm(pb[:], wa[:, kc, :], h0T[:, kc, ts], kc == 0, False, [war] + h0T_res[4 * tb:4 * tb + 4], [pbr])
                    for kc in range(8):
                        mm(pb[:], wb[:, kc, :], shT[:, kc, ts], False, kc == 7, [wbr, shr], [pbr])
                    if cc < 12:
                        dest, fn = rkv[:, cc, ts], AF.Copy
                    else:
                        dest, fn = wag[:, cc - 12, ts], (AF.Tanh, AF.Copy, AF.Sigmoid)[cc - 12]
                    k.op("act", lambda e: e.activation(out=dest, in_=pb[:], func=fn), reads=[pbr], writes=[rkv_res[cc][tb]])
            barrier()
        dbg("r0", rkv[:, 0, :], rkv_res[0], [128, SEQ])
        dbg("k0", rkv[:, 4, :], rkv_res[4], [128, SEQ])
        dbg("wd", wag[:, 0, :], rkv_res[12], [128, SEQ])
        if stop == "B":
            finish()
            return nc, dbg_outs


        barrier()
        yaT = rkv[:, 0:4, :]
        yaT_res = [rkv_res[hp] for hp in range(4)]

        class Carver:
            def __init__(self, a_kb, b_kb):
                self.p = a_kb * 1024; self.end = b_kb * 1024

            def alloc(self, shape, dt):
                nb = int(np.prod(shape[1:])) * (2 if dt == BF16 else 4)
                nb = (nb + 31) // 32 * 32
                assert self.p + nb <= self.end, "carver overflow"
                v = RAW[:, self.p // 4:(self.p + nb) // 4]
                self.p += nb
                if dt != F32:
                    v = v.bitcast(dt)
                n = int(np.prod(shape[1:]))
                v = v[:, 0:n]
                if len(shape) == 3:
                    v = v.rearrange("p (a b) -> p a b", a=shape[1])
                return v

        def scan_round(rnd):
            hps = [2 * rnd, 2 * rnd + 1]
            carve = Carver(64, 128)
            carve2 = Carver(144, 160)
            yacc = view(128, 2 * SEQ * 4, F32, "p (c t) -> p c t", c=2)
            yacc_res = [[Res("yacc") for _ in range(NCH)] for _ in range(2)]
            if scan_steps:
                for hl_ in range(2):
                    k.op("pool", lambda e: e.memset(yacc[:, hl_, :], 0.0), writes=yacc_res[hl_])
                    for r_ in yacc_res[hl_]:
                        r_.w = None
            with ExitStack() as ph:
                def T(shape, dt, name):
                    return (phase_sb(ph, shape, dt, name), Res(name))
                tmp = {n: T([128, BLK], F32, n) for n in ("sw", "sa", "cs", "pin", "pex", "ege", "egi", "kr", "rn", "kk", "nkk", "t1", "kd", "bb")}
                ksq = T([128, BLK], BF16, "ksq")
                yb = T([128, BLK], BF16, "yb"); ysqb = T([128, BLK], BF16, "ysqb")
                NBK = NCH // 2
                streams = []
                for z in (0, 1):
                    for hl, hp in enumerate(hps):
                        st = dict(z=z, hp=hp, hl=hl, ui=len(streams))
                        st["A3"] = [dict(AR=carve.alloc([128, 2, 192], BF16), eg=carve.alloc([128, BLK], F32), res=Res("prepA")) for _ in range(3)]
                        st["K2"] = [dict(Kb=carve.alloc([128, 2, 128], BF16), Bb=carve.alloc([128, 2, 128], BF16), Vb=carve.alloc([128, 2, 128], BF16), res=Res("prepK")) for _ in range(2)]
                        for sl in st["A3"]:
                            k.op("pool", lambda e: e.memset(sl["AR"], 0.0), writes=[sl["res"]])
                        for sl in st["K2"]:
                            for nm in ("Kb", "Bb", "Vb"):
                                k.op("pool", lambda e: e.memset(sl[nm], 0.0), writes=[sl["res"]])
                        st["Tm"] = [[(carve2.alloc([128, 512], BF16), Res("Tm")) for _ in range(2)] for _ in range(2)]
                        st["WT"] = T([128, 128], BF16, "WT"); st["UT"] = T([128, 128], BF16, "UT")
                        st["S"] = [T([128, 128], BF16, "S") for _ in range(2)]
                        k.op("pool", lambda e: e.memset(st["S"][0][0][:], 0.0), writes=[st["S"][0][1]])
                        st["si"] = 0
                        streams.append(st)
                KTG = [[(carve.alloc([128, 4, 384], BF16), Res("KTG")) for _ in range(2)] for _ in range(2)]
                MVG = [[(carve.alloc([128, 4, 128], BF16), Res("MVG")) for _ in range(2)] for _ in range(2)]
                IVG = [[(carve.alloc([128, 3, 512], BF16), Res("IVG")) for _ in range(2)] for _ in range(2)]
                chain_res = [Res("chain%d" % i, True) for i in range(2)]
                psbf = [(psb[i][0][:].bitcast(F32), psb[i][1]) for i in range(2)]
                lvl_banks = [[psf[0], psf[1], psf[2]], [psf[3], psbf[0], psbf[1]]]

                def blk_of(st, bi):
                    return bi if st["z"] == 0 else NBK - 1 - bi

                def chunk_of(st, bi, g):
                    b = blk_of(st, bi)
                    return 2 * b + g if st["z"] == 0 else 2 * b + 1 - g

                def prep(st, bi, tmp, ksq, pcol):
                    z, hp = st["z"], st["hp"]
                    b = blk_of(st, bi)
                    sa_ = st["A3"][bi % 3]; sk_ = st["K2"][bi % 2]
                    t0 = b * BLK; ts = slice(t0, t0 + BLK); tb = t0 // 512
                    zs = slice(z * 64, (z + 1) * 64)
                    hc = slice(hp * 128, (hp + 1) * 128)
                    r_ap, k_ap, v_ap = rkv[:, hp, ts], rkv[:, 4 + hp, ts], rkv[:, 8 + hp, ts]
                    rr, kr_, vr_ = [rkv_res[hp][tb]], [rkv_res[4 + hp][tb]], [rkv_res[8 + hp][tb]]
                    pb, pbr = psbf[0][0][:, pcol:pcol + 256], psbf[0][1]
                    mm(pb[:, 0:BLK], w2b[zs, hc], wag[zs, 0, ts], True, True, [cst, rkv_res[12][tb]], [pbr])
                    mm(pb[:, BLK:2 * BLK], a2b[zs, hc], wag[zs, 1, ts], True, True, [cst, rkv_res[13][tb]], [pbr])
                    (sw, swr), (sa, sar), (cs, csr) = tmp["sw"], tmp["sa"], tmp["cs"]
                    (pin, pinr), (pex, pexr), (ege, eger), (egi, egir) = tmp["pin"], tmp["pex"], tmp["ege"], tmp["egi"]
                    act_sigmoid(k.op, sw[:], pb[:, 0:BLK], nw0T[:, z * 4 + hp:z * 4 + hp + 1], [pbr], [swr])
                    act_sigmoid(k.op, sa[:], pb[:, BLK:2 * BLK], na0T[:, z * 4 + hp:z * 4 + hp + 1], [pbr], [sar])
                    k.mark()
                    k.op("dve", lambda e: e.tensor_tensor_scan(out=cs[:], data0=rst[:], data1=sw[:], initial=0.0, op0=ALU.mult, op1=ALU.add), reads=[swr, cst], writes=[csr])
                    v3 = lambda t_: t_.rearrange("p (c t) -> p c t", t=CH)
                    if z == 0:
                        k.op("dve", lambda e: e.tensor_tensor(out=pex[:], in0=cs[:], in1=sw[:], op=ALU.subtract), reads=[csr, swr], writes=[pexr])
                        pin_t, pin_r = cs, csr
                    else:
                        k.op("dve", lambda e: e.tensor_tensor(out=v3(pex[:]), in0=v3(cs[:])[:, :, CH - 1:CH].to_broadcast([128, 2, CH]), in1=v3(cs[:]), op=ALU.subtract), reads=[csr], writes=[pexr])
                        k.op("dve", lambda e: e.tensor_tensor(out=pin[:], in0=pex[:], in1=sw[:], op=ALU.add), reads=[pexr, swr], writes=[pinr])
                        pin_t, pin_r = pin, pinr
                    eg = sa_["eg"]; rA = sa_["res"]; rK = sk_["res"]
                    k.op("act", lambda e: e.activation(out=eg, in_=pin_t[:], func=AF.Exp, scale=-C0), reads=[pin_r], writes=[rA])
                    k.op("act", lambda e: e.activation(out=ege[:], in_=pex[:], func=AF.Exp, scale=-C0), reads=[pexr], writes=[eger])
                    k.op("act", lambda e: e.activation(out=egi[:], in_=pin_t[:], func=AF.Exp, scale=C0), reads=[pin_r], writes=[egir])
                    k.mark()
                    (kr, krr), (rn, rnr), (kk, kkr) = tmp["kr"], tmp["rn"], tmp["kk"]
                    k.op("dve", lambda e: e.tensor_scalar(kr[:], k_ap, KK(hp), None, ALU.mult), reads=kr_ + [cst], writes=[krr])
                    k.op("dve", lambda e: e.tensor_tensor(out=ksq[0][:], in0=kr[:], in1=kr[:], op=ALU.mult), reads=[krr], writes=[ksq[1]])
                    pb2, pb2r = psbf[1][0][:, pcol:pcol + 256], psbf[1][1]
                    mm(pb2[:, 0:BLK], onesbd[:], ksq[0][:], True, True, [cst, ksq[1]], [pb2r])
                    k.op("dve", lambda e: e.tensor_scalar(rn[:], pb2[:, 0:BLK], 1e-24, None, ALU.max), reads=[pb2r], writes=[rnr])
                    k.mark()
                    k.op("act", lambda e: e.activation(out=rn[:], in_=rn[:], func=AF.Ln), reads=[rnr], writes=[rnr])
                    k.op("act", lambda e: e.activation(out=rn[:], in_=rn[:], func=AF.Exp, scale=-0.5), reads=[rnr], writes=[rnr])
                    k.op("dve", lambda e: e.tensor_tensor(out=kk[:], in0=kr[:], in1=rn[:], op=ALU.mult), reads=[krr, rnr], writes=[kkr])
                    nkk, nkkr = tmp["nkk"]
                    k.op("dve", lambda e: e.tensor_scalar(nkk[:], kk[:], -1.0, None, ALU.mult), reads=[kkr], writes=[nkkr])
                    AR, Kb, Bb, Vb = sa_["AR"], sk_["Kb"], sk_["Bb"], sk_["Vb"]
                    k.op("dve", lambda e: e.tensor_tensor(out=AR[:, :, 128:192], in0=v3(r_ap), in1=v3(eg), op=ALU.mult), reads=rr + [rA], writes=[rA])
                    (t1, t1r), (kd, kdr), (bb, bbr) = tmp["t1"], tmp["kd"], tmp["bb"]
                    k.op("dve", lambda e: e.tensor_scalar(t1[:], sa[:], KA(hp), ppx[:, hp:hp + 1], ALU.mult, ALU.add), reads=[sar, cst], writes=[t1r])
                    k.op("dve", lambda e: e.tensor_tensor(out=kd[:], in0=t1[:], in1=k_ap, op=ALU.mult), reads=[t1r] + kr_, writes=[kdr])
                    k.op("dve", lambda e: e.tensor_tensor(out=bb[:], in0=kk[:], in1=sa[:], op=ALU.mult), reads=[kkr, sar], writes=[bbr])
                    k.mark()

                    def half_ops(half):
                        hs = slice(half * 64, (half + 1) * 64); cs_ = slice(half * 64, (half + 1) * 64)
                        eng = "dve" if half == 0 else "pool"
                        k.op(eng, lambda e: e.tensor_tensor(out=AR[hs, :, cs_], in0=v3(nkk[hs, :]), in1=v3(ege[hs, :]), op=ALU.mult), reads=[nkkr, eger, rA], writes=[rA])
                        k.op(eng, lambda e: e.tensor_tensor(out=Kb[hs, :, cs_], in0=v3(kd[hs, :]), in1=v3(egi[hs, :]), op=ALU.mult), reads=[kdr, egir, rK], writes=[rK])
                        k.op(eng, lambda e: e.tensor_tensor(out=Bb[hs, :, cs_], in0=v3(bb[hs, :]), in1=v3(egi[hs, :]), op=ALU.mult), reads=[bbr, egir, rK], writes=[rK])
                        k.op("act", lambda e: e.activation(out=Vb[hs, :, cs_], in_=v3(v_ap)[hs], func=AF.Copy), reads=vr_ + [rK], writes=[rK])
                    half_ops(0)
                    half_ops(1)
                    k.mark()

                def unit_ctx(st, bi, g):
                    bp = bi % 2
                    sa_ = st["A3"][bi % 3]; sk_ = st["K2"][bi % 2]
                    c = chunk_of(st, bi, g); ci = c % 2
                    return dict(st=st, ui=st["ui"], c=c, ci=ci, bp=bp, g=g, rA=sa_["res"], rK=sk_["res"], eg=sa_["eg"],
                                AR=sa_["AR"][:, ci, :], Kb=sk_["Kb"][:, ci, :], Bb=sk_["Bb"][:, ci, :], Vb=sk_["Vb"][:, ci, :],
                                Tm=st["Tm"][bp][g], KT=(KTG[bp][g][0][:, st["ui"], :], KTG[bp][g][1]), Minv=(MVG[bp][g][0][:, st["ui"], :], MVG[bp][g][1]))

                def pre_block(bi):
                    groups = [[unit_ctx(st, bi, g) for st in streams] for g in range(2)]
                    macro = []

                    def t_pe(g):
                        def f():
                            for u in groups[g]:
                                ui = u["ui"]
                                pb, pbr = psf[ui]; rd = [u["rA"], u["rK"]]
                                mm(pb[:, 0:192], u["Kb"], u["AR"], True, True, rd, [pbr])
                                mm(pb[:, 192:384], u["Bb"], u["AR"], True, True, rd, [pbr])
                                mm(pb[:, 384:512], u["AR"][:, 0:128], u["Bb"], True, True, rd, [pbr])
                                pt_, ptr_ = psb[ui // 2]; pt = pt_[:, (ui % 2) * 384:(ui % 2) * 384 + 384]
                                for i, nm in enumerate(("Kb", "Bb", "Vb")):
                                    k.op("pe", lambda e: e.transpose(pt[:, i * 128:(i + 1) * 128], u[nm], ident[:]), reads=rd + [cst], writes=[ptr_])
                        return f

                    def t_ev(g):
                        def f():
                            bp = bi % 2
                            IA, IAr = IVG[g][0]
                            for u in groups[g]:
                                ui = u["ui"]
                                pb, pbr = psf[ui]; Tm, Tmr = u["Tm"]
                                msk = maskF if u["st"]["z"] == 0 else maskB
                                k.op("dve", lambda e: e.tensor_tensor(out=Tm, in0=pb[:], in1=msk[:], op=ALU.mult), reads=[pbr, cst], writes=[Tmr])
                            KTt, KTr = KTG[bp][g]
                            for j in range(2):
                                pt_, ptr_ = psb[j]
                                k.op("act", lambda e: e.activation(out=KTt[:, 2 * j:2 * j + 2, :], in_=pt_[:, 0:768].rearrange("p (u c) -> p u c", u=2), func=AF.Copy), reads=[ptr_], writes=[KTr])
                            for u in groups[g]:
                                ui = u["ui"]; Tm, Tmr = u["Tm"]
                                k.op("dve", lambda e: e.tensor_tensor(out=IA[:, 2, ui * 128:(ui + 1) * 128], in0=Tm[:, 192:320], in1=ident[:], op=ALU.add), reads=[Tmr, cst], writes=[IAr])
                        return f
                    for g in range(2):
                        macro.append([t_pe(g), t_ev(g)])

                    def lvl_pe(l, g):
                        def f():
                            (bP, bPr), (bQ, bQr), (bX, bXr) = lvl_banks[g]
                            for u in groups[g]:
                                ui = u["ui"]; cs_ = slice(ui * 128, (ui + 1) * 128)
                                if l == 1:
                                    Tm, Tmr = u["Tm"]
                                    P, Q, rd = Tm[:, 192:320], Tm[:, 384:512], [Tmr]
                                    mm(bP[:, cs_], Q, P, True, True, rd, [bPr])
                                    mm(bQ[:, cs_], P, Q, True, True, rd, [bQr])
                                else:
                                    src, srcr = IVG[g][l % 2]
                                    P, Q, X = src[:, 0, cs_], src[:, 1, cs_], src[:, 2, cs_]
                                    if l <= 4:
                                        mm(bP[:, cs_], Q, P, True, True, [srcr], [bPr])
                                    if l <= 5:
                                        mm(bQ[:, cs_], P, Q, True, True, [srcr], [bQr])
                                    mm(bX[:, cs_], ident[:], X, True, False, [srcr, cst], [bXr])
                                    mm(bX[:, cs_], Q, X, False, True, [srcr], [bXr])
                        return f

                    def lvl_ev(l, g):
                        def f():
                            (bP, bPr), (bQ, bQr), (bX, bXr) = lvl_banks[g]
                            bp = bi % 2
                            jobs = []
                            if l == 1:
                                dst, dstr = IVG[g][0]
                                jobs = [(dst[:, 0, :], bP, bPr, dstr), (dst[:, 1, :], bQ, bQr, dstr)]
                            elif l <= 5:
                                dst, dstr = IVG[g][(l + 1) % 2]
                                if l <= 4:
                                    jobs.append((dst[:, 0, :], bP, bPr, dstr))
                                jobs.append((dst[:, 1, :], bQ, bQr, dstr))
                                jobs.append((dst[:, 2, :], bX, bXr, dstr))
                            else:
                                mv, mvr = MVG[bp][g]
                                jobs = [(mv[:].rearrange("p u c -> p (u c)"), bX, bXr, mvr)]
                            for i, (o, bk, bkr, dr) in enumerate(jobs):
                                if (i + l + g) % 2 == 0:
                                    k.op("act", lambda e: e.activation(out=o, in_=bk[:, 0:512], func=AF.Copy), reads=[bkr], writes=[dr])
                                else:
                                    k.op("dve", lambda e: e.tensor_copy(o, bk[:, 0:512]), reads=[bkr], writes=[dr])
                        return f
                    for l in range(1, 7):
                        macro.append([lvl_pe(l, 0), lvl_pe(l, 1), lvl_ev(l, 0), lvl_ev(l, 1)])
                    return macro

                def chain_stages(bi, g):
                    ctx = [unit_ctx(st, bi, g) for st in streams]

                    def w_pe():
                        for u in ctx:
                            st = u["st"]; Tm, Tmr = u["Tm"]; KT, KTr = u["KT"]
                            S, Sr = st["S"][st["si"] % 2]
                            ui = u["ui"]
                            pb = psf[4 + ui // 2][0][:, (ui % 2) * 192:(ui % 2) * 192 + 192]; pbr = chain_res[ui // 2]; u["pC"] = (pb, pbr)
                            mm(pb[:, 0:128], Tm[:, 0:128], KT[:, 256:384], True, False, [Tmr, KTr], [pbr])
                            mm(pb[:, 0:128], u["AR"][:, 0:128], S[:], False, True, [u["rA"], Sr], [pbr])

                    def w_ev():
                        for u in ctx:
                            pb, pbr = u["pC"]; WT, WTr = u["st"]["WT"]
                            k.op("act", lambda e: e.activation(out=WT[:], in_=pb[:, 0:128], func=AF.Copy), reads=[pbr], writes=[WTr])

                    def u_pe():
                        for u in ctx:
                            pb, pbr = u["pC"]; WT, WTr = u["st"]["WT"]; Mi, Mir = u["Minv"]
                            mm(pb[:, 0:128], Mi, WT[:], True, True, [Mir, WTr], [pbr])

                    def u_ev():
                        for u in ctx:
                            pb, pbr = u["pC"]; UT, UTr = u["st"]["UT"]
                            k.op("dve", lambda e: e.tensor_copy(UT[:], pb[:, 0:128]), reads=[pbr], writes=[UTr])

                    def ys_pe():
                        for u in ctx:
                            st = u["st"]; Tm, Tmr = u["Tm"]; KT, KTr = u["KT"]; UT, UTr = st["UT"]
                            S, Sr = st["S"][st["si"] % 2]
                            pb, pbr = u["pC"]; rA = u["rA"]
                            mm(pb[:, 128:192], KT[:, 256:384], Tm[:, 128:192], True, False, [KTr, Tmr], [pbr])
                            mm(pb[:, 128:192], S[:], u["AR"][:, 128:192], False, False, [Sr, rA], [pbr])
                            mm(pb[:, 128:192], UT[:], Tm[:, 320:384], False, True, [UTr, Tmr], [pbr])
                            mm(pb[:, 0:128], KT[:, 0:128], KT[:, 256:384], True, False, [KTr], [pbr])
                            mm(pb[:, 0:128], ident[:], S[:], False, False, [cst, Sr], [pbr])
                            mm(pb[:, 0:128], KT[:, 128:256], UT[:], False, True, [KTr, UTr], [pbr])

                    def ys_ev():
                        for u in ctx:
                            st = u["st"]; pb, pbr = u["pC"]; c = u["c"]; ci = u["ci"]
                            Sn, Snr = st["S"][(st["si"] + 1) % 2]
                            eg = u["eg"]
                            col = ci * CH + (CH - 1 if st["z"] == 0 else 0)
                            k.op("act", lambda e: e.activation(out=Sn[:], in_=pb[:, 0:128], func=AF.Identity, scale=eg[:, col:col + 1]), reads=[pbr, u["rA"]], writes=[Snr])
                            st["si"] += 1
                            yr = yacc_res[st["hl"]][c]
                            ydst = yacc[:, st["hl"], c * CH:(c + 1) * CH]
                            if yr.w is None:
                                k.op("dve", lambda e: e.tensor_copy(ydst, pb[:, 128:192]), reads=[pbr], writes=[yr])
                            else:
                                k.op("dve", lambda e: e.tensor_tensor(out=ydst, in0=ydst, in1=pb[:, 128:192], op=ALU.add), reads=[pbr, yr], writes=[yr])
                    return [[w_pe, w_ev], [u_pe, u_ev], [ys_pe, ys_ev]]

                def replay(chunk):
                    for (e_, fn_, rd_, wr_) in chunk:
                        k.op(e_, fn_, rd_, wr_)

                tmpB = {n_: (carve.alloc([128, BLK], F32), Res(n_ + "B")) for n_ in tmp}
                ksqB = (carve.alloc([128, BLK], BF16), Res("ksqB"))

                def record_prep(bi_):
                    per_stream = []
                    for si_, st in enumerate(streams):
                        k.defer = []
                        if si_ % 2 == 0:
                            prep(st, bi_, tmp, ksq, 0)
                        else:
                            prep(st, bi_, tmpB, ksqB, 256)
                        rec = k.defer; k.defer = None
                        chunks = [[]]
                        for it in rec:
                            if it is None:
                                chunks.append([])
                            else:
                                chunks[-1].append(it)
                        per_stream.append(chunks)
                    out = []
                    for p_ in range(0, len(streams), 2):
                        ca, cb = per_stream[p_], per_stream[p_ + 1]
                        for j_ in range(max(len(ca), len(cb))):
                            a_ = ca[j_] if j_ < len(ca) else []
                            b_ = cb[j_] if j_ < len(cb) else []
                            m_ = []
                            for i_ in range(max(len(a_), len(b_))):
                                if i_ < len(a_):
                                    m_.append(a_[i_])
                                if i_ < len(b_):
                                    m_.append(b_[i_])
                            if m_:
                                out.append(m_)
                    return out
                nblk = (scan_steps + 1) // 2 if scan_steps else NBK
                for bi_ in range(min(2, nblk)):
                    for c_ in record_prep(bi_):
                        replay(c_)
                for ms in pre_block(0):
                    for f in ms:
                        f()
                for bi in range(nblk):
                    A = pre_block(bi + 1) if bi + 1 < nblk else []
                    B = [f for pr in (chain_stages(bi, 0) + chain_stages(bi, 1)) for f in pr]
                    C = record_prep(bi + 2) if bi + 2 < nblk else []
                    Af = []
                    for ms in A:
                        flags = [False, True] if len(ms) == 2 else [True, False, False, True]
                        Af += list(zip(ms, flags))
                    nb_per = max(1, (len(B) + max(1, len(Af)) - 1) // max(1, len(Af)))
                    while Af or B or C:
                        safe = True
                        if Af:
                            f, safe = Af.pop(0)
                            f()
                        for _ in range(nb_per if Af else len(B)):
                            if B:
                                B.pop(0)()
                        if C and (safe or not any(op_[0] == "pe" for op_ in C[0])):
                            replay(C.pop(0))
                        if not Af and not B:
                            while C:
                                replay(C.pop(0))
                dbg("yacc%d" % rnd, yacc[:, 0, :], yacc_res[0], [128, SEQ])
                fo = k.op
                fmm = mm
                fctr = [0]

                def FPS():
                    it = psf[fctr[0] % 4]; fctr[0] += 1
                    return it
                def fin_block(hl, hp, b, tmp, ksq, yb, ysqb, bankA, bankB):
                    hc = slice(hp * 128, (hp + 1) * 128)
                    t0 = b * BLK; ts = slice(t0, t0 + BLK); tb = t0 // 512
                    y = yacc[:, hl, ts]; yres = yacc_res[hl][2 * b:2 * b + 2]
                    r_ap, k_ap, v_ap = rkv[:, hp, ts], rkv[:, 4 + hp, ts], rkv[:, 8 + hp, ts]
                    rr, kr_, vr_ = [rkv_res[hp][tb]], [rkv_res[4 + hp][tb]], [rkv_res[8 + hp][tb]]
                    (ysq, ysqr), (mean, meanr), (msq, msqr), (var, varr) = tmp["sw"], tmp["sa"], tmp["cs"], tmp["pin"]
                    (rs, rsr), (yn, ynr), (s0, s0r), (s1, s1r), (bon, bonr) = tmp["pex"], tmp["ege"], tmp["egi"], tmp["kr"], tmp["rn"]
                    fo("dve", lambda e: e.tensor_tensor(out=ysqb[0][:], in0=y, in1=y, op=ALU.mult), reads=yres, writes=[ysqb[1]])
                    fo("act", lambda e: e.activation(out=yb[0][:], in_=y, func=AF.Copy), reads=yres, writes=[yb[1]])
                    pa, par = bankA
                    fmm(pa[:, 0:BLK], onesbd[:], yb[0][:], True, True, [cst, yb[1]], [par])
                    fmm(pa[:, BLK:2 * BLK], onesbd[:], ysqb[0][:], True, True, [cst, ysqb[1]], [par])
                    fo("act", lambda e: e.activation(out=mean[:], in_=pa[:, 0:BLK], func=AF.Copy, scale=1.0 / 64.0), reads=[par], writes=[meanr])
                    fo("dve", lambda e: e.tensor_tensor(out=msq[:], in0=mean[:], in1=mean[:], op=ALU.mult), reads=[meanr], writes=[msqr])
                    fo("dve", lambda e: e.scalar_tensor_tensor(out=var[:], in0=pa[:, BLK:2 * BLK], scalar=1.0 / 64.0, in1=msq[:], op0=ALU.mult, op1=ALU.subtract), reads=[par, msqr], writes=[varr])
                    fo("act", lambda e: e.activation(out=rs[:], in_=var[:], func=AF.Ln, bias=epsgn[:, 0:1], scale=1.0), reads=[varr, cst], writes=[rsr])
                    fo("act", lambda e: e.activation(out=rs[:], in_=rs[:], func=AF.Exp, scale=-0.5), reads=[rsr], writes=[rsr])
                    fo("dve", lambda e: e.tensor_tensor(out=yn[:], in0=y, in1=mean[:], op=ALU.subtract), reads=yres + [meanr], writes=[ynr])
                    fo("dve", lambda e: e.tensor_tensor(out=yn[:], in0=yn[:], in1=rs[:], op=ALU.mult), reads=[ynr, rsr], writes=[ynr])
                    fo("dve", lambda e: e.tensor_scalar(yn[:], yn[:], GNG(hp), GNB(hp), ALU.mult, ALU.add), reads=[ynr, cst], writes=[ynr])
                    pb_, pbr_ = bankA
                    fmm(pb_[:, 0:BLK], a2b[0:64, hc], wag[0:64, 1, ts], True, True, [cst, rkv_res[13][tb]], [pbr_])
                    pb2_, pbr2_ = bankB
                    fmm(pb2_[:, 0:BLK], a2b[64:128, hc], wag[64:128, 1, ts], True, True, [cst, rkv_res[13][tb]], [pbr2_])
                    act_sigmoid(fo, s0[:], pb_[:, 0:BLK], na0T[:, hp:hp + 1], [pbr_], [s0r])
                    act_sigmoid(fo, s1[:], pb2_[:, 0:BLK], na0T[:, 4 + hp:5 + hp], [pbr2_], [s1r])
                    fo("dve", lambda e: e.tensor_tensor(out=s0[:], in0=s0[:], in1=s1[:], op=ALU.add), reads=[s0r, s1r], writes=[s0r])
                    fo("dve", lambda e: e.tensor_scalar(s0[:], s0[:], KA(hp), ppx[:, 4 + hp:5 + hp], ALU.mult, ALU.add), reads=[s0r, cst], writes=[s0r])
                    fo("dve", lambda e: e.tensor_tensor(out=s0[:], in0=s0[:], in1=k_ap, op=ALU.mult), reads=[s0r] + kr_, writes=[s0r])
                    fo("dve", lambda e: e.scalar_tensor_tensor(out=ksq[0][:], in0=r_ap, scalar=RK(hp), in1=s0[:], op0=ALU.mult, op1=ALU.mult), reads=rr + [s0r, cst], writes=[ksq[1]])
                    pc_, pcr_ = bankA
                    fmm(pc_[:, 0:BLK], onesbd[:], ksq[0][:], True, True, [cst, ksq[1]], [pcr_])
                    fmm(pc_[:, BLK:2 * BLK], g2b[:, hc], wag[:, 2, ts], True, True, [cst, rkv_res[14][tb]], [pcr_])
                    fo("dve", lambda e: e.tensor_tensor(out=bon[:], in0=pc_[:, 0:BLK], in1=v_ap, op=ALU.mult), reads=[pcr_] + vr_, writes=[bonr])
                    fo("dve", lambda e: e.tensor_tensor(out=yn[:], in0=yn[:], in1=bon[:], op=ALU.add), reads=[ynr, bonr], writes=[ynr])
                    fo("dve", lambda e: e.tensor_tensor(out=yaT[:, hp, ts], in0=yn[:], in1=pc_[:, BLK:2 * BLK], op=ALU.mult), reads=[ynr, pcr_], writes=[yaT_res[hp][tb]])
                barrier()
                cvf = Carver(64, 128)
                tmp2 = {n_: (cvf.alloc([128, BLK], F32), Res(n_ + "2")) for n_ in tmp}
                ksq2 = (cvf.alloc([128, BLK], BF16), Res("ksq2")); yb2 = (cvf.alloc([128, BLK], BF16), Res("yb2")); ysqb2 = (cvf.alloc([128, BLK], BF16), Res("ysqb2"))
                for b in range(NBLK):
                    recs = []
                    for hl, hp in enumerate(hps):
                        k.defer = []
                        if hl == 0:
                            fin_block(hl, hp, b, tmp, ksq, yb, ysqb, psf[0], psf[1])
                        else:
                            fin_block(hl, hp, b, tmp2, ksq2, yb2, ysqb2, psf[2], psf[3])
                        recs.append([it for it in k.defer if it is not None]); k.defer = None
                    for i_ in range(max(len(r_) for r_ in recs)):
                        for r_ in recs:
                            if i_ < len(r_):
                                k.op(*r_[i_])
                return yacc, yacc_res

        for rnd in range(2):
            yacc, yacc_res = scan_round(rnd)
            if stop == "C0":
                dbg("yaT0", yaT[:, 0, :], yaT_res[0], [128, SEQ])
                finish()
                return nc, dbg_outs
            barrier()
        dbg("yaT0", yaT[:, 0, :], yaT_res[0], [128, SEQ])
        dbg("yaT3", yaT[:, 3, :], yaT_res[3], [128, SEQ])
        if stop == "C":
            finish()
            return nc, dbg_outs

        barrier()
        h0T_res = [Res("h0T%d" % t) for t in range(NT)]
        phase_h0T(h0T, h0T_res)
        ybT = view(16, 4 * SEQ * 2, BF16, "p (c t) -> p c t", c=4)
        ybT_res = [Res("ybT%d" % t) for t in range(NT)]
        with ExitStack() as ph:
            Wu = view(96, 8 * 512 * 2, BF16, "p (k c) -> p k c", k=8); Wv = view(104, 8 * 512 * 2, BF16, "p (k c) -> p k c", k=8)
            wr_ = Res("WuWv")
            k.dma("pool", lambda e: e.dma_start(out=Wu, in_=w_in[:, GM0:GM0 + 512].rearrange("(k p) c -> p k c", p=128)), writes=[wr_])
            k.dma("pool", lambda e: e.dma_start(out=Wv, in_=w_in[:, GM0 + 512:GM0 + 1024].rearrange("(k p) c -> p k c", p=128)), writes=[wr_])
            wsT = phase_sb(ph, [128, 8, 128], BF16, "wsT"); bsF = phase_sb(ph, [128, 4, 128], F32, "bsF")
            k.dma("pool", lambda e: e.dma_start(out=wsT[:], in_=wsT_d), writes=[wr_])
            k.dma("sp", lambda e: e.dma_start(out=bsF[:], in_=bsF_d), writes=[wr_])
            gg, ggr = load_bc(ph, gln_g_d, 512, "gg"); gb, gbr = load_bc(ph, gln_b_d, 512, "gb")
            uTs = [(view(112 + 4 * i, 4 * 512 * 2, BF16, "p (c t) -> p c t", c=4), Res("uT")) for i in range(2)]
            vgs = PRing(ph, 2, [128, 512], F32, "vg")
            vnEs = PRing(ph, 2, [128, 512], BF16, "vnE"); vnOs = PRing(ph, 2, [128, 512], BF16, "vnO")
            svs = PRing(ph, 2, [128, 512], F32, "sv")
            small = PRing(ph, 4, [128, 16], F32, "lnsm")
            for (t_, r_) in vnEs.items + vnOs.items:
                k.op("pool", lambda e: e.memset(t_[:], 0.0), writes=[r_])
            g4 = lambda ap, g: ap.rearrange("p (c g d) -> p c g d", g=2, d=64)[:, :, g, :]
            for tb in range(4):
                ts = slice(tb * 512, (tb + 1) * 512)
                uT, uTr = uTs[tb % 2]
                for cu in range(4):
                    pb, pbr = PSF()
                    for kc in range(8):
                        mm(pb[:], Wu[:, kc, cu * 128:(cu + 1) * 128], h0T[:, kc, ts], kc == 0, kc == 7, [wr_] + h0T_res[4 * tb:4 * tb + 4], [pbr])
                    k.op("act", lambda e: e.activation(out=uT[:, cu, :], in_=pb[:], func=AF.Gelu), reads=[pbr], writes=[uTr])
                for tt in range(4):
                    t = 4 * tb + tt
                    tsl = slice(t * 128, (t + 1) * 128)
                    pb, pbr = PSF()
                    for kc in range(8):
                        mm(pb[:], h0T[:, kc, tsl], Wv[:, kc, :], kc == 0, kc == 7, [wr_, h0T_res[t]], [pbr])
                    vg, vgr = vgs.get()
                    k.op("act", lambda e: e.activation(out=vg[:], in_=pb[:], func=AF.Gelu), reads=[pbr], writes=[vgr])
                    layer_norm(vg[:], [vgr], gg[:], gb[:], [ggr, gbr], vg[:], [vgr], small, n=512)
                    vnE, vnEr = vnEs.get(); vnO, vnOr = vnOs.get()
                    k.op("act", lambda e: e.activation(out=g4(vnE[:], 0), in_=g4(vg[:], 0), func=AF.Copy), reads=[vgr], writes=[vnEr])
                    k.op("pool", lambda e: e.tensor_copy(g4(vnO[:], 1), g4(vg[:], 1)), reads=[vgr], writes=[vnOr])
                    ps, psr = PSF()
                    for c in range(4):
                        cs_ = slice(c * 128, (c + 1) * 128)
                        mm(ps[:, cs_], vnE[:, cs_], wsT[:, 2 * c, :], True, False, [vnEr, wr_], [psr])
                        mm(ps[:, cs_], vnO[:, cs_], wsT[:, 2 * c + 1, :], False, True, [vnOr, wr_], [psr])
                    sv, svr = svs.get()
                    k.op("dve", lambda e: e.tensor_tensor(out=sv[:], in0=ps[:], in1=bsF[:].rearrange("p c t -> p (c t)"), op=ALU.add), reads=[psr, wr_], writes=[svr])
                    k.op("dve", lambda e: e.tensor_tensor(out=ybT[:, :, tsl], in0=sv[:].rearrange("p (c t) -> p c t", c=4), in1=uT[:, :, tt * 128:(tt + 1) * 128], op=ALU.mult), reads=[svr, uTr], writes=[ybT_res[t]])
            barrier()
        dbg("ybT0", ybT[:, 0, :], ybT_res, [128, SEQ])
        if stop == "D":
            finish()
            return nc, dbg_outs

        mgT = view(128, 8 * SEQ * 2, BF16, "p (c t) -> p c t", c=8)
        mg_res = [Res("mg%d" % tb) for tb in range(4)]
        with ExitStack() as ph:
            wbr = view(96, 8 * 1024 * 2, BF16, "p (k d) -> p k d", k=8); wbr_r = Res("wbr")
            for kc in range(8):
                k.dma("pool", lambda e: e.dma_start(out=wbr[:, kc, :], in_=wbr_d[kc * 128:(kc + 1) * 128, :]), writes=[wbr_r])
            wgs = [(view(112 + 2 * i, 8 * 128 * 2, BF16, "p (k c) -> p k c", k=8), Res("wg")) for i in range(4)]
            sigs = PRing(ph, 3, [128, 512], F32, "sig")
            prs = PRing(ph, 2, [128, 512], F32, "pr")
            wi = 0
            for dc in range(8):
                wg2 = []
                for n in range(2):
                    wg, wgr = wgs[wi % 4]; wi += 1
                    c0 = GT0 + n * 1024 + dc * 128
                    k.dma("pool", lambda e: e.dma_start(out=wg, in_=w_in[:, c0:c0 + 128].rearrange("(k p) c -> p k c", p=128)), writes=[wgr])
                    wg2.append((wg, wgr))
                for tb in range(4):
                    ts = slice(tb * 512, (tb + 1) * 512)
                    hr = h0T_res[4 * tb:4 * tb + 4]
                    acc = None
                    for n in range(2):
                        wg, wgr = wg2[n]
                        pg, pgr = PSF()
                        for kc in range(8):
                            mm(pg[:], wg[:, kc, :], h0T[:, kc, ts], kc == 0, kc == 7, [wgr] + hr, [pgr])
                        sg, sgr = sigs.get()
                        k.op("act", lambda e: e.activation(out=sg[:], in_=pg[:], func=AF.Sigmoid), reads=[pgr], writes=[sgr])
                        pbn, pbnr = PSF()
                        yT, yres = (yaT, [yaT_res[c][tb] for c in range(4)]) if n == 0 else (ybT, ybT_res[4 * tb:4 * tb + 4])
                        for c in range(4):
                            mm(pbn[:], wbr[:, n * 4 + c, dc * 128:(dc + 1) * 128], yT[:, c, ts], c == 0, c == 3, [wbr_r] + yres, [pbnr])
                        if n == 0:
                            acc, accr = prs.get()
                            k.op("dve", lambda e: e.tensor_tensor(out=acc[:], in0=sg[:], in1=pbn[:], op=ALU.mult), reads=[sgr, pbnr], writes=[accr])
                        else:
                            k.op("dve", lambda e: e.tensor_tensor(out=sg[:], in0=sg[:], in1=pbn[:], op=ALU.mult), reads=[sgr, pbnr], writes=[sgr])
                            k.op("pool", lambda e: e.tensor_tensor(out=mgT[:, dc, ts], in0=acc[:], in1=sg[:], op=ALU.add), reads=[sgr, accr], writes=[mg_res[tb]])
            barrier()
        dbg("mgT0", mgT[:, 0, :], mg_res, [128, SEQ])
        if stop == "E":
            finish()
            return nc, dbg_outs

        H = view(64, NT * D * 4, F32, "p (t d) -> p t d", t=NT)
        H_res = [Res("H%d" % t) for t in range(NT)]

        def load_w_fm(dst, d_ap, res, q="pool"):
            for kc in range(8):
                k.dma(q, lambda e: e.dma_start(out=dst[:, kc, :], in_=d_ap[kc * 128:(kc + 1) * 128, :]), writes=[res])

        with ExitStack() as ph:
            wmix = view(0, 8 * 1024 * 2, BF16, "p (k d) -> p k d", k=8); wmix_r = Res("wmix")
            load_w_fm(wmix, wmix_d, wmix_r)
            g0, g0r = load_bc(ph, lng["ln_emb_g"], D, "g0"); b0, b0r = load_bc(ph, lng["ln_emb_b"], D, "b0")
            g1, g1r = load_bc(ph, lng["ln1_g"], D, "g1"); b1, b1r = load_bc(ph, lng["ln1_b"], D, "b1")
            xs = PRing(ph, 2, [128, D], F32, "xs")
            small = PRing(ph, 4, [128, 16], F32, "lnsm")
            def tileF(t):
                tsl = slice(t * 128, (t + 1) * 128)
                xt, xr = xs.get()
                k.dma("sp", lambda e: e.dma_start(out=xt[:], in_=x[tsl, :]), writes=[xr])
                layer_norm(xt[:], [xr], g0[:], b0[:], [g0r, b0r], xt[:], [xr], small)

                def half_(half):
                    hs = slice(half * 512, (half + 1) * 512)
                    pm, pmr = PSF()
                    for kc in range(8):
                        mm(pm[:], mgT[:, kc, tsl], wmix[:, kc, hs], kc == 0, kc == 7, [mg_res[t // 4], wmix_r], [pmr])
                    k.op("dve", lambda e: e.scalar_tensor_tensor(out=xt[:, hs], in0=xt[:, hs], scalar=ALPHA, in1=pm[:], op0=ALU.mult, op1=ALU.add), reads=[xr, pmr], writes=[xr])
                half_(0)
                half_(1)
                layer_norm(xt[:], [xr], g1[:], b1[:], [g1r, b1r], H[:, t, :], [H_res[t]], small)
            run_pairs(NT, tileF)
            barrier()
        dbg("h1_t0", H[:, 0, :], [H_res[0]], [128, D])
        dbg("h1_t9", H[:, 9, :], [H_res[9]], [128, D])
        if stop == "F":
            finish()
            return nc, dbg_outs

        with ExitStack() as ph:
            wq = view(0, 8 * 1024 * 2, BF16, "p (k d) -> p k d", k=8); wo = view(16, 8 * 1024 * 2, BF16, "p (k d) -> p k d", k=8)
            wkK = view(32, 8 * 1024 * 2, BF16, "p (k d) -> p k d", k=8); wkV = view(48, 8 * 1024 * 2, BF16, "p (k d) -> p k d", k=8)
            KT = view(128, 8 * 256 * 2, BF16, "p (c m) -> p c m", c=8); Vm = view(132, 2 * 1024 * 2, BF16, "p (m d) -> p m d", m=2)
            memT = view(136, 8 * 256 * 2, BF16, "p (c m) -> p c m", c=8)
            wres = Res("xw"); kvres = Res("kv"); memr = [Res("memT0"), Res("memT1")]
            load_w_fm(wq, wq_d, wres); load_w_fm(wo, wo_d, wres)
            load_w_fm(wkK, wkv_d[:, 0:D], wres); load_w_fm(wkV, wkv_d[:, D:2 * D], wres)
            small = PRing(ph, 4, [128, 16], F32, "lnsm")
            xs = PRing(ph, 2, [128, D], F32, "xs")
            with ExitStack() as ph2:
                gm, gmr = load_bc(ph2, lng["mem_ln_g"], D, "gm"); bm, bmr = load_bc(ph2, lng["mem_ln_b"], D, "bm")
                hbs0 = PRing(ph2, 2, [128, D], BF16, "hb")
                for mt in range(2):
                    xt, xr = xs.get()
                    k.dma("sp", lambda e: e.dma_start(out=xt[:], in_=mem[mt * 128:(mt + 1) * 128, :]), writes=[xr])
                    layer_norm(xt[:], [xr], gm[:], bm[:], [gmr, bmr], xt[:], [xr], small)
                    hb, hbr = hbs0.get()
                    k.op("act", lambda e: e.activation(out=hb[:], in_=xt[:], func=AF.Copy), reads=[xr], writes=[hbr])
                    to_fm(hb, hbr, memT, [memr[mt]], mt)
                barrier()
            g2_, g2r = load_bc(ph, lng["ln2_g"], D, "g2"); b2_, b2r = load_bc(ph, lng["ln2_b"], D, "b2")
            for c in range(8):
                pk, pkr = PSF()
                for kc in range(8):
                    mm(pk[:, 0:256], wkK[:, kc, c * 128:(c + 1) * 128], memT[:, kc, :], kc == 0, kc == 7, [wres] + memr, [pkr])
                k.op("act", lambda e: e.activation(out=KT[:, c, :], in_=pk[:, 0:256], func=AF.Copy), reads=[pkr], writes=[kvres])
            for mt in range(2):
                for half in range(2):
                    hs = slice(half * 512, (half + 1) * 512)
                    pv, pvr = PSF()
                    for kc in range(8):
                        mm(pv[:], memT[:, kc, mt * 128:(mt + 1) * 128], wkV[:, kc, hs], kc == 0, kc == 7, [wres] + memr, [pvr])
                    k.op("act", lambda e: e.activation(out=Vm[:, mt, hs], in_=pv[:], func=AF.Copy), reads=[pvr], writes=[kvres])
            barrier()
            cv = Carver(32, 64)

            class CRing:
                def __init__(self, n, shape, dt, name):
                    self.items = [(cv.alloc(shape, dt), Res(name)) for _ in range(n)]
                    self.i = 0

                def get(self):
                    it = self.items[self.i % len(self.items)]; self.i += 1
                    return it
            hbs = CRing(2, [128, D], BF16, "hb")
            hTs = CRing(2, [128, 8, 128], BF16, "hT")
            qTs = CRing(2, [128, 8, 128], BF16, "qT")
            pexs = CRing(2, [128, 4, 256], BF16, "pex")
            pTs = CRing(2, [128, 8, 128], BF16, "pT")
            oTs = CRing(2, [128, 8, 128], BF16, "oT")
            sm2 = PRing(ph, 2, [128, 16], F32, "sm2")
            SCL = 256.0 ** -0.5
            tab_res = Res("tables")
            stg = [(view(140 + 8 * i, 4 * D * 2, BF16, "p (a f) -> p a f", a=4), Res("stg%d" % i)) for i in range(2)]
            conv_jobs = [(src, dst, ch) for (src, dst) in ((pu_d, Ub_d), (pv_d, Vb_d)) for ch in range(32)]

            def conv_some(n_):
                for _ in range(n_):
                    if not conv_jobs:
                        return
                    src, dst, ch = conv_jobs.pop(0)
                    st_, str_ = stg[ch % 2]
                    k.dma("pool", lambda e: e.dma_start(out=st_, in_=src[ch * 4:(ch + 1) * 4].rearrange("a p f -> p a f")), writes=[str_])
                    k.dma("sp", lambda e: e.dma_start(out=dst[ch * 4:(ch + 1) * 4].rearrange("a p f -> p a f"), in_=st_), reads=[str_], writes=[tab_res])
            for t in range(NT):
                conv_some(4)
                h1 = H[:, t, :]; h1r = H_res[t]
                hb, hbr = hbs.get()
                k.op("act", lambda e: e.activation(out=hb[:], in_=h1, func=AF.Copy), reads=[h1r], writes=[hbr])
                hT, hTr = hTs.get()
                to_fm(hb, hbr, hT, [hTr], 0)
                qT, qTr = qTs.get()
                for g in range(2):
                    pq, pqr = PSF()
                    for cc in range(4):
                        c = g * 4 + cc
                        for kc in range(8):
                            mm(pq[:, cc * 128:(cc + 1) * 128], wq[:, kc, c * 128:(c + 1) * 128], hT[:, kc, :], kc == 0, kc == 7, [wres, hTr], [pqr])
                    k.op("act", lambda e: e.activation(out=qT[:, g * 4:(g + 1) * 4, :], in_=pq[:].rearrange("p (c t) -> p c t", c=4), func=AF.Copy), reads=[pqr], writes=[qTr])
                sm, smr = sm2.get()
                pex, pexr = pexs.get()
                pss = []
                for g in range(2):
                    ps_, psr_ = PSF(); pss.append((ps_, psr_))
                    for hh in range(2):
                        h = g * 2 + hh
                        for j in range(2):
                            mm(ps_[:, hh * 256:(hh + 1) * 256], qT[:, 2 * h + j, :], KT[:, 2 * h + j, :], j == 0, j == 1, [qTr, kvres], [psr_])
                    k.op("dve", lambda e: e.tensor_reduce(out=sm[:, g * 2:(g + 1) * 2], in_=ps_[:].rearrange("p (h m) -> p h m", h=2), axis=AX.X, op=ALU.max), reads=[psr_], writes=[smr])
                k.op("dve", lambda e: e.tensor_scalar(sm[:, 4:8], sm[:, 0:4], -SCL, None, ALU.mult), reads=[smr], writes=[smr])
                for h in range(4):
                    ps_, psr_ = pss[h // 2]
                    k.op("act", lambda e: e.activation(out=pex[:, h, :], in_=ps_[:, (h % 2) * 256:(h % 2 + 1) * 256], func=AF.Exp, bias=sm[:, 4 + h:5 + h], scale=SCL, accum_out=sm[:, 8 + h:9 + h]), reads=[psr_, smr], writes=[pexr, smr])
                k.op("dve", lambda e: e.reciprocal(out=sm[:, 12:16], in_=sm[:, 8:12]), reads=[smr], writes=[smr])
                k.op("dve", lambda e: e.tensor_tensor(out=pex[:], in0=pex[:], in1=sm[:, 12:16].unsqueeze(2).to_broadcast([128, 4, 256]), op=ALU.mult), reads=[pexr, smr], writes=[pexr])
                pt, ptr_ = PSB()
                for h in range(4):
                    for mt in range(2):
                        i = h * 2 + mt
                        k.op("pe", lambda e: e.transpose(pt[:, i * 128:(i + 1) * 128], pex[:, h, mt * 128:(mt + 1) * 128], ident[:]), reads=[pexr, cst], writes=[ptr_])
                pT, pTr = pTs.get()
                k.op("act", lambda e: e.activation(out=pT[:], in_=pt[:].rearrange("p (c t) -> p c t", t=128), func=AF.Copy), reads=[ptr_], writes=[pTr])
                oT, oTr = oTs.get()
                for g in range(2):
                    po, por = PSF()
                    for cc in range(4):
                        c = g * 4 + cc; h = c // 2
                        for mt in range(2):
                            mm(po[:, cc * 128:(cc + 1) * 128], Vm[:, mt, c * 128:(c + 1) * 128], pT[:, h * 2 + mt, :], mt == 0, mt == 1, [kvres, pTr], [por])
                    k.op("dve", lambda e: e.tensor_copy(oT[:, g * 4:(g + 1) * 4, :], po[:].rearrange("p (c t) -> p c t", c=4)), reads=[por], writes=[oTr])
                xt, xr = xs.get()
                for half in range(2):
                    hs = slice(half * 512, (half + 1) * 512)
                    px, pxr = PSF()
                    for c in range(8):
                        mm(px[:], oT[:, c, :], wo[:, c, hs], c == 0, c == 7, [oTr, wres], [pxr])
                    k.op("dve", lambda e: e.scalar_tensor_tensor(out=xt[:, hs], in0=h1[:, hs], scalar=ALPHA, in1=px[:], op0=ALU.mult, op1=ALU.add), reads=[h1r, pxr], writes=[xr])
                layer_norm(xt[:], [xr], g2_[:], b2_[:], [g2r, b2r], H[:, t, :], [H_res[t]], small)
            barrier()
        dbg("h2_t0", H[:, 0, :], [H_res[0]], [128, D])
        dbg("h2_t9", H[:, 9, :], [H_res[9]], [128, D])
        if stop == "H":
            finish()
            return nc, dbg_outs

        with ExitStack() as pho:
            SI1 = phase_sb(pho, [128, NT, 128], F32, "SI1"); SI2 = phase_sb(pho, [128, NT, 128], F32, "SI2"); SG = phase_sb(pho, [128, NT, 128], F32, "SG")
            slot_res = [Res("slot%d" % t) for t in range(NT)]
            with ExitStack() as ph:
                pwq = view(0, 8 * 2048 * 2, BF16, "p (k d) -> p k d", k=8); pw_r = Res("pwq")
                load_w_fm(pwq, pwq_d, pw_r)
                skT = phase_sb(ph, [128, 2, 128], BF16, "skT")
                k.dma("pool", lambda e: e.dma_start(out=skT[:], in_=skT_d), writes=[pw_r])
                sc = view(48, 16 * 128 * 4, F32, "p (c k) -> p c k", c=16); scr = Res("sc")
                cv = Carver(128, 160)

                def CT(shape, dt, name, c=cv):
                    return (c.alloc(shape, dt), Res(name))
                hb, hbr = CT([128, D], BF16, "hb"); hT, hTr = CT([128, 8, 128], BF16, "hT")
                pqT, pqTr = CT([128, 16, 128], BF16, "pqT")
                sc2, sc2r = CT([128, 256], F32, "sc2"); sc2b, sc2br = CT([128, 256], F32, "sc2b")
                ts_c = [Res("ts%d" % c_) for c_ in range(16)]; ti_c = [Res("ti%d" % c_) for c_ in range(16)]
                bs_h = [Res("bs%d" % h_) for h_ in range(8)]; bp_h = [Res("bp%d" % h_) for h_ in range(8)]
                top_s, tsr = CT([128, 256], F32, "top_s"); top_i, tir = CT([128, 256], U32, "top_i"); top_f, tfr = CT([128, 256], F32, "top_f")
                cand, cdr = CT([128, 2048], F32, "cand")
                best_s, bsr = CT([128, 128], F32, "best_s"); best_p, bpr = CT([128, 128], U32, "best_p")
                pf, pfr = CT([128, 128], F32, "pf"); k1f, k1r = CT([128, 128], F32, "k1f"); k2f, k2r = CT([128, 128], F32, "k2f")
                gsum, gsr = CT([128, 16], F32, "gsum")
                eq = cand
                v4 = lambda ap: ap.rearrange("p (h z k) -> p h z k", h=8, z=2)
                v3k = lambda ap: ap.rearrange("p (h k) -> p h k", h=8)
                c4 = lambda ap: ap.rearrange("p (h a b) -> p h a b", h=8, a=16)
                sc_b = [(sc, scr), (view(32, 16 * 128 * 4, F32, "p (c k) -> p c k", c=16), Res("scB"))]
                hb_b = [(hb, hbr), (view(40, D * 2, BF16), Res("hbB"))]
                hT_b = [(hT, hTr), (view(42, 8 * 128 * 2, BF16, "p (c t) -> p c t", c=8), Res("hTB"))]
                pq_b = [(pqT, pqTr), (view(44, 16 * 128 * 2, BF16, "p (c t) -> p c t", c=16), Res("pqTB"))]

                def head(t):
                    hb, hbr = hb_b[t % 2]; hT, hTr = hT_b[t % 2]; pqT, pqTr = pq_b[t % 2]; sc, scr = sc_b[t % 2]
                    h2 = H[:, t, :]; h2r = H_res[t]
                    k.op("act", lambda e: e.activation(out=hb, in_=h2, func=AF.Copy), reads=[h2r], writes=[hbr])
                    to_fm(hb, hbr, hT, [hTr], 0)
                    for g in range(4):
                        pq, pqr = PSF()
                        for cc in range(4):
                            c = g * 4 + cc
                            for kc in range(8):
                                mm(pq[:, cc * 128:(cc + 1) * 128], pwq[:, kc, c * 128:(c + 1) * 128], hT[:, kc, :], kc == 0, kc == 7, [pw_r, hTr], [pqr])
                        k.op("act", lambda e: e.activation(out=pqT[:, g * 4:(g + 1) * 4, :], in_=pq[:].rearrange("p (c t) -> p c t", c=4), func=AF.Copy), reads=[pqr], writes=[pqTr])
                    for g in range(4):
                        ps_, psr_ = PSF()
                        for cc in range(4):
                            c = g * 4 + cc
                            mm(ps_[:, cc * 128:(cc + 1) * 128], pqT[:, c, :], skT[:, c % 2, :], True, True, [pqTr, pw_r], [psr_])
                        k.op("act", lambda e: e.activation(out=sc[:, g * 4:(g + 1) * 4, :], in_=ps_[:].rearrange("p (c k) -> p c k", c=4), func=AF.Copy), reads=[psr_], writes=[scr])

                def tail(t):
                    sc, scr = sc_b[t % 2]
                    def lvl1_ops(c, buf, bufr):
                        lo = slice(c * 16, c * 16 + 8); hi = slice(c * 16 + 8, c * 16 + 16)
                        tr_, ir_ = ts_c[c], ti_c[c]
                        return [
                            lambda: k.op("dve", lambda e: e.max(out=top_s[:, lo], in_=sc[:, c, :]), reads=[scr], writes=[tr_]),
                            lambda: k.op("dve", lambda e: e.max_index(out=top_i[:, lo], in_max=top_s[:, lo], in_values=sc[:, c, :]), reads=[scr, tr_], writes=[ir_]),
                            lambda: k.op("dve", lambda e: e.match_replace(out=buf[:, 0:128], in_to_replace=top_s[:, lo], in_values=sc[:, c, :], imm_value=-1e30), reads=[scr, tr_], writes=[bufr]),
                            lambda: k.op("dve", lambda e: e.max(out=top_s[:, hi], in_=buf[:, 0:128]), reads=[bufr], writes=[tr_]),
                            lambda: k.op("dve", lambda e: e.max_index(out=top_i[:, hi], in_max=top_s[:, hi], in_values=buf[:, 0:128]), reads=[bufr, tr_], writes=[ir_]),
                        ]
                    for c in range(0, 16, 2):
                        oa_ = lvl1_ops(c, sc2, sc2r); ob_ = lvl1_ops(c + 1, sc2b, sc2br)
                        for fa_, fb_ in zip(oa_, ob_):
                            fa_(); fb_()
                    k.op("dve", lambda e: e.tensor_copy(top_f, top_i), reads=ti_c, writes=[tfr])
                    k.op("dve", lambda e: e.tensor_tensor(out=c4(cand), in0=v4(top_s)[:, :, 0, :].unsqueeze(3).to_broadcast([128, 8, 16, 16]),
                                                          in1=v4(top_s)[:, :, 1, :].unsqueeze(2).to_broadcast([128, 8, 16, 16]), op=ALU.add), reads=ts_c, writes=[cdr])
                    candh = cand.rearrange("p (h c) -> p h c", h=8)
                    def lvl2_ops(h, buf, bufr):
                        lo = slice(h * 16, h * 16 + 8); hi = slice(h * 16 + 8, h * 16 + 16)
                        br_, pr_ = bs_h[h], bp_h[h]
                        return [
                            lambda: k.op("dve", lambda e: e.max(out=best_s[:, lo], in_=candh[:, h, :]), reads=[cdr], writes=[br_]),
                            lambda: k.op("dve", lambda e: e.max_index(out=best_p[:, lo], in_max=best_s[:, lo], in_values=candh[:, h, :]), reads=[cdr, br_], writes=[pr_]),
                            lambda: k.op("dve", lambda e: e.match_replace(out=buf, in_to_replace=best_s[:, lo], in_values=candh[:, h, :], imm_value=-1e30), reads=[cdr, br_], writes=[bufr]),
                            lambda: k.op("dve", lambda e: e.max(out=best_s[:, hi], in_=buf), reads=[bufr], writes=[br_]),
                            lambda: k.op("dve", lambda e: e.max_index(out=best_p[:, hi], in_max=best_s[:, hi], in_values=buf), reads=[bufr, br_], writes=[pr_]),
                        ]
                    for h in range(0, 8, 2):
                        oa_ = lvl2_ops(h, sc2, sc2r); ob_ = lvl2_ops(h + 1, sc2b, sc2br)
                        for fa_, fb_ in zip(oa_, ob_):
                            fa_(); fb_()
                    pfu = pf.bitcast(U32)
                    k.op("dve", lambda e: e.tensor_single_scalar(out=pfu, in_=best_p, scalar=4, op=ALU.logical_shift_right), reads=bp_h, writes=[pfr])
                    k.op("dve", lambda e: e.tensor_copy(k1f, pfu), reads=[pfr], writes=[k1r])
                    k.op("dve", lambda e: e.tensor_single_scalar(out=pfu, in_=best_p, scalar=15, op=ALU.bitwise_and), reads=bp_h + [k1r, pfr], writes=[pfr])
                    k.op("dve", lambda e: e.tensor_copy(k2f, pfu), reads=[pfr], writes=[k2r])
                    io4 = iota16[:, :].unsqueeze(1).unsqueeze(1).to_broadcast([128, 8, 16, 16])
                    sr = slot_res[t]
                    for (kf, kr_, z, dst) in ((k1f, k1r, 0, SI1[:, t, :]), (k2f, k2r, 1, SI2[:, t, :])):
                        k.op("dve", lambda e: e.tensor_tensor(out=c4(eq), in0=io4, in1=v3k(kf).unsqueeze(3).to_broadcast([128, 8, 16, 16]), op=ALU.is_equal), reads=[cst, kr_, cdr], writes=[cdr])
                        k.op("dve", lambda e: e.tensor_tensor(out=c4(eq), in0=c4(eq), in1=v4(top_f)[:, :, z, :].unsqueeze(2).to_broadcast([128, 8, 16, 16]), op=ALU.mult), reads=[cdr, tfr], writes=[cdr])
                        k.op("dve", lambda e: e.tensor_reduce(out=dst, in_=eq.rearrange("p (a b) -> p a b", b=16), axis=AX.X, op=ALU.add), reads=[cdr], writes=[sr])
                    gate = SG[:, t, :]
                    k.op("dve", lambda e: e.tensor_tensor(out=v3k(gate), in0=v3k(best_s), in1=v3k(best_s)[:, :, 0:1].to_broadcast([128, 8, 16]), op=ALU.subtract), reads=bs_h, writes=[sr])
                    k.op("act", lambda e: e.activation(out=gate, in_=gate, func=AF.Exp), reads=[sr], writes=[sr])
                    k.op("dve", lambda e: e.tensor_reduce(out=gsum[:, 0:8], in_=v3k(gate), axis=AX.X, op=ALU.add), reads=[sr], writes=[gsr])
                    k.op("dve", lambda e: e.reciprocal(out=gsum[:, 8:16], in_=gsum[:, 0:8]), reads=[gsr], writes=[gsr])
                    k.op("dve", lambda e: e.tensor_tensor(out=v3k(gate), in0=v3k(gate), in1=gsum[:, 8:16].unsqueeze(2).to_broadcast([128, 8, 16]), op=ALU.mult), reads=[sr, gsr], writes=[sr])
                head(0)
                for t in range(NT):
                    if t + 1 < NT:
                        head(t + 1)
                    tail(t)
                barrier()
            dbg("si1", SI1[:, 0, :], [slot_res[0]], [128, 128]); dbg("sg", SG[:, 0, :], [slot_res[0]], [128, 128])
            if stop == "I":
                finish()
                return nc, dbg_outs

            TBK = 256
            with ExitStack() as ph:
                cv = Carver(128, 160)
                NU = 3
                utiles = [(cv.alloc([128, 8, 128], BF16), Res("ut")) for _ in range(NU)]
                vtiles = [(cv.alloc([128, D], BF16), Res("vt")) for _ in range(NU)]
                hTb = cv.alloc([128, 8, TBK], BF16); hTb_res = [Res("hTb0"), Res("hTb1")]
                accs = [(cv.alloc([128, D], F32), Res("acc")) for _ in range(2)]
                hbJ = (cv.alloc([128, D], BF16), Res("hbJ"))
                g3, g3r = load_bc(ph, lng["ln3_g"], D, "g3"); b3, b3r = load_bc(ph, lng["ln3_b"], D, "b3")
                small = PRing(ph, 4, [128, 16], F32, "lnsm")
                iota128 = phase_sb(ph, [128, 128], F32, "iota128"); ior = Res("iota128")
                k.op("pool", lambda e: e.iota(iota128[:], pattern=[[1, 128]], base=0, channel_multiplier=0, allow_small_or_imprecise_dtypes=True), writes=[ior])
                slTs = [(phase_sb(ph, [128, 3, TBK], BF16, "slT"), Res("slT")) for _ in range(2)]
                oh1s = PRing(ph, 8, [128, 128], BF16, "oh1"); oh2s = PRing(ph, 8, [128, 64], BF16, "oh2")

                class VRing:
                    def __init__(self, n, shape, dt, name):
                        self.items = [(cv.alloc(shape, dt), Res(name)) for _ in range(n)]
                        self.i = 0

                    def get(self):
                        it = self.items[self.i % len(self.items)]; self.i += 1
                        return it
                gels = VRing(2, [128, TBK], F32, "gel"); pbs = VRing(2, [128, TBK], BF16, "pb")
                ui = [0]
                NTB = SEQ // TBK
                Gh = [(view(32 * h_, TBK * 64 * 2, BF16, "p (t i) -> p t i", t=TBK), Res("G%d" % h_)) for h_ in range(2)]
                psbf = [(psb[i][0][:].bitcast(F32), psb[i][1]) for i in range(2)]
                gq = [0]

                def prep_block(tb):
                    slT, slTr = slTs[tb % 2]
                    for tt in range(2):
                        pt_, ptr_ = psbf[tt]
                        for a_, arr in enumerate((SI1, SI2, SG)):
                            k.op("pe", lambda e: e.transpose(pt_[:, a_ * 128:(a_ + 1) * 128], arr[:, tb * 2 + tt, :], identf[:]), reads=[slot_res[tb * 2 + tt], cst], writes=[ptr_])
                        k.op("act", lambda e: e.activation(out=slT[:, :, tt * 128:(tt + 1) * 128], in_=pt_[:, 0:384].rearrange("p (a t) -> p a t", a=3), func=AF.Copy), reads=[ptr_], writes=[slTr])

                def g_build_jobs(tb, half):
                    slT, slTr = slTs[tb % 2]
                    Gt, Gtr = Gh[half]
                    jobs = []
                    state = {}
                    pend_pe = []
                    for tk in range(TBK):
                        def job(tk=tk):
                            j = tk % 8
                            if j == 0:
                                state["pg"] = psbf[gq[0] % 2]; gq[0] += 1
                            pg, pgr = state["pg"]
                            o1, o1r = oh1s.get(); o2, o2r = oh2s.get()
                            k.op("dve", lambda e: e.tensor_scalar(o1[:], iota128[:], slT[:, 0, tk:tk + 1], slT[:, 2, tk:tk + 1], ALU.is_equal, ALU.mult), reads=[ior, slTr], writes=[o1r])
                            k.op("dve", lambda e: e.tensor_scalar(o2[:], iota128[:, half * 64:(half + 1) * 64], slT[:, 1, tk:tk + 1], None, ALU.is_equal), reads=[ior, slTr], writes=[o2r])
                            def pe_part():
                                mm(pg[:, j * 64:(j + 1) * 64], o1[:], o2[:], True, True, [o1r, o2r], [pgr])
                                if j == 7:
                                    tq = tk // 8
                                    k.op("act", lambda e: e.activation(out=Gt[:, tq * 8:tq * 8 + 8, :], in_=pg[:].rearrange("p (t i) -> p t i", t=8), func=AF.Copy), reads=[pgr], writes=[Gtr])
                            pend_pe.append(pe_part)
                            while len(pend_pe) > 6:
                                pend_pe.pop(0)()
                        jobs.append(job)

                    def flush():
                        while pend_pe:
                            pend_pe.pop(0)()
                    jobs.append(flush)
                    return jobs

                def issue_act(tb, i2):
                    ut, utr = utiles[ui[0] % NU]; vt, vtr = vtiles[ui[0] % NU]; ui[0] += 1
                    k.dma("sp", lambda e: e.dma_start(out=ut, in_=Ub_d[i2].rearrange("p (k i) -> p k i", k=8)), writes=[utr])
                    k.dma("sp", lambda e: e.dma_start(out=vt, in_=Vb_d[i2]), writes=[vtr])
                    pa, par = psf[4 + i2 % 2]
                    for kc in range(8):
                        mm(pa[:, 0:TBK], ut[:, kc, :], hTb[:, kc, :], kc == 0, kc == 7, [utr] + hTb_res, [par])
                    gel, gelr = gels.get()
                    k.op("act", lambda e: e.activation(out=gel, in_=pa[:, 0:TBK], func=AF.Gelu), reads=[par], writes=[gelr])
                    pb_, pbr_ = pbs.get()
                    Gt, Gtr = Gh[i2 // 64]
                    k.op("dve", lambda e: e.tensor_tensor(out=pb_, in0=gel, in1=Gt[:, :, i2 % 64], op=ALU.mult), reads=[gelr, Gtr], writes=[pbr_])
                    return (pb_, pbr_, vt, vtr)

                def issue_y(i2, st_):
                    pb_, pbr_, vt, vtr = st_
                    for tt in range(2):
                        for half in range(2):
                            py, pyr = psf[tt * 2 + half]
                            mm(py[:], pb_[:, tt * 128:(tt + 1) * 128], vt[:, half * 512:(half + 1) * 512], i2 == 0, i2 == 127, [pbr_, vtr], [pyr])
                prep_block(0)
                for job in g_build_jobs(0, 0):
                    job()
                for tb in range(NTB):
                    t0 = tb * 2
                    for tt in range(2):
                        hb, hbr = hbJ
                        k.op("act", lambda e: e.activation(out=hb, in_=H[:, t0 + tt, :], func=AF.Copy), reads=[H_res[t0 + tt]], writes=[hbr])
                        to_fm(hb, hbr, hTb, [hTb_res[tt]], tt)
                    if tb + 1 < NTB:
                        prep_block(tb + 1)
                    jobs_lo = g_build_jobs(tb, 1)
                    jobs_hi = g_build_jobs(tb + 1, 0) if tb + 1 < NTB else []
                    pend = issue_act(tb, 0)
                    for i2 in range(128):
                        jl = jobs_lo if i2 < 64 else jobs_hi
                        nxt = issue_act(tb, i2 + 1) if i2 + 1 < 128 else None
                        for _ in range(2):
                            if jl:
                                jl.pop(0)()
                        issue_y(i2, pend)
                        pend = nxt
                        for _ in range(2 if i2 % 64 < 62 else 1000):
                            if jl:
                                jl.pop(0)()
                    while jobs_hi:
                        jobs_hi.pop(0)()
                    for tt in range(2):
                        t = t0 + tt
                        acc, accr = accs[tt]
                        for half in range(2):
                            hs = slice(half * 512, (half + 1) * 512)
                            py, pyr = psf[tt * 2 + half]
                            k.op("dve", lambda e: e.scalar_tensor_tensor(out=acc[:, hs], in0=H[:, t, hs], scalar=ALPHA, in1=py[:], op0=ALU.mult, op1=ALU.add), reads=[H_res[t], pyr], writes=[accr])
                        layer_norm(acc, [accr], g3[:], b3[:], [g3r, b3r], acc, [accr], small)
                        k.dma("sp", lambda e: e.dma_start(out=out_d[t * 128:(t + 1) * 128, :], in_=acc), reads=[accr])
                barrier()

        finish()
    return nc, dbg_outs


def _consts():
    c = {}
    c["c_ident"] = np.eye(128, dtype=np.float32)
    ob = np.zeros((128, 128), np.float32); ob[:64, :64] = 1.0; ob[64:, 64:] = 1.0
    c["c_onesbd"] = ob
    s = np.arange(64)
    lt = (s[:, None] < s[None, :]).astype(np.float32)
    le = (s[:, None] <= s[None, :]).astype(np.float32)

    def mk(strict, incl):
        m = np.zeros((128, 512), np.float32)
        bd = np.zeros((128, 128), np.float32); bd[:64, :64] = strict; bd[64:, 64:] = strict
        pl = np.concatenate([incl, incl], axis=0)
        m[:, 0:128] = bd; m[:, 128:192] = pl; m[:, 192:320] = bd; m[:, 320:384] = pl
        bdT = np.zeros((128, 128), np.float32); bdT[:64, :64] = strict.T; bdT[64:, 64:] = strict.T
        m[:, 384:512] = bdT
        return m
    c["c_maskF"] = mk(lt, le)
    c["c_maskB"] = mk(lt.T.copy(), le.T.copy())
    r = np.ones((128, BLK), np.float32); r[:, ::CH] = 0.0
    c["c_rst"] = r
    c["c_iota"] = np.broadcast_to(np.arange(16, dtype=np.float32), (128, 16)).copy()
    return c


def prep_shared(inp):
    f = lambda a: np.ascontiguousarray(np.asarray(a, dtype=np.float32))
    sh = {}
    for n in ("ln_emb_g", "ln_emb_b"):
        sh[n] = f(inp[n]).reshape(1, D)
    for n in ("ln1_g", "ln1_b", "ln2_g", "ln2_b", "ln3_g", "ln3_b", "mem_ln_g", "mem_ln_b"):
        sh[n] = f(inp[n][0]).reshape(1, D)
    sh["w_in"] = f(inp["w_in"][0])
    sh["mu"] = f(inp["rwkv_mu"][0]).reshape(1, RWKV_COLS)
    tr = lambda a: f(np.asarray(a).reshape(-1, 4, 128).transpose(2, 0, 1).reshape(128, -1))
    sh["w0T"] = tr(inp["rwkv_w0"][0]); sh["a0T"] = tr(inp["rwkv_a0"][0])
    sh["w2"] = f(inp["rwkv_w2"][0]).reshape(128, RW); sh["a2"] = f(inp["rwkv_a2"][0]).reshape(128, RW)
    sh["g2"] = f(inp["rwkv_g2"][0])
    cols = [inp["rwkv_k_k"][0], inp["rwkv_k_a"][0], np.asarray(inp["rwkv_r_k"][0]).reshape(-1), inp["rwkv_gn_g"][0], inp["rwkv_gn_b"][0]]
    sh["pp"] = f(np.concatenate([np.asarray(c).reshape(4, 128).T for c in cols], axis=1))
    sh["gln_g"] = f(inp["gmlp_ln_g"][0]).reshape(1, 512); sh["gln_b"] = f(inp["gmlp_ln_b"][0]).reshape(1, 512)
    sh["wsT"] = f(np.asarray(inp["gmlp_w_s"][0]).transpose(2, 0, 1))
    bs = np.repeat(np.asarray(inp["gmlp_b_s"][0]), 64, axis=0)
    sh["bsF"] = f(bs.reshape(4, 128, 128).transpose(1, 0, 2))
    sh["w_branch"] = f(inp["w_branch"][0]).reshape(1024, D)
    sh["w_mix"] = f(inp["w_mix_out"][0])
    sh["wq"] = f(inp["xattn_w_q"][0]); sh["wkv"] = f(inp["xattn_w_kv"][0]); sh["wo"] = f(inp["xattn_w_o"][0])
    sh["pwq"] = f(inp["peer_w_query"][0])
    sh["skT"] = f(np.asarray(inp["peer_sub_keys"][0]).transpose(2, 0, 1))
    sh["puT"] = f(np.asarray(inp["peer_u"][0]).reshape(128, 128, 8, 128).transpose(1, 3, 2, 0).reshape(128, 128, D))
    sh["pvP"] = f(np.asarray(inp["peer_v"][0]).reshape(128, 128, D).transpose(1, 0, 2))
    sh.update(_consts())
    return sh


def make_in_maps(inp, cores):
    sh = prep_shared(inp)
    maps = []
    for b in cores:
        m = dict(sh)
        m["x"] = np.ascontiguousarray(np.asarray(inp["x"][b], dtype=np.float32))
        m["mem"] = np.ascontiguousarray(np.asarray(inp["mem"][b], dtype=np.float32))
        maps.append(m)
    return maps


def kernel(**inputs):
    nc, _ = build_program()
    maps = make_in_maps(inputs, list(range(N_CORES)))
    res = run_bass_kernel_spmd(nc, maps, core_ids=list(range(N_CORES)))
    return np.stack([np.asarray(r["out"], dtype=np.float32) for r in res.results], axis=0)
```

```python
from contextlib import ExitStack

import numpy as np
import concourse.bass as bass
import concourse.mybir as mybir
from concourse.bass_utils import run_bass_kernel_spmd

F32 = mybir.dt.float32
BF16 = mybir.dt.bfloat16
I32 = mybir.dt.int32
U32 = mybir.dt.uint32
AF = mybir.ActivationFunctionType
ALU = mybir.AluOpType
AX = mybir.AxisListType

N_CORES = 8
SEQ = 2048
D = 1024
NT = SEQ // 128


class Res:
    __slots__ = ("name", "w", "r", "excl")

    def __init__(self, name="", excl=False):
        self.name = name
        self.excl = excl
        self.w = None
        self.r = {}


class KB:
    def __init__(self, nc, es):
        self.nc = nc
        self.es = es
        self.eng = {"pe": nc.tensor, "act": nc.scalar, "dve": nc.vector, "pool": nc.gpsimd, "sp": nc.sync}
        self.sem = {}
        self.cnt = {}
        self.seen = {}
        for e in self.eng:
            self.sem[e] = es.enter_context(nc.semaphore("sem_" + e))
            self.cnt[e] = 0
            self.seen[e] = {}
        self.dsem = {}
        for q, n in (("sp", 8), ("pool", 8), ("act", 2)):
            self.dsem[q] = [[es.enter_context(nc.semaphore("dsem_%s%d" % (q, i))), 0] for i in range(n)]
        self.dptr = {q: 0 for q in self.dsem}
        self.n_ins = 0
        self.n_wait = 0
        self._id = 0
        self.defer = None

    def sb(self, shape, dt, name=None):
        self._id += 1
        return self.es.enter_context(self.nc.sbuf_tensor("%s_%d" % (name or "t", self._id), list(shape), dt))

    def ps(self, shape, dt, name=None):
        self._id += 1
        return self.es.enter_context(self.nc.psum_tensor("%s_%d" % (name or "p", self._id), list(shape), dt))

    def _deps(self, reads, writes):
        deps = []
        for r in reads:
            if r.w is not None:
                deps.append(r.w)
        for w in writes:
            if w.w is not None:
                deps.append(w.w)
            deps.extend(w.r.values())
        return deps

    def _wait(self, e, deps):
        eng = self.eng[e]
        seen = self.seen[e]
        best = {}
        for (sem, val) in deps:
            k = id(sem)
            if seen.get(k, 0) >= val:
                continue
            if k not in best or best[k][1] < val:
                best[k] = (sem, val)
        for k, (sem, val) in best.items():
            if e == "pe" and sem is self.sem["pe"]:
                continue
            eng.wait_ge(sem, val)
            self.n_wait += 1
            seen[k] = val

    def _mark(self, tok, reads, writes):
        for r in reads:
            k = id(tok[0])
            r.r[k] = tok
        for w in writes:
            w.w = tok
            w.r = {}

    def mark(self):
        if self.defer is not None:
            self.defer.append(None)

    def op(self, e, fn, reads=(), writes=()):
        if self.defer is not None:
            self.defer.append((e, fn, list(reads), list(writes)))
            return None
        ex = [r for r in reads if r.excl and r not in writes]
        if ex:
            writes = list(writes) + ex
        self._wait(e, self._deps(reads, writes))
        ins = fn(self.eng[e])
        self.cnt[e] += 1
        ins.then_inc(self.sem[e], 1)
        tok = (self.sem[e], self.cnt[e])
        self._mark(tok, reads, writes)
        self.n_ins += 1
        return tok

    def dma(self, q, fn, reads=(), writes=()):
        slots = self.dsem[q]
        slot = slots[self.dptr[q] % len(slots)]
        self.dptr[q] += 1
        deps = self._deps(reads, writes)
        if slot[1] > 0:
            deps.append((slot[0], slot[1]))
        self._wait(q, deps)
        ins = fn(self.eng[q])
        slot[1] += 16
        ins.then_inc(slot[0], 16)
        tok = (slot[0], slot[1])
        self._mark(tok, reads, writes)
        self.n_ins += 1
        return tok

    def wait_all(self, e, ress):
        deps = []
        for r in ress:
            if r.w is not None:
                deps.append(r.w)
            deps.extend(r.r.values())
        self._wait(e, deps)


class Ring:
    def __init__(self, k, n, shape, dt, name="ring"):
        self.items = [(k.sb(shape, dt, name), Res(name)) for _ in range(n)]
        self.i = 0

    def get(self):
        it = self.items[self.i % len(self.items)]
        self.i += 1
        return it


RW = 512
NHP = 4
RWKV_COLS = 1920
GM0 = 1920
GT0 = 2944
C0 = float(np.exp(-0.5))
ALPHA = float(2.0 ** 0.25)
LN_EPS = 1e-5
GN_EPS = 64e-5
CH = 64
BLK = 128
NCH = SEQ // CH
NBLK = SEQ // BLK


def build_program(debug=(), stop=None, scan_steps=None, scan_sub=99):
    nc = bass.Bass("TRN2", target_bir_lowering=False)
    dbg_outs = {}

    def din(name, shape, dt=F32):
        return nc.dram_tensor(name, list(shape), dt, kind="ExternalInput").ap()

    x = din("x", [SEQ, D]); mem = din("mem", [256, D])
    lng = {n: din(n, [1, D]) for n in ("ln_emb_g", "ln_emb_b", "ln1_g", "ln1_b", "ln2_g", "ln2_b", "ln3_g", "ln3_b", "mem_ln_g", "mem_ln_b")}
    w_in = din("w_in", [D, 4992]); mu_d = din("mu", [1, RWKV_COLS])
    w0T_d = din("w0T", [128, 8]); a0T_d = din("a0T", [128, 8])
    w2_d = din("w2", [128, RW]); a2_d = din("a2", [128, RW]); g2_d = din("g2", [128, RW])
    pp_d = din("pp", [128, 20])
    gln_g_d = din("gln_g", [1, 512]); gln_b_d = din("gln_b", [1, 512])
    wsT_d = din("wsT", [128, 8, 128]); bsF_d = din("bsF", [128, 4, 128])
    wbr_d = din("w_branch", [1024, D]); wmix_d = din("w_mix", [D, D])
    wq_d = din("wq", [D, D]); wkv_d = din("wkv", [D, 2 * D]); wo_d = din("wo", [D, D])
    pwq_d = din("pwq", [D, 2048]); skT_d = din("skT", [128, 2, 128])
    pu_d = din("puT", [128, 128, D]); pv_d = din("pvP", [128, 128, D])
    Ub_d = nc.dram_tensor("Ub", [128, 128, D], BF16, kind="Internal").ap()
    Vb_d = nc.dram_tensor("Vb", [128, 128, D], BF16, kind="Internal").ap()
    ident_d = din("c_ident", [128, 128]); onesbd_d = din("c_onesbd", [128, 128])
    maskF_d = din("c_maskF", [128, 512]); maskB_d = din("c_maskB", [128, 512]); rst_d = din("c_rst", [128, BLK])
    iota_d = din("c_iota", [128, 16])
    out_d = nc.dram_tensor("out", [SEQ, D], F32, kind="ExternalOutput").ap()

    es = ExitStack()
    with es:
        k = KB(nc, es)
        RAW = k.sb([128, 40960], F32, "raw")

        def view(off_kb, nbytes, dt, pattern=None, **kw):
            w0 = int(off_kb * 256)
            v = RAW[:, w0:w0 + nbytes // 4]
            if dt != F32:
                v = v.bitcast(dt)
            if pattern:
                v = v.rearrange(pattern, **kw)
            return v

        def barrier():
            toks = [(k.sem[e], k.cnt[e]) for e in k.eng if k.cnt[e] > 0]
            for q in k.dsem:
                for s in k.dsem[q]:
                    if s[1] > 0:
                        toks.append((s[0], s[1]))
            for e in k.eng:
                k._wait(e, toks)

        dbg_res = []

        def dbg(name, ap, res, shape):
            if name not in debug:
                return
            o = nc.dram_tensor("dbg_" + name, list(shape), F32, kind="ExternalOutput").ap()
            dbg_outs[name] = o
            barrier()
            if ap.dtype != F32:
                tmp = k.sb(list(shape), F32, "dbgtmp"); tr = Res()
                k.op("dve", lambda e: e.tensor_copy(tmp[:], ap), reads=res, writes=[tr])
                k.dma("sp", lambda e: e.dma_start(out=o, in_=tmp[:]), reads=[tr])
                dbg_res.append(tr)
            else:
                rr = Res()
                k.dma("sp", lambda e: e.dma_start(out=o, in_=ap), reads=res, writes=[rr])
                dbg_res.append(rr)

        def finish():
            barrier()

        cst = Res("consts")
        ident = k.sb([128, 128], BF16, "ident"); identf = k.sb([128, 128], F32, "identf")
        onesbd = k.sb([128, 128], BF16, "onesbd"); ones64 = k.sb([128, 128], F32, "ones64")
        maskF = k.sb([128, 512], BF16, "maskF"); maskB = k.sb([128, 512], BF16, "maskB")
        rst = k.sb([128, BLK], F32, "rst"); iota16 = k.sb([128, 16], F32, "iota16")
        w0T = k.sb([128, 8], F32, "w0T"); a0T = k.sb([128, 8], F32, "a0T"); pp = k.sb([128, 20], F32, "pp")
        ppx = k.sb([128, 8], F32, "ppx")
        w2b = k.sb([128, RW], BF16, "w2b"); a2b = k.sb([128, RW], BF16, "a2b"); g2b = k.sb([128, RW], BF16, "g2b")
        epsln = k.sb([128, 1], F32, "epsln"); epsgn = k.sb([128, 1], F32, "epsgn")
        for (t, d_) in ((ident, ident_d), (onesbd, onesbd_d), (maskF, maskF_d), (maskB, maskB_d), (w2b, w2_d), (a2b, a2_d), (g2b, g2_d)):
            k.dma("pool", lambda e: e.dma_start(out=t[:], in_=d_), writes=[cst])
        for (t, d_) in ((identf, ident_d), (rst, rst_d), (iota16, iota_d), (w0T, w0T_d), (a0T, a0T_d), (pp, pp_d)):
            k.dma("sp", lambda e: e.dma_start(out=t[:], in_=d_), writes=[cst])
        nw0T = k.sb([128, 8], F32, "nw0T"); na0T = k.sb([128, 8], F32, "na0T"); one1 = k.sb([128, 1], F32, "one1")
        k.op("dve", lambda e: e.memset(one1[:], 1.0), writes=[cst])
        k.op("dve", lambda e: e.memset(epsln[:], LN_EPS), writes=[cst])
        k.op("dve", lambda e: e.memset(epsgn[:], GN_EPS), writes=[cst])
        k.op("dve", lambda e: e.tensor_scalar(ones64[:], identf[:], 0.0, 0.0, ALU.mult, ALU.add), reads=[cst], writes=[cst])
        k.op("dve", lambda e: e.tensor_scalar(ones64[:], onesbd[:], 1.0 / 64.0, None, ALU.mult), reads=[cst], writes=[cst])
        k.op("dve", lambda e: e.tensor_scalar(nw0T[:], w0T[:], -1.0, None, ALU.mult), reads=[cst], writes=[cst])
        k.op("dve", lambda e: e.tensor_scalar(na0T[:], a0T[:], -1.0, None, ALU.mult), reads=[cst], writes=[cst])
        k.op("dve", lambda e: e.tensor_scalar(ppx[:, 0:4], pp[:, 4:8], -1.0, 1.0, ALU.mult, ALU.add), reads=[cst], writes=[cst])

        def act_sigmoid(eng_op, out, in_, nbias, reads, writes):
            eng_op("act", lambda e: e.activation(out=out, in_=in_, func=AF.Exp, bias=nbias, scale=-1.0), reads=reads + [cst], writes=writes)
            eng_op("act", lambda e: e.activation(out=out, in_=out, func=AF.Ln, bias=one1[:, 0:1], scale=1.0), reads=writes + [cst], writes=writes)
            eng_op("act", lambda e: e.activation(out=out, in_=out, func=AF.Exp, scale=-1.0), reads=writes, writes=writes)
        k.op("dve", lambda e: e.tensor_scalar(ppx[:, 4:8], pp[:, 4:8], -2.0, 2.0, ALU.mult, ALU.add), reads=[cst], writes=[cst])
        KK = lambda hp: pp[:, hp:hp + 1]
        KA = lambda hp: pp[:, 4 + hp:5 + hp]
        RK = lambda hp: pp[:, 8 + hp:9 + hp]
        GNG = lambda hp: pp[:, 12 + hp:13 + hp]
        GNB = lambda hp: pp[:, 16 + hp:17 + hp]

        psf = [(k.ps([128, 512], F32, "psf"), Res("psf%d" % i, True)) for i in range(6)]
        psb = [(k.ps([128, 1024], BF16, "psb"), Res("psb%d" % i, True)) for i in range(2)]
        pctr = {"f": 0, "b": 0}

        def PSF():
            it = psf[pctr["f"] % 6]; pctr["f"] += 1
            return it

        def PSB():
            it = psb[pctr["b"] % 2]; pctr["b"] += 1
            return it

        def mm(out, lhsT, rhs, start, stop, reads, writes):
            k.op("pe", lambda e: e.matmul(out, lhsT, rhs, start=start, stop=stop), reads=reads, writes=writes)

        def load_bc(ph, d_ap, n, name):
            t = ph.enter_context(nc.sbuf_tensor(name + "_%d" % k._id, [128, n], F32)); k._id += 1
            r = Res(name)
            k.dma("sp", lambda e: e.dma_start(out=t[:], in_=d_ap.partition_broadcast(128)), writes=[r])
            return t, r

        def phase_sb(ph, shape, dt, name):
            k._id += 1
            return ph.enter_context(nc.sbuf_tensor("%s_%d" % (name, k._id), list(shape), dt))

        class PRing:
            def __init__(self, ph, n, shape, dt, name):
                self.items = [(phase_sb(ph, shape, dt, name), Res(name)) for _ in range(n)]
                self.i = 0

            def get(self):
                it = self.items[self.i % len(self.items)]; self.i += 1
                return it

        def layer_norm(src, src_res, gam, bet, gbres, out, out_res, small, n=1024, eps=None):
            eps = eps if eps is not None else epsln
            nchk = n // 512
            st, sr = small.get()
            for c in range(nchk):
                k.op("dve", lambda e, c=c: e.bn_stats(out=st[:, c * 6:(c + 1) * 6], in_=src[:, c * 512:(c + 1) * 512]), reads=src_res, writes=[sr])
            k.op("dve", lambda e: e.bn_aggr(out=st[:, 12:14], in_=st[:, 0:6 * nchk]), reads=[sr], writes=[sr])
            k.op("act", lambda e: e.activation(out=st[:, 14:15], in_=st[:, 13:14], func=AF.Sqrt, bias=eps[:, 0:1], scale=1.0), reads=[sr, cst], writes=[sr])
            k.op("dve", lambda e: e.reciprocal(out=st[:, 14:15], in_=st[:, 14:15]), reads=[sr], writes=[sr])
            k.op("dve", lambda e: e.tensor_scalar(st[:, 15:16], st[:, 12:13], st[:, 14:15], -1.0, ALU.mult, ALU.mult), reads=[sr], writes=[sr])
            k.op("act", lambda e: e.activation(out=out, in_=src, func=AF.Identity, bias=st[:, 15:16], scale=st[:, 14:15]), reads=src_res + [sr], writes=out_res)
            k.op("dve", lambda e: e.tensor_tensor(out=out, in0=out, in1=gam, op=ALU.mult), reads=out_res + gbres, writes=out_res)
            k.op("dve", lambda e: e.tensor_tensor(out=out, in0=out, in1=bet, op=ALU.add), reads=out_res + gbres, writes=out_res)

        def to_fm(hb, hbr, dstT, dst_res, t):
            pt, ptr_ = PSB()
            for c in range(8):
                k.op("pe", lambda e, c=c: e.transpose(pt[:, c * 128:(c + 1) * 128], hb[:, c * 128:(c + 1) * 128], ident[:]), reads=[hbr, cst], writes=[ptr_])
            k.op("act", lambda e: e.activation(out=dstT[:, :, t * 128:(t + 1) * 128], in_=pt[:].rearrange("p (c t) -> p c t", t=128), func=AF.Copy), reads=[ptr_], writes=dst_res)

        def run_pairs(n, body):
            for t_ in range(0, n, 2):
                recs = []
                for tt_ in (t_, t_ + 1):
                    if tt_ >= n:
                        continue
                    k.defer = []
                    body(tt_)
                    recs.append([x_ for x_ in k.defer if x_ is not None]); k.defer = None
                for i_ in range(max(len(r_) for r_ in recs)):
                    for r_ in recs:
                        if i_ < len(r_):
                            k.op(*r_[i_])

        def phase_h0T(h0T, h0T_res):
            with ExitStack() as ph:
                g_t, g_r = load_bc(ph, lng["ln_emb_g"], D, "g")
                b_t, b_r = load_bc(ph, lng["ln_emb_b"], D, "b")
                xs = PRing(ph, 2, [128, D], F32, "xs")
                hbs = PRing(ph, 2, [128, D], BF16, "hb")
                small = PRing(ph, 4, [128, 16], F32, "lnsm")
                def tile_body(t):
                    xt, xr = xs.get()
                    k.dma("sp", lambda e: e.dma_start(out=xt[:], in_=x[t * 128:(t + 1) * 128, :]), writes=[xr])
                    layer_norm(xt[:], [xr], g_t[:], b_t[:], [g_r, b_r], xt[:], [xr], small)
                    hb, hbr = hbs.get()
                    k.op("act", lambda e: e.activation(out=hb[:], in_=xt[:], func=AF.Copy), reads=[xr], writes=[hbr])
                    to_fm(hb, hbr, h0T, [h0T_res[t]], t)
                for t_ in range(NT):
                    tile_body(t_)
                barrier()

        h0T = view(64, 32768, BF16, "p (c t) -> p c t", c=8)
        h0T_res = [Res("h0T%d" % t) for t in range(NT)]
        phase_h0T(h0T, h0T_res)
        dbg("h0T", h0T[:, 0, :], h0T_res, [128, SEQ])
        if stop == "A":
            finish()
            return nc, dbg_outs

        shT = view(96, 32768, BF16, "p (c t) -> p c t", c=8); shr = Res("shT")
        rkv = view(0, 12 * SEQ * 2, BF16, "p (c t) -> p c t", c=12)
        wag = view(48, 3 * SEQ * 2, BF16, "p (c t) -> p c t", c=3)
        rkv_res = [[Res("rkv") for _ in range(4)] for _ in range(15)]
        for c in range(8):
            k.op("dve", lambda e: e.tensor_tensor(out=shT[:, c, 1:SEQ - 1], in0=h0T[:, c, 0:SEQ - 2], in1=h0T[:, c, 2:SEQ], op=ALU.add), reads=h0T_res, writes=[shr])
        k.op("dve", lambda e: e.tensor_copy(shT[:, :, 0:1], h0T[:, :, 1:2]), reads=h0T_res, writes=[shr])
        k.op("dve", lambda e: e.tensor_copy(shT[:, :, SEQ - 1:SEQ], h0T[:, :, SEQ - 2:SEQ - 1]), reads=h0T_res, writes=[shr])
        with ExitStack() as ph:
            wsts = PRing(ph, 2, [128, 8, 128], F32, "wst")
            was = PRing(ph, 2, [128, 8, 128], BF16, "wa")
            wbs = PRing(ph, 2, [128, 8, 128], BF16, "wb")
            mus = PRing(ph, 2, [128, 384], F32, "mu")
            for cc in range(15):
                c0 = cc * 128
                wst, wsr = wsts.get()
                k.dma("sp", lambda e: e.dma_start(out=wst[:], in_=w_in[:, c0:c0 + 128].rearrange("(k p) c -> p k c", p=128)), writes=[wsr])
                mt, mr = mus.get()
                k.dma("sp", lambda e: e.dma_start(out=mt[:, 0:128], in_=mu_d[:, c0:c0 + 128].partition_broadcast(128)), writes=[mr])
                k.op("dve", lambda e: e.tensor_scalar(mt[:, 128:256], mt[:, 0:128], -1.0, 1.0, ALU.mult, ALU.add), reads=[mr], writes=[mr])
                k.op("dve", lambda e: e.tensor_scalar(mt[:, 256:384], mt[:, 0:128], 0.5, None, ALU.mult), reads=[mr], writes=[mr])
                wa, war = was.get(); wb, wbr = wbs.get()
                k.op("dve", lambda e: e.tensor_tensor(out=wa[:], in0=wst[:], in1=mt[:, 128:256].unsqueeze(1).to_broadcast([128, 8, 128]), op=ALU.mult), reads=[wsr, mr], writes=[war])
                k.op("pool", lambda e: e.tensor_tensor(out=wb[:], in0=wst[:], in1=mt[:, 256:384].unsqueeze(1).to_broadcast([128, 8, 128]), op=ALU.mult), reads=[wsr, mr], writes=[wbr])
                for tb in range(4):
                    pb, pbr = PSF()
                    ts = slice(tb * 512, (tb + 1) * 512)
                    for kc in range(8):
                        mm(pb[:], wa[:, kc, :], h0T[:, kc, ts], kc == 0, False, [war] + h0T_res[4 * tb:4 * tb + 4], [pbr])
                    for kc in range(8):
                        mm(pb[:], wb[:, kc, :], shT[:, kc, ts], False, kc == 7, [wbr, shr], [pbr])
                    if cc < 12:
                        dest, fn = rkv[:, cc, ts], AF.Copy
                    else:
                        dest, fn = wag[:, cc - 12, ts], (AF.Tanh, AF.Copy, AF.Sigmoid)[cc - 12]
                    k.op("act", lambda e: e.activation(out=dest, in_=pb[:], func=fn), reads=[pbr], writes=[rkv_res[cc][tb]])
            barrier()
        dbg("r0", rkv[:, 0, :], rkv_res[0], [128, SEQ])
        dbg("k0", rkv[:, 4, :], rkv_res[4], [128, SEQ])
        dbg("wd", wag[:, 0, :], rkv_res[12], [128, SEQ])
        if stop == "B":
            finish()
            return nc, dbg_outs


        barrier()
        yaT = rkv[:, 0:4, :]
        yaT_res = [rkv_res[hp] for hp in range(4)]

        class Carver:
            def __init__(self, a_kb, b_kb):
                self.p = a_kb * 1024; self.end = b_kb * 1024

            def alloc(self, shape, dt):
                nb = int(np.prod(shape[1:])) * (2 if dt == BF16 else 4)
                nb = (nb + 31) // 32 * 32
                assert self.p + nb <= self.end, "carver overflow"
                v = RAW[:, self.p // 4:(self.p + nb) // 4]
                self.p += nb
                if dt != F32:
                    v = v.bitcast(dt)
                n = int(np.prod(shape[1:]))
                v = v[:, 0:n]
                if len(shape) == 3:
                    v = v.rearrange("p (a b) -> p a b", a=shape[1])
                return v

        def scan_round(rnd):
            hps = [2 * rnd, 2 * rnd + 1]
            carve = Carver(64, 128)
            carve2 = Carver(144, 160)
            yacc = view(128, 2 * SEQ * 4, F32, "p (c t) -> p c t", c=2)
            yacc_res = [[Res("yacc") for _ in range(NCH)] for _ in range(2)]
            if scan_steps:
                for hl_ in range(2):
                    k.op("pool", lambda e: e.memset(yacc[:, hl_, :], 0.0), writes=yacc_res[hl_])
                    for r_ in yacc_res[hl_]:
                        r_.w = None
            with ExitStack() as ph:
                def T(shape, dt, name):
                    return (phase_sb(ph, shape, dt, name), Res(name))
                tmp = {n: T([128, BLK], F32, n) for n in ("sw", "sa", "cs", "pin", "pex", "ege", "egi", "kr", "rn", "kk", "nkk", "t1", "kd", "bb")}
                ksq = T([128, BLK], BF16, "ksq")
                yb = T([128, BLK], BF16, "yb"); ysqb = T([128, BLK], BF16, "ysqb")
                NBK = NCH // 2
                streams = []
                for z in (0, 1):
                    for hl, hp in enumerate(hps):
                        st = dict(z=z, hp=hp, hl=hl, ui=len(streams))
                        st["A3"] = [dict(AR=carve.alloc([128, 2, 192], BF16), eg=carve.alloc([128, BLK], F32), res=Res("prepA")) for _ in range(3)]
                        st["K2"] = [dict(Kb=carve.alloc([128, 2, 128], BF16), Bb=carve.alloc([128, 2, 128], BF16), Vb=carve.alloc([128, 2, 128], BF16), res=Res("prepK")) for _ in range(2)]
                        for sl in st["A3"]:
                            k.op("pool", lambda e: e.memset(sl["AR"], 0.0), writes=[sl["res"]])
                        for sl in st["K2"]:
                            for nm in ("Kb", "Bb", "Vb"):
                                k.op("pool", lambda e: e.memset(sl[nm], 0.0), writes=[sl["res"]])
                        st["Tm"] = [[(carve2.alloc([128, 512], BF16), Res("Tm")) for _ in range(2)] for _ in range(2)]
                        st["WT"] = T([128, 128], BF16, "WT"); st["UT"] = T([128, 128], BF16, "UT")
                        st["S"] = [T([128, 128], BF16, "S") for _ in range(2)]
                        k.op("pool", lambda e: e.memset(st["S"][0][0][:], 0.0), writes=[st["S"][0][1]])
                        st["si"] = 0
                        streams.append(st)
                KTG = [[(carve.alloc([128, 4, 384], BF16), Res("KTG")) for _ in range(2)] for _ in range(2)]
                MVG = [[(carve.alloc([128, 4, 128], BF16), Res("MVG")) for _ in range(2)] for _ in range(2)]
                IVG = [[(carve.alloc([128, 3, 512], BF16), Res("IVG")) for _ in range(2)] for _ in range(2)]
                chain_res = [Res("chain%d" % i, True) for i in range(2)]
                psbf = [(psb[i][0][:].bitcast(F32), psb[i][1]) for i in range(2)]
                lvl_banks = [[psf[0], psf[1], psf[2]], [psf[3], psbf[0], psbf[1]]]

                def blk_of(st, bi):
                    return bi if st["z"] == 0 else NBK - 1 - bi

                def chunk_of(st, bi, g):
                    b = blk_of(st, bi)
                    return 2 * b + g if st["z"] == 0 else 2 * b + 1 - g

                def prep(st, bi, tmp, ksq, pcol):
                    z, hp = st["z"], st["hp"]
                    b = blk_of(st, bi)
                    sa_ = st["A3"][bi % 3]; sk_ = st["K2"][bi % 2]
                    t0 = b * BLK; ts = slice(t0, t0 + BLK); tb = t0 // 512
                    zs = slice(z * 64, (z + 1) * 64)
                    hc = slice(hp * 128, (hp + 1) * 128)
                    r_ap, k_ap, v_ap = rkv[:, hp, ts], rkv[:, 4 + hp, ts], rkv[:, 8 + hp, ts]
                    rr, kr_, vr_ = [rkv_res[hp][tb]], [rkv_res[4 + hp][tb]], [rkv_res[8 + hp][tb]]
                    pb, pbr = psbf[0][0][:, pcol:pcol + 256], psbf[0][1]
                    mm(pb[:, 0:BLK], w2b[zs, hc], wag[zs, 0, ts], True, True, [cst, rkv_res[12][tb]], [pbr])
                    mm(pb[:, BLK:2 * BLK], a2b[zs, hc], wag[zs, 1, ts], True, True, [cst, rkv_res[13][tb]], [pbr])
                    (sw, swr), (sa, sar), (cs, csr) = tmp["sw"], tmp["sa"], tmp["cs"]
                    (pin, pinr), (pex, pexr), (ege, eger), (egi, egir) = tmp["pin"], tmp["pex"], tmp["ege"], tmp["egi"]
                    act_sigmoid(k.op, sw[:], pb[:, 0:BLK], nw0T[:, z * 4 + hp:z * 4 + hp + 1], [pbr], [swr])
                    act_sigmoid(k.op, sa[:], pb[:, BLK:2 * BLK], na0T[:, z * 4 + hp:z * 4 + hp + 1], [pbr], [sar])
                    k.mark()
                    k.op("dve", lambda e: e.tensor_tensor_scan(out=cs[:], data0=rst[:], data1=sw[:], initial=0.0, op0=ALU.mult, op1=ALU.add), reads=[swr, cst], writes=[csr])
                    v3 = lambda t_: t_.rearrange("p (c t) -> p c t", t=CH)
                    if z == 0:
                        k.op("dve", lambda e: e.tensor_tensor(out=pex[:], in0=cs[:], in1=sw[:], op=ALU.subtract), reads=[csr, swr], writes=[pexr])
                        pin_t, pin_r = cs, csr
                    else:
                        k.op("dve", lambda e: e.tensor_tensor(out=v3(pex[:]), in0=v3(cs[:])[:, :, CH - 1:CH].to_broadcast([128, 2, CH]), in1=v3(cs[:]), op=ALU.subtract), reads=[csr], writes=[pexr])
                        k.op("dve", lambda e: e.tensor_tensor(out=pin[:], in0=pex[:], in1=sw[:], op=ALU.add), reads=[pexr, swr], writes=[pinr])
                        pin_t, pin_r = pin, pinr
                    eg = sa_["eg"]; rA = sa_["res"]; rK = sk_["res"]
                    k.op("act", lambda e: e.activation(out=eg, in_=pin_t[:], func=AF.Exp, scale=-C0), reads=[pin_r], writes=[rA])
                    k.op("act", lambda e: e.activation(out=ege[:], in_=pex[:], func=AF.Exp, scale=-C0), reads=[pexr], writes=[eger])
                    k.op("act", lambda e: e.activation(out=egi[:], in_=pin_t[:], func=AF.Exp, scale=C0), reads=[pin_r], writes=[egir])
                    k.mark()
                    (kr, krr), (rn, rnr), (kk, kkr) = tmp["kr"], tmp["rn"], tmp["kk"]
                    k.op("dve", lambda e: e.tensor_scalar(kr[:], k_ap, KK(hp), None, ALU.mult), reads=kr_ + [cst], writes=[krr])
                    k.op("dve", lambda e: e.tensor_tensor(out=ksq[0][:], in0=kr[:], in1=kr[:], op=ALU.mult), reads=[krr], writes=[ksq[1]])
                    pb2, pb2r = psbf[1][0][:, pcol:pcol + 256], psbf[1][1]
                    mm(pb2[:, 0:BLK], onesbd[:], ksq[0][:], True, True, [cst, ksq[1]], [pb2r])
                    k.op("dve", lambda e: e.tensor_scalar(rn[:], pb2[:, 0:BLK], 1e-24, None, ALU.max), reads=[pb2r], writes=[rnr])
                    k.mark()
                    k.op("act", lambda e: e.activation(out=rn[:], in_=rn[:], func=AF.Ln), reads=[rnr], writes=[rnr])
                    k.op("act", lambda e: e.activation(out=rn[:], in_=rn[:], func=AF.Exp, scale=-0.5), reads=[rnr], writes=[rnr])
                    k.op("dve", lambda e: e.tensor_tensor(out=kk[:], in0=kr[:], in1=rn[:], op=ALU.mult), reads=[krr, rnr], writes=[kkr])
                    nkk, nkkr = tmp["nkk"]
                    k.op("dve", lambda e: e.tensor_scalar(nkk[:], kk[:], -1.0, None, ALU.mult), reads=[kkr], writes=[nkkr])
                    AR, Kb, Bb, Vb = sa_["AR"], sk_["Kb"], sk_["Bb"], sk_["Vb"]
                    k.op("dve", lambda e: e.tensor_tensor(out=AR[:, :, 128:192], in0=v3(r_ap), in1=v3(eg), op=ALU.mult), reads=rr + [rA], writes=[rA])
                    (t1, t1r), (kd, kdr), (bb, bbr) = tmp["t1"], tmp["kd"], tmp["bb"]
                    k.op("dve", lambda e: e.tensor_scalar(t1[:], sa[:], KA(hp), ppx[:, hp:hp + 1], ALU.mult, ALU.add), reads=[sar, cst], writes=[t1r])
                    k.op("dve", lambda e: e.tensor_tensor(out=kd[:], in0=t1[:], in1=k_ap, op=ALU.mult), reads=[t1r] + kr_, writes=[kdr])
                    k.op("dve", lambda e: e.tensor_tensor(out=bb[:], in0=kk[:], in1=sa[:], op=ALU.mult), reads=[kkr, sar], writes=[bbr])
                    k.mark()

                    def half_ops(half):
                        hs = slice(half * 64, (half + 1) * 64); cs_ = slice(half * 64, (half + 1) * 64)
                        eng = "dve" if half == 0 else "pool"
                        k.op(eng, lambda e: e.tensor_tensor(out=AR[hs, :, cs_], in0=v3(nkk[hs, :]), in1=v3(ege[hs, :]), op=ALU.mult), reads=[nkkr, eger, rA], writes=[rA])
                        k.op(eng, lambda e: e.tensor_tensor(out=Kb[hs, :, cs_], in0=v3(kd[hs, :]), in1=v3(egi[hs, :]), op=ALU.mult), reads=[kdr, egir, rK], writes=[rK])
                        k.op(eng, lambda e: e.tensor_tensor(out=Bb[hs, :, cs_], in0=v3(bb[hs, :]), in1=v3(egi[hs, :]), op=ALU.mult), reads=[bbr, egir, rK], writes=[rK])
                        k.op("act", lambda e: e.activation(out=Vb[hs, :, cs_], in_=v3(v_ap)[hs], func=AF.Copy), reads=vr_ + [rK], writes=[rK])
                    half_ops(0)
                    half_ops(1)
                    k.mark()

                def unit_ctx(st, bi, g):
                    bp = bi % 2
                    sa_ = st["A3"][bi % 3]; sk_ = st["K2"][bi % 2]
                    c = chunk_of(st, bi, g); ci = c % 2
                    return dict(st=st, ui=st["ui"], c=c, ci=ci, bp=bp, g=g, rA=sa_["res"], rK=sk_["res"], eg=sa_["eg"],
                                AR=sa_["AR"][:, ci, :], Kb=sk_["Kb"][:, ci, :], Bb=sk_["Bb"][:, ci, :], Vb=sk_["Vb"][:, ci, :],
                                Tm=st["Tm"][bp][g], KT=(KTG[bp][g][0][:, st["ui"], :], KTG[bp][g][1]), Minv=(MVG[bp][g][0][:, st["ui"], :], MVG[bp][g][1]))

                def pre_block(bi):
                    groups = [[unit_ctx(st, bi, g) for st in streams] for g in range(2)]
                    macro = []

                    def t_pe(g):
                        def f():
                            for u in groups[g]:
                                ui = u["ui"]
                                pb, pbr = psf[ui]; rd = [u["rA"], u["rK"]]
                                mm(pb[:, 0:192], u["Kb"], u["AR"], True, True, rd, [pbr])
                                mm(pb[:, 192:384], u["Bb"], u["AR"], True, True, rd, [pbr])
                                mm(pb[:, 384:512], u["AR"][:, 0:128], u["Bb"], True, True, rd, [pbr])
                                pt_, ptr_ = psb[ui // 2]; pt = pt_[:, (ui % 2) * 384:(ui % 2) * 384 + 384]
                                for i, nm in enumerate(("Kb", "Bb", "Vb")):
                                    k.op("pe", lambda e: e.transpose(pt[:, i * 128:(i + 1) * 128], u[nm], ident[:]), reads=rd + [cst], writes=[ptr_])
                        return f

                    def t_ev(g):
                        def f():
                            bp = bi % 2
                            IA, IAr = IVG[g][0]
                            for u in groups[g]:
                                ui = u["ui"]
                                pb, pbr = psf[ui]; Tm, Tmr = u["Tm"]
                                msk = maskF if u["st"]["z"] == 0 else maskB
                                k.op("dve", lambda e: e.tensor_tensor(out=Tm, in0=pb[:], in1=msk[:], op=ALU.mult), reads=[pbr, cst], writes=[Tmr])
                            KTt, KTr = KTG[bp][g]
                            for j in range(2):
                                pt_, ptr_ = psb[j]
                                k.op("act", lambda e: e.activation(out=KTt[:, 2 * j:2 * j + 2, :], in_=pt_[:, 0:768].rearrange("p (u c) -> p u c", u=2), func=AF.Copy), reads=[ptr_], writes=[KTr])
                            for u in groups[g]:
                                ui = u["ui"]; Tm, Tmr = u["Tm"]
                                k.op("dve", lambda e: e.tensor_tensor(out=IA[:, 2, ui * 128:(ui + 1) * 128], in0=Tm[:, 192:320], in1=ident[:], op=ALU.add), reads=[Tmr, cst], writes=[IAr])
                        return f
                    for g in range(2):
                        macro.append([t_pe(g), t_ev(g)])

                    def lvl_pe(l, g):
                        def f():
                            (bP, bPr), (bQ, bQr), (bX, bXr) = lvl_banks[g]
                            for u in groups[g]:
                                ui = u["ui"]; cs_ = slice(ui * 128, (ui + 1) * 128)
                                if l == 1:
                                    Tm, Tmr = u["Tm"]
                                    P, Q, rd = Tm[:, 192:320], Tm[:, 384:512], [Tmr]
                                    mm(bP[:, cs_], Q, P, True, True, rd, [bPr])
                                    mm(bQ[:, cs_], P, Q, True, True, rd, [bQr])
                                else:
                                    src, srcr = IVG[g][l % 2]
                                    P, Q, X = src[:, 0, cs_], src[:, 1, cs_], src[:, 2, cs_]
                                    if l <= 4:
                                        mm(bP[:, cs_], Q, P, True, True, [srcr], [bPr])
                                    if l <= 5:
                                        mm(bQ[:, cs_], P, Q, True, True, [srcr], [bQr])
                                    mm(bX[:, cs_], ident[:], X, True, False, [srcr, cst], [bXr])
                                    mm(bX[:, cs_], Q, X, False, True, [srcr], [bXr])
                        return f

                    def lvl_ev(l, g):
                        def f():
                            (bP, bPr), (bQ, bQr), (bX, bXr) = lvl_banks[g]
                            bp = bi % 2
                            jobs = []
                            if l == 1:
                                dst, dstr = IVG[g][0]
                                jobs = [(dst[:, 0, :], bP, bPr, dstr), (dst[:, 1, :], bQ, bQr, dstr)]
                            elif l <= 5:
                                dst, dstr = IVG[g][(l + 1) % 2]
                                if l <= 4:
                                    jobs.append((dst[:, 0, :], bP, bPr, dstr))
                                jobs.append((dst[:, 1, :], bQ, bQr, dstr))
                                jobs.append((dst[:, 2, :], bX, bXr, dstr))
                            else:
                                mv, mvr = MVG[bp][g]
                                jobs = [(mv[:].rearrange("p u c -> p (u c)"), bX, bXr, mvr)]
                            for i, (o, bk, bkr, dr) in enumerate(jobs):
                                if (i + l + g) % 2 == 0:
                                    k.op("act", lambda e: e.activation(out=o, in_=bk[:, 0:512], func=AF.Copy), reads=[bkr], writes=[dr])
                                else:
                                    k.op("dve", lambda e: e.tensor_copy(o, bk[:, 0:512]), reads=[bkr], writes=[dr])
                        return f
                    for l in range(1, 7):
                        macro.append([lvl_pe(l, 0), lvl_pe(l, 1), lvl_ev(l, 0), lvl_ev(l, 1)])
                    return macro

                def chain_stages(bi, g):
                    ctx = [unit_ctx(st, bi, g) for st in streams]

                    def w_pe():
                        for u in ctx:
                            st = u["st"]; Tm, Tmr = u["Tm"]; KT, KTr = u["KT"]
                            S, Sr = st["S"][st["si"] % 2]
                            ui = u["ui"]
                            pb = psf[4 + ui // 2][0][:, (ui % 2) * 192:(ui % 2) * 192 + 192]; pbr = chain_res[ui // 2]; u["pC"] = (pb, pbr)
                            mm(pb[:, 0:128], Tm[:, 0:128], KT[:, 256:384], True, False, [Tmr, KTr], [pbr])
                            mm(pb[:, 0:128], u["AR"][:, 0:128], S[:], False, True, [u["rA"], Sr], [pbr])

                    def w_ev():
                        for u in ctx:
                            pb, pbr = u["pC"]; WT, WTr = u["st"]["WT"]
                            k.op("act", lambda e: e.activation(out=WT[:], in_=pb[:, 0:128], func=AF.Copy), reads=[pbr], writes=[WTr])

                    def u_pe():
                        for u in ctx:
                            pb, pbr = u["pC"]; WT, WTr = u["st"]["WT"]; Mi, Mir = u["Minv"]
                            mm(pb[:, 0:128], Mi, WT[:], True, True, [Mir, WTr], [pbr])

                    def u_ev():
                        for u in ctx:
                            pb, pbr = u["pC"]; UT, UTr = u["st"]["UT"]
                            k.op("dve", lambda e: e.tensor_copy(UT[:], pb[:, 0:128]), reads=[pbr], writes=[UTr])

                    def ys_pe():
                        for u in ctx:
                            st = u["st"]; Tm, Tmr = u["Tm"]; KT, KTr = u["KT"]; UT, UTr = st["UT"]
                            S, Sr = st["S"][st["si"] % 2]
                            pb, pbr = u["pC"]; rA = u["rA"]
                            mm(pb[:, 128:192], KT[:, 256:384], Tm[:, 128:192], True, False, [KTr, Tmr], [pbr])
                            mm(pb[:, 128:192], S[:], u["AR"][:, 128:192], False, False, [Sr, rA], [pbr])
                            mm(pb[:, 128:192], UT[:], Tm[:, 320:384], False, True, [UTr, Tmr], [pbr])
                            mm(pb[:, 0:128], KT[:, 0:128], KT[:, 256:384], True, False, [KTr], [pbr])
                            mm(pb[:, 0:128], ident[:], S[:], False, False, [cst, Sr], [pbr])
                            mm(pb[:, 0:128], KT[:, 128:256], UT[:], False, True, [KTr, UTr], [pbr])

                    def ys_ev():
                        for u in ctx:
                            st = u["st"]; pb, pbr = u["pC"]; c = u["c"]; ci = u["ci"]
                            Sn, Snr = st["S"][(st["si"] + 1) % 2]
                            eg = u["eg"]
                            col = ci * CH + (CH - 1 if st["z"] == 0 else 0)
                            k.op("act", lambda e: e.activation(out=Sn[:], in_=pb[:, 0:128], func=AF.Identity, scale=eg[:, col:col + 1]), reads=[pbr, u["rA"]], writes=[Snr])
                            st["si"] += 1
                            yr = yacc_res[st["hl"]][c]
                            ydst = yacc[:, st["hl"], c * CH:(c + 1) * CH]
                            if yr.w is None:
                                k.op("dve", lambda e: e.tensor_copy(ydst, pb[:, 128:192]), reads=[pbr], writes=[yr])
                            else:
                                k.op("dve", lambda e: e.tensor_tensor(out=ydst, in0=ydst, in1=pb[:, 128:192], op=ALU.add), reads=[pbr, yr], writes=[yr])
                    return [[w_pe, w_ev], [u_pe, u_ev], [ys_pe, ys_ev]]

                def replay(chunk):
                    for (e_, fn_, rd_, wr_) in chunk:
                        k.op(e_, fn_, rd_, wr_)

                tmpB = {n_: (carve.alloc([128, BLK], F32), Res(n_ + "B")) for n_ in tmp}
                ksqB = (carve.alloc([128, BLK], BF16), Res("ksqB"))

                def record_prep(bi_):
                    per_stream = []
                    for si_, st in enumerate(streams):
                        k.defer = []
                        if si_ % 2 == 0:
                            prep(st, bi_, tmp, ksq, 0)
                        else:
                            prep(st, bi_, tmpB, ksqB, 256)
                        rec = k.defer; k.defer = None
                        chunks = [[]]
                        for it in rec:
                            if it is None:
                                chunks.append([])
                            else:
                                chunks[-1].append(it)
                        per_stream.append(chunks)
                    out = []
                    for p_ in range(0, len(streams), 2):
                        ca, cb = per_stream[p_], per_stream[p_ + 1]
                        for j_ in range(max(len(ca), len(cb))):
                            a_ = ca[j_] if j_ < len(ca) else []
                            b_ = cb[j_] if j_ < len(cb) else []
                            m_ = []
                            for i_ in range(max(len(a_), len(b_))):
                                if i_ < len(a_):
                                    m_.append(a_[i_])
                                if i_ < len(b_):
                                    m_.append(b_[i_])
                            if m_:
                                out.append(m_)
                    return out
                nblk = (scan_steps + 1) // 2 if scan_steps else NBK
                for bi_ in range(min(2, nblk)):
                    for c_ in record_prep(bi_):
                        replay(c_)
                for ms in pre_block(0):
                    for f in ms:
                        f()
                for bi in range(nblk):
                    A = pre_block(bi + 1) if bi + 1 < nblk else []
                    B = [f for pr in (chain_stages(bi, 0) + chain_stages(bi, 1)) for f in pr]
                    C = record_prep(bi + 2) if bi + 2 < nblk else []
                    Af = []
                    for ms in A:
                        flags = [False, True] if len(ms) == 2 else [True, False, False, True]
                        Af += list(zip(ms, flags))
                    nb_per = max(1, (len(B) + max(1, len(Af)) - 1) // max(1, len(Af)))
                    while Af or B or C:
                        safe = True
                        if Af:
                            f, safe = Af.pop(0)
                            f()
                        for _ in range(nb_per if Af else len(B)):
                            if B:
                                B.pop(0)()
                        if C and (safe or not any(op_[0] == "pe" for op_ in C[0])):
                            replay(C.pop(0))
                        if not Af and not B:
                            while C:
                                replay(C.pop(0))
                dbg("yacc%d" % rnd, yacc[:, 0, :], yacc_res[0], [128, SEQ])
                fo = k.op
                fmm = mm
                fctr = [0]

                def FPS():
                    it = psf[fctr[0] % 4]; fctr[0] += 1
                    return it
                def fin_block(hl, hp, b, tmp, ksq, yb, ysqb, bankA, bankB):
                    hc = slice(hp * 128, (hp + 1) * 128)
                    t0 = b * BLK; ts = slice(t0, t0 + BLK); tb = t0 // 512
                    y = yacc[:, hl, ts]; yres = yacc_res[hl][2 * b:2 * b + 2]
                    r_ap, k_ap, v_ap = rkv[:, hp, ts], rkv[:, 4 + hp, ts], rkv[:, 8 + hp, ts]
                    rr, kr_, vr_ = [rkv_res[hp][tb]], [rkv_res[4 + hp][tb]], [rkv_res[8 + hp][tb]]
                    (ysq, ysqr), (mean, meanr), (msq, msqr), (var, varr) = tmp["sw"], tmp["sa"], tmp["cs"], tmp["pin"]
                    (rs, rsr), (yn, ynr), (s0, s0r), (s1, s1r), (bon, bonr) = tmp["pex"], tmp["ege"], tmp["egi"], tmp["kr"], tmp["rn"]
                    fo("dve", lambda e: e.tensor_tensor(out=ysqb[0][:], in0=y, in1=y, op=ALU.mult), reads=yres, writes=[ysqb[1]])
                    fo("act", lambda e: e.activation(out=yb[0][:], in_=y, func=AF.Copy), reads=yres, writes=[yb[1]])
                    pa, par = bankA
                    fmm(pa[:, 0:BLK], onesbd[:], yb[0][:], True, True, [cst, yb[1]], [par])
                    fmm(pa[:, BLK:2 * BLK], onesbd[:], ysqb[0][:], True, True, [cst, ysqb[1]], [par])
                    fo("act", lambda e: e.activation(out=mean[:], in_=pa[:, 0:BLK], func=AF.Copy, scale=1.0 / 64.0), reads=[par], writes=[meanr])
                    fo("dve", lambda e: e.tensor_tensor(out=msq[:], in0=mean[:], in1=mean[:], op=ALU.mult), reads=[meanr], writes=[msqr])
                    fo("dve", lambda e: e.scalar_tensor_tensor(out=var[:], in0=pa[:, BLK:2 * BLK], scalar=1.0 / 64.0, in1=msq[:], op0=ALU.mult, op1=ALU.subtract), reads=[par, msqr], writes=[varr])
                    fo("act", lambda e: e.activation(out=rs[:], in_=var[:], func=AF.Ln, bias=epsgn[:, 0:1], scale=1.0), reads=[varr, cst], writes=[rsr])
                    fo("act", lambda e: e.activation(out=rs[:], in_=rs[:], func=AF.Exp, scale=-0.5), reads=[rsr], writes=[rsr])
                    fo("dve", lambda e: e.tensor_tensor(out=yn[:], in0=y, in1=mean[:], op=ALU.subtract), reads=yres + [meanr], writes=[ynr])
                    fo("dve", lambda e: e.tensor_tensor(out=yn[:], in0=yn[:], in1=rs[:], op=ALU.mult), reads=[ynr, rsr], writes=[ynr])
                    fo("dve", lambda e: e.tensor_scalar(yn[:], yn[:], GNG(hp), GNB(hp), ALU.mult, ALU.add), reads=[ynr, cst], writes=[ynr])
                    pb_, pbr_ = bankA
                    fmm(pb_[:, 0:BLK], a2b[0:64, hc], wag[0:64, 1, ts], True, True, [cst, rkv_res[13][tb]], [pbr_])
                    pb2_, pbr2_ = bankB
                    fmm(pb2_[:, 0:BLK], a2b[64:128, hc], wag[64:128, 1, ts], True, True, [cst, rkv_res[13][tb]], [pbr2_])
                    act_sigmoid(fo, s0[:], pb_[:, 0:BLK], na0T[:, hp:hp + 1], [pbr_], [s0r])
                    act_sigmoid(fo, s1[:], pb2_[:, 0:BLK], na0T[:, 4 + hp:5 + hp], [pbr2_], [s1r])
                    fo("dve", lambda e: e.tensor_tensor(out=s0[:], in0=s0[:], in1=s1[:], op=ALU.add), reads=[s0r, s1r], writes=[s0r])
                    fo("dve", lambda e: e.tensor_scalar(s0[:], s0[:], KA(hp), ppx[:, 4 + hp:5 + hp], ALU.mult, ALU.add), reads=[s0r, cst], writes=[s0r])
                    fo("dve", lambda e: e.tensor_tensor(out=s0[:], in0=s0[:], in1=k_ap, op=ALU.mult), reads=[s0r] + kr_, writes=[s0r])
                    fo("dve", lambda e: e.scalar_tensor_tensor(out=ksq[0][:], in0=r_ap, scalar=RK(hp), in1=s0[:], op0=ALU.mult, op1=ALU.mult), reads=rr + [s0r, cst], writes=[ksq[1]])
                    pc_, pcr_ = bankA
                    fmm(pc_[:, 0:BLK], onesbd[:], ksq[0][:], True, True, [cst, ksq[1]], [pcr_])
                    fmm(pc_[:, BLK:2 * BLK], g2b[:, hc], wag[:, 2, ts], True, True, [cst, rkv_res[14][tb]], [pcr_])
                    fo("dve", lambda e: e.tensor_tensor(out=bon[:], in0=pc_[:, 0:BLK], in1=v_ap, op=ALU.mult), reads=[pcr_] + vr_, writes=[bonr])
                    fo("dve", lambda e: e.tensor_tensor(out=yn[:], in0=yn[:], in1=bon[:], op=ALU.add), reads=[ynr, bonr], writes=[ynr])
                    fo("dve", lambda e: e.tensor_tensor(out=yaT[:, hp, ts], in0=yn[:], in1=pc_[:, BLK:2 * BLK], op=ALU.mult), reads=[ynr, pcr_], writes=[yaT_res[hp][tb]])
                barrier()
                cvf = Carver(64, 128)
                tmp2 = {n_: (cvf.alloc([128, BLK], F32), Res(n_ + "2")) for n_ in tmp}
                ksq2 = (cvf.alloc([128, BLK], BF16), Res("ksq2")); yb2 = (cvf.alloc([128, BLK], BF16), Res("yb2")); ysqb2 = (cvf.alloc([128, BLK], BF16), Res("ysqb2"))
                for b in range(NBLK):
                    recs = []
                    for hl, hp in enumerate(hps):
                        k.defer = []
                        if hl == 0:
                            fin_block(hl, hp, b, tmp, ksq, yb, ysqb, psf[0], psf[1])
                        else:
                            fin_block(hl, hp, b, tmp2, ksq2, yb2, ysqb2, psf[2], psf[3])
                        recs.append([it for it in k.defer if it is not None]); k.defer = None
                    for i_ in range(max(len(r_) for r_ in recs)):
                        for r_ in recs:
                            if i_ < len(r_):
                                k.op(*r_[i_])
                return yacc, yacc_res

        for rnd in range(2):
            yacc, yacc_res = scan_round(rnd)
            if stop == "C0":
                dbg("yaT0", yaT[:, 0, :], yaT_res[0], [128, SEQ])
                finish()
                return nc, dbg_outs
            barrier()
        dbg("yaT0", yaT[:, 0, :], yaT_res[0], [128, SEQ])
        dbg("yaT3", yaT[:, 3, :], yaT_res[3], [128, SEQ])
        if stop == "C":
            finish()
            return nc, dbg_outs

        barrier()
        h0T_res = [Res("h0T%d" % t) for t in range(NT)]
        phase_h0T(h0T, h0T_res)
        ybT = view(16, 4 * SEQ * 2, BF16, "p (c t) -> p c t", c=4)
        ybT_res = [Res("ybT%d" % t) for t in range(NT)]
        with ExitStack() as ph:
            Wu = view(96, 8 * 512 * 2, BF16, "p (k c) -> p k c", k=8); Wv = view(104, 8 * 512 * 2, BF16, "p (k c) -> p k c", k=8)
            wr_ = Res("WuWv")
            k.dma("pool", lambda e: e.dma_start(out=Wu, in_=w_in[:, GM0:GM0 + 512].rearrange("(k p) c -> p k c", p=128)), writes=[wr_])
            k.dma("pool", lambda e: e.dma_start(out=Wv, in_=w_in[:, GM0 + 512:GM0 + 1024].rearrange("(k p) c -> p k c", p=128)), writes=[wr_])
            wsT = phase_sb(ph, [128, 8, 128], BF16, "wsT"); bsF = phase_sb(ph, [128, 4, 128], F32, "bsF")
            k.dma("pool", lambda e: e.dma_start(out=wsT[:], in_=wsT_d), writes=[wr_])
            k.dma("sp", lambda e: e.dma_start(out=bsF[:], in_=bsF_d), writes=[wr_])
            gg, ggr = load_bc(ph, gln_g_d, 512, "gg"); gb, gbr = load_bc(ph, gln_b_d, 512, "gb")
            uTs = [(view(112 + 4 * i, 4 * 512 * 2, BF16, "p (c t) -> p c t", c=4), Res("uT")) for i in range(2)]
            vgs = PRing(ph, 2, [128, 512], F32, "vg")
            vnEs = PRing(ph, 2, [128, 512], BF16, "vnE"); vnOs = PRing(ph, 2, [128, 512], BF16, "vnO")
            svs = PRing(ph, 2, [128, 512], F32, "sv")
            small = PRing(ph, 4, [128, 16], F32, "lnsm")
            for (t_, r_) in vnEs.items + vnOs.items:
                k.op("pool", lambda e: e.memset(t_[:], 0.0), writes=[r_])
            g4 = lambda ap, g: ap.rearrange("p (c g d) -> p c g d", g=2, d=64)[:, :, g, :]
            for tb in range(4):
                ts = slice(tb * 512, (tb + 1) * 512)
                uT, uTr = uTs[tb % 2]
                for cu in range(4):
                    pb, pbr = PSF()
                    for kc in range(8):
                        mm(pb[:], Wu[:, kc, cu * 128:(cu + 1) * 128], h0T[:, kc, ts], kc == 0, kc == 7, [wr_] + h0T_res[4 * tb:4 * tb + 4], [pbr])
                    k.op("act", lambda e: e.activation(out=uT[:, cu, :], in_=pb[:], func=AF.Gelu), reads=[pbr], writes=[uTr])
                for tt in range(4):
                    t = 4 * tb + tt
                    tsl = slice(t * 128, (t + 1) * 128)
                    pb, pbr = PSF()
                    for kc in range(8):
                        mm(pb[:], h0T[:, kc, tsl], Wv[:, kc, :], kc == 0, kc == 7, [wr_, h0T_res[t]], [pbr])
                    vg, vgr = vgs.get()
                    k.op("act", lambda e: e.activation(out=vg[:], in_=pb[:], func=AF.Gelu), reads=[pbr], writes=[vgr])
                    layer_norm(vg[:], [vgr], gg[:], gb[:], [ggr, gbr], vg[:], [vgr], small, n=512)
                    vnE, vnEr = vnEs.get(); vnO, vnOr = vnOs.get()
                    k.op("act", lambda e: e.activation(out=g4(vnE[:], 0), in_=g4(vg[:], 0), func=AF.Copy), reads=[vgr], writes=[vnEr])
                    k.op("pool", lambda e: e.tensor_copy(g4(vnO[:], 1), g4(vg[:], 1)), reads=[vgr], writes=[vnOr])
                    ps, psr = PSF()
                    for c in range(4):
                        cs_ = slice(c * 128, (c + 1) * 128)
                        mm(ps[:, cs_], vnE[:, cs_], wsT[:, 2 * c, :], True, False, [vnEr, wr_], [psr])
                        mm(ps[:, cs_], vnO[:, cs_], wsT[:, 2 * c + 1, :], False, True, [vnOr, wr_], [psr])
                    sv, svr = svs.get()
                    k.op("dve", lambda e: e.tensor_tensor(out=sv[:], in0=ps[:], in1=bsF[:].rearrange("p c t -> p (c t)"), op=ALU.add), reads=[psr, wr_], writes=[svr])
                    k.op("dve", lambda e: e.tensor_tensor(out=ybT[:, :, tsl], in0=sv[:].rearrange("p (c t) -> p c t", c=4), in1=uT[:, :, tt * 128:(tt + 1) * 128], op=ALU.mult), reads=[svr, uTr], writes=[ybT_res[t]])
            barrier()
        dbg("ybT0", ybT[:, 0, :], ybT_res, [128, SEQ])
        if stop == "D":
            finish()
            return nc, dbg_outs

        mgT = view(128, 8 * SEQ * 2, BF16, "p (c t) -> p c t", c=8)
        mg_res = [Res("mg%d" % tb) for tb in range(4)]
        with ExitStack() as ph:
            wbr = view(96, 8 * 1024 * 2, BF16, "p (k d) -> p k d", k=8); wbr_r = Res("wbr")
            for kc in range(8):
                k.dma("pool", lambda e: e.dma_start(out=wbr[:, kc, :], in_=wbr_d[kc * 128:(kc + 1) * 128, :]), writes=[wbr_r])
            wgs = [(view(112 + 2 * i, 8 * 128 * 2, BF16, "p (k c) -> p k c", k=8), Res("wg")) for i in range(4)]
            sigs = PRing(ph, 3, [128, 512], F32, "sig")
            prs = PRing(ph, 2, [128, 512], F32, "pr")
            wi = 0
            for dc in range(8):
                wg2 = []
                for n in range(2):
                    wg, wgr = wgs[wi % 4]; wi += 1
                    c0 = GT0 + n * 1024 + dc * 128
                    k.dma("pool", lambda e: e.dma_start(out=wg, in_=w_in[:, c0:c0 + 128].rearrange("(k p) c -> p k c", p=128)), writes=[wgr])
                    wg2.append((wg, wgr))
                for tb in range(4):
                    ts = slice(tb * 512, (tb + 1) * 512)
                    hr = h0T_res[4 * tb:4 * tb + 4]
                    acc = None
                    for n in range(2):
                        wg, wgr = wg2[n]
                        pg, pgr = PSF()
                        for kc in range(8):
                            mm(pg[:], wg[:, kc, :], h0T[:, kc, ts], kc == 0, kc == 7, [wgr] + hr, [pgr])
                        sg, sgr = sigs.get()
                        k.op("act", lambda e: e.activation(out=sg[:], in_=pg[:], func=AF.Sigmoid), reads=[pgr], writes=[sgr])
                        pbn, pbnr = PSF()
                        yT, yres = (yaT, [yaT_res[c][tb] for c in range(4)]) if n == 0 else (ybT, ybT_res[4 * tb:4 * tb + 4])
                        for c in range(4):
                            mm(pbn[:], wbr[:, n * 4 + c, dc * 128:(dc + 1) * 128], yT[:, c, ts], c == 0, c == 3, [wbr_r] + yres, [pbnr])
                        if n == 0:
                            acc, accr = prs.get()
                            k.op("dve", lambda e: e.tensor_tensor(out=acc[:], in0=sg[:], in1=pbn[:], op=ALU.mult), reads=[sgr, pbnr], writes=[accr])
                        else:
                            k.op("dve", lambda e: e.tensor_tensor(out=sg[:], in0=sg[:], in1=pbn[:], op=ALU.mult), reads=[sgr, pbnr], writes=[sgr])
                            k.op("pool", lambda e: e.tensor_tensor(out=mgT[:, dc, ts], in0=acc[:], in1=sg[:], op=ALU.add), reads=[sgr, accr], writes=[mg_res[tb]])
            barrier()
        dbg("mgT0", mgT[:, 0, :], mg_res, [128, SEQ])
        if stop == "E":
            finish()
            return nc, dbg_outs

        H = view(64, NT * D * 4, F32, "p (t d) -> p t d", t=NT)
        H_res = [Res("H%d" % t) for t in range(NT)]

        def load_w_fm(dst, d_ap, res, q="pool"):
            for kc in range(8):
                k.dma(q, lambda e: e.dma_start(out=dst[:, kc, :], in_=d_ap[kc * 128:(kc + 1) * 128, :]), writes=[res])

        with ExitStack() as ph:
            wmix = view(0, 8 * 1024 * 2, BF16, "p (k d) -> p k d", k=8); wmix_r = Res("wmix")
            load_w_fm(wmix, wmix_d, wmix_r)
            g0, g0r = load_bc(ph, lng["ln_emb_g"], D, "g0"); b0, b0r = load_bc(ph, lng["ln_emb_b"], D, "b0")
            g1, g1r = load_bc(ph, lng["ln1_g"], D, "g1"); b1, b1r = load_bc(ph, lng["ln1_b"], D, "b1")
            xs = PRing(ph, 2, [128, D], F32, "xs")
            small = PRing(ph, 4, [128, 16], F32, "lnsm")
            def tileF(t):
                tsl = slice(t * 128, (t + 1) * 128)
                xt, xr = xs.get()
                k.dma("sp", lambda e: e.dma_start(out=xt[:], in_=x[tsl, :]), writes=[xr])
                layer_norm(xt[:], [xr], g0[:], b0[:], [g0r, b0r], xt[:], [xr], small)

                def half_(half):
                    hs = slice(half * 512, (half + 1) * 512)
                    pm, pmr = PSF()
                    for kc in range(8):
                        mm(pm[:], mgT[:, kc, tsl], wmix[:, kc, hs], kc == 0, kc == 7, [mg_res[t // 4], wmix_r], [pmr])
                    k.op("dve", lambda e: e.scalar_tensor_tensor(out=xt[:, hs], in0=xt[:, hs], scalar=ALPHA, in1=pm[:], op0=ALU.mult, op1=ALU.add), reads=[xr, pmr], writes=[xr])
                half_(0)
                half_(1)
                layer_norm(xt[:], [xr], g1[:], b1[:], [g1r, b1r], H[:, t, :], [H_res[t]], small)
            run_pairs(NT, tileF)
            barrier()
        dbg("h1_t0", H[:, 0, :], [H_res[0]], [128, D])
        dbg("h1_t9", H[:, 9, :], [H_res[9]], [128, D])
        if stop == "F":
            finish()
            return nc, dbg_outs

        with ExitStack() as ph:
            wq = view(0, 8 * 1024 * 2, BF16, "p (k d) -> p k d", k=8); wo = view(16, 8 * 1024 * 2, BF16, "p (k d) -> p k d", k=8)
            wkK = view(32, 8 * 1024 * 2, BF16, "p (k d) -> p k d", k=8); wkV = view(48, 8 * 1024 * 2, BF16, "p (k d) -> p k d", k=8)
            KT = view(128, 8 * 256 * 2, BF16, "p (c m) -> p c m", c=8); Vm = view(132, 2 * 1024 * 2, BF16, "p (m d) -> p m d", m=2)
            memT = view(136, 8 * 256 * 2, BF16, "p (c m) -> p c m", c=8)
            wres = Res("xw"); kvres = Res("kv"); memr = [Res("memT0"), Res("memT1")]
            load_w_fm(wq, wq_d, wres); load_w_fm(wo, wo_d, wres)
            load_w_fm(wkK, wkv_d[:, 0:D], wres); load_w_fm(wkV, wkv_d[:, D:2 * D], wres)
            small = PRing(ph, 4, [128, 16], F32, "lnsm")
            xs = PRing(ph, 2, [128, D], F32, "xs")
            with ExitStack() as ph2:
                gm, gmr = load_bc(ph2, lng["mem_ln_g"], D, "gm"); bm, bmr = load_bc(ph2, lng["mem_ln_b"], D, "bm")
                hbs0 = PRing(ph2, 2, [128, D], BF16, "hb")
                for mt in range(2):
                    xt, xr = xs.get()
                    k.dma("sp", lambda e: e.dma_start(out=xt[:], in_=mem[mt * 128:(mt + 1) * 128, :]), writes=[xr])
                    layer_norm(xt[:], [xr], gm[:], bm[:], [gmr, bmr], xt[:], [xr], small)
                    hb, hbr = hbs0.get()
                    k.op("act", lambda e: e.activation(out=hb[:], in_=xt[:], func=AF.Copy), reads=[xr], writes=[hbr])
                    to_fm(hb, hbr, memT, [memr[mt]], mt)
                barrier()
            g2_, g2r = load_bc(ph, lng["ln2_g"], D, "g2"); b2_, b2r = load_bc(ph, lng["ln2_b"], D, "b2")
            for c in range(8):
                pk, pkr = PSF()
                for kc in range(8):
                    mm(pk[:, 0:256], wkK[:, kc, c * 128:(c + 1) * 128], memT[:, kc, :], kc == 0, kc == 7, [wres] + memr, [pkr])
                k.op("act", lambda e: e.activation(out=KT[:, c, :], in_=pk[:, 0:256], func=AF.Copy), reads=[pkr], writes=[kvres])
            for mt in range(2):
                for half in range(2):
                    hs = slice(half * 512, (half + 1) * 512)
                    pv, pvr = PSF()
                    for kc in range(8):
                        mm(pv[:], memT[:, kc, mt * 128:(mt + 1) * 128], wkV[:, kc, hs], kc == 0, kc == 7, [wres] + memr, [pvr])
                    k.op("act", lambda e: e.activation(out=Vm[:, mt, hs], in_=pv[:], func=AF.Copy), reads=[pvr], writes=[kvres])
            barrier()
            cv = Carver(32, 64)

            class CRing:
                def __init__(self, n, shape, dt, name):
                    self.items = [(cv.alloc(shape, dt), Res(name)) for _ in range(n)]
                    self.i = 0

                def get(self):
                    it = self.items[self.i % len(self.items)]; self.i += 1
                    return it
            hbs = CRing(2, [128, D], BF16, "hb")
            hTs = CRing(2, [128, 8, 128], BF16, "hT")
            qTs = CRing(2, [128, 8, 128], BF16, "qT")
            pexs = CRing(2, [128, 4, 256], BF16, "pex")
            pTs = CRing(2, [128, 8, 128], BF16, "pT")
            oTs = CRing(2, [128, 8, 128], BF16, "oT")
            sm2 = PRing(ph, 2, [128, 16], F32, "sm2")
            SCL = 256.0 ** -0.5
            tab_res = Res("tables")
            stg = [(view(140 + 8 * i, 4 * D * 2, BF16, "p (a f) -> p a f", a=4), Res("stg%d" % i)) for i in range(2)]
            conv_jobs = [(src, dst, ch) for (src, dst) in ((pu_d, Ub_d), (pv_d, Vb_d)) for ch in range(32)]

            def conv_some(n_):
                for _ in range(n_):
                    if not conv_jobs:
                        return
                    src, dst, ch = conv_jobs.pop(0)
                    st_, str_ = stg[ch % 2]
                    k.dma("pool", lambda e: e.dma_start(out=st_, in_=src[ch * 4:(ch + 1) * 4].rearrange("a p f -> p a f")), writes=[str_])
                    k.dma("sp", lambda e: e.dma_start(out=dst[ch * 4:(ch + 1) * 4].rearrange("a p f -> p a f"), in_=st_), reads=[str_], writes=[tab_res])
            for t in range(NT):
                conv_some(4)
                h1 = H[:, t, :]; h1r = H_res[t]
                hb, hbr = hbs.get()
                k.op("act", lambda e: e.activation(out=hb[:], in_=h1, func=AF.Copy), reads=[h1r], writes=[hbr])
                hT, hTr = hTs.get()
                to_fm(hb, hbr, hT, [hTr], 0)
                qT, qTr = qTs.get()
                for g in range(2):
                    pq, pqr = PSF()
                    for cc in range(4):
                        c = g * 4 + cc
                        for kc in range(8):
                            mm(pq[:, cc * 128:(cc + 1) * 128], wq[:, kc, c * 128:(c + 1) * 128], hT[:, kc, :], kc == 0, kc == 7, [wres, hTr], [pqr])
                    k.op("act", lambda e: e.activation(out=qT[:, g * 4:(g + 1) * 4, :], in_=pq[:].rearrange("p (c t) -> p c t", c=4), func=AF.Copy), reads=[pqr], writes=[qTr])
                sm, smr = sm2.get()
                pex, pexr = pexs.get()
                pss = []
                for g in range(2):
                    ps_, psr_ = PSF(); pss.append((ps_, psr_))
                    for hh in range(2):
                        h = g * 2 + hh
                        for j in range(2):
                            mm(ps_[:, hh * 256:(hh + 1) * 256], qT[:, 2 * h + j, :], KT[:, 2 * h + j, :], j == 0, j == 1, [qTr, kvres], [psr_])
                    k.op("dve", lambda e: e.tensor_reduce(out=sm[:, g * 2:(g + 1) * 2], in_=ps_[:].rearrange("p (h m) -> p h m", h=2), axis=AX.X, op=ALU.max), reads=[psr_], writes=[smr])
                k.op("dve", lambda e: e.tensor_scalar(sm[:, 4:8], sm[:, 0:4], -SCL, None, ALU.mult), reads=[smr], writes=[smr])
                for h in range(4):
                    ps_, psr_ = pss[h // 2]
                    k.op("act", lambda e: e.activation(out=pex[:, h, :], in_=ps_[:, (h % 2) * 256:(h % 2 + 1) * 256], func=AF.Exp, bias=sm[:, 4 + h:5 + h], scale=SCL, accum_out=sm[:, 8 + h:9 + h]), reads=[psr_, smr], writes=[pexr, smr])
                k.op("dve", lambda e: e.reciprocal(out=sm[:, 12:16], in_=sm[:, 8:12]), reads=[smr], writes=[smr])
                k.op("dve", lambda e: e.tensor_tensor(out=pex[:], in0=pex[:], in1=sm[:, 12:16].unsqueeze(2).to_broadcast([128, 4, 256]), op=ALU.mult), reads=[pexr, smr], writes=[pexr])
                pt, ptr_ = PSB()
                for h in range(4):
                    for mt in range(2):
                        i = h * 2 + mt
                        k.op("pe", lambda e: e.transpose(pt[:, i * 128:(i + 1) * 128], pex[:, h, mt * 128:(mt + 1) * 128], ident[:]), reads=[pexr, cst], writes=[ptr_])
                pT, pTr = pTs.get()
                k.op("act", lambda e: e.activation(out=pT[:], in_=pt[:].rearrange("p (c t) -> p c t", t=128), func=AF.Copy), reads=[ptr_], writes=[pTr])
                oT, oTr = oTs.get()
                for g in range(2):
                    po, por = PSF()
                    for cc in range(4):
                        c = g * 4 + cc; h = c // 2
                        for mt in range(2):
                            mm(po[:, cc * 128:(cc + 1) * 128], Vm[:, mt, c * 128:(c + 1) * 128], pT[:, h * 2 + mt, :], mt == 0, mt == 1, [kvres, pTr], [por])
                    k.op("dve", lambda e: e.tensor_copy(oT[:, g * 4:(g + 1) * 4, :], po[:].rearrange("p (c t) -> p c t", c=4)), reads=[por], writes=[oTr])
                xt, xr = xs.get()
                for half in range(2):
                    hs = slice(half * 512, (half + 1) * 512)
                    px, pxr = PSF()
                    for c in range(8):
                        mm(px[:], oT[:, c, :], wo[:, c, hs], c == 0, c == 7, [oTr, wres], [pxr])
                    k.op("dve", lambda e: e.scalar_tensor_tensor(out=xt[:, hs], in0=h1[:, hs], scalar=ALPHA, in1=px[:], op0=ALU.mult, op1=ALU.add), reads=[h1r, pxr], writes=[xr])
                layer_norm(xt[:], [xr], g2_[:], b2_[:], [g2r, b2r], H[:, t, :], [H_res[t]], small)
            barrier()
        dbg("h2_t0", H[:, 0, :], [H_res[0]], [128, D])
        dbg("h2_t9", H[:, 9, :], [H_res[9]], [128, D])
        if stop == "H":
            finish()
            return nc, dbg_outs

        with ExitStack() as pho:
            SI1 = phase_sb(pho, [128, NT, 128], F32, "SI1"); SI2 = phase_sb(pho, [128, NT, 128], F32, "SI2"); SG = phase_sb(pho, [128, NT, 128], F32, "SG")
            slot_res = [Res("slot%d" % t) for t in range(NT)]
            with ExitStack() as ph:
                pwq = view(0, 8 * 2048 * 2, BF16, "p (k d) -> p k d", k=8); pw_r = Res("pwq")
                load_w_fm(pwq, pwq_d, pw_r)
                skT = phase_sb(ph, [128, 2, 128], BF16, "skT")
                k.dma("pool", lambda e: e.dma_start(out=skT[:], in_=skT_d), writes=[pw_r])
                sc = view(48, 16 * 128 * 4, F32, "p (c k) -> p c k", c=16); scr = Res("sc")
                cv = Carver(128, 160)

                def CT(shape, dt, name, c=cv):
                    return (c.alloc(shape, dt), Res(name))
                hb, hbr = CT([128, D], BF16, "hb"); hT, hTr = CT([128, 8, 128], BF16, "hT")
                pqT, pqTr = CT([128, 16, 128], BF16, "pqT")
                sc2, sc2r = CT([128, 256], F32, "sc2"); sc2b, sc2br = CT([128, 256], F32, "sc2b")
                ts_c = [Res("ts%d" % c_) for c_ in range(16)]; ti_c = [Res("ti%d" % c_) for c_ in range(16)]
                bs_h = [Res("bs%d" % h_) for h_ in range(8)]; bp_h = [Res("bp%d" % h_) for h_ in range(8)]
                top_s, tsr = CT([128, 256], F32, "top_s"); top_i, tir = CT([128, 256], U32, "top_i"); top_f, tfr = CT([128, 256], F32, "top_f")
                cand, cdr = CT([128, 2048], F32, "cand")
                best_s, bsr = CT([128, 128], F32, "best_s"); best_p, bpr = CT([128, 128], U32, "best_p")
                pf, pfr = CT([128, 128], F32, "pf"); k1f, k1r = CT([128, 128], F32, "k1f"); k2f, k2r = CT([128, 128], F32, "k2f")
                gsum, gsr = CT([128, 16], F32, "gsum")
                eq = cand
                v4 = lambda ap: ap.rearrange("p (h z k) -> p h z k", h=8, z=2)
                v3k = lambda ap: ap.rearrange("p (h k) -> p h k", h=8)
                c4 = lambda ap: ap.rearrange("p (h a b) -> p h a b", h=8, a=16)
                sc_b = [(sc, scr), (view(32, 16 * 128 * 4, F32, "p (c k) -> p c k", c=16), Res("scB"))]
                hb_b = [(hb, hbr), (view(40, D * 2, BF16), Res("hbB"))]
                hT_b = [(hT, hTr), (view(42, 8 * 128 * 2, BF16, "p (c t) -> p c t", c=8), Res("hTB"))]
                pq_b = [(pqT, pqTr), (view(44, 16 * 128 * 2, BF16, "p (c t) -> p c t", c=16), Res("pqTB"))]

                def head(t):
                    hb, hbr = hb_b[t % 2]; hT, hTr = hT_b[t % 2]; pqT, pqTr = pq_b[t % 2]; sc, scr = sc_b[t % 2]
                    h2 = H[:, t, :]; h2r = H_res[t]
                    k.op("act", lambda e: e.activation(out=hb, in_=h2, func=AF.Copy), reads=[h2r], writes=[hbr])
                    to_fm(hb, hbr, hT, [hTr], 0)
                    for g in range(4):
                        pq, pqr = PSF()
                        for cc in range(4):
                            c = g * 4 + cc
                            for kc in range(8):
                                mm(pq[:, cc * 128:(cc + 1) * 128], pwq[:, kc, c * 128:(c + 1) * 128], hT[:, kc, :], kc == 0, kc == 7, [pw_r, hTr], [pqr])
                        k.op("act", lambda e: e.activation(out=pqT[:, g * 4:(g + 1) * 4, :], in_=pq[:].rearrange("p (c t) -> p c t", c=4), func=AF.Copy), reads=[pqr], writes=[pqTr])
                    for g in range(4):
                        ps_, psr_ = PSF()
                        for cc in range(4):
                            c = g * 4 + cc
                            mm(ps_[:, cc * 128:(cc + 1) * 128], pqT[:, c, :], skT[:, c % 2, :], True, True, [pqTr, pw_r], [psr_])
                        k.op("act", lambda e: e.activation(out=sc[:, g * 4:(g + 1) * 4, :], in_=ps_[:].rearrange("p (c k) -> p c k", c=4), func=AF.Copy), reads=[psr_], writes=[scr])

                def tail(t):
                    sc, scr = sc_b[t % 2]
                    def lvl1_ops(c, buf, bufr):
                        lo = slice(c * 16, c * 16 + 8); hi = slice(c * 16 + 8, c * 16 + 16)
                        tr_, ir_ = ts_c[c], ti_c[c]
                        return [
                            lambda: k.op("dve", lambda e: e.max(out=top_s[:, lo], in_=sc[:, c, :]), reads=[scr], writes=[tr_]),
                            lambda: k.op("dve", lambda e: e.max_index(out=top_i[:, lo], in_max=top_s[:, lo], in_values=sc[:, c, :]), reads=[scr, tr_], writes=[ir_]),
                            lambda: k.op("dve", lambda e: e.match_replace(out=buf[:, 0:128], in_to_replace=top_s[:, lo], in_values=sc[:, c, :], imm_value=-1e30), reads=[scr, tr_], writes=[bufr]),
                            lambda: k.op("dve", lambda e: e.max(out=top_s[:, hi], in_=buf[:, 0:128]), reads=[bufr], writes=[tr_]),
                            lambda: k.op("dve", lambda e: e.max_index(out=top_i[:, hi], in_max=top_s[:, hi], in_values=buf[:, 0:128]), reads=[bufr, tr_], writes=[ir_]),
                        ]
                    for c in range(0, 16, 2):
                        oa_ = lvl1_ops(c, sc2, sc2r); ob_ = lvl1_ops(c + 1, sc2b, sc2br)
                        for fa_, fb_ in zip(oa_, ob_):
                            fa_(); fb_()
                    k.op("dve", lambda e: e.tensor_copy(top_f, top_i), reads=ti_c, writes=[tfr])
                    k.op("dve", lambda e: e.tensor_tensor(out=c4(cand), in0=v4(top_s)[:, :, 0, :].unsqueeze(3).to_broadcast([128, 8, 16, 16]),
                                                          in1=v4(top_s)[:, :, 1, :].unsqueeze(2).to_broadcast([128, 8, 16, 16]), op=ALU.add), reads=ts_c, writes=[cdr])
                    candh = cand.rearrange("p (h c) -> p h c", h=8)
                    def lvl2_ops(h, buf, bufr):
                        lo = slice(h * 16, h * 16 + 8); hi = slice(h * 16 + 8, h * 16 + 16)
                        br_, pr_ = bs_h[h], bp_h[h]
                        return [
                            lambda: k.op("dve", lambda e: e.max(out=best_s[:, lo], in_=candh[:, h, :]), reads=[cdr], writes=[br_]),
                            lambda: k.op("dve", lambda e: e.max_index(out=best_p[:, lo], in_max=best_s[:, lo], in_values=candh[:, h, :]), reads=[cdr, br_], writes=[pr_]),
                            lambda: k.op("dve", lambda e: e.match_replace(out=buf, in_to_replace=best_s[:, lo], in_values=candh[:, h, :], imm_value=-1e30), reads=[cdr, br_], writes=[bufr]),
                            lambda: k.op("dve", lambda e: e.max(out=best_s[:, hi], in_=buf), reads=[bufr], writes=[br_]),
                            lambda: k.op("dve", lambda e: e.max_index(out=best_p[:, hi], in_max=best_s[:, hi], in_values=buf), reads=[bufr, br_], writes=[pr_]),
                        ]
                    for h in range(0, 8, 2):
                        oa_ = lvl2_ops(h, sc2, sc2r); ob_ = lvl2_ops(h + 1, sc2b, sc2br)
                        for fa_, fb_ in zip(oa_, ob_):
                            fa_(); fb_()
                    pfu = pf.bitcast(U32)
                    k.op("dve", lambda e: e.tensor_single_scalar(out=pfu, in_=best_p, scalar=4, op=ALU.logical_shift_right), reads=bp_h, writes=[pfr])
                    k.op("dve", lambda e: e.tensor_copy(k1f, pfu), reads=[pfr], writes=[k1r])
                    k.op("dve", lambda e: e.tensor_single_scalar(out=pfu, in_=best_p, scalar=15, op=ALU.bitwise_and), reads=bp_h + [k1r, pfr], writes=[pfr])
                    k.op("dve", lambda e: e.tensor_copy(k2f, pfu), reads=[pfr], writes=[k2r])
                    io4 = iota16[:, :].unsqueeze(1).unsqueeze(1).to_broadcast([128, 8, 16, 16])
                    sr = slot_res[t]
                    for (kf, kr_, z, dst) in ((k1f, k1r, 0, SI1[:, t, :]), (k2f, k2r, 1, SI2[:, t, :])):
                        k.op("dve", lambda e: e.tensor_tensor(out=c4(eq), in0=io4, in1=v3k(kf).unsqueeze(3).to_broadcast([128, 8, 16, 16]), op=ALU.is_equal), reads=[cst, kr_, cdr], writes=[cdr])
                        k.op("dve", lambda e: e.tensor_tensor(out=c4(eq), in0=c4(eq), in1=v4(top_f)[:, :, z, :].unsqueeze(2).to_broadcast([128, 8, 16, 16]), op=ALU.mult), reads=[cdr, tfr], writes=[cdr])
                        k.op("dve", lambda e: e.tensor_reduce(out=dst, in_=eq.rearrange("p (a b) -> p a b", b=16), axis=AX.X, op=ALU.add), reads=[cdr], writes=[sr])
                    gate = SG[:, t, :]
                    k.op("dve", lambda e: e.tensor_tensor(out=v3k(gate), in0=v3k(best_s), in1=v3k(best_s)[:, :, 0:1].to_broadcast([128, 8, 16]), op=ALU.subtract), reads=bs_h, writes=[sr])
                    k.op("act", lambda e: e.activation(out=gate, in_=gate, func=AF.Exp), reads=[sr], writes=[sr])
                    k.op("dve", lambda e: e.tensor_reduce(out=gsum[:, 0:8], in_=v3k(gate), axis=AX.X, op=ALU.add), reads=[sr], writes=[gsr])
                    k.op("dve", lambda e: e.reciprocal(out=gsum[:, 8:16], in_=gsum[:, 0:8]), reads=[gsr], writes=[gsr])
                    k.op("dve", lambda e: e.tensor_tensor(out=v3k(gate), in0=v3k(gate), in1=gsum[:, 8:16].unsqueeze(2).to_broadcast([128, 8, 16]), op=ALU.mult), reads=[sr, gsr], writes=[sr])
                head(0)
                for t in range(NT):
                    if t + 1 < NT:
                        head(t + 1)
                    tail(t)
                barrier()
            dbg("si1", SI1[:, 0, :], [slot_res[0]], [128, 128]); dbg("sg", SG[:, 0, :], [slot_res[0]], [128, 128])
            if stop == "I":
                finish()
                return nc, dbg_outs

            TBK = 256
            with ExitStack() as ph:
                cv = Carver(128, 160)
                NU = 3
                utiles = [(cv.alloc([128, 8, 128], BF16), Res("ut")) for _ in range(NU)]
                vtiles = [(cv.alloc([128, D], BF16), Res("vt")) for _ in range(NU)]
                hTb = cv.alloc([128, 8, TBK], BF16); hTb_res = [Res("hTb0"), Res("hTb1")]
                accs = [(cv.alloc([128, D], F32), Res("acc")) for _ in range(2)]
                hbJ = (cv.alloc([128, D], BF16), Res("hbJ"))
                g3, g3r = load_bc(ph, lng["ln3_g"], D, "g3"); b3, b3r = load_bc(ph, lng["ln3_b"], D, "b3")
                small = PRing(ph, 4, [128, 16], F32, "lnsm")
                iota128 = phase_sb(ph, [128, 128], F32, "iota128"); ior = Res("iota128")
                k.op("pool", lambda e: e.iota(iota128[:], pattern=[[1, 128]], base=0, channel_multiplier=0, allow_small_or_imprecise_dtypes=True), writes=[ior])
                slTs = [(phase_sb(ph, [128, 3, TBK], BF16, "slT"), Res("slT")) for _ in range(2)]
                oh1s = PRing(ph, 8, [128, 128], BF16, "oh1"); oh2s = PRing(ph, 8, [128, 64], BF16, "oh2")

                class VRing:
                    def __init__(self, n, shape, dt, name):
                        self.items = [(cv.alloc(shape, dt), Res(name)) for _ in range(n)]
                        self.i = 0

                    def get(self):
                        it = self.items[self.i % len(self.items)]; self.i += 1
                        return it
                gels = VRing(2, [128, TBK], F32, "gel"); pbs = VRing(2, [128, TBK], BF16, "pb")
                ui = [0]
                NTB = SEQ // TBK
                Gh = [(view(32 * h_, TBK * 64 * 2, BF16, "p (t i) -> p t i", t=TBK), Res("G%d" % h_)) for h_ in range(2)]
                psbf = [(psb[i][0][:].bitcast(F32), psb[i][1]) for i in range(2)]
                gq = [0]

                def prep_block(tb):
                    slT, slTr = slTs[tb % 2]
                    for tt in range(2):
                        pt_, ptr_ = psbf[tt]
                        for a_, arr in enumerate((SI1, SI2, SG)):
                            k.op("pe", lambda e: e.transpose(pt_[:, a_ * 128:(a_ + 1) * 128], arr[:, tb * 2 + tt, :], identf[:]), reads=[slot_res[tb * 2 + tt], cst], writes=[ptr_])
                        k.op("act", lambda e: e.activation(out=slT[:, :, tt * 128:(tt + 1) * 128], in_=pt_[:, 0:384].rearrange("p (a t) -> p a t", a=3), func=AF.Copy), reads=[ptr_], writes=[slTr])

                def g_build_jobs(tb, half):
                    slT, slTr = slTs[tb % 2]
                    Gt, Gtr = Gh[half]
                    jobs = []
                    state = {}
                    pend_pe = []
                    for tk in range(TBK):
                        def job(tk=tk):
                            j = tk % 8
                            if j == 0:
                                state["pg"] = psbf[gq[0] % 2]; gq[0] += 1
                            pg, pgr = state["pg"]
                            o1, o1r = oh1s.get(); o2, o2r = oh2s.get()
                            k.op("dve", lambda e: e.tensor_scalar(o1[:], iota128[:], slT[:, 0, tk:tk + 1], slT[:, 2, tk:tk + 1], ALU.is_equal, ALU.mult), reads=[ior, slTr], writes=[o1r])
                            k.op("dve", lambda e: e.tensor_scalar(o2[:], iota128[:, half * 64:(half + 1) * 64], slT[:, 1, tk:tk + 1], None, ALU.is_equal), reads=[ior, slTr], writes=[o2r])
                            def pe_part():
                                mm(pg[:, j * 64:(j + 1) * 64], o1[:], o2[:], True, True, [o1r, o2r], [pgr])
                                if j == 7:
                                    tq = tk // 8
                                    k.op("act", lambda e: e.activation(out=Gt[:, tq * 8:tq * 8 + 8, :], in_=pg[:].rearrange("p (t i) -> p t i", t=8), func=AF.Copy), reads=[pgr], writes=[Gtr])
                            pend_pe.append(pe_part)
                            while len(pend_pe) > 6:
                                pend_pe.pop(0)()
                        jobs.append(job)

                    def flush():
                        while pend_pe:
                            pend_pe.pop(0)()
                    jobs.append(flush)
                    return jobs

                def issue_act(tb, i2):
                    ut, utr = utiles[ui[0] % NU]; vt, vtr = vtiles[ui[0] % NU]; ui[0] += 1
                    k.dma("sp", lambda e: e.dma_start(out=ut, in_=Ub_d[i2].rearrange("p (k i) -> p k i", k=8)), writes=[utr])
                    k.dma("sp", lambda e: e.dma_start(out=vt, in_=Vb_d[i2]), writes=[vtr])
                    pa, par = psf[4 + i2 % 2]
                    for kc in range(8):
                        mm(pa[:, 0:TBK], ut[:, kc, :], hTb[:, kc, :], kc == 0, kc == 7, [utr] + hTb_res, [par])
                    gel, gelr = gels.get()
                    k.op("act", lambda e: e.activation(out=gel, in_=pa[:, 0:TBK], func=AF.Gelu), reads=[par], writes=[gelr])
                    pb_, pbr_ = pbs.get()
                    Gt, Gtr = Gh[i2 // 64]
                    k.op("dve", lambda e: e.tensor_tensor(out=pb_, in0=gel, in1=Gt[:, :, i2 % 64], op=ALU.mult), reads=[gelr, Gtr], writes=[pbr_])
                    return (pb_, pbr_, vt, vtr)

                def issue_y(i2, st_):
                    pb_, pbr_, vt, vtr = st_
                    for tt in range(2):
                        for half in range(2):
                            py, pyr = psf[tt * 2 + half]
                            mm(py[:], pb_[:, tt * 128:(tt + 1) * 128], vt[:, half * 512:(half + 1) * 512], i2 == 0, i2 == 127, [pbr_, vtr], [pyr])
                prep_block(0)
                for job in g_build_jobs(0, 0):
                    job()
                for tb in range(NTB):
                    t0 = tb * 2
                    for tt in range(2):
                        hb, hbr = hbJ
                        k.op("act", lambda e: e.activation(out=hb, in_=H[:, t0 + tt, :], func=AF.Copy), reads=[H_res[t0 + tt]], writes=[hbr])
                        to_fm(hb, hbr, hTb, [hTb_res[tt]], tt)
                    if tb + 1 < NTB:
                        prep_block(tb + 1)
                    jobs_lo = g_build_jobs(tb, 1)
                    jobs_hi = g_build_jobs(tb + 1, 0) if tb + 1 < NTB else []
                    pend = issue_act(tb, 0)
                    for i2 in range(128):
                        jl = jobs_lo if i2 < 64 else jobs_hi
                        nxt = issue_act(tb, i2 + 1) if i2 + 1 < 128 else None
                        for _ in range(2):
                            if jl:
                                jl.pop(0)()
                        issue_y(i2, pend)
                        pend = nxt
                        for _ in range(2 if i2 % 64 < 62 else 1000):
                            if jl:
                                jl.pop(0)()
                    while jobs_hi:
                        jobs_hi.pop(0)()
                    for tt in range(2):
                        t = t0 + tt
                        acc, accr = accs[tt]
                        for half in range(2):
                            hs = slice(half * 512, (half + 1) * 512)
                            py, pyr = psf[tt * 2 + half]
                            k.op("dve", lambda e: e.scalar_tensor_tensor(out=acc[:, hs], in0=H[:, t, hs], scalar=ALPHA, in1=py[:], op0=ALU.mult, op1=ALU.add), reads=[H_res[t], pyr], writes=[accr])
                        layer_norm(acc, [accr], g3[:], b3[:], [g3r, b3r], acc, [accr], small)
                        k.dma("sp", lambda e: e.dma_start(out=out_d[t * 128:(t + 1) * 128, :], in_=acc), reads=[accr])
                barrier()

        finish()
    return nc, dbg_outs


def _consts():
    c = {}
    c["c_ident"] = np.eye(128, dtype=np.float32)
    ob = np.zeros((128, 128), np.float32); ob[:64, :64] = 1.0; ob[64:, 64:] = 1.0
    c["c_onesbd"] = ob
    s = np.arange(64)
    lt = (s[:, None] < s[None, :]).astype(np.float32)
    le = (s[:, None] <= s[None, :]).astype(np.float32)

    def mk(strict, incl):
        m = np.zeros((128, 512), np.float32)
        bd = np.zeros((128, 128), np.float32); bd[:64, :64] = strict; bd[64:, 64:] = strict
        pl = np.concatenate([incl, incl], axis=0)
        m[:, 0:128] = bd; m[:, 128:192] = pl; m[:, 192:320] = bd; m[:, 320:384] = pl
        bdT = np.zeros((128, 128), np.float32); bdT[:64, :64] = strict.T; bdT[64:, 64:] = strict.T
        m[:, 384:512] = bdT
        return m
    c["c_maskF"] = mk(lt, le)
    c["c_maskB"] = mk(lt.T.copy(), le.T.copy())
    r = np.ones((128, BLK), np.float32); r[:, ::CH] = 0.0
    c["c_rst"] = r
    c["c_iota"] = np.broadcast_to(np.arange(16, dtype=np.float32), (128, 16)).copy()
    return c


def prep_shared(inp):
    f = lambda a: np.ascontiguousarray(np.asarray(a, dtype=np.float32))
    sh = {}
    for n in ("ln_emb_g", "ln_emb_b"):
        sh[n] = f(inp[n]).reshape(1, D)
    for n in ("ln1_g", "ln1_b", "ln2_g", "ln2_b", "ln3_g", "ln3_b", "mem_ln_g", "mem_ln_b"):
        sh[n] = f(inp[n][0]).reshape(1, D)
    sh["w_in"] = f(inp["w_in"][0])
    sh["mu"] = f(inp["rwkv_mu"][0]).reshape(1, RWKV_COLS)
    tr = lambda a: f(np.asarray(a).reshape(-1, 4, 128).transpose(2, 0, 1).reshape(128, -1))
    sh["w0T"] = tr(inp["rwkv_w0"][0]); sh["a0T"] = tr(inp["rwkv_a0"][0])
    sh["w2"] = f(inp["rwkv_w2"][0]).reshape(128, RW); sh["a2"] = f(inp["rwkv_a2"][0]).reshape(128, RW)
    sh["g2"] = f(inp["rwkv_g2"][0])
    cols = [inp["rwkv_k_k"][0], inp["rwkv_k_a"][0], np.asarray(inp["rwkv_r_k"][0]).reshape(-1), inp["rwkv_gn_g"][0], inp["rwkv_gn_b"][0]]
    sh["pp"] = f(np.concatenate([np.asarray(c).reshape(4, 128).T for c in cols], axis=1))
    sh["gln_g"] = f(inp["gmlp_ln_g"][0]).reshape(1, 512); sh["gln_b"] = f(inp["gmlp_ln_b"][0]).reshape(1, 512)
    sh["wsT"] = f(np.asarray(inp["gmlp_w_s"][0]).transpose(2, 0, 1))
    bs = np.repeat(np.asarray(inp["gmlp_b_s"][0]), 64, axis=0)
    sh["bsF"] = f(bs.reshape(4, 128, 128).transpose(1, 0, 2))
    sh["w_branch"] = f(inp["w_branch"][0]).reshape(1024, D)
    sh["w_mix"] = f(inp["w_mix_out"][0])
    sh["wq"] = f(inp["xattn_w_q"][0]); sh["wkv"] = f(inp["xattn_w_kv"][0]); sh["wo"] = f(inp["xattn_w_o"][0])
    sh["pwq"] = f(inp["peer_w_query"][0])
    sh["skT"] = f(np.asarray(inp["peer_sub_keys"][0]).transpose(2, 0, 1))
    sh["puT"] = f(np.asarray(inp["peer_u"][0]).reshape(128, 128, 8, 128).transpose(1, 3, 2, 0).reshape(128, 128, D))
    sh["pvP"] = f(np.asarray(inp["peer_v"][0]).reshape(128, 128, D).transpose(1, 0, 2))
    sh.update(_consts())
    return sh


def make_in_maps(inp, cores):
    sh = prep_shared(inp)
    maps = []
    for b in cores:
        m = dict(sh)
        m["x"] = np.ascontiguousarray(np.asarray(inp["x"][b], dtype=np.float32))
        m["mem"] = np.ascontiguousarray(np.asarray(inp["mem"][b], dtype=np.float32))
        maps.append(m)
    return maps


def kernel(**inputs):
    nc, _ = build_program()
    maps = make_in_maps(inputs, list(range(N_CORES)))
    res = run_bass_kernel_spmd(nc, maps, core_ids=list(range(N_CORES)))
    return np.stack([np.asarray(r["out"], dtype=np.float32) for r in res.results], axis=0)
```

```python
from contextlib import ExitStack

import numpy as np
import concourse.bass as bass
import concourse.mybir as mybir
from concourse.bass_utils import run_bass_kernel_spmd

F32 = mybir.dt.float32
BF16 = mybir.dt.bfloat16
I32 = mybir.dt.int32
U32 = mybir.dt.uint32
AF = mybir.ActivationFunctionType
ALU = mybir.AluOpType
AX = mybir.AxisListType

N_CORES = 8
SEQ = 2048
D = 1024
NT = SEQ // 128


class Res:
    __slots__ = ("name", "w", "r", "excl")

    def __init__(self, name="", excl=False):
        self.name = name
        self.excl = excl
        self.w = None
        self.r = {}


class KB:
    def __init__(self, nc, es):
        self.nc = nc
        self.es = es
        self.eng = {"pe": nc.tensor, "act": nc.scalar, "dve": nc.vector, "pool": nc.gpsimd, "sp": nc.sync}
        self.sem = {}
        self.cnt = {}
        self.seen = {}
        for e in self.eng:
            self.sem[e] = es.enter_context(nc.semaphore("sem_" + e))
            self.cnt[e] = 0
            self.seen[e] = {}
        self.dsem = {}
        for q, n in (("sp", 8), ("pool", 8), ("act", 2)):
            self.dsem[q] = [[es.enter_context(nc.semaphore("dsem_%s%d" % (q, i))), 0] for i in range(n)]
        self.dptr = {q: 0 for q in self.dsem}
        self.n_ins = 0
        self.n_wait = 0
        self._id = 0
        self.defer = None

    def sb(self, shape, dt, name=None):
        self._id += 1
        return self.es.enter_context(self.nc.sbuf_tensor("%s_%d" % (name or "t", self._id), list(shape), dt))

    def ps(self, shape, dt, name=None):
        self._id += 1
        return self.es.enter_context(self.nc.psum_tensor("%s_%d" % (name or "p", self._id), list(shape), dt))

    def _deps(self, reads, writes):
        deps = []
        for r in reads:
            if r.w is not None:
                deps.append(r.w)
        for w in writes:
            if w.w is not None:
                deps.append(w.w)
            deps.extend(w.r.values())
        return deps

    def _wait(self, e, deps):
        eng = self.eng[e]
        seen = self.seen[e]
        best = {}
        for (sem, val) in deps:
            k = id(sem)
            if seen.get(k, 0) >= val:
                continue
            if k not in best or best[k][1] < val:
                best[k] = (sem, val)
        for k, (sem, val) in best.items():
            if e == "pe" and sem is self.sem["pe"]:
                continue
            eng.wait_ge(sem, val)
            self.n_wait += 1
            seen[k] = val

    def _mark(self, tok, reads, writes):
        for r in reads:
            k = id(tok[0])
            r.r[k] = tok
        for w in writes:
            w.w = tok
            w.r = {}

    def mark(self):
        if self.defer is not None:
            self.defer.append(None)

    def op(self, e, fn, reads=(), writes=()):
        if self.defer is not None:
            self.defer.append((e, fn, list(reads), list(writes)))
            return None
        ex = [r for r in reads if r.excl and r not in writes]
        if ex:
            writes = list(writes) + ex
        self._wait(e, self._deps(reads, writes))
        ins = fn(self.eng[e])
        self.cnt[e] += 1
        ins.then_inc(self.sem[e], 1)
        tok = (self.sem[e], self.cnt[e])
        self._mark(tok, reads, writes)
        self.n_ins += 1
        return tok

    def dma(self, q, fn, reads=(), writes=()):
        slots = self.dsem[q]
        slot = slots[self.dptr[q] % len(slots)]
        self.dptr[q] += 1
        deps = self._deps(reads, writes)
        if slot[1] > 0:
            deps.append((slot[0], slot[1]))
        self._wait(q, deps)
        ins = fn(self.eng[q])
        slot[1] += 16
        ins.then_inc(slot[0], 16)
        tok = (slot[0], slot[1])
        self._mark(tok, reads, writes)
        self.n_ins += 1
        return tok

    def wait_all(self, e, ress):
        deps = []
        for r in ress:
            if r.w is not None:
                deps.append(r.w)
            deps.extend(r.r.values())
        self._wait(e, deps)


class Ring:
    def __init__(self, k, n, shape, dt, name="ring"):
        self.items = [(k.sb(shape, dt, name), Res(name)) for _ in range(n)]
        self.i = 0

    def get(self):
        it = self.items[self.i % len(self.items)]
        self.i += 1
        return it


RW = 512
NHP = 4
RWKV_COLS = 1920
GM0 = 1920
GT0 = 2944
C0 = float(np.exp(-0.5))
ALPHA = float(2.0 ** 0.25)
LN_EPS = 1e-5
GN_EPS = 64e-5
CH = 64
BLK = 128
NCH = SEQ // CH
NBLK = SEQ // BLK


def build_program(debug=(), stop=None, scan_steps=None, scan_sub=99):
    nc = bass.Bass("TRN2", target_bir_lowering=False)
    dbg_outs = {}

    def din(name, shape, dt=F32):
        return nc.dram_tensor(name, list(shape), dt, kind="ExternalInput").ap()

    x = din("x", [SEQ, D]); mem = din("mem", [256, D])
    lng = {n: din(n, [1, D]) for n in ("ln_emb_g", "ln_emb_b", "ln1_g", "ln1_b", "ln2_g", "ln2_b", "ln3_g", "ln3_b", "mem_ln_g", "mem_ln_b")}
    w_in = din("w_in", [D, 4992]); mu_d = din("mu", [1, RWKV_COLS])
    w0T_d = din("w0T", [128, 8]); a0T_d = din("a0T", [128, 8])
    w2_d = din("w2", [128, RW]); a2_d = din("a2", [128, RW]); g2_d = din("g2", [128, RW])
    pp_d = din("pp", [128, 20])
    gln_g_d = din("gln_g", [1, 512]); gln_b_d = din("gln_b", [1, 512])
    wsT_d = din("wsT", [128, 8, 128]); bsF_d = din("bsF", [128, 4, 128])
    wbr_d = din("w_branch", [1024, D]); wmix_d = din("w_mix", [D, D])
    wq_d = din("wq", [D, D]); wkv_d = din("wkv", [D, 2 * D]); wo_d = din("wo", [D, D])
    pwq_d = din("pwq", [D, 2048]); skT_d = din("skT", [128, 2, 128])
    pu_d = din("puT", [128, 128, D]); pv_d = din("pvP", [128, 128, D])
    Ub_d = nc.dram_tensor("Ub", [128, 128, D], BF16, kind="Internal").ap()
    Vb_d = nc.dram_tensor("Vb", [128, 128, D], BF16, kind="Internal").ap()
    ident_d = din("c_ident", [128, 128]); onesbd_d = din("c_onesbd", [128, 128])
    maskF_d = din("c_maskF", [128, 512]); maskB_d = din("c_maskB", [128, 512]); rst_d = din("c_rst", [128, BLK])
    iota_d = din("c_iota", [128, 16])
    out_d = nc.dram_tensor("out", [SEQ, D], F32, kind="ExternalOutput").ap()

    es = ExitStack()
    with es:
        k = KB(nc, es)
        RAW = k.sb([128, 40960], F32, "raw")

        def view(off_kb, nbytes, dt, pattern=None, **kw):
            w0 = int(off_kb * 256)
            v = RAW[:, w0:w0 + nbytes // 4]
            if dt != F32:
                v = v.bitcast(dt)
            if pattern:
                v = v.rearrange(pattern, **kw)
            return v

        def barrier():
            toks = [(k.sem[e], k.cnt[e]) for e in k.eng if k.cnt[e] > 0]
            for q in k.dsem:
                for s in k.dsem[q]:
                    if s[1] > 0:
                        toks.append((s[0], s[1]))
            for e in k.eng:
                k._wait(e, toks)

        dbg_res = []

        def dbg(name, ap, res, shape):
            if name not in debug:
                return
            o = nc.dram_tensor("dbg_" + name, list(shape), F32, kind="ExternalOutput").ap()
            dbg_outs[name] = o
            barrier()
            if ap.dtype != F32:
                tmp = k.sb(list(shape), F32, "dbgtmp"); tr = Res()
                k.op("dve", lambda e: e.tensor_copy(tmp[:], ap), reads=res, writes=[tr])
                k.dma("sp", lambda e: e.dma_start(out=o, in_=tmp[:]), reads=[tr])
                dbg_res.append(tr)
            else:
                rr = Res()
                k.dma("sp", lambda e: e.dma_start(out=o, in_=ap), reads=res, writes=[rr])
                dbg_res.append(rr)

        def finish():
            barrier()

        cst = Res("consts")
        ident = k.sb([128, 128], BF16, "ident"); identf = k.sb([128, 128], F32, "identf")
        onesbd = k.sb([128, 128], BF16, "onesbd"); ones64 = k.sb([128, 128], F32, "ones64")
        maskF = k.sb([128, 512], BF16, "maskF"); maskB = k.sb([128, 512], BF16, "maskB")
        rst = k.sb([128, BLK], F32, "rst"); iota16 = k.sb([128, 16], F32, "iota16")
        w0T = k.sb([128, 8], F32, "w0T"); a0T = k.sb([128, 8], F32, "a0T"); pp = k.sb([128, 20], F32, "pp")
        ppx = k.sb([128, 8], F32, "ppx")
        w2b = k.sb([128, RW], BF16, "w2b"); a2b = k.sb([128, RW], BF16, "a2b"); g2b = k.sb([128, RW], BF16, "g2b")
        epsln = k.sb([128, 1], F32, "epsln"); epsgn = k.sb([128, 1], F32, "epsgn")
        for (t, d_) in ((ident, ident_d), (onesbd, onesbd_d), (maskF, maskF_d), (maskB, maskB_d), (w2b, w2_d), (a2b, a2_d), (g2b, g2_d)):
            k.dma("pool", lambda e: e.dma_start(out=t[:], in_=d_), writes=[cst])
        for (t, d_) in ((identf, ident_d), (rst, rst_d), (iota16, iota_d), (w0T, w0T_d), (a0T, a0T_d), (pp, pp_d)):
            k.dma("sp", lambda e: e.dma_start(out=t[:], in_=d_), writes=[cst])
        nw0T = k.sb([128, 8], F32, "nw0T"); na0T = k.sb([128, 8], F32, "na0T"); one1 = k.sb([128, 1], F32, "one1")
        k.op("dve", lambda e: e.memset(one1[:], 1.0), writes=[cst])
        k.op("dve", lambda e: e.memset(epsln[:], LN_EPS), writes=[cst])
        k.op("dve", lambda e: e.memset(epsgn[:], GN_EPS), writes=[cst])
        k.op("dve", lambda e: e.tensor_scalar(ones64[:], identf[:], 0.0, 0.0, ALU.mult, ALU.add), reads=[cst], writes=[cst])
        k.op("dve", lambda e: e.tensor_scalar(ones64[:], onesbd[:], 1.0 / 64.0, None, ALU.mult), reads=[cst], writes=[cst])
        k.op("dve", lambda e: e.tensor_scalar(nw0T[:], w0T[:], -1.0, None, ALU.mult), reads=[cst], writes=[cst])
        k.op("dve", lambda e: e.tensor_scalar(na0T[:], a0T[:], -1.0, None, ALU.mult), reads=[cst], writes=[cst])
        k.op("dve", lambda e: e.tensor_scalar(ppx[:, 0:4], pp[:, 4:8], -1.0, 1.0, ALU.mult, ALU.add), reads=[cst], writes=[cst])

        def act_sigmoid(eng_op, out, in_, nbias, reads, writes):
            eng_op("act", lambda e: e.activation(out=out, in_=in_, func=AF.Exp, bias=nbias, scale=-1.0), reads=reads + [cst], writes=writes)
            eng_op("act", lambda e: e.activation(out=out, in_=out, func=AF.Ln, bias=one1[:, 0:1], scale=1.0), reads=writes + [cst], writes=writes)
            eng_op("act", lambda e: e.activation(out=out, in_=out, func=AF.Exp, scale=-1.0), reads=writes, writes=writes)
        k.op("dve", lambda e: e.tensor_scalar(ppx[:, 4:8], pp[:, 4:8], -2.0, 2.0, ALU.mult, ALU.add), reads=[cst], writes=[cst])
        KK = lambda hp: pp[:, hp:hp + 1]
        KA = lambda hp: pp[:, 4 + hp:5 + hp]
        RK = lambda hp: pp[:, 8 + hp:9 + hp]
        GNG = lambda hp: pp[:, 12 + hp:13 + hp]
        GNB = lambda hp: pp[:, 16 + hp:17 + hp]

        psf = [(k.ps([128, 512], F32, "psf"), Res("psf%d" % i, True)) for i in range(6)]
        psb = [(k.ps([128, 1024], BF16, "psb"), Res("psb%d" % i, True)) for i in range(2)]
        pctr = {"f": 0, "b": 0}

        def PSF():
            it = psf[pctr["f"] % 6]; pctr["f"] += 1
            return it

        def PSB():
            it = psb[pctr["b"] % 2]; pctr["b"] += 1
            return it

        def mm(out, lhsT, rhs, start, stop, reads, writes):
            k.op("pe", lambda e: e.matmul(out, lhsT, rhs, start=start, stop=stop), reads=reads, writes=writes)

        def load_bc(ph, d_ap, n, name):
            t = ph.enter_context(nc.sbuf_tensor(name + "_%d" % k._id, [128, n], F32)); k._id += 1
            r = Res(name)
            k.dma("sp", lambda e: e.dma_start(out=t[:], in_=d_ap.partition_broadcast(128)), writes=[r])
            return t, r

        def phase_sb(ph, shape, dt, name):
            k._id += 1
            return ph.enter_context(nc.sbuf_tensor("%s_%d" % (name, k._id), list(shape), dt))

        class PRing:
            def __init__(self, ph, n, shape, dt, name):
                self.items = [(phase_sb(ph, shape, dt, name), Res(name)) for _ in range(n)]
                self.i = 0

            def get(self):
                it = self.items[self.i % len(self.items)]; self.i += 1
                return it

        def layer_norm(src, src_res, gam, bet, gbres, out, out_res, small, n=1024, eps=None):
            eps = eps if eps is not None else epsln
            nchk = n // 512
            st, sr = small.get()
            for c in range(nchk):
                k.op("dve", lambda e, c=c: e.bn_stats(out=st[:, c * 6:(c + 1) * 6], in_=src[:, c * 512:(c + 1) * 512]), reads=src_res, writes=[sr])
            k.op("dve", lambda e: e.bn_aggr(out=st[:, 12:14], in_=st[:, 0:6 * nchk]), reads=[sr], writes=[sr])
            k.op("act", lambda e: e.activation(out=st[:, 14:15], in_=st[:, 13:14], func=AF.Sqrt, bias=eps[:, 0:1], scale=1.0), reads=[sr, cst], writes=[sr])
            k.op("dve", lambda e: e.reciprocal(out=st[:, 14:15], in_=st[:, 14:15]), reads=[sr], writes=[sr])
            k.op("dve", lambda e: e.tensor_scalar(st[:, 15:16], st[:, 12:13], st[:, 14:15], -1.0, ALU.mult, ALU.mult), reads=[sr], writes=[sr])
            k.op("act", lambda e: e.activation(out=out, in_=src, func=AF.Identity, bias=st[:, 15:16], scale=st[:, 14:15]), reads=src_res + [sr], writes=out_res)
            k.op("dve", lambda e: e.tensor_tensor(out=out, in0=out, in1=gam, op=ALU.mult), reads=out_res + gbres, writes=out_res)
            k.op("dve", lambda e: e.tensor_tensor(out=out, in0=out, in1=bet, op=ALU.add), reads=out_res + gbres, writes=out_res)

        def to_fm(hb, hbr, dstT, dst_res, t):
            pt, ptr_ = PSB()
            for c in range(8):
                k.op("pe", lambda e, c=c: e.transpose(pt[:, c * 128:(c + 1) * 128], hb[:, c * 128:(c + 1) * 128], ident[:]), reads=[hbr, cst], writes=[ptr_])
            k.op("act", lambda e: e.activation(out=dstT[:, :, t * 128:(t + 1) * 128], in_=pt[:].rearrange("p (c t) -> p c t", t=128), func=AF.Copy), reads=[ptr_], writes=dst_res)

        def run_pairs(n, body):
            for t_ in range(0, n, 2):
                recs = []
                for tt_ in (t_, t_ + 1):
                    if tt_ >= n:
                        continue
                    k.defer = []
                    body(tt_)
                    recs.append([x_ for x_ in k.defer if x_ is not None]); k.defer = None
                for i_ in range(max(len(r_) for r_ in recs)):
                    for r_ in recs:
                        if i_ < len(r_):
                            k.op(*r_[i_])

        def phase_h0T(h0T, h0T_res):
            with ExitStack() as ph:
                g_t, g_r = load_bc(ph, lng["ln_emb_g"], D, "g")
                b_t, b_r = load_bc(ph, lng["ln_emb_b"], D, "b")
                xs = PRing(ph, 2, [128, D], F32, "xs")
                hbs = PRing(ph, 2, [128, D], BF16, "hb")
                small = PRing(ph, 4, [128, 16], F32, "lnsm")
                def tile_body(t):
                    xt, xr = xs.get()
                    k.dma("sp", lambda e: e.dma_start(out=xt[:], in_=x[t * 128:(t + 1) * 128, :]), writes=[xr])
                    layer_norm(xt[:], [xr], g_t[:], b_t[:], [g_r, b_r], xt[:], [xr], small)
                    hb, hbr = hbs.get()
                    k.op("act", lambda e: e.activation(out=hb[:], in_=xt[:], func=AF.Copy), reads=[xr], writes=[hbr])
                    to_fm(hb, hbr, h0T, [h0T_res[t]], t)
                for t_ in range(NT):
                    tile_body(t_)
                barrier()

        h0T = view(64, 32768, BF16, "p (c t) -> p c t", c=8)
        h0T_res = [Res("h0T%d" % t) for t in range(NT)]
        phase_h0T(h0T, h0T_res)
        dbg("h0T", h0T[:, 0, :], h0T_res, [128, SEQ])
        if stop == "A":
            finish()
            return nc, dbg_outs

        shT = view(96, 32768, BF16, "p (c t) -> p c t", c=8); shr = Res("shT")
        rkv = view(0, 12 * SEQ * 2, BF16, "p (c t) -> p c t", c=12)
        wag = view(48, 3 * SEQ * 2, BF16, "p (c t) -> p c t", c=3)
        rkv_res = [[Res("rkv") for _ in range(4)] for _ in range(15)]
        for c in range(8):
            k.op("dve", lambda e: e.tensor_tensor(out=shT[:, c, 1:SEQ - 1], in0=h0T[:, c, 0:SEQ - 2], in1=h0T[:, c, 2:SEQ], op=ALU.add), reads=h0T_res, writes=[shr])
        k.op("dve", lambda e: e.tensor_copy(shT[:, :, 0:1], h0T[:, :, 1:2]), reads=h0T_res, writes=[shr])
        k.op("dve", lambda e: e.tensor_copy(shT[:, :, SEQ - 1:SEQ], h0T[:, :, SEQ - 2:SEQ - 1]), reads=h0T_res, writes=[shr])
        with ExitStack() as ph:
            wsts = PRing(ph, 2, [128, 8, 128], F32, "wst")
            was = PRing(ph, 2, [128, 8, 128], BF16, "wa")
            wbs = PRing(ph, 2, [128, 8, 128], BF16, "wb")
            mus = PRing(ph, 2, [128, 384], F32, "mu")
            for cc in range(15):
                c0 = cc * 128
                wst, wsr = wsts.get()
                k.dma("sp", lambda e: e.dma_start(out=wst[:], in_=w_in[:, c0:c0 + 128].rearrange("(k p) c -> p k c", p=128)), writes=[wsr])
                mt, mr = mus.get()
                k.dma("sp", lambda e: e.dma_start(out=mt[:, 0:128], in_=mu_d[:, c0:c0 + 128].partition_broadcast(128)), writes=[mr])
                k.op("dve", lambda e: e.tensor_scalar(mt[:, 128:256], mt[:, 0:128], -1.0, 1.0, ALU.mult, ALU.add), reads=[mr], writes=[mr])
                k.op("dve", lambda e: e.tensor_scalar(mt[:, 256:384], mt[:, 0:128], 0.5, None, ALU.mult), reads=[mr], writes=[mr])
                wa, war = was.get(); wb, wbr = wbs.get()
                k.op("dve", lambda e: e.tensor_tensor(out=wa[:], in0=wst[:], in1=mt[:, 128:256].unsqueeze(1).to_broadcast([128, 8, 128]), op=ALU.mult), reads=[wsr, mr], writes=[war])
                k.op("pool", lambda e: e.tensor_tensor(out=wb[:], in0=wst[:], in1=mt[:, 256:384].unsqueeze(1).to_broadcast([128, 8, 128]), op=ALU.mult), reads=[wsr, mr], writes=[wbr])
                for tb in range(4):
                    pb, pbr = PSF()
                    ts = slice(tb * 512, (tb + 1) * 512)
                    for kc in range(8):
                        mm(pb[:], wa[:, kc, :], h0T[:, kc, ts], kc == 0, False, [war] + h0T_res[4 * tb:4 * tb + 4], [pbr])
                    for kc in range(8):
                        mm(pb[:], wb[:, kc, :], shT[:, kc, ts], False, kc == 7, [wbr, shr], [pbr])
                    if cc < 12:
                        dest, fn = rkv[:, cc, ts], AF.Copy
                    else:
                        dest, fn = wag[:, cc - 12, ts], (AF.Tanh, AF.Copy, AF.Sigmoid)[cc - 12]
                    k.op("act", lambda e: e.activation(out=dest, in_=pb[:], func=fn), reads=[pbr], writes=[rkv_res[cc][tb]])
            barrier()
        dbg("r0", rkv[:, 0, :], rkv_res[0], [128, SEQ])
        dbg("k0", rkv[:, 4, :], rkv_res[4], [128, SEQ])
        dbg("wd", wag[:, 0, :], rkv_res[12], [128, SEQ])
        if stop == "B":
            finish()
            return nc, dbg_outs


        barrier()
        yaT = rkv[:, 0:4, :]
        yaT_res = [rkv_res[hp] for hp in range(4)]

        class Carver:
            def __init__(self, a_kb, b_kb):
                self.p = a_kb * 1024; self.end = b_kb * 1024

            def alloc(self, shape, dt):
                nb = int(np.prod(shape[1:])) * (2 if dt == BF16 else 4)
                nb = (nb + 31) // 32 * 32
                assert self.p + nb <= self.end, "carver overflow"
                v = RAW[:, self.p // 4:(self.p + nb) // 4]
                self.p += nb
                if dt != F32:
                    v = v.bitcast(dt)
                n = int(np.prod(shape[1:]))
                v = v[:, 0:n]
                if len(shape) == 3:
                    v = v.rearrange("p (a b) -> p a b", a=shape[1])
                return v

        def scan_round(rnd):
            hps = [2 * rnd, 2 * rnd + 1]
            carve = Carver(64, 128)
            carve2 = Carver(144, 160)
            yacc = view(128, 2 * SEQ * 4, F32, "p (c t) -> p c t", c=2)
            yacc_res = [[Res("yacc") for _ in range(NCH)] for _ in range(2)]
            if scan_steps:
                for hl_ in range(2):
                    k.op("pool", lambda e: e.memset(yacc[:, hl_, :], 0.0), writes=yacc_res[hl_])
                    for r_ in yacc_res[hl_]:
                        r_.w = None
            with ExitStack() as ph:
                def T(shape, dt, name):
                    return (phase_sb(ph, shape, dt, name), Res(name))
                tmp = {n: T([128, BLK], F32, n) for n in ("sw", "sa", "cs", "pin", "pex", "ege", "egi", "kr", "rn", "kk", "nkk", "t1", "kd", "bb")}
                ksq = T([128, BLK], BF16, "ksq")
                yb = T([128, BLK], BF16, "yb"); ysqb = T([128, BLK], BF16, "ysqb")
                NBK = NCH // 2
                streams = []
                for z in (0, 1):
                    for hl, hp in enumerate(hps):
                        st = dict(z=z, hp=hp, hl=hl, ui=len(streams))
                        st["A3"] = [dict(AR=carve.alloc([128, 2, 192], BF16), eg=carve.alloc([128, BLK], F32), res=Res("prepA")) for _ in range(3)]
                        st["K2"] = [dict(Kb=carve.alloc([128, 2, 128], BF16), Bb=carve.alloc([128, 2, 128], BF16), Vb=carve.alloc([128, 2, 128], BF16), res=Res("prepK")) for _ in range(2)]
                        for sl in st["A3"]:
                            k.op("pool", lambda e: e.memset(sl["AR"], 0.0), writes=[sl["res"]])
                        for sl in st["K2"]:
                            for nm in ("Kb", "Bb", "Vb"):
                                k.op("pool", lambda e: e.memset(sl[nm], 0.0), writes=[sl["res"]])
                        st["Tm"] = [[(carve2.alloc([128, 512], BF16), Res("Tm")) for _ in range(2)] for _ in range(2)]
                        st["WT"] = T([128, 128], BF16, "WT"); st["UT"] = T([128, 128], BF16, "UT")
                        st["S"] = [T([128, 128], BF16, "S") for _ in range(2)]
                        k.op("pool", lambda e: e.memset(st["S"][0][0][:], 0.0), writes=[st["S"][0][1]])
                        st["si"] = 0
                        streams.append(st)
                KTG = [[(carve.alloc([128, 4, 384], BF16), Res("KTG")) for _ in range(2)] for _ in range(2)]
                MVG = [[(carve.alloc([128, 4, 128], BF16), Res("MVG")) for _ in range(2)] for _ in range(2)]
                IVG = [[(carve.alloc([128, 3, 512], BF16), Res("IVG")) for _ in range(2)] for _ in range(2)]
                chain_res = [Res("chain%d" % i, True) for i in range(2)]
                psbf = [(psb[i][0][:].bitcast(F32), psb[i][1]) for i in range(2)]
                lvl_banks = [[psf[0], psf[1], psf[2]], [psf[3], psbf[0], psbf[1]]]

                def blk_of(st, bi):
                    return bi if st["z"] == 0 else NBK - 1 - bi

                def chunk_of(st, bi, g):
                    b = blk_of(st, bi)
                    return 2 * b + g if st["z"] == 0 else 2 * b + 1 - g

                def prep(st, bi, tmp, ksq, pcol):
                    z, hp = st["z"], st["hp"]
                    b = blk_of(st, bi)
                    sa_ = st["A3"][bi % 3]; sk_ = st["K2"][bi % 2]
                    t0 = b * BLK; ts = slice(t0, t0 + BLK); tb = t0 // 512
                    zs = slice(z * 64, (z + 1) * 64)
                    hc = slice(hp * 128, (hp + 1) * 128)
                    r_ap, k_ap, v_ap = rkv[:, hp, ts], rkv[:, 4 + hp, ts], rkv[:, 8 + hp, ts]
                    rr, kr_, vr_ = [rkv_res[hp][tb]], [rkv_res[4 + hp][tb]], [rkv_res[8 + hp][tb]]
                    pb, pbr = psbf[0][0][:, pcol:pcol + 256], psbf[0][1]
                    mm(pb[:, 0:BLK], w2b[zs, hc], wag[zs, 0, ts], True, True, [cst, rkv_res[12][tb]], [pbr])
                    mm(pb[:, BLK:2 * BLK], a2b[zs, hc], wag[zs, 1, ts], True, True, [cst, rkv_res[13][tb]], [pbr])
                    (sw, swr), (sa, sar), (cs, csr) = tmp["sw"], tmp["sa"], tmp["cs"]
                    (pin, pinr), (pex, pexr), (ege, eger), (egi, egir) = tmp["pin"], tmp["pex"], tmp["ege"], tmp["egi"]
                    act_sigmoid(k.op, sw[:], pb[:, 0:BLK], nw0T[:, z * 4 + hp:z * 4 + hp + 1], [pbr], [swr])
                    act_sigmoid(k.op, sa[:], pb[:, BLK:2 * BLK], na0T[:, z * 4 + hp:z * 4 + hp + 1], [pbr], [sar])
                    k.mark()
                    k.op("dve", lambda e: e.tensor_tensor_scan(out=cs[:], data0=rst[:], data1=sw[:], initial=0.0, op0=ALU.mult, op1=ALU.add), reads=[swr, cst], writes=[csr])
                    v3 = lambda t_: t_.rearrange("p (c t) -> p c t", t=CH)
                    if z == 0:
                        k.op("dve", lambda e: e.tensor_tensor(out=pex[:], in0=cs[:], in1=sw[:], op=ALU.subtract), reads=[csr, swr], writes=[pexr])
                        pin_t, pin_r = cs, csr
                    else:
                        k.op("dve", lambda e: e.tensor_tensor(out=v3(pex[:]), in0=v3(cs[:])[:, :, CH - 1:CH].to_broadcast([128, 2, CH]), in1=v3(cs[:]), op=ALU.subtract), reads=[csr], writes=[pexr])
                        k.op("dve", lambda e: e.tensor_tensor(out=pin[:], in0=pex[:], in1=sw[:], op=ALU.add), reads=[pexr, swr], writes=[pinr])
                        pin_t, pin_r = pin, pinr
                    eg = sa_["eg"]; rA = sa_["res"]; rK = sk_["res"]
                    k.op("act", lambda e: e.activation(out=eg, in_=pin_t[:], func=AF.Exp, scale=-C0), reads=[pin_r], writes=[rA])
                    k.op("act", lambda e: e.activation(out=ege[:], in_=pex[:], func=AF.Exp, scale=-C0), reads=[pexr], writes=[eger])
                    k.op("act", lambda e: e.activation(out=egi[:], in_=pin_t[:], func=AF.Exp, scale=C0), reads=[pin_r], writes=[egir])
                    k.mark()
                    (kr, krr), (rn, rnr), (kk, kkr) = tmp["kr"], tmp["rn"], tmp["kk"]
                    k.op("dve", lambda e: e.tensor_scalar(kr[:], k_ap, KK(hp), None, ALU.mult), reads=kr_ + [cst], writes=[krr])
                    k.op("dve", lambda e: e.tensor_tensor(out=ksq[0][:], in0=kr[:], in1=kr[:], op=ALU.mult), reads=[krr], writes=[ksq[1]])
                    pb2, pb2r = psbf[1][0][:, pcol:pcol + 256], psbf[1][1]
                    mm(pb2[:, 0:BLK], onesbd[:], ksq[0][:], True, True, [cst, ksq[1]], [pb2r])
                    k.op("dve", lambda e: e.tensor_scalar(rn[:], pb2[:, 0:BLK], 1e-24, None, ALU.max), reads=[pb2r], writes=[rnr])
                    k.mark()
                    k.op("act", lambda e: e.activation(out=rn[:], in_=rn[:], func=AF.Ln), reads=[rnr], writes=[rnr])
                    k.op("act", lambda e: e.activation(out=rn[:], in_=rn[:], func=AF.Exp, scale=-0.5), reads=[rnr], writes=[rnr])
                    k.op("dve", lambda e: e.tensor_tensor(out=kk[:], in0=kr[:], in1=rn[:], op=ALU.mult), reads=[krr, rnr], writes=[kkr])
                    nkk, nkkr = tmp["nkk"]
                    k.op("dve", lambda e: e.tensor_scalar(nkk[:], kk[:], -1.0, None, ALU.mult), reads=[kkr], writes=[nkkr])
                    AR, Kb, Bb, Vb = sa_["AR"], sk_["Kb"], sk_["Bb"], sk_["Vb"]
                    k.op("dve", lambda e: e.tensor_tensor(out=AR[:, :, 128:192], in0=v3(r_ap), in1=v3(eg), op=ALU.mult), reads=rr + [rA], writes=[rA])
                    (t1, t1r), (kd, kdr), (bb, bbr) = tmp["t1"], tmp["kd"], tmp["bb"]
                    k.op("dve", lambda e: e.tensor_scalar(t1[:], sa[:], KA(hp), ppx[:, hp:hp + 1], ALU.mult, ALU.add), reads=[sar, cst], writes=[t1r])
                    k.op("dve", lambda e: e.tensor_tensor(out=kd[:], in0=t1[:], in1=k_ap, op=ALU.mult), reads=[t1r] + kr_, writes=[kdr])
                    k.op("dve", lambda e: e.tensor_tensor(out=bb[:], in0=kk[:], in1=sa[:], op=ALU.mult), reads=[kkr, sar], writes=[bbr])
                    k.mark()

                    def half_ops(half):
                        hs = slice(half * 64, (half + 1) * 64); cs_ = slice(half * 64, (half + 1) * 64)
                        eng = "dve" if half == 0 else "pool"
                        k.op(eng, lambda e: e.tensor_tensor(out=AR[hs, :, cs_], in0=v3(nkk[hs, :]), in1=v3(ege[hs, :]), op=ALU.mult), reads=[nkkr, eger, rA], writes=[rA])
                        k.op(eng, lambda e: e.tensor_tensor(out=Kb[hs, :, cs_], in0=v3(kd[hs, :]), in1=v3(egi[hs, :]), op=ALU.mult), reads=[kdr, egir, rK], writes=[rK])
                        k.op(eng, lambda e: e.tensor_tensor(out=Bb[hs, :, cs_], in0=v3(bb[hs, :]), in1=v3(egi[hs, :]), op=ALU.mult), reads=[bbr, egir, rK], writes=[rK])
                        k.op("act", lambda e: e.activation(out=Vb[hs, :, cs_], in_=v3(v_ap)[hs], func=AF.Copy), reads=vr_ + [rK], writes=[rK])
                    half_ops(0)
                    half_ops(1)
                    k.mark()

                def unit_ctx(st, bi, g):
                    bp = bi % 2
                    sa_ = st["A3"][bi % 3]; sk_ = st["K2"][bi % 2]
                    c = chunk_of(st, bi, g); ci = c % 2
                    return dict(st=st, ui=st["ui"], c=c, ci=ci, bp=bp, g=g, rA=sa_["res"], rK=sk_["res"], eg=sa_["eg"],
                                AR=sa_["AR"][:, ci, :], Kb=sk_["Kb"][:, ci, :], Bb=sk_["Bb"][:, ci, :], Vb=sk_["Vb"][:, ci, :],
                                Tm=st["Tm"][bp][g], KT=(KTG[bp][g][0][:, st["ui"], :], KTG[bp][g][1]), Minv=(MVG[bp][g][0][:, st["ui"], :], MVG[bp][g][1]))

                def pre_block(bi):
                    groups = [[unit_ctx(st, bi, g) for st in streams] for g in range(2)]
                    macro = []

                    def t_pe(g):
                        def f():
                            for u in groups[g]:
                                ui = u["ui"]
                                pb, pbr = psf[ui]; rd = [u["rA"], u["rK"]]
                                mm(pb[:, 0:192], u["Kb"], u["AR"], True, True, rd, [pbr])
                                mm(pb[:, 192:384], u["Bb"], u["AR"], True, True, rd, [pbr])
                                mm(pb[:, 384:512], u["AR"][:, 0:128], u["Bb"], True, True, rd, [pbr])
                                pt_, ptr_ = psb[ui // 2]; pt = pt_[:, (ui % 2) * 384:(ui % 2) * 384 + 384]
                                for i, nm in enumerate(("Kb", "Bb", "Vb")):
                                    k.op("pe", lambda e: e.transpose(pt[:, i * 128:(i + 1) * 128], u[nm], ident[:]), reads=rd + [cst], writes=[ptr_])
                        return f

                    def t_ev(g):
                        def f():
                            bp = bi % 2
                            IA, IAr = IVG[g][0]
                            for u in groups[g]:
                                ui = u["ui"]
                                pb, pbr = psf[ui]; Tm, Tmr = u["Tm"]
                                msk = maskF if u["st"]["z"] == 0 else maskB
                                k.op("dve", lambda e: e.tensor_tensor(out=Tm, in0=pb[:], in1=msk[:], op=ALU.mult), reads=[pbr, cst], writes=[Tmr])
                            KTt, KTr = KTG[bp][g]
                            for j in range(2):
                                pt_, ptr_ = psb[j]
                                k.op("act", lambda e: e.activation(out=KTt[:, 2 * j:2 * j + 2, :], in_=pt_[:, 0:768].rearrange("p (u c) -> p u c", u=2), func=AF.Copy), reads=[ptr_], writes=[KTr])
                            for u in groups[g]:
                                ui = u["ui"]; Tm, Tmr = u["Tm"]
                                k.op("dve", lambda e: e.tensor_tensor(out=IA[:, 2, ui * 128:(ui + 1) * 128], in0=Tm[:, 192:320], in1=ident[:], op=ALU.add), reads=[Tmr, cst], writes=[IAr])
                        return f
                    for g in range(2):
                        macro.append([t_pe(g), t_ev(g)])

                    def lvl_pe(l, g):
                        def f():
                            (bP, bPr), (bQ, bQr), (bX, bXr) = lvl_banks[g]
                            for u in groups[g]:
                                ui = u["ui"]; cs_ = slice(ui * 128, (ui + 1) * 128)
                                if l == 1:
                                    Tm, Tmr = u["Tm"]
                                    P, Q, rd = Tm[:, 192:320], Tm[:, 384:512], [Tmr]
                                    mm(bP[:, cs_], Q, P, True, True, rd, [bPr])
                                    mm(bQ[:, cs_], P, Q, True, True, rd, [bQr])
                                else:
                                    src, srcr = IVG[g][l % 2]
                                    P, Q, X = src[:, 0, cs_], src[:, 1, cs_], src[:, 2, cs_]
                                    if l <= 4:
                                        mm(bP[:, cs_], Q, P, True, True, [srcr], [bPr])
                                    if l <= 5:
                                        mm(bQ[:, cs_], P, Q, True, True, [srcr], [bQr])
                                    mm(bX[:, cs_], ident[:], X, True, False, [srcr, cst], [bXr])
                                    mm(bX[:, cs_], Q, X, False, True, [srcr], [bXr])
                        return f

                    def lvl_ev(l, g):
                        def f():
                            (bP, bPr), (bQ, bQr), (bX, bXr) = lvl_banks[g]
                            bp = bi % 2
                            jobs = []
                            if l == 1:
                                dst, dstr = IVG[g][0]
                                jobs = [(dst[:, 0, :], bP, bPr, dstr), (dst[:, 1, :], bQ, bQr, dstr)]
                            elif l <= 5:
                                dst, dstr = IVG[g][(l + 1) % 2]
                                if l <= 4:
                                    jobs.append((dst[:, 0, :], bP, bPr, dstr))
                                jobs.append((dst[:, 1, :], bQ, bQr, dstr))
                                jobs.append((dst[:, 2, :], bX, bXr, dstr))
                            else:
                                mv, mvr = MVG[bp][g]
                                jobs = [(mv[:].rearrange("p u c -> p (u c)"), bX, bXr, mvr)]
                            for i, (o, bk, bkr, dr) in enumerate(jobs):
                                if (i + l + g) % 2 == 0:
                                    k.op("act", lambda e: e.activation(out=o, in_=bk[:, 0:512], func=AF.Copy), reads=[bkr], writes=[dr])
                                else:
                                    k.op("dve", lambda e: e.tensor_copy(o, bk[:, 0:512]), reads=[bkr], writes=[dr])
                        return f
                    for l in range(1, 7):
                        macro.append([lvl_pe(l, 0), lvl_pe(l, 1), lvl_ev(l, 0), lvl_ev(l, 1)])
                    return macro

                def chain_stages(bi, g):
                    ctx = [unit_ctx(st, bi, g) for st in streams]

                    def w_pe():
                        for u in ctx:
                            st = u["st"]; Tm, Tmr = u["Tm"]; KT, KTr = u["KT"]
                            S, Sr = st["S"][st["si"] % 2]
                            ui = u["ui"]
                            pb = psf[4 + ui // 2][0][:, (ui % 2) * 192:(ui % 2) * 192 + 192]; pbr = chain_res[ui // 2]; u["pC"] = (pb, pbr)
                            mm(pb[:, 0:128], Tm[:, 0:128], KT[:, 256:384], True, False, [Tmr, KTr], [pbr])
                            mm(pb[:, 0:128], u["AR"][:, 0:128], S[:], False, True, [u["rA"], Sr], [pbr])

                    def w_ev():
                        for u in ctx:
                            pb, pbr = u["pC"]; WT, WTr = u["st"]["WT"]
                            k.op("act", lambda e: e.activation(out=WT[:], in_=pb[:, 0:128], func=AF.Copy), reads=[pbr], writes=[WTr])

                    def u_pe():
                        for u in ctx:
                            pb, pbr = u["pC"]; WT, WTr = u["st"]["WT"]; Mi, Mir = u["Minv"]
                            mm(pb[:, 0:128], Mi, WT[:], True, True, [Mir, WTr], [pbr])

                    def u_ev():
                        for u in ctx:
                            pb, pbr = u["pC"]; UT, UTr = u["st"]["UT"]
                            k.op("dve", lambda e: e.tensor_copy(UT[:], pb[:, 0:128]), reads=[pbr], writes=[UTr])

                    def ys_pe():
                        for u in ctx:
                            st = u["st"]; Tm, Tmr = u["Tm"]; KT, KTr = u["KT"]; UT, UTr = st["UT"]
                            S, Sr = st["S"][st["si"] % 2]
                            pb, pbr = u["pC"]; rA = u["rA"]
                            mm(pb[:, 128:192], KT[:, 256:384], Tm[:, 128:192], True, False, [KTr, Tmr], [pbr])
                            mm(pb[:, 128:192], S[:], u["AR"][:, 128:192], False, False, [Sr, rA], [pbr])
                            mm(pb[:, 128:192], UT[:], Tm[:, 320:384], False, True, [UTr, Tmr], [pbr])
                            mm(pb[:, 0:128], KT[:, 0:128], KT[:, 256:384], True, False, [KTr], [pbr])
                            mm(pb[:, 0:128], ident[:], S[:], False, False, [cst, Sr], [pbr])
                            mm(pb[:, 0:128], KT[:, 128:256], UT[:], False, True, [KTr, UTr], [pbr])

                    def ys_ev():
                        for u in ctx:
                            st = u["st"]; pb, pbr = u["pC"]; c = u["c"]; ci = u["ci"]
                            Sn, Snr = st["S"][(st["si"] + 1) % 2]
                            eg = u["eg"]
                            col = ci * CH + (CH - 1 if st["z"] == 0 else 0)
                            k.op("act", lambda e: e.activation(out=Sn[:], in_=pb[:, 0:128], func=AF.Identity, scale=eg[:, col:col + 1]), reads=[pbr, u["rA"]], writes=[Snr])
                            st["si"] += 1
                            yr = yacc_res[st["hl"]][c]
                            ydst = yacc[:, st["hl"], c * CH:(c + 1) * CH]
                            if yr.w is None:
                                k.op("dve", lambda e: e.tensor_copy(ydst, pb[:, 128:192]), reads=[pbr], writes=[yr])
                            else:
                                k.op("dve", lambda e: e.tensor_tensor(out=ydst, in0=ydst, in1=pb[:, 128:192], op=ALU.add), reads=[pbr, yr], writes=[yr])
                    return [[w_pe, w_ev], [u_pe, u_ev], [ys_pe, ys_ev]]

                def replay(chunk):
                    for (e_, fn_, rd_, wr_) in chunk:
                        k.op(e_, fn_, rd_, wr_)

                tmpB = {n_: (carve.alloc([128, BLK], F32), Res(n_ + "B")) for n_ in tmp}
                ksqB = (carve.alloc([128, BLK], BF16), Res("ksqB"))

                def record_prep(bi_):
                    per_stream = []
                    for si_, st in enumerate(streams):
                        k.defer = []
                        if si_ % 2 == 0:
                            prep(st, bi_, tmp, ksq, 0)
                        else:
                            prep(st, bi_, tmpB, ksqB, 256)
                        rec = k.defer; k.defer = None
                        chunks = [[]]
                        for it in rec:
                            if it is None:
                                chunks.append([])
                            else:
                                chunks[-1].append(it)
                        per_stream.append(chunks)
                    out = []
                    for p_ in range(0, len(streams), 2):
                        ca, cb = per_stream[p_], per_stream[p_ + 1]
                        for j_ in range(max(len(ca), len(cb))):
                            a_ = ca[j_] if j_ < len(ca) else []
                            b_ = cb[j_] if j_ < len(cb) else []
                            m_ = []
                            for i_ in range(max(len(a_), len(b_))):
                                if i_ < len(a_):
                                    m_.append(a_[i_])
                                if i_ < len(b_):
                                    m_.append(b_[i_])
                            if m_:
                                out.append(m_)
                    return out
                nblk = (scan_steps + 1) // 2 if scan_steps else NBK
                for bi_ in range(min(2, nblk)):
                    for c_ in record_prep(bi_):
                        replay(c_)
                for ms in pre_block(0):
                    for f in ms:
                        f()
                for bi in range(nblk):
                    A = pre_block(bi + 1) if bi + 1 < nblk else []
                    B = [f for pr in (chain_stages(bi, 0) + chain_stages(bi, 1)) for f in pr]
                    C = record_prep(bi + 2) if bi + 2 < nblk else []
                    Af = []
                    for ms in A:
                        flags = [False, True] if len(ms) == 2 else [True, False, False, True]
                        Af += list(zip(ms, flags))
                    nb_per = max(1, (len(B) + max(1, len(Af)) - 1) // max(1, len(Af)))
                    while Af or B or C:
                        safe = True
                        if Af:
                            f, safe = Af.pop(0)
                            f()
                        for _ in range(nb_per if Af else len(B)):
                            if B:
                                B.pop(0)()
                        if C and (safe or not any(op_[0] == "pe" for op_ in C[0])):
                            replay(C.pop(0))
                        if not Af and not B:
                            while C:
                                replay(C.pop(0))
                dbg("yacc%d" % rnd, yacc[:, 0, :], yacc_res[0], [128, SEQ])
                fo = k.op
                fmm = mm
                fctr = [0]

                def FPS():
                    it = psf[fctr[0] % 4]; fctr[0] += 1
                    return it
                def fin_block(hl, hp, b, tmp, ksq, yb, ysqb, bankA, bankB):
                    hc = slice(hp * 128, (hp + 1) * 128)
                    t0 = b * BLK; ts = slice(t0, t0 + BLK); tb = t0 // 512
                    y = yacc[:, hl, ts]; yres = yacc_res[hl][2 * b:2 * b + 2]
                    r_ap, k_ap, v_ap = rkv[:, hp, ts], rkv[:, 4 + hp, ts], rkv[:, 8 + hp, ts]
                    rr, kr_, vr_ = [rkv_res[hp][tb]], [rkv_res[4 + hp][tb]], [rkv_res[8 + hp][tb]]
                    (ysq, ysqr), (mean, meanr), (msq, msqr), (var, varr) = tmp["sw"], tmp["sa"], tmp["cs"], tmp["pin"]
                    (rs, rsr), (yn, ynr), (s0, s0r), (s1, s1r), (bon, bonr) = tmp["pex"], tmp["ege"], tmp["egi"], tmp["kr"], tmp["rn"]
                    fo("dve", lambda e: e.tensor_tensor(out=ysqb[0][:], in0=y, in1=y, op=ALU.mult), reads=yres, writes=[ysqb[1]])
                    fo("act", lambda e: e.activation(out=yb[0][:], in_=y, func=AF.Copy), reads=yres, writes=[yb[1]])
                    pa, par = bankA
                    fmm(pa[:, 0:BLK], onesbd[:], yb[0][:], True, True, [cst, yb[1]], [par])
                    fmm(pa[:, BLK:2 * BLK], onesbd[:], ysqb[0][:], True, True, [cst, ysqb[1]], [par])
                    fo("act", lambda e: e.activation(out=mean[:], in_=pa[:, 0:BLK], func=AF.Copy, scale=1.0 / 64.0), reads=[par], writes=[meanr])
                    fo("dve", lambda e: e.tensor_tensor(out=msq[:], in0=mean[:], in1=mean[:], op=ALU.mult), reads=[meanr], writes=[msqr])
                    fo("dve", lambda e: e.scalar_tensor_tensor(out=var[:], in0=pa[:, BLK:2 * BLK], scalar=1.0 / 64.0, in1=msq[:], op0=ALU.mult, op1=ALU.subtract), reads=[par, msqr], writes=[varr])
                    fo("act", lambda e: e.activation(out=rs[:], in_=var[:], func=AF.Ln, bias=epsgn[:, 0:1], scale=1.0), reads=[varr, cst], writes=[rsr])
                    fo("act", lambda e: e.activation(out=rs[:], in_=rs[:], func=AF.Exp, scale=-0.5), reads=[rsr], writes=[rsr])
                    fo("dve", lambda e: e.tensor_tensor(out=yn[:], in0=y, in1=mean[:], op=ALU.subtract), reads=yres + [meanr], writes=[ynr])
                    fo("dve", lambda e: e.tensor_tensor(out=yn[:], in0=yn[:], in1=rs[:], op=ALU.mult), reads=[ynr, rsr], writes=[ynr])
                    fo("dve", lambda e: e.tensor_scalar(yn[:], yn[:], GNG(hp), GNB(hp), ALU.mult, ALU.add), reads=[ynr, cst], writes=[ynr])
                    pb_, pbr_ = bankA
                    fmm(pb_[:, 0:BLK], a2b[0:64, hc], wag[0:64, 1, ts], True, True, [cst, rkv_res[13][tb]], [pbr_])
                    pb2_, pbr2_ = bankB
                    fmm(pb2_[:, 0:BLK], a2b[64:128, hc], wag[64:128, 1, ts], True, True, [cst, rkv_res[13][tb]], [pbr2_])
                    act_sigmoid(fo, s0[:], pb_[:, 0:BLK], na0T[:, hp:hp + 1], [pbr_], [s0r])
                    act_sigmoid(fo, s1[:], pb2_[:, 0:BLK], na0T[:, 4 + hp:5 + hp], [pbr2_], [s1r])
                    fo("dve", lambda e: e.tensor_tensor(out=s0[:], in0=s0[:], in1=s1[:], op=ALU.add), reads=[s0r, s1r], writes=[s0r])
                    fo("dve", lambda e: e.tensor_scalar(s0[:], s0[:], KA(hp), ppx[:, 4 + hp:5 + hp], ALU.mult, ALU.add), reads=[s0r, cst], writes=[s0r])
                    fo("dve", lambda e: e.tensor_tensor(out=s0[:], in0=s0[:], in1=k_ap, op=ALU.mult), reads=[s0r] + kr_, writes=[s0r])
                    fo("dve", lambda e: e.scalar_tensor_tensor(out=ksq[0][:], in0=r_ap, scalar=RK(hp), in1=s0[:], op0=ALU.mult, op1=ALU.mult), reads=rr + [s0r, cst], writes=[ksq[1]])
                    pc_, pcr_ = bankA
                    fmm(pc_[:, 0:BLK], onesbd[:], ksq[0][:], True, True, [cst, ksq[1]], [pcr_])
                    fmm(pc_[:, BLK:2 * BLK], g2b[:, hc], wag[:, 2, ts], True, True, [cst, rkv_res[14][tb]], [pcr_])
                    fo("dve", lambda e: e.tensor_tensor(out=bon[:], in0=pc_[:, 0:BLK], in1=v_ap, op=ALU.mult), reads=[pcr_] + vr_, writes=[bonr])
                    fo("dve", lambda e: e.tensor_tensor(out=yn[:], in0=yn[:], in1=bon[:], op=ALU.add), reads=[ynr, bonr], writes=[ynr])
                    fo("dve", lambda e: e.tensor_tensor(out=yaT[:, hp, ts], in0=yn[:], in1=pc_[:, BLK:2 * BLK], op=ALU.mult), reads=[ynr, pcr_], writes=[yaT_res[hp][tb]])
                barrier()
                cvf = Carver(64, 128)
                tmp2 = {n_: (cvf.alloc([128, BLK], F32), Res(n_ + "2")) for n_ in tmp}
                ksq2 = (cvf.alloc([128, BLK], BF16), Res("ksq2")); yb2 = (cvf.alloc([128, BLK], BF16), Res("yb2")); ysqb2 = (cvf.alloc([128, BLK], BF16), Res("ysqb2"))
                for b in range(NBLK):
                    recs = []
                    for hl, hp in enumerate(hps):
                        k.defer = []
                        if hl == 0:
                            fin_block(hl, hp, b, tmp, ksq, yb, ysqb, psf[0], psf[1])
                        else:
                            fin_block(hl, hp, b, tmp2, ksq2, yb2, ysqb2, psf[2], psf[3])
                        recs.append([it for it in k.defer if it is not None]); k.defer = None
                    for i_ in range(max(len(r_) for r_ in recs)):
                        for r_ in recs:
                            if i_ < len(r_):
                                k.op(*r_[i_])
                return yacc, yacc_res

        for rnd in range(2):
            yacc, yacc_res = scan_round(rnd)
            if stop == "C0":
                dbg("yaT0", yaT[:, 0, :], yaT_res[0], [128, SEQ])
                finish()
                return nc, dbg_outs
            barrier()
        dbg("yaT0", yaT[:, 0, :], yaT_res[0], [128, SEQ])
        dbg("yaT3", yaT[:, 3, :], yaT_res[3], [128, SEQ])
        if stop == "C":
            finish()
            return nc, dbg_outs

        barrier()
        h0T_res = [Res("h0T%d" % t) for t in range(NT)]
        phase_h0T(h0T, h0T_res)
        ybT = view(16, 4 * SEQ * 2, BF16, "p (c t) -> p c t", c=4)
        ybT_res = [Res("ybT%d" % t) for t in range(NT)]
        with ExitStack() as ph:
            Wu = view(96, 8 * 512 * 2, BF16, "p (k c) -> p k c", k=8); Wv = view(104, 8 * 512 * 2, BF16, "p (k c) -> p k c", k=8)
            wr_ = Res("WuWv")
            k.dma("pool", lambda e: e.dma_start(out=Wu, in_=w_in[:, GM0:GM0 + 512].rearrange("(k p) c -> p k c", p=128)), writes=[wr_])
            k.dma("pool", lambda e: e.dma_start(out=Wv, in_=w_in[:, GM0 + 512:GM0 + 1024].rearrange("(k p) c -> p k c", p=128)), writes=[wr_])
            wsT = phase_sb(ph, [128, 8, 128], BF16, "wsT"); bsF = phase_sb(ph, [128, 4, 128], F32, "bsF")
            k.dma("pool", lambda e: e.dma_start(out=wsT[:], in_=wsT_d), writes=[wr_])
            k.dma("sp", lambda e: e.dma_start(out=bsF[:], in_=bsF_d), writes=[wr_])
            gg, ggr = load_bc(ph, gln_g_d, 512, "gg"); gb, gbr = load_bc(ph, gln_b_d, 512, "gb")
            uTs = [(view(112 + 4 * i, 4 * 512 * 2, BF16, "p (c t) -> p c t", c=4), Res("uT")) for i in range(2)]
            vgs = PRing(ph, 2, [128, 512], F32, "vg")
            vnEs = PRing(ph, 2, [128, 512], BF16, "vnE"); vnOs = PRing(ph, 2, [128, 512], BF16, "vnO")
            svs = PRing(ph, 2, [128, 512], F32, "sv")
            small = PRing(ph, 4, [128, 16], F32, "lnsm")
            for (t_, r_) in vnEs.items + vnOs.items:
                k.op("pool", lambda e: e.memset(t_[:], 0.0), writes=[r_])
            g4 = lambda ap, g: ap.rearrange("p (c g d) -> p c g d", g=2, d=64)[:, :, g, :]
            for tb in range(4):
                ts = slice(tb * 512, (tb + 1) * 512)
                uT, uTr = uTs[tb % 2]
                for cu in range(4):
                    pb, pbr = PSF()
                    for kc in range(8):
                        mm(pb[:], Wu[:, kc, cu * 128:(cu + 1) * 128], h0T[:, kc, ts], kc == 0, kc == 7, [wr_] + h0T_res[4 * tb:4 * tb + 4], [pbr])
                    k.op("act", lambda e: e.activation(out=uT[:, cu, :], in_=pb[:], func=AF.Gelu), reads=[pbr], writes=[uTr])
                def tileD(tt, tb=tb, uT=uT, uTr=uTr):
                    t = 4 * tb + tt
                    tsl = slice(t * 128, (t + 1) * 128)
                    pb, pbr = PSF()
                    for kc in range(8):
                        mm(pb[:], h0T[:, kc, tsl], Wv[:, kc, :], kc == 0, kc == 7, [wr_, h0T_res[t]], [pbr])
                    vg, vgr = vgs.get()
                    k.op("act", lambda e: e.activation(out=vg[:], in_=pb[:], func=AF.Gelu), reads=[pbr], writes=[vgr])
                    layer_norm(vg[:], [vgr], gg[:], gb[:], [ggr, gbr], vg[:], [vgr], small, n=512)
                    vnE, vnEr = vnEs.get(); vnO, vnOr = vnOs.get()
                    k.op("act", lambda e: e.activation(out=g4(vnE[:], 0), in_=g4(vg[:], 0), func=AF.Copy), reads=[vgr], writes=[vnEr])
                    k.op("pool", lambda e: e.tensor_copy(g4(vnO[:], 1), g4(vg[:], 1)), reads=[vgr], writes=[vnOr])
                    ps, psr = PSF()
                    for c in range(4):
                        cs_ = slice(c * 128, (c + 1) * 128)
                        mm(ps[:, cs_], vnE[:, cs_], wsT[:, 2 * c, :], True, False, [vnEr, wr_], [psr])
                        mm(ps[:, cs_], vnO[:, cs_], wsT[:, 2 * c + 1, :], False, True, [vnOr, wr_], [psr])
                    sv, svr = svs.get()
                    k.op("dve", lambda e: e.tensor_tensor(out=sv[:], in0=ps[:], in1=bsF[:].rearrange("p c t -> p (c t)"), op=ALU.add), reads=[psr, wr_], writes=[svr])
                    k.op("dve", lambda e: e.tensor_tensor(out=ybT[:, :, tsl], in0=sv[:].rearrange("p (c t) -> p c t", c=4), in1=uT[:, :, tt * 128:(tt + 1) * 128], op=ALU.mult), reads=[svr, uTr], writes=[ybT_res[t]])
                run_pairs(4, tileD)
            barrier()
        dbg("ybT0", ybT[:, 0, :], ybT_res, [128, SEQ])
        if stop == "D":
            finish()
            return nc, dbg_outs

        mgT = view(128, 8 * SEQ * 2, BF16, "p (c t) -> p c t", c=8)
        mg_res = [Res("mg%d" % tb) for tb in range(4)]
        with ExitStack() as ph:
            wbr = view(96, 8 * 1024 * 2, BF16, "p (k d) -> p k d", k=8); wbr_r = Res("wbr")
            for kc in range(8):
                k.dma("pool", lambda e: e.dma_start(out=wbr[:, kc, :], in_=wbr_d[kc * 128:(kc + 1) * 128, :]), writes=[wbr_r])
            wgs = [(view(112 + 2 * i, 8 * 128 * 2, BF16, "p (k c) -> p k c", k=8), Res("wg")) for i in range(4)]
            sigs = PRing(ph, 3, [128, 512], F32, "sig")
            prs = PRing(ph, 2, [128, 512], F32, "pr")
            wi = 0
            for dc in range(8):
                wg2 = []
                for n in range(2):
                    wg, wgr = wgs[wi % 4]; wi += 1
                    c0 = GT0 + n * 1024 + dc * 128
                    k.dma("pool", lambda e: e.dma_start(out=wg, in_=w_in[:, c0:c0 + 128].rearrange("(k p) c -> p k c", p=128)), writes=[wgr])
                    wg2.append((wg, wgr))
                for tb in range(4):
                    ts = slice(tb * 512, (tb + 1) * 512)
                    hr = h0T_res[4 * tb:4 * tb + 4]
                    acc = None
                    for n in range(2):
                        wg, wgr = wg2[n]
                        pg, pgr = PSF()
                        for kc in range(8):
                            mm(pg[:], wg[:, kc, :], h0T[:, kc, ts], kc == 0, kc == 7, [wgr] + hr, [pgr])
                        sg, sgr = sigs.get()
                        k.op("act", lambda e: e.activation(out=sg[:], in_=pg[:], func=AF.Sigmoid), reads=[pgr], writes=[sgr])
                        pbn, pbnr = PSF()
                        yT, yres = (yaT, [yaT_res[c][tb] for c in range(4)]) if n == 0 else (ybT, ybT_res[4 * tb:4 * tb + 4])
                        for c in range(4):
                            mm(pbn[:], wbr[:, n * 4 + c, dc * 128:(dc + 1) * 128], yT[:, c, ts], c == 0, c == 3, [wbr_r] + yres, [pbnr])
                        if n == 0:
                            acc, accr = prs.get()
                            k.op("dve", lambda e: e.tensor_tensor(out=acc[:], in0=sg[:], in1=pbn[:], op=ALU.mult), reads=[sgr, pbnr], writes=[accr])
                        else:
                            k.op("dve", lambda e: e.tensor_tensor(out=sg[:], in0=sg[:], in1=pbn[:], op=ALU.mult), reads=[sgr, pbnr], writes=[sgr])
                            k.op("pool", lambda e: e.tensor_tensor(out=mgT[:, dc, ts], in0=acc[:], in1=sg[:], op=ALU.add), reads=[sgr, accr], writes=[mg_res[tb]])
            barrier()
        dbg("mgT0", mgT[:, 0, :], mg_res, [128, SEQ])
        if stop == "E":
            finish()
            return nc, dbg_outs

        H = view(64, NT * D * 4, F32, "p (t d) -> p t d", t=NT)
        H_res = [Res("H%d" % t) for t in range(NT)]

        def load_w_fm(dst, d_ap, res, q="pool"):
            for kc in range(8):
                k.dma(q, lambda e: e.dma_start(out=dst[:, kc, :], in_=d_ap[kc * 128:(kc + 1) * 128, :]), writes=[res])

        with ExitStack() as ph:
            wmix = view(0, 8 * 1024 * 2, BF16, "p (k d) -> p k d", k=8); wmix_r = Res("wmix")
            load_w_fm(wmix, wmix_d, wmix_r)
            g0, g0r = load_bc(ph, lng["ln_emb_g"], D, "g0"); b0, b0r = load_bc(ph, lng["ln_emb_b"], D, "b0")
            g1, g1r = load_bc(ph, lng["ln1_g"], D, "g1"); b1, b1r = load_bc(ph, lng["ln1_b"], D, "b1")
            xs = PRing(ph, 2, [128, D], F32, "xs")
            small = PRing(ph, 4, [128, 16], F32, "lnsm")
            def tileF(t):
                tsl = slice(t * 128, (t + 1) * 128)
                xt, xr = xs.get()
                k.dma("sp", lambda e: e.dma_start(out=xt[:], in_=x[tsl, :]), writes=[xr])
                layer_norm(xt[:], [xr], g0[:], b0[:], [g0r, b0r], xt[:], [xr], small)

                def half_(half):
                    hs = slice(half * 512, (half + 1) * 512)
                    pm, pmr = PSF()
                    for kc in range(8):
                        mm(pm[:], mgT[:, kc, tsl], wmix[:, kc, hs], kc == 0, kc == 7, [mg_res[t // 4], wmix_r], [pmr])
                    k.op("dve", lambda e: e.scalar_tensor_tensor(out=xt[:, hs], in0=xt[:, hs], scalar=ALPHA, in1=pm[:], op0=ALU.mult, op1=ALU.add), reads=[xr, pmr], writes=[xr])
                half_(0)
                half_(1)
                layer_norm(xt[:], [xr], g1[:], b1[:], [g1r, b1r], H[:, t, :], [H_res[t]], small)
            run_pairs(NT, tileF)
            barrier()
        dbg("h1_t0", H[:, 0, :], [H_res[0]], [128, D])
        dbg("h1_t9", H[:, 9, :], [H_res[9]], [128, D])
        if stop == "F":
            finish()
            return nc, dbg_outs

        with ExitStack() as ph:
            wq = view(0, 8 * 1024 * 2, BF16, "p (k d) -> p k d", k=8); wo = view(16, 8 * 1024 * 2, BF16, "p (k d) -> p k d", k=8)
            wkK = view(32, 8 * 1024 * 2, BF16, "p (k d) -> p k d", k=8); wkV = view(48, 8 * 1024 * 2, BF16, "p (k d) -> p k d", k=8)
            KT = view(128, 8 * 256 * 2, BF16, "p (c m) -> p c m", c=8); Vm = view(132, 2 * 1024 * 2, BF16, "p (m d) -> p m d", m=2)
            memT = view(136, 8 * 256 * 2, BF16, "p (c m) -> p c m", c=8)
            wres = Res("xw"); kvres = Res("kv"); memr = [Res("memT0"), Res("memT1")]
            load_w_fm(wq, wq_d, wres); load_w_fm(wo, wo_d, wres)
            load_w_fm(wkK, wkv_d[:, 0:D], wres); load_w_fm(wkV, wkv_d[:, D:2 * D], wres)
            small = PRing(ph, 4, [128, 16], F32, "lnsm")
            xs = PRing(ph, 2, [128, D], F32, "xs")
            with ExitStack() as ph2:
                gm, gmr = load_bc(ph2, lng["mem_ln_g"], D, "gm"); bm, bmr = load_bc(ph2, lng["mem_ln_b"], D, "bm")
                hbs0 = PRing(ph2, 2, [128, D], BF16, "hb")
                for mt in range(2):
                    xt, xr = xs.get()
                    k.dma("sp", lambda e: e.dma_start(out=xt[:], in_=mem[mt * 128:(mt + 1) * 128, :]), writes=[xr])
                    layer_norm(xt[:], [xr], gm[:], bm[:], [gmr, bmr], xt[:], [xr], small)
                    hb, hbr = hbs0.get()
                    k.op("act", lambda e: e.activation(out=hb[:], in_=xt[:], func=AF.Copy), reads=[xr], writes=[hbr])
                    to_fm(hb, hbr, memT, [memr[mt]], mt)
                barrier()
            g2_, g2r = load_bc(ph, lng["ln2_g"], D, "g2"); b2_, b2r = load_bc(ph, lng["ln2_b"], D, "b2")
            for c in range(8):
                pk, pkr = PSF()
                for kc in range(8):
                    mm(pk[:, 0:256], wkK[:, kc, c * 128:(c + 1) * 128], memT[:, kc, :], kc == 0, kc == 7, [wres] + memr, [pkr])
                k.op("act", lambda e: e.activation(out=KT[:, c, :], in_=pk[:, 0:256], func=AF.Copy), reads=[pkr], writes=[kvres])
            for mt in range(2):
                for half in range(2):
                    hs = slice(half * 512, (half + 1) * 512)
                    pv, pvr = PSF()
                    for kc in range(8):
                        mm(pv[:], memT[:, kc, mt * 128:(mt + 1) * 128], wkV[:, kc, hs], kc == 0, kc == 7, [wres] + memr, [pvr])
                    k.op("act", lambda e: e.activation(out=Vm[:, mt, hs], in_=pv[:], func=AF.Copy), reads=[pvr], writes=[kvres])
            barrier()
            cv = Carver(32, 64)

            class CRing:
                def __init__(self, n, shape, dt, name):
                    self.items = [(cv.alloc(shape, dt), Res(name)) for _ in range(n)]
                    self.i = 0

                def get(self):
                    it = self.items[self.i % len(self.items)]; self.i += 1
                    return it
            hbs = CRing(2, [128, D], BF16, "hb")
            hTs = CRing(2, [128, 8, 128], BF16, "hT")
            qTs = CRing(2, [128, 8, 128], BF16, "qT")
            pexs = CRing(2, [128, 4, 256], BF16, "pex")
            pTs = CRing(2, [128, 8, 128], BF16, "pT")
            oTs = CRing(2, [128, 8, 128], BF16, "oT")
            sm2 = PRing(ph, 2, [128, 16], F32, "sm2")
            SCL = 256.0 ** -0.5
            tab_res = Res("tables")
            stg = [(view(140 + 8 * i, 4 * D * 2, BF16, "p (a f) -> p a f", a=4), Res("stg%d" % i)) for i in range(2)]
            conv_jobs = [(src, dst, ch) for (src, dst) in ((pu_d, Ub_d), (pv_d, Vb_d)) for ch in range(32)]

            def conv_some(n_):
                for _ in range(n_):
                    if not conv_jobs:
                        return
                    src, dst, ch = conv_jobs.pop(0)
                    st_, str_ = stg[ch % 2]
                    k.dma("pool", lambda e: e.dma_start(out=st_, in_=src[ch * 4:(ch + 1) * 4].rearrange("a p f -> p a f")), writes=[str_])
                    k.dma("sp", lambda e: e.dma_start(out=dst[ch * 4:(ch + 1) * 4].rearrange("a p f -> p a f"), in_=st_), reads=[str_], writes=[tab_res])
            for t in range(NT):
                conv_some(4)
                h1 = H[:, t, :]; h1r = H_res[t]
                hb, hbr = hbs.get()
                k.op("act", lambda e: e.activation(out=hb[:], in_=h1, func=AF.Copy), reads=[h1r], writes=[hbr])
                hT, hTr = hTs.get()
                to_fm(hb, hbr, hT, [hTr], 0)
                qT, qTr = qTs.get()
                for g in range(2):
                    pq, pqr = PSF()
                    for cc in range(4):
                        c = g * 4 + cc
                        for kc in range(8):
                            mm(pq[:, cc * 128:(cc + 1) * 128], wq[:, kc, c * 128:(c + 1) * 128], hT[:, kc, :], kc == 0, kc == 7, [wres, hTr], [pqr])
                    k.op("act", lambda e: e.activation(out=qT[:, g * 4:(g + 1) * 4, :], in_=pq[:].rearrange("p (c t) -> p c t", c=4), func=AF.Copy), reads=[pqr], writes=[qTr])
                sm, smr = sm2.get()
                pex, pexr = pexs.get()
                pss = []
                for g in range(2):
                    ps_, psr_ = PSF(); pss.append((ps_, psr_))
                    for hh in range(2):
                        h = g * 2 + hh
                        for j in range(2):
                            mm(ps_[:, hh * 256:(hh + 1) * 256], qT[:, 2 * h + j, :], KT[:, 2 * h + j, :], j == 0, j == 1, [qTr, kvres], [psr_])
                    k.op("dve", lambda e: e.tensor_reduce(out=sm[:, g * 2:(g + 1) * 2], in_=ps_[:].rearrange("p (h m) -> p h m", h=2), axis=AX.X, op=ALU.max), reads=[psr_], writes=[smr])
                k.op("dve", lambda e: e.tensor_scalar(sm[:, 4:8], sm[:, 0:4], -SCL, None, ALU.mult), reads=[smr], writes=[smr])
                for h in range(4):
                    ps_, psr_ = pss[h // 2]
                    k.op("act", lambda e: e.activation(out=pex[:, h, :], in_=ps_[:, (h % 2) * 256:(h % 2 + 1) * 256], func=AF.Exp, bias=sm[:, 4 + h:5 + h], scale=SCL, accum_out=sm[:, 8 + h:9 + h]), reads=[psr_, smr], writes=[pexr, smr])
                k.op("dve", lambda e: e.reciprocal(out=sm[:, 12:16], in_=sm[:, 8:12]), reads=[smr], writes=[smr])
                k.op("dve", lambda e: e.tensor_tensor(out=pex[:], in0=pex[:], in1=sm[:, 12:16].unsqueeze(2).to_broadcast([128, 4, 256]), op=ALU.mult), reads=[pexr, smr], writes=[pexr])
                pt, ptr_ = PSB()
                for h in range(4):
                    for mt in range(2):
                        i = h * 2 + mt
                        k.op("pe", lambda e: e.transpose(pt[:, i * 128:(i + 1) * 128], pex[:, h, mt * 128:(mt + 1) * 128], ident[:]), reads=[pexr, cst], writes=[ptr_])
                pT, pTr = pTs.get()
                k.op("act", lambda e: e.activation(out=pT[:], in_=pt[:].rearrange("p (c t) -> p c t", t=128), func=AF.Copy), reads=[ptr_], writes=[pTr])
                oT, oTr = oTs.get()
                for g in range(2):
                    po, por = PSF()
                    for cc in range(4):
                        c = g * 4 + cc; h = c // 2
                        for mt in range(2):
                            mm(po[:, cc * 128:(cc + 1) * 128], Vm[:, mt, c * 128:(c + 1) * 128], pT[:, h * 2 + mt, :], mt == 0, mt == 1, [kvres, pTr], [por])
                    k.op("dve", lambda e: e.tensor_copy(oT[:, g * 4:(g + 1) * 4, :], po[:].rearrange("p (c t) -> p c t", c=4)), reads=[por], writes=[oTr])
                xt, xr = xs.get()
                for half in range(2):
                    hs = slice(half * 512, (half + 1) * 512)
                    px, pxr = PSF()
                    for c in range(8):
                        mm(px[:], oT[:, c, :], wo[:, c, hs], c == 0, c == 7, [oTr, wres], [pxr])
                    k.op("dve", lambda e: e.scalar_tensor_tensor(out=xt[:, hs], in0=h1[:, hs], scalar=ALPHA, in1=px[:], op0=ALU.mult, op1=ALU.add), reads=[h1r, pxr], writes=[xr])
                layer_norm(xt[:], [xr], g2_[:], b2_[:], [g2r, b2r], H[:, t, :], [H_res[t]], small)
            barrier()
        dbg("h2_t0", H[:, 0, :], [H_res[0]], [128, D])
        dbg("h2_t9", H[:, 9, :], [H_res[9]], [128, D])
        if stop == "H":
            finish()
            return nc, dbg_outs

        with ExitStack() as pho:
            SI1 = phase_sb(pho, [128, NT, 128], F32, "SI1"); SI2 = phase_sb(pho, [128, NT, 128], F32, "SI2"); SG = phase_sb(pho, [128, NT, 128], F32, "SG")
            slot_res = [Res("slot%d" % t) for t in range(NT)]
            with ExitStack() as ph:
                pwq = view(0, 8 * 2048 * 2, BF16, "p (k d) -> p k d", k=8); pw_r = Res("pwq")
                load_w_fm(pwq, pwq_d, pw_r)
                skT = phase_sb(ph, [128, 2, 128], BF16, "skT")
                k.dma("pool", lambda e: e.dma_start(out=skT[:], in_=skT_d), writes=[pw_r])
                sc = view(48, 16 * 128 * 4, F32, "p (c k) -> p c k", c=16); scr = Res("sc")
                cv = Carver(128, 160)

                def CT(shape, dt, name, c=cv):
                    return (c.alloc(shape, dt), Res(name))
                hb, hbr = CT([128, D], BF16, "hb"); hT, hTr = CT([128, 8, 128], BF16, "hT")
                pqT, pqTr = CT([128, 16, 128], BF16, "pqT")
                sc2, sc2r = CT([128, 256], F32, "sc2"); sc2b, sc2br = CT([128, 256], F32, "sc2b")
                ts_c = [Res("ts%d" % c_) for c_ in range(16)]; ti_c = [Res("ti%d" % c_) for c_ in range(16)]
                bs_h = [Res("bs%d" % h_) for h_ in range(8)]; bp_h = [Res("bp%d" % h_) for h_ in range(8)]
                top_s, tsr = CT([128, 256], F32, "top_s"); top_i, tir = CT([128, 256], U32, "top_i"); top_f, tfr = CT([128, 256], F32, "top_f")
                cand, cdr = CT([128, 2048], F32, "cand")
                best_s, bsr = CT([128, 128], F32, "best_s"); best_p, bpr = CT([128, 128], U32, "best_p")
                pf, pfr = CT([128, 128], F32, "pf"); k1f, k1r = CT([128, 128], F32, "k1f"); k2f, k2r = CT([128, 128], F32, "k2f")
                gsum, gsr = CT([128, 16], F32, "gsum")
                eq = cand
                v4 = lambda ap: ap.rearrange("p (h z k) -> p h z k", h=8, z=2)
                v3k = lambda ap: ap.rearrange("p (h k) -> p h k", h=8)
                c4 = lambda ap: ap.rearrange("p (h a b) -> p h a b", h=8, a=16)
                sc_b = [(sc, scr), (view(32, 16 * 128 * 4, F32, "p (c k) -> p c k", c=16), Res("scB"))]
                hb_b = [(hb, hbr), (view(40, D * 2, BF16), Res("hbB"))]
                hT_b = [(hT, hTr), (view(42, 8 * 128 * 2, BF16, "p (c t) -> p c t", c=8), Res("hTB"))]
                pq_b = [(pqT, pqTr), (view(44, 16 * 128 * 2, BF16, "p (c t) -> p c t", c=16), Res("pqTB"))]

                def head(t):
                    hb, hbr = hb_b[t % 2]; hT, hTr = hT_b[t % 2]; pqT, pqTr = pq_b[t % 2]; sc, scr = sc_b[t % 2]
                    h2 = H[:, t, :]; h2r = H_res[t]
                    k.op("act", lambda e: e.activation(out=hb, in_=h2, func=AF.Copy), reads=[h2r], writes=[hbr])
                    to_fm(hb, hbr, hT, [hTr], 0)
                    for g in range(4):
                        pq, pqr = PSF()
                        for cc in range(4):
                            c = g * 4 + cc
                            for kc in range(8):
                                mm(pq[:, cc * 128:(cc + 1) * 128], pwq[:, kc, c * 128:(c + 1) * 128], hT[:, kc, :], kc == 0, kc == 7, [pw_r, hTr], [pqr])
                        k.op("act", lambda e: e.activation(out=pqT[:, g * 4:(g + 1) * 4, :], in_=pq[:].rearrange("p (c t) -> p c t", c=4), func=AF.Copy), reads=[pqr], writes=[pqTr])
                    for g in range(4):
                        ps_, psr_ = PSF()
                        for cc in range(4):
                            c = g * 4 + cc
                            mm(ps_[:, cc * 128:(cc + 1) * 128], pqT[:, c, :], skT[:, c % 2, :], True, True, [pqTr, pw_r], [psr_])
                        k.op("act", lambda e: e.activation(out=sc[:, g * 4:(g + 1) * 4, :], in_=ps_[:].rearrange("p (c k) -> p c k", c=4), func=AF.Copy), reads=[psr_], writes=[scr])

                def tail(t):
                    sc, scr = sc_b[t % 2]
                    def lvl1_ops(c, buf, bufr):
                        lo = slice(c * 16, c * 16 + 8); hi = slice(c * 16 + 8, c * 16 + 16)
                        tr_, ir_ = ts_c[c], ti_c[c]
                        return [
                            lambda: k.op("dve", lambda e: e.max(out=top_s[:, lo], in_=sc[:, c, :]), reads=[scr], writes=[tr_]),
                            lambda: k.op("dve", lambda e: e.max_index(out=top_i[:, lo], in_max=top_s[:, lo], in_values=sc[:, c, :]), reads=[scr, tr_], writes=[ir_]),
                            lambda: k.op("dve", lambda e: e.match_replace(out=buf[:, 0:128], in_to_replace=top_s[:, lo], in_values=sc[:, c, :], imm_value=-1e30), reads=[scr, tr_], writes=[bufr]),
                            lambda: k.op("dve", lambda e: e.max(out=top_s[:, hi], in_=buf[:, 0:128]), reads=[bufr], writes=[tr_]),
                            lambda: k.op("dve", lambda e: e.max_index(out=top_i[:, hi], in_max=top_s[:, hi], in_values=buf[:, 0:128]), reads=[bufr, tr_], writes=[ir_]),
                        ]
                    for c in range(0, 16, 2):
                        oa_ = lvl1_ops(c, sc2, sc2r); ob_ = lvl1_ops(c + 1, sc2b, sc2br)
                        for fa_, fb_ in zip(oa_, ob_):
                            fa_(); fb_()
                    k.op("dve", lambda e: e.tensor_copy(top_f, top_i), reads=ti_c, writes=[tfr])
                    k.op("dve", lambda e: e.tensor_tensor(out=c4(cand), in0=v4(top_s)[:, :, 0, :].unsqueeze(3).to_broadcast([128, 8, 16, 16]),
                                                          in1=v4(top_s)[:, :, 1, :].unsqueeze(2).to_broadcast([128, 8, 16, 16]), op=ALU.add), reads=ts_c, writes=[cdr])
                    candh = cand.rearrange("p (h c) -> p h c", h=8)
                    def lvl2_ops(h, buf, bufr):
                        lo = slice(h * 16, h * 16 + 8); hi = slice(h * 16 + 8, h * 16 + 16)
                        br_, pr_ = bs_h[h], bp_h[h]
                        return [
                            lambda: k.op("dve", lambda e: e.max(out=best_s[:, lo], in_=candh[:, h, :]), reads=[cdr], writes=[br_]),
                            lambda: k.op("dve", lambda e: e.max_index(out=best_p[:, lo], in_max=best_s[:, lo], in_values=candh[:, h, :]), reads=[cdr, br_], writes=[pr_]),
                            lambda: k.op("dve", lambda e: e.match_replace(out=buf, in_to_replace=best_s[:, lo], in_values=candh[:, h, :], imm_value=-1e30), reads=[cdr, br_], writes=[bufr]),
                            lambda: k.op("dve", lambda e: e.max(out=best_s[:, hi], in_=buf), reads=[bufr], writes=[br_]),
                            lambda: k.op("dve", lambda e: e.max_index(out=best_p[:, hi], in_max=best_s[:, hi], in_values=buf), reads=[bufr, br_], writes=[pr_]),
                        ]
                    for h in range(0, 8, 2):
                        oa_ = lvl2_ops(h, sc2, sc2r); ob_ = lvl2_ops(h + 1, sc2b, sc2br)
                        for fa_, fb_ in zip(oa_, ob_):
                            fa_(); fb_()
                    pfu = pf.bitcast(U32)
                    k.op("dve", lambda e: e.tensor_single_scalar(out=pfu, in_=best_p, scalar=4, op=ALU.logical_shift_right), reads=bp_h, writes=[pfr])
                    k.op("dve", lambda e: e.tensor_copy(k1f, pfu), reads=[pfr], writes=[k1r])
                    k.op("dve", lambda e: e.tensor_single_scalar(out=pfu, in_=best_p, scalar=15, op=ALU.bitwise_and), reads=bp_h + [k1r, pfr], writes=[pfr])
                    k.op("dve", lambda e: e.tensor_copy(k2f, pfu), reads=[pfr], writes=[k2r])
                    io4 = iota16[:, :].unsqueeze(1).unsqueeze(1).to_broadcast([128, 8, 16, 16])
                    sr = slot_res[t]
                    for (kf, kr_, z, dst) in ((k1f, k1r, 0, SI1[:, t, :]), (k2f, k2r, 1, SI2[:, t, :])):
                        k.op("dve", lambda e: e.tensor_tensor(out=c4(eq), in0=io4, in1=v3k(kf).unsqueeze(3).to_broadcast([128, 8, 16, 16]), op=ALU.is_equal), reads=[cst, kr_, cdr], writes=[cdr])
                        k.op("dve", lambda e: e.tensor_tensor(out=c4(eq), in0=c4(eq), in1=v4(top_f)[:, :, z, :].unsqueeze(2).to_broadcast([128, 8, 16, 16]), op=ALU.mult), reads=[cdr, tfr], writes=[cdr])
                        k.op("dve", lambda e: e.tensor_reduce(out=dst, in_=eq.rearrange("p (a b) -> p a b", b=16), axis=AX.X, op=ALU.add), reads=[cdr], writes=[sr])
                    gate = SG[:, t, :]
                    k.op("dve", lambda e: e.tensor_tensor(out=v3k(gate), in0=v3k(best_s), in1=v3k(best_s)[:, :, 0:1].to_broadcast([128, 8, 16]), op=ALU.subtract), reads=bs_h, writes=[sr])
                    k.op("act", lambda e: e.activation(out=gate, in_=gate, func=AF.Exp), reads=[sr], writes=[sr])
                    k.op("dve", lambda e: e.tensor_reduce(out=gsum[:, 0:8], in_=v3k(gate), axis=AX.X, op=ALU.add), reads=[sr], writes=[gsr])
                    k.op("dve", lambda e: e.reciprocal(out=gsum[:, 8:16], in_=gsum[:, 0:8]), reads=[gsr], writes=[gsr])
                    k.op("dve", lambda e: e.tensor_tensor(out=v3k(gate), in0=v3k(gate), in1=gsum[:, 8:16].unsqueeze(2).to_broadcast([128, 8, 16]), op=ALU.mult), reads=[sr, gsr], writes=[sr])
                head(0)
                for t in range(NT):
                    if t + 1 < NT:
                        head(t + 1)
                    tail(t)
                barrier()
            dbg("si1", SI1[:, 0, :], [slot_res[0]], [128, 128]); dbg("sg", SG[:, 0, :], [slot_res[0]], [128, 128])
            if stop == "I":
                finish()
                return nc, dbg_outs

            TBK = 256
            with ExitStack() as ph:
                cv = Carver(128, 160)
                NU = 3
                utiles = [(cv.alloc([128, 8, 128], BF16), Res("ut")) for _ in range(NU)]
                vtiles = [(cv.alloc([128, D], BF16), Res("vt")) for _ in range(NU)]
                hTb = cv.alloc([128, 8, TBK], BF16); hTb_res = [Res("hTb0"), Res("hTb1")]
                accs = [(cv.alloc([128, D], F32), Res("acc")) for _ in range(2)]
                hbJ = (cv.alloc([128, D], BF16), Res("hbJ"))
                g3, g3r = load_bc(ph, lng["ln3_g"], D, "g3"); b3, b3r = load_bc(ph, lng["ln3_b"], D, "b3")
                small = PRing(ph, 4, [128, 16], F32, "lnsm")
                iota128 = phase_sb(ph, [128, 128], F32, "iota128"); ior = Res("iota128")
                k.op("pool", lambda e: e.iota(iota128[:], pattern=[[1, 128]], base=0, channel_multiplier=0, allow_small_or_imprecise_dtypes=True), writes=[ior])
                slTs = [(phase_sb(ph, [128, 3, TBK], BF16, "slT"), Res("slT")) for _ in range(2)]
                oh1s = PRing(ph, 8, [128, 128], BF16, "oh1"); oh2s = PRing(ph, 8, [128, 64], BF16, "oh2")

                class VRing:
                    def __init__(self, n, shape, dt, name):
                        self.items = [(cv.alloc(shape, dt), Res(name)) for _ in range(n)]
                        self.i = 0

                    def get(self):
                        it = self.items[self.i % len(self.items)]; self.i += 1
                        return it
                gels = VRing(2, [128, TBK], F32, "gel"); pbs = VRing(2, [128, TBK], BF16, "pb")
                ui = [0]
                NTB = SEQ // TBK
                Gh = [(view(32 * h_, TBK * 64 * 2, BF16, "p (t i) -> p t i", t=TBK), Res("G%d" % h_)) for h_ in range(2)]
                psbf = [(psb[i][0][:].bitcast(F32), psb[i][1]) for i in range(2)]
                gq = [0]

                def prep_block(tb):
                    slT, slTr = slTs[tb % 2]
                    for tt in range(2):
                        pt_, ptr_ = psbf[tt]
                        for a_, arr in enumerate((SI1, SI2, SG)):
                            k.op("pe", lambda e: e.transpose(pt_[:, a_ * 128:(a_ + 1) * 128], arr[:, tb * 2 + tt, :], identf[:]), reads=[slot_res[tb * 2 + tt], cst], writes=[ptr_])
                        k.op("act", lambda e: e.activation(out=slT[:, :, tt * 128:(tt + 1) * 128], in_=pt_[:, 0:384].rearrange("p (a t) -> p a t", a=3), func=AF.Copy), reads=[ptr_], writes=[slTr])

                def g_build_jobs(tb, half):
                    slT, slTr = slTs[tb % 2]
                    Gt, Gtr = Gh[half]
                    jobs = []
                    state = {}
                    pend_pe = []
                    for tk in range(TBK):
                        def job(tk=tk):
                            j = tk % 8
                            if j == 0:
                                state["pg"] = psbf[gq[0] % 2]; gq[0] += 1
                            pg, pgr = state["pg"]
                            o1, o1r = oh1s.get(); o2, o2r = oh2s.get()
                            k.op("dve", lambda e: e.tensor_scalar(o1[:], iota128[:], slT[:, 0, tk:tk + 1], slT[:, 2, tk:tk + 1], ALU.is_equal, ALU.mult), reads=[ior, slTr], writes=[o1r])
                            k.op("dve", lambda e: e.tensor_scalar(o2[:], iota128[:, half * 64:(half + 1) * 64], slT[:, 1, tk:tk + 1], None, ALU.is_equal), reads=[ior, slTr], writes=[o2r])
                            def pe_part():
                                mm(pg[:, j * 64:(j + 1) * 64], o1[:], o2[:], True, True, [o1r, o2r], [pgr])
                                if j == 7:
                                    tq = tk // 8
                                    k.op("act", lambda e: e.activation(out=Gt[:, tq * 8:tq * 8 + 8, :], in_=pg[:].rearrange("p (t i) -> p t i", t=8), func=AF.Copy), reads=[pgr], writes=[Gtr])
                            pend_pe.append(pe_part)
                            while len(pend_pe) > 6:
                                pend_pe.pop(0)()
                        jobs.append(job)

                    def flush():
                        while pend_pe:
                            pend_pe.pop(0)()
                    jobs.append(flush)
                    return jobs

                def issue_act(tb, i2):
                    ut, utr = utiles[ui[0] % NU]; vt, vtr = vtiles[ui[0] % NU]; ui[0] += 1
                    k.dma("sp", lambda e: e.dma_start(out=ut, in_=Ub_d[i2].rearrange("p (k i) -> p k i", k=8)), writes=[utr])
                    k.dma("sp", lambda e: e.dma_start(out=vt, in_=Vb_d[i2]), writes=[vtr])
                    pa, par = psf[4 + i2 % 2]
                    for kc in range(8):
                        mm(pa[:, 0:TBK], ut[:, kc, :], hTb[:, kc, :], kc == 0, kc == 7, [utr] + hTb_res, [par])
                    gel, gelr = gels.get()
                    k.op("act", lambda e: e.activation(out=gel, in_=pa[:, 0:TBK], func=AF.Gelu), reads=[par], writes=[gelr])
                    pb_, pbr_ = pbs.get()
                    Gt, Gtr = Gh[i2 // 64]
                    k.op("dve", lambda e: e.tensor_tensor(out=pb_, in0=gel, in1=Gt[:, :, i2 % 64], op=ALU.mult), reads=[gelr, Gtr], writes=[pbr_])
                    return (pb_, pbr_, vt, vtr)

                def issue_y(i2, st_):
                    pb_, pbr_, vt, vtr = st_
                    for tt in range(2):
                        for half in range(2):
                            py, pyr = psf[tt * 2 + half]
                            mm(py[:], pb_[:, tt * 128:(tt + 1) * 128], vt[:, half * 512:(half + 1) * 512], i2 == 0, i2 == 127, [pbr_, vtr], [pyr])
                prep_block(0)
                for job in g_build_jobs(0, 0):
                    job()
                for tb in range(NTB):
                    t0 = tb * 2
                    for tt in range(2):
                        hb, hbr = hbJ
                        k.op("act", lambda e: e.activation(out=hb, in_=H[:, t0 + tt, :], func=AF.Copy), reads=[H_res[t0 + tt]], writes=[hbr])
                        to_fm(hb, hbr, hTb, [hTb_res[tt]], tt)
                    if tb + 1 < NTB:
                        prep_block(tb + 1)
                    jobs_lo = g_build_jobs(tb, 1)
                    jobs_hi = g_build_jobs(tb + 1, 0) if tb + 1 < NTB else []
                    pend = issue_act(tb, 0)
                    for i2 in range(128):
                        jl = jobs_lo if i2 < 64 else jobs_hi
                        nxt = issue_act(tb, i2 + 1) if i2 + 1 < 128 else None
                        for _ in range(2):
                            if jl:
                                jl.pop(0)()
                        issue_y(i2, pend)
                        pend = nxt
                        for _ in range(2 if i2 % 64 < 62 else 1000):
                            if jl:
                                jl.pop(0)()
                    while jobs_hi:
                        jobs_hi.pop(0)()
                    for tt in range(2):
                        t = t0 + tt
                        acc, accr = accs[tt]
                        for half in range(2):
                            hs = slice(half * 512, (half + 1) * 512)
                            py, pyr = psf[tt * 2 + half]
                            k.op("dve", lambda e: e.scalar_tensor_tensor(out=acc[:, hs], in0=H[:, t, hs], scalar=ALPHA, in1=py[:], op0=ALU.mult, op1=ALU.add), reads=[H_res[t], pyr], writes=[accr])
                        layer_norm(acc, [accr], g3[:], b3[:], [g3r, b3r], acc, [accr], small)
                        k.dma("sp", lambda e: e.dma_start(out=out_d[t * 128:(t + 1) * 128, :], in_=acc), reads=[accr])
                barrier()

        finish()
    return nc, dbg_outs


def _consts():
    c = {}
    c["c_ident"] = np.eye(128, dtype=np.float32)
    ob = np.zeros((128, 128), np.float32); ob[:64, :64] = 1.0; ob[64:, 64:] = 1.0
    c["c_onesbd"] = ob
    s = np.arange(64)
    lt = (s[:, None] < s[None, :]).astype(np.float32)
    le = (s[:, None] <= s[None, :]).astype(np.float32)

    def mk(strict, incl):
        m = np.zeros((128, 512), np.float32)
        bd = np.zeros((128, 128), np.float32); bd[:64, :64] = strict; bd[64:, 64:] = strict
        pl = np.concatenate([incl, incl], axis=0)
        m[:, 0:128] = bd; m[:, 128:192] = pl; m[:, 192:320] = bd; m[:, 320:384] = pl
        bdT = np.zeros((128, 128), np.float32); bdT[:64, :64] = strict.T; bdT[64:, 64:] = strict.T
        m[:, 384:512] = bdT
        return m
    c["c_maskF"] = mk(lt, le)
    c["c_maskB"] = mk(lt.T.copy(), le.T.copy())
    r = np.ones((128, BLK), np.float32); r[:, ::CH] = 0.0
    c["c_rst"] = r
    c["c_iota"] = np.broadcast_to(np.arange(16, dtype=np.float32), (128, 16)).copy()
    return c


def prep_shared(inp):
    f = lambda a: np.ascontiguousarray(np.asarray(a, dtype=np.float32))
    sh = {}
    for n in ("ln_emb_g", "ln_emb_b"):
        sh[n] = f(inp[n]).reshape(1, D)
    for n in ("ln1_g", "ln1_b", "ln2_g", "ln2_b", "ln3_g", "ln3_b", "mem_ln_g", "mem_ln_b"):
        sh[n] = f(inp[n][0]).reshape(1, D)
    sh["w_in"] = f(inp["w_in"][0])
    sh["mu"] = f(inp["rwkv_mu"][0]).reshape(1, RWKV_COLS)
    tr = lambda a: f(np.asarray(a).reshape(-1, 4, 128).transpose(2, 0, 1).reshape(128, -1))
    sh["w0T"] = tr(inp["rwkv_w0"][0]); sh["a0T"] = tr(inp["rwkv_a0"][0])
    sh["w2"] = f(inp["rwkv_w2"][0]).reshape(128, RW); sh["a2"] = f(inp["rwkv_a2"][0]).reshape(128, RW)
    sh["g2"] = f(inp["rwkv_g2"][0])
    cols = [inp["rwkv_k_k"][0], inp["rwkv_k_a"][0], np.asarray(inp["rwkv_r_k"][0]).reshape(-1), inp["rwkv_gn_g"][0], inp["rwkv_gn_b"][0]]
    sh["pp"] = f(np.concatenate([np.asarray(c).reshape(4, 128).T for c in cols], axis=1))
    sh["gln_g"] = f(inp["gmlp_ln_g"][0]).reshape(1, 512); sh["gln_b"] = f(inp["gmlp_ln_b"][0]).reshape(1, 512)
    sh["wsT"] = f(np.asarray(inp["gmlp_w_s"][0]).transpose(2, 0, 1))
    bs = np.repeat(np.asarray(inp["gmlp_b_s"][0]), 64, axis=0)
    sh["bsF"] = f(bs.reshape(4, 128, 128).transpose(1, 0, 2))
    sh["w_branch"] = f(inp["w_branch"][0]).reshape(1024, D)
    sh["w_mix"] = f(inp["w_mix_out"][0])
    sh["wq"] = f(inp["xattn_w_q"][0]); sh["wkv"] = f(inp["xattn_w_kv"][0]); sh["wo"] = f(inp["xattn_w_o"][0])
    sh["pwq"] = f(inp["peer_w_query"][0])
    sh["skT"] = f(np.asarray(inp["peer_sub_keys"][0]).transpose(2, 0, 1))
    sh["puT"] = f(np.asarray(inp["peer_u"][0]).reshape(128, 128, 8, 128).transpose(1, 3, 2, 0).reshape(128, 128, D))
    sh["pvP"] = f(np.asarray(inp["peer_v"][0]).reshape(128, 128, D).transpose(1, 0, 2))
    sh.update(_consts())
    return sh


def make_in_maps(inp, cores):
    sh = prep_shared(inp)
    maps = []
    for b in cores:
        m = dict(sh)
        m["x"] = np.ascontiguousarray(np.asarray(inp["x"][b], dtype=np.float32))
        m["mem"] = np.ascontiguousarray(np.asarray(inp["mem"][b], dtype=np.float32))
        maps.append(m)
    return maps


def kernel(**inputs):
    nc, _ = build_program()
    maps = make_in_maps(inputs, list(range(N_CORES)))
    res = run_bass_kernel_spmd(nc, maps, core_ids=list(range(N_CORES)))
    return np.stack([np.asarray(r["out"], dtype=np.float32) for r in res.results], axis=0)
```

```python
from contextlib import ExitStack

import numpy as np
import concourse.bass as bass
import concourse.mybir as mybir
from concourse.bass_utils import run_bass_kernel_spmd

F32 = mybir.dt.float32
BF16 = mybir.dt.bfloat16
I32 = mybir.dt.int32
U32 = mybir.dt.uint32
AF = mybir.ActivationFunctionType
ALU = mybir.AluOpType
AX = mybir.AxisListType

N_CORES = 8
SEQ = 2048
D = 1024
NT = SEQ // 128


class Res:
    __slots__ = ("name", "w", "r", "excl")

    def __init__(self, name="", excl=False):
        self.name = name
        self.excl = excl
        self.w = None
        self.r = {}


class KB:
    def __init__(self, nc, es):
        self.nc = nc
        self.es = es
        self.eng = {"pe": nc.tensor, "act": nc.scalar, "dve": nc.vector, "pool": nc.gpsimd, "sp": nc.sync}
        self.sem = {}
        self.cnt = {}
        self.seen = {}
        for e in self.eng:
            self.sem[e] = es.enter_context(nc.semaphore("sem_" + e))
            self.cnt[e] = 0
            self.seen[e] = {}
        self.dsem = {}
        for q, n in (("sp", 8), ("pool", 8), ("act", 2)):
            self.dsem[q] = [[es.enter_context(nc.semaphore("dsem_%s%d" % (q, i))), 0] for i in range(n)]
        self.dptr = {q: 0 for q in self.dsem}
        self.n_ins = 0
        self.n_wait = 0
        self._id = 0
        self.defer = None

    def sb(self, shape, dt, name=None):
        self._id += 1
        return self.es.enter_context(self.nc.sbuf_tensor("%s_%d" % (name or "t", self._id), list(shape), dt))

    def ps(self, shape, dt, name=None):
        self._id += 1
        return self.es.enter_context(self.nc.psum_tensor("%s_%d" % (name or "p", self._id), list(shape), dt))

    def _deps(self, reads, writes):
        deps = []
        for r in reads:
            if r.w is not None:
                deps.append(r.w)
        for w in writes:
            if w.w is not None:
                deps.append(w.w)
            deps.extend(w.r.values())
        return deps

    def _wait(self, e, deps):
        eng = self.eng[e]
        seen = self.seen[e]
        best = {}
        for (sem, val) in deps:
            k = id(sem)
            if seen.get(k, 0) >= val:
                continue
            if k not in best or best[k][1] < val:
                best[k] = (sem, val)
        for k, (sem, val) in best.items():
            if e == "pe" and sem is self.sem["pe"]:
                continue
            eng.wait_ge(sem, val)
            self.n_wait += 1
            seen[k] = val

    def _mark(self, tok, reads, writes):
        for r in reads:
            k = id(tok[0])
            r.r[k] = tok
        for w in writes:
            w.w = tok
            w.r = {}

    def mark(self):
        if self.defer is not None:
            self.defer.append(None)

    def op(self, e, fn, reads=(), writes=()):
        if self.defer is not None:
            self.defer.append((e, fn, list(reads), list(writes)))
            return None
        ex = [r for r in reads if r.excl and r not in writes]
        if ex:
            writes = list(writes) + ex
        self._wait(e, self._deps(reads, writes))
        ins = fn(self.eng[e])
        self.cnt[e] += 1
        ins.then_inc(self.sem[e], 1)
        tok = (self.sem[e], self.cnt[e])
        self._mark(tok, reads, writes)
        self.n_ins += 1
        return tok

    def dma(self, q, fn, reads=(), writes=()):
        slots = self.dsem[q]
        slot = slots[self.dptr[q] % len(slots)]
        self.dptr[q] += 1
        deps = self._deps(reads, writes)
        if slot[1] > 0:
            deps.append((slot[0], slot[1]))
        self._wait(q, deps)
        ins = fn(self.eng[q])
        slot[1] += 16
        ins.then_inc(slot[0], 16)
        tok = (slot[0], slot[1])
        self._mark(tok, reads, writes)
        self.n_ins += 1
        return tok

    def wait_all(self, e, ress):
        deps = []
        for r in ress:
            if r.w is not None:
                deps.append(r.w)
            deps.extend(r.r.values())
        self._wait(e, deps)


class Ring:
    def __init__(self, k, n, shape, dt, name="ring"):
        self.items = [(k.sb(shape, dt, name), Res(name)) for _ in range(n)]
        self.i = 0

    def get(self):
        it = self.items[self.i % len(self.items)]
        self.i += 1
        return it


RW = 512
NHP = 4
RWKV_COLS = 1920
GM0 = 1920
GT0 = 2944
C0 = float(np.exp(-0.5))
ALPHA = float(2.0 ** 0.25)
LN_EPS = 1e-5
GN_EPS = 64e-5
CH = 64
BLK = 128
NCH = SEQ // CH
NBLK = SEQ // BLK


def build_program(debug=(), stop=None, scan_steps=None, scan_sub=99):
    nc = bass.Bass("TRN2", target_bir_lowering=False)
    dbg_outs = {}

    def din(name, shape, dt=F32):
        return nc.dram_tensor(name, list(shape), dt, kind="ExternalInput").ap()

    x = din("x", [SEQ, D]); mem = din("mem", [256, D])
    lng = {n: din(n, [1, D]) for n in ("ln_emb_g", "ln_emb_b", "ln1_g", "ln1_b", "ln2_g", "ln2_b", "ln3_g", "ln3_b", "mem_ln_g", "mem_ln_b")}
    w_in = din("w_in", [D, 4992]); mu_d = din("mu", [1, RWKV_COLS])
    w0T_d = din("w0T", [128, 8]); a0T_d = din("a0T", [128, 8])
    w2_d = din("w2", [128, RW]); a2_d = din("a2", [128, RW]); g2_d = din("g2", [128, RW])
    pp_d = din("pp", [128, 20])
    gln_g_d = din("gln_g", [1, 512]); gln_b_d = din("gln_b", [1, 512])
    wsT_d = din("wsT", [128, 8, 128]); bsF_d = din("bsF", [128, 4, 128])
    wbr_d = din("w_branch", [1024, D]); wmix_d = din("w_mix", [D, D])
    wq_d = din("wq", [D, D]); wkv_d = din("wkv", [D, 2 * D]); wo_d = din("wo", [D, D])
    pwq_d = din("pwq", [D, 2048]); skT_d = din("skT", [128, 2, 128])
    pu_d = din("puT", [128, 128, D]); pv_d = din("pvP", [128, 128, D])
    Ub_d = nc.dram_tensor("Ub", [128, 128, D], BF16, kind="Internal").ap()
    Vb_d = nc.dram_tensor("Vb", [128, 128, D], BF16, kind="Internal").ap()
    ident_d = din("c_ident", [128, 128]); onesbd_d = din("c_onesbd", [128, 128])
    maskF_d = din("c_maskF", [128, 512]); maskB_d = din("c_maskB", [128, 512]); rst_d = din("c_rst", [128, BLK])
    iota_d = din("c_iota", [128, 16])
    out_d = nc.dram_tensor("out", [SEQ, D], F32, kind="ExternalOutput").ap()

    es = ExitStack()
    with es:
        k = KB(nc, es)
        RAW = k.sb([128, 40960], F32, "raw")

        def view(off_kb, nbytes, dt, pattern=None, **kw):
            w0 = int(off_kb * 256)
            v = RAW[:, w0:w0 + nbytes // 4]
            if dt != F32:
                v = v.bitcast(dt)
            if pattern:
                v = v.rearrange(pattern, **kw)
            return v

        def barrier():
            toks = [(k.sem[e], k.cnt[e]) for e in k.eng if k.cnt[e] > 0]
            for q in k.dsem:
                for s in k.dsem[q]:
                    if s[1] > 0:
                        toks.append((s[0], s[1]))
            for e in k.eng:
                k._wait(e, toks)

        dbg_res = []

        def dbg(name, ap, res, shape):
            if name not in debug:
                return
            o = nc.dram_tensor("dbg_" + name, list(shape), F32, kind="ExternalOutput").ap()
            dbg_outs[name] = o
            barrier()
            if ap.dtype != F32:
                tmp = k.sb(list(shape), F32, "dbgtmp"); tr = Res()
                k.op("dve", lambda e: e.tensor_copy(tmp[:], ap), reads=res, writes=[tr])
                k.dma("sp", lambda e: e.dma_start(out=o, in_=tmp[:]), reads=[tr])
                dbg_res.append(tr)
            else:
                rr = Res()
                k.dma("sp", lambda e: e.dma_start(out=o, in_=ap), reads=res, writes=[rr])
                dbg_res.append(rr)

        def finish():
            barrier()

        cst = Res("consts")
        ident = k.sb([128, 128], BF16, "ident"); identf = k.sb([128, 128], F32, "identf")
        onesbd = k.sb([128, 128], BF16, "onesbd"); ones64 = k.sb([128, 128], F32, "ones64")
        maskF = k.sb([128, 512], BF16, "maskF"); maskB = k.sb([128, 512], BF16, "maskB")
        rst = k.sb([128, BLK], F32, "rst"); iota16 = k.sb([128, 16], F32, "iota16")
        w0T = k.sb([128, 8], F32, "w0T"); a0T = k.sb([128, 8], F32, "a0T"); pp = k.sb([128, 20], F32, "pp")
        ppx = k.sb([128, 8], F32, "ppx")
        w2b = k.sb([128, RW], BF16, "w2b"); a2b = k.sb([128, RW], BF16, "a2b"); g2b = k.sb([128, RW], BF16, "g2b")
        epsln = k.sb([128, 1], F32, "epsln"); epsgn = k.sb([128, 1], F32, "epsgn")
        for (t, d_) in ((ident, ident_d), (onesbd, onesbd_d), (maskF, maskF_d), (maskB, maskB_d), (w2b, w2_d), (a2b, a2_d), (g2b, g2_d)):
            k.dma("pool", lambda e: e.dma_start(out=t[:], in_=d_), writes=[cst])
        for (t, d_) in ((identf, ident_d), (rst, rst_d), (iota16, iota_d), (w0T, w0T_d), (a0T, a0T_d), (pp, pp_d)):
            k.dma("sp", lambda e: e.dma_start(out=t[:], in_=d_), writes=[cst])
        nw0T = k.sb([128, 8], F32, "nw0T"); na0T = k.sb([128, 8], F32, "na0T"); one1 = k.sb([128, 1], F32, "one1")
        k.op("dve", lambda e: e.memset(one1[:], 1.0), writes=[cst])
        k.op("dve", lambda e: e.memset(epsln[:], LN_EPS), writes=[cst])
        k.op("dve", lambda e: e.memset(epsgn[:], GN_EPS), writes=[cst])
        k.op("dve", lambda e: e.tensor_scalar(ones64[:], identf[:], 0.0, 0.0, ALU.mult, ALU.add), reads=[cst], writes=[cst])
        k.op("dve", lambda e: e.tensor_scalar(ones64[:], onesbd[:], 1.0 / 64.0, None, ALU.mult), reads=[cst], writes=[cst])
        k.op("dve", lambda e: e.tensor_scalar(nw0T[:], w0T[:], -1.0, None, ALU.mult), reads=[cst], writes=[cst])
        k.op("dve", lambda e: e.tensor_scalar(na0T[:], a0T[:], -1.0, None, ALU.mult), reads=[cst], writes=[cst])
        k.op("dve", lambda e: e.tensor_scalar(ppx[:, 0:4], pp[:, 4:8], -1.0, 1.0, ALU.mult, ALU.add), reads=[cst], writes=[cst])

        def act_sigmoid(eng_op, out, in_, nbias, reads, writes):
            eng_op("act", lambda e: e.activation(out=out, in_=in_, func=AF.Exp, bias=nbias, scale=-1.0), reads=reads + [cst], writes=writes)
            eng_op("act", lambda e: e.activation(out=out, in_=out, func=AF.Ln, bias=one1[:, 0:1], scale=1.0), reads=writes + [cst], writes=writes)
            eng_op("act", lambda e: e.activation(out=out, in_=out, func=AF.Exp, scale=-1.0), reads=writes, writes=writes)
        k.op("dve", lambda e: e.tensor_scalar(ppx[:, 4:8], pp[:, 4:8], -2.0, 2.0, ALU.mult, ALU.add), reads=[cst], writes=[cst])
        KK = lambda hp: pp[:, hp:hp + 1]
        KA = lambda hp: pp[:, 4 + hp:5 + hp]
        RK = lambda hp: pp[:, 8 + hp:9 + hp]
        GNG = lambda hp: pp[:, 12 + hp:13 + hp]
        GNB = lambda hp: pp[:, 16 + hp:17 + hp]

        psf = [(k.ps([128, 512], F32, "psf"), Res("psf%d" % i, True)) for i in range(6)]
        psb = [(k.ps([128, 1024], BF16, "psb"), Res("psb%d" % i, True)) for i in range(2)]
        pctr = {"f": 0, "b": 0}

        def PSF():
            it = psf[pctr["f"] % 6]; pctr["f"] += 1
            return it

        def PSB():
            it = psb[pctr["b"] % 2]; pctr["b"] += 1
            return it

        def mm(out, lhsT, rhs, start, stop, reads, writes):
            k.op("pe", lambda e: e.matmul(out, lhsT, rhs, start=start, stop=stop), reads=reads, writes=writes)

        def load_bc(ph, d_ap, n, name):
            t = ph.enter_context(nc.sbuf_tensor(name + "_%d" % k._id, [128, n], F32)); k._id += 1
            r = Res(name)
            k.dma("sp", lambda e: e.dma_start(out=t[:], in_=d_ap.partition_broadcast(128)), writes=[r])
            return t, r

        def phase_sb(ph, shape, dt, name):
            k._id += 1
            return ph.enter_context(nc.sbuf_tensor("%s_%d" % (name, k._id), list(shape), dt))

        class PRing:
            def __init__(self, ph, n, shape, dt, name):
                self.items = [(phase_sb(ph, shape, dt, name), Res(name)) for _ in range(n)]
                self.i = 0

            def get(self):
                it = self.items[self.i % len(self.items)]; self.i += 1
                return it

        def layer_norm(src, src_res, gam, bet, gbres, out, out_res, small, n=1024, eps=None):
            eps = eps if eps is not None else epsln
            nchk = n // 512
            st, sr = small.get()
            for c in range(nchk):
                k.op("dve", lambda e, c=c: e.bn_stats(out=st[:, c * 6:(c + 1) * 6], in_=src[:, c * 512:(c + 1) * 512]), reads=src_res, writes=[sr])
            k.op("dve", lambda e: e.bn_aggr(out=st[:, 12:14], in_=st[:, 0:6 * nchk]), reads=[sr], writes=[sr])
            k.op("act", lambda e: e.activation(out=st[:, 14:15], in_=st[:, 13:14], func=AF.Sqrt, bias=eps[:, 0:1], scale=1.0), reads=[sr, cst], writes=[sr])
            k.op("dve", lambda e: e.reciprocal(out=st[:, 14:15], in_=st[:, 14:15]), reads=[sr], writes=[sr])
            k.op("dve", lambda e: e.tensor_scalar(st[:, 15:16], st[:, 12:13], st[:, 14:15], -1.0, ALU.mult, ALU.mult), reads=[sr], writes=[sr])
            k.op("act", lambda e: e.activation(out=out, in_=src, func=AF.Identity, bias=st[:, 15:16], scale=st[:, 14:15]), reads=src_res + [sr], writes=out_res)
            k.op("dve", lambda e: e.tensor_tensor(out=out, in0=out, in1=gam, op=ALU.mult), reads=out_res + gbres, writes=out_res)
            k.op("dve", lambda e: e.tensor_tensor(out=out, in0=out, in1=bet, op=ALU.add), reads=out_res + gbres, writes=out_res)

        def to_fm(hb, hbr, dstT, dst_res, t):
            pt, ptr_ = PSB()
            for c in range(8):
                k.op("pe", lambda e, c=c: e.transpose(pt[:, c * 128:(c + 1) * 128], hb[:, c * 128:(c + 1) * 128], ident[:]), reads=[hbr, cst], writes=[ptr_])
            k.op("act", lambda e: e.activation(out=dstT[:, :, t * 128:(t + 1) * 128], in_=pt[:].rearrange("p (c t) -> p c t", t=128), func=AF.Copy), reads=[ptr_], writes=dst_res)

        def run_pairs(n, body):
            for t_ in range(0, n, 2):
                recs = []
                for tt_ in (t_, t_ + 1):
                    if tt_ >= n:
                        continue
                    k.defer = []
                    body(tt_)
                    recs.append([x_ for x_ in k.defer if x_ is not None]); k.defer = None
                for i_ in range(max(len(r_) for r_ in recs)):
                    for r_ in recs:
                        if i_ < len(r_):
                            k.op(*r_[i_])

        def phase_h0T(h0T, h0T_res, paired=False):
            with ExitStack() as ph:
                g_t, g_r = load_bc(ph, lng["ln_emb_g"], D, "g")
                b_t, b_r = load_bc(ph, lng["ln_emb_b"], D, "b")
                xs = PRing(ph, 2, [128, D], F32, "xs")
                hbs = PRing(ph, 2, [128, D], BF16, "hb")
                small = PRing(ph, 4, [128, 16], F32, "lnsm")
                def tile_body(t):
                    xt, xr = xs.get()
                    k.dma("sp", lambda e: e.dma_start(out=xt[:], in_=x[t * 128:(t + 1) * 128, :]), writes=[xr])
                    layer_norm(xt[:], [xr], g_t[:], b_t[:], [g_r, b_r], xt[:], [xr], small)
                    hb, hbr = hbs.get()
                    k.op("act", lambda e: e.activation(out=hb[:], in_=xt[:], func=AF.Copy), reads=[xr], writes=[hbr])
                    to_fm(hb, hbr, h0T, [h0T_res[t]], t)
                if paired:
                    run_pairs(NT, tile_body)
                else:
                    for t_ in range(NT):
                        tile_body(t_)
                barrier()

        h0T = view(64, 32768, BF16, "p (c t) -> p c t", c=8)
        h0T_res = [Res("h0T%d" % t) for t in range(NT)]
        phase_h0T(h0T, h0T_res)
        dbg("h0T", h0T[:, 0, :], h0T_res, [128, SEQ])
        if stop == "A":
            finish()
            return nc, dbg_outs

        shT = view(96, 32768, BF16, "p (c t) -> p c t", c=8); shr = Res("shT")
        rkv = view(0, 12 * SEQ * 2, BF16, "p (c t) -> p c t", c=12)
        wag = view(48, 3 * SEQ * 2, BF16, "p (c t) -> p c t", c=3)
        rkv_res = [[Res("rkv") for _ in range(4)] for _ in range(15)]
        for c in range(8):
            k.op("dve", lambda e: e.tensor_tensor(out=shT[:, c, 1:SEQ - 1], in0=h0T[:, c, 0:SEQ - 2], in1=h0T[:, c, 2:SEQ], op=ALU.add), reads=h0T_res, writes=[shr])
        k.op("dve", lambda e: e.tensor_copy(shT[:, :, 0:1], h0T[:, :, 1:2]), reads=h0T_res, writes=[shr])
        k.op("dve", lambda e: e.tensor_copy(shT[:, :, SEQ - 1:SEQ], h0T[:, :, SEQ - 2:SEQ - 1]), reads=h0T_res, writes=[shr])
        with ExitStack() as ph:
            wsts = PRing(ph, 2, [128, 8, 128], F32, "wst")
            was = PRing(ph, 2, [128, 8, 128], BF16, "wa")
            wbs = PRing(ph, 2, [128, 8, 128], BF16, "wb")
            mus = PRing(ph, 2, [128, 384], F32, "mu")
            for cc in range(15):
                c0 = cc * 128
                wst, wsr = wsts.get()
                k.dma("sp", lambda e: e.dma_start(out=wst[:], in_=w_in[:, c0:c0 + 128].rearrange("(k p) c -> p k c", p=128)), writes=[wsr])
                mt, mr = mus.get()
                k.dma("sp", lambda e: e.dma_start(out=mt[:, 0:128], in_=mu_d[:, c0:c0 + 128].partition_broadcast(128)), writes=[mr])
                k.op("dve", lambda e: e.tensor_scalar(mt[:, 128:256], mt[:, 0:128], -1.0, 1.0, ALU.mult, ALU.add), reads=[mr], writes=[mr])
                k.op("dve", lambda e: e.tensor_scalar(mt[:, 256:384], mt[:, 0:128], 0.5, None, ALU.mult), reads=[mr], writes=[mr])
                wa, war = was.get(); wb, wbr = wbs.get()
                k.op("dve", lambda e: e.tensor_tensor(out=wa[:], in0=wst[:], in1=mt[:, 128:256].unsqueeze(1).to_broadcast([128, 8, 128]), op=ALU.mult), reads=[wsr, mr], writes=[war])
                k.op("pool", lambda e: e.tensor_tensor(out=wb[:], in0=wst[:], in1=mt[:, 256:384].unsqueeze(1).to_broadcast([128, 8, 128]), op=ALU.mult), reads=[wsr, mr], writes=[wbr])
                for tb in range(4):
                    pb, pbr = PSF()
                    ts = slice(tb * 512, (tb + 1) * 512)
                    for kc in range(8):
                        mm(pb[:], wa[:, kc, :], h0T[:, kc, ts], kc == 0, False, [war] + h0T_res[4 * tb:4 * tb + 4], [pbr])
                    for kc in range(8):
                        mm(pb[:], wb[:, kc, :], shT[:, kc, ts], False, kc == 7, [wbr, shr], [pbr])
                    if cc < 12:
                        dest, fn = rkv[:, cc, ts], AF.Copy
                    else:
                        dest, fn = wag[:, cc - 12, ts], (AF.Tanh, AF.Copy, AF.Sigmoid)[cc - 12]
                    k.op("act", lambda e: e.activation(out=dest, in_=pb[:], func=fn), reads=[pbr], writes=[rkv_res[cc][tb]])
            barrier()
        dbg("r0", rkv[:, 0, :], rkv_res[0], [128, SEQ])
        dbg("k0", rkv[:, 4, :], rkv_res[4], [128, SEQ])
        dbg("wd", wag[:, 0, :], rkv_res[12], [128, SEQ])
        if stop == "B":
            finish()
            return nc, dbg_outs


        barrier()
        yaT = rkv[:, 0:4, :]
        yaT_res = [rkv_res[hp] for hp in range(4)]

        class Carver:
            def __init__(self, a_kb, b_kb):
                self.p = a_kb * 1024; self.end = b_kb * 1024

            def alloc(self, shape, dt):
                nb = int(np.prod(shape[1:])) * (2 if dt == BF16 else 4)
                nb = (nb + 31) // 32 * 32
                assert self.p + nb <= self.end, "carver overflow"
                v = RAW[:, self.p // 4:(self.p + nb) // 4]
                self.p += nb
                if dt != F32:
                    v = v.bitcast(dt)
                n = int(np.prod(shape[1:]))
                v = v[:, 0:n]
                if len(shape) == 3:
                    v = v.rearrange("p (a b) -> p a b", a=shape[1])
                return v

        def scan_round(rnd):
            hps = [2 * rnd, 2 * rnd + 1]
            carve = Carver(64, 128)
            carve2 = Carver(144, 160)
            yacc = view(128, 2 * SEQ * 4, F32, "p (c t) -> p c t", c=2)
            yacc_res = [[Res("yacc") for _ in range(NCH)] for _ in range(2)]
            if scan_steps:
                for hl_ in range(2):
                    k.op("pool", lambda e: e.memset(yacc[:, hl_, :], 0.0), writes=yacc_res[hl_])
                    for r_ in yacc_res[hl_]:
                        r_.w = None
            with ExitStack() as ph:
                def T(shape, dt, name):
                    return (phase_sb(ph, shape, dt, name), Res(name))
                tmp = {n: T([128, BLK], F32, n) for n in ("sw", "sa", "cs", "pin", "pex", "ege", "egi", "kr", "rn", "kk", "nkk", "t1", "kd", "bb")}
                ksq = T([128, BLK], BF16, "ksq")
                yb = T([128, BLK], BF16, "yb"); ysqb = T([128, BLK], BF16, "ysqb")
                NBK = NCH // 2
                streams = []
                for z in (0, 1):
                    for hl, hp in enumerate(hps):
                        st = dict(z=z, hp=hp, hl=hl, ui=len(streams))
                        st["A3"] = [dict(AR=carve.alloc([128, 2, 192], BF16), eg=carve.alloc([128, BLK], F32), res=Res("prepA")) for _ in range(3)]
                        st["K2"] = [dict(Kb=carve.alloc([128, 2, 128], BF16), Bb=carve.alloc([128, 2, 128], BF16), Vb=carve.alloc([128, 2, 128], BF16), res=Res("prepK")) for _ in range(2)]
                        for sl in st["A3"]:
                            k.op("pool", lambda e: e.memset(sl["AR"], 0.0), writes=[sl["res"]])
                        for sl in st["K2"]:
                            for nm in ("Kb", "Bb", "Vb"):
                                k.op("pool", lambda e: e.memset(sl[nm], 0.0), writes=[sl["res"]])
                        st["Tm"] = [[(carve2.alloc([128, 512], BF16), Res("Tm")) for _ in range(2)] for _ in range(2)]
                        st["WT"] = T([128, 128], BF16, "WT"); st["UT"] = T([128, 128], BF16, "UT")
                        st["S"] = [T([128, 128], BF16, "S") for _ in range(2)]
                        k.op("pool", lambda e: e.memset(st["S"][0][0][:], 0.0), writes=[st["S"][0][1]])
                        st["si"] = 0
                        streams.append(st)
                KTG = [[(carve.alloc([128, 4, 384], BF16), Res("KTG")) for _ in range(2)] for _ in range(2)]
                MVG = [[(carve.alloc([128, 4, 128], BF16), Res("MVG")) for _ in range(2)] for _ in range(2)]
                IVG = [[(carve.alloc([128, 3, 512], BF16), Res("IVG")) for _ in range(2)] for _ in range(2)]
                chain_res = [Res("chain%d" % i, True) for i in range(2)]
                psbf = [(psb[i][0][:].bitcast(F32), psb[i][1]) for i in range(2)]
                lvl_banks = [[psf[0], psf[1], psf[2]], [psf[3], psbf[0], psbf[1]]]

                def blk_of(st, bi):
                    return bi if st["z"] == 0 else NBK - 1 - bi

                def chunk_of(st, bi, g):
                    b = blk_of(st, bi)
                    return 2 * b + g if st["z"] == 0 else 2 * b + 1 - g

                def prep(st, bi, tmp, ksq, pcol):
                    z, hp = st["z"], st["hp"]
                    b = blk_of(st, bi)
                    sa_ = st["A3"][bi % 3]; sk_ = st["K2"][bi % 2]
                    t0 = b * BLK; ts = slice(t0, t0 + BLK); tb = t0 // 512
                    zs = slice(z * 64, (z + 1) * 64)
                    hc = slice(hp * 128, (hp + 1) * 128)
                    r_ap, k_ap, v_ap = rkv[:, hp, ts], rkv[:, 4 + hp, ts], rkv[:, 8 + hp, ts]
                    rr, kr_, vr_ = [rkv_res[hp][tb]], [rkv_res[4 + hp][tb]], [rkv_res[8 + hp][tb]]
                    pb, pbr = psbf[0][0][:, pcol:pcol + 256], psbf[0][1]
                    mm(pb[:, 0:BLK], w2b[zs, hc], wag[zs, 0, ts], True, True, [cst, rkv_res[12][tb]], [pbr])
                    mm(pb[:, BLK:2 * BLK], a2b[zs, hc], wag[zs, 1, ts], True, True, [cst, rkv_res[13][tb]], [pbr])
                    (sw, swr), (sa, sar), (cs, csr) = tmp["sw"], tmp["sa"], tmp["cs"]
                    (pin, pinr), (pex, pexr), (ege, eger), (egi, egir) = tmp["pin"], tmp["pex"], tmp["ege"], tmp["egi"]
                    act_sigmoid(k.op, sw[:], pb[:, 0:BLK], nw0T[:, z * 4 + hp:z * 4 + hp + 1], [pbr], [swr])
                    act_sigmoid(k.op, sa[:], pb[:, BLK:2 * BLK], na0T[:, z * 4 + hp:z * 4 + hp + 1], [pbr], [sar])
                    k.mark()
                    k.op("dve", lambda e: e.tensor_tensor_scan(out=cs[:], data0=rst[:], data1=sw[:], initial=0.0, op0=ALU.mult, op1=ALU.add), reads=[swr, cst], writes=[csr])
                    v3 = lambda t_: t_.rearrange("p (c t) -> p c t", t=CH)
                    if z == 0:
                        k.op("dve", lambda e: e.tensor_tensor(out=pex[:], in0=cs[:], in1=sw[:], op=ALU.subtract), reads=[csr, swr], writes=[pexr])
                        pin_t, pin_r = cs, csr
                    else:
                        k.op("dve", lambda e: e.tensor_tensor(out=v3(pex[:]), in0=v3(cs[:])[:, :, CH - 1:CH].to_broadcast([128, 2, CH]), in1=v3(cs[:]), op=ALU.subtract), reads=[csr], writes=[pexr])
                        k.op("dve", lambda e: e.tensor_tensor(out=pin[:], in0=pex[:], in1=sw[:], op=ALU.add), reads=[pexr, swr], writes=[pinr])
                        pin_t, pin_r = pin, pinr
                    eg = sa_["eg"]; rA = sa_["res"]; rK = sk_["res"]
                    k.op("act", lambda e: e.activation(out=eg, in_=pin_t[:], func=AF.Exp, scale=-C0), reads=[pin_r], writes=[rA])
                    k.op("act", lambda e: e.activation(out=ege[:], in_=pex[:], func=AF.Exp, scale=-C0), reads=[pexr], writes=[eger])
                    k.op("act", lambda e: e.activation(out=egi[:], in_=pin_t[:], func=AF.Exp, scale=C0), reads=[pin_r], writes=[egir])
                    k.mark()
                    (kr, krr), (rn, rnr), (kk, kkr) = tmp["kr"], tmp["rn"], tmp["kk"]
                    k.op("dve", lambda e: e.tensor_scalar(kr[:], k_ap, KK(hp), None, ALU.mult), reads=kr_ + [cst], writes=[krr])
                    k.op("dve", lambda e: e.tensor_tensor(out=ksq[0][:], in0=kr[:], in1=kr[:], op=ALU.mult), reads=[krr], writes=[ksq[1]])
                    pb2, pb2r = psbf[1][0][:, pcol:pcol + 256], psbf[1][1]
                    mm(pb2[:, 0:BLK], onesbd[:], ksq[0][:], True, True, [cst, ksq[1]], [pb2r])
                    k.op("dve", lambda e: e.tensor_scalar(rn[:], pb2[:, 0:BLK], 1e-24, None, ALU.max), reads=[pb2r], writes=[rnr])
                    k.mark()
                    k.op("act", lambda e: e.activation(out=rn[:], in_=rn[:], func=AF.Ln), reads=[rnr], writes=[rnr])
                    k.op("act", lambda e: e.activation(out=rn[:], in_=rn[:], func=AF.Exp, scale=-0.5), reads=[rnr], writes=[rnr])
                    k.op("dve", lambda e: e.tensor_tensor(out=kk[:], in0=kr[:], in1=rn[:], op=ALU.mult), reads=[krr, rnr], writes=[kkr])
                    nkk, nkkr = tmp["nkk"]
                    k.op("dve", lambda e: e.tensor_scalar(nkk[:], kk[:], -1.0, None, ALU.mult), reads=[kkr], writes=[nkkr])
                    AR, Kb, Bb, Vb = sa_["AR"], sk_["Kb"], sk_["Bb"], sk_["Vb"]
                    k.op("dve", lambda e: e.tensor_tensor(out=AR[:, :, 128:192], in0=v3(r_ap), in1=v3(eg), op=ALU.mult), reads=rr + [rA], writes=[rA])
                    (t1, t1r), (kd, kdr), (bb, bbr) = tmp["t1"], tmp["kd"], tmp["bb"]
                    k.op("dve", lambda e: e.tensor_scalar(t1[:], sa[:], KA(hp), ppx[:, hp:hp + 1], ALU.mult, ALU.add), reads=[sar, cst], writes=[t1r])
                    k.op("dve", lambda e: e.tensor_tensor(out=kd[:], in0=t1[:], in1=k_ap, op=ALU.mult), reads=[t1r] + kr_, writes=[kdr])
                    k.op("dve", lambda e: e.tensor_tensor(out=bb[:], in0=kk[:], in1=sa[:], op=ALU.mult), reads=[kkr, sar], writes=[bbr])
                    k.mark()

                    def half_ops(half):
                        hs = slice(half * 64, (half + 1) * 64); cs_ = slice(half * 64, (half + 1) * 64)
                        eng = "dve" if half == 0 else "pool"
                        k.op(eng, lambda e: e.tensor_tensor(out=AR[hs, :, cs_], in0=v3(nkk[hs, :]), in1=v3(ege[hs, :]), op=ALU.mult), reads=[nkkr, eger, rA], writes=[rA])
                        k.op(eng, lambda e: e.tensor_tensor(out=Kb[hs, :, cs_], in0=v3(kd[hs, :]), in1=v3(egi[hs, :]), op=ALU.mult), reads=[kdr, egir, rK], writes=[rK])
                        k.op(eng, lambda e: e.tensor_tensor(out=Bb[hs, :, cs_], in0=v3(bb[hs, :]), in1=v3(egi[hs, :]), op=ALU.mult), reads=[bbr, egir, rK], writes=[rK])
                        k.op("act", lambda e: e.activation(out=Vb[hs, :, cs_], in_=v3(v_ap)[hs], func=AF.Copy), reads=vr_ + [rK], writes=[rK])
                    half_ops(0)
                    half_ops(1)
                    k.mark()

                def unit_ctx(st, bi, g):
                    bp = bi % 2
                    sa_ = st["A3"][bi % 3]; sk_ = st["K2"][bi % 2]
                    c = chunk_of(st, bi, g); ci = c % 2
                    return dict(st=st, ui=st["ui"], c=c, ci=ci, bp=bp, g=g, rA=sa_["res"], rK=sk_["res"], eg=sa_["eg"],
                                AR=sa_["AR"][:, ci, :], Kb=sk_["Kb"][:, ci, :], Bb=sk_["Bb"][:, ci, :], Vb=sk_["Vb"][:, ci, :],
                                Tm=st["Tm"][bp][g], KT=(KTG[bp][g][0][:, st["ui"], :], KTG[bp][g][1]), Minv=(MVG[bp][g][0][:, st["ui"], :], MVG[bp][g][1]))

                def pre_block(bi):
                    groups = [[unit_ctx(st, bi, g) for st in streams] for g in range(2)]
                    macro = []

                    def t_pe(g):
                        def f():
                            for u in groups[g]:
                                ui = u["ui"]
                                pb, pbr = psf[ui]; rd = [u["rA"], u["rK"]]
                                mm(pb[:, 0:192], u["Kb"], u["AR"], True, True, rd, [pbr])
                                mm(pb[:, 192:384], u["Bb"], u["AR"], True, True, rd, [pbr])
                                mm(pb[:, 384:512], u["AR"][:, 0:128], u["Bb"], True, True, rd, [pbr])
                                pt_, ptr_ = psb[ui // 2]; pt = pt_[:, (ui % 2) * 384:(ui % 2) * 384 + 384]
                                for i, nm in enumerate(("Kb", "Bb", "Vb")):
                                    k.op("pe", lambda e: e.transpose(pt[:, i * 128:(i + 1) * 128], u[nm], ident[:]), reads=rd + [cst], writes=[ptr_])
                        return f

                    def t_ev(g):
                        def f():
                            bp = bi % 2
                            IA, IAr = IVG[g][0]
                            for u in groups[g]:
                                ui = u["ui"]
                                pb, pbr = psf[ui]; Tm, Tmr = u["Tm"]
                                msk = maskF if u["st"]["z"] == 0 else maskB
                                k.op("dve", lambda e: e.tensor_tensor(out=Tm, in0=pb[:], in1=msk[:], op=ALU.mult), reads=[pbr, cst], writes=[Tmr])
                            KTt, KTr = KTG[bp][g]
                            for j in range(2):
                                pt_, ptr_ = psb[j]
                                k.op("act", lambda e: e.activation(out=KTt[:, 2 * j:2 * j + 2, :], in_=pt_[:, 0:768].rearrange("p (u c) -> p u c", u=2), func=AF.Copy), reads=[ptr_], writes=[KTr])
                            for u in groups[g]:
                                ui = u["ui"]; Tm, Tmr = u["Tm"]
                                k.op("dve", lambda e: e.tensor_tensor(out=IA[:, 2, ui * 128:(ui + 1) * 128], in0=Tm[:, 192:320], in1=ident[:], op=ALU.add), reads=[Tmr, cst], writes=[IAr])
                        return f
                    for g in range(2):
                        macro.append([t_pe(g), t_ev(g)])

                    def lvl_pe(l, g):
                        def f():
                            (bP, bPr), (bQ, bQr), (bX, bXr) = lvl_banks[g]
                            for u in groups[g]:
                                ui = u["ui"]; cs_ = slice(ui * 128, (ui + 1) * 128)
                                if l == 1:
                                    Tm, Tmr = u["Tm"]
                                    P, Q, rd = Tm[:, 192:320], Tm[:, 384:512], [Tmr]
                                    mm(bP[:, cs_], Q, P, True, True, rd, [bPr])
                                    mm(bQ[:, cs_], P, Q, True, True, rd, [bQr])
                                else:
                                    src, srcr = IVG[g][l % 2]
                                    P, Q, X = src[:, 0, cs_], src[:, 1, cs_], src[:, 2, cs_]
                                    if l <= 4:
                                        mm(bP[:, cs_], Q, P, True, True, [srcr], [bPr])
                                    if l <= 5:
                                        mm(bQ[:, cs_], P, Q, True, True, [srcr], [bQr])
                                    mm(bX[:, cs_], ident[:], X, True, False, [srcr, cst], [bXr])
                                    mm(bX[:, cs_], Q, X, False, True, [srcr], [bXr])
                        return f

                    def lvl_ev(l, g):
                        def f():
                            (bP, bPr), (bQ, bQr), (bX, bXr) = lvl_banks[g]
                            bp = bi % 2
                            jobs = []
                            if l == 1:
                                dst, dstr = IVG[g][0]
                                jobs = [(dst[:, 0, :], bP, bPr, dstr), (dst[:, 1, :], bQ, bQr, dstr)]
                            elif l <= 5:
                                dst, dstr = IVG[g][(l + 1) % 2]
                                if l <= 4:
                                    jobs.append((dst[:, 0, :], bP, bPr, dstr))
                                jobs.append((dst[:, 1, :], bQ, bQr, dstr))
                                jobs.append((dst[:, 2, :], bX, bXr, dstr))
                            else:
                                mv, mvr = MVG[bp][g]
                                jobs = [(mv[:].rearrange("p u c -> p (u c)"), bX, bXr, mvr)]
                            for i, (o, bk, bkr, dr) in enumerate(jobs):
                                if (i + l + g) % 2 == 0:
                                    k.op("act", lambda e: e.activation(out=o, in_=bk[:, 0:512], func=AF.Copy), reads=[bkr], writes=[dr])
                                else:
                                    k.op("dve", lambda e: e.tensor_copy(o, bk[:, 0:512]), reads=[bkr], writes=[dr])
                        return f
                    for l in range(1, 7):
                        macro.append([lvl_pe(l, 0), lvl_pe(l, 1), lvl_ev(l, 0), lvl_ev(l, 1)])
                    return macro

                def chain_stages(bi, g):
                    ctx = [unit_ctx(st, bi, g) for st in streams]

                    def w_pe():
                        for u in ctx:
                            st = u["st"]; Tm, Tmr = u["Tm"]; KT, KTr = u["KT"]
                            S, Sr = st["S"][st["si"] % 2]
                            ui = u["ui"]
                            pb = psf[4 + ui // 2][0][:, (ui % 2) * 192:(ui % 2) * 192 + 192]; pbr = chain_res[ui // 2]; u["pC"] = (pb, pbr)
                            mm(pb[:, 0:128], Tm[:, 0:128], KT[:, 256:384], True, False, [Tmr, KTr], [pbr])
                            mm(pb[:, 0:128], u["AR"][:, 0:128], S[:], False, True, [u["rA"], Sr], [pbr])

                    def w_ev():
                        for u in ctx:
                            pb, pbr = u["pC"]; WT, WTr = u["st"]["WT"]
                            k.op("act", lambda e: e.activation(out=WT[:], in_=pb[:, 0:128], func=AF.Copy), reads=[pbr], writes=[WTr])

                    def u_pe():
                        for u in ctx:
                            pb, pbr = u["pC"]; WT, WTr = u["st"]["WT"]; Mi, Mir = u["Minv"]
                            mm(pb[:, 0:128], Mi, WT[:], True, True, [Mir, WTr], [pbr])

                    def u_ev():
                        for u in ctx:
                            pb, pbr = u["pC"]; UT, UTr = u["st"]["UT"]
                            k.op("dve", lambda e: e.tensor_copy(UT[:], pb[:, 0:128]), reads=[pbr], writes=[UTr])

                    def ys_pe():
                        for u in ctx:
                            st = u["st"]; Tm, Tmr = u["Tm"]; KT, KTr = u["KT"]; UT, UTr = st["UT"]
                            S, Sr = st["S"][st["si"] % 2]
                            pb, pbr = u["pC"]; rA = u["rA"]
                            mm(pb[:, 128:192], KT[:, 256:384], Tm[:, 128:192], True, False, [KTr, Tmr], [pbr])
                            mm(pb[:, 128:192], S[:], u["AR"][:, 128:192], False, False, [Sr, rA], [pbr])
                            mm(pb[:, 128:192], UT[:], Tm[:, 320:384], False, True, [UTr, Tmr], [pbr])
                            mm(pb[:, 0:128], KT[:, 0:128], KT[:, 256:384], True, False, [KTr], [pbr])
                            mm(pb[:, 0:128], ident[:], S[:], False, False, [cst, Sr], [pbr])
                            mm(pb[:, 0:128], KT[:, 128:256], UT[:], False, True, [KTr, UTr], [pbr])

                    def ys_ev():
                        for u in ctx:
                            st = u["st"]; pb, pbr = u["pC"]; c = u["c"]; ci = u["ci"]
                            Sn, Snr = st["S"][(st["si"] + 1) % 2]
                            eg = u["eg"]
                            col = ci * CH + (CH - 1 if st["z"] == 0 else 0)
                            k.op("act", lambda e: e.activation(out=Sn[:], in_=pb[:, 0:128], func=AF.Identity, scale=eg[:, col:col + 1]), reads=[pbr, u["rA"]], writes=[Snr])
                            st["si"] += 1
                            yr = yacc_res[st["hl"]][c]
                            ydst = yacc[:, st["hl"], c * CH:(c + 1) * CH]
                            if yr.w is None:
                                k.op("dve", lambda e: e.tensor_copy(ydst, pb[:, 128:192]), reads=[pbr], writes=[yr])
                            else:
                                k.op("dve", lambda e: e.tensor_tensor(out=ydst, in0=ydst, in1=pb[:, 128:192], op=ALU.add), reads=[pbr, yr], writes=[yr])
                    return [[w_pe, w_ev], [u_pe, u_ev], [ys_pe, ys_ev]]

                def replay(chunk):
                    for (e_, fn_, rd_, wr_) in chunk:
                        k.op(e_, fn_, rd_, wr_)

                tmpB = {n_: (carve.alloc([128, BLK], F32), Res(n_ + "B")) for n_ in tmp}
                ksqB = (carve.alloc([128, BLK], BF16), Res("ksqB"))

                def record_prep(bi_):
                    per_stream = []
                    for si_, st in enumerate(streams):
                        k.defer = []
                        if si_ % 2 == 0:
                            prep(st, bi_, tmp, ksq, 0)
                        else:
                            prep(st, bi_, tmpB, ksqB, 256)
                        rec = k.defer; k.defer = None
                        chunks = [[]]
                        for it in rec:
                            if it is None:
                                chunks.append([])
                            else:
                                chunks[-1].append(it)
                        per_stream.append(chunks)
                    out = []
                    for p_ in range(0, len(streams), 2):
                        ca, cb = per_stream[p_], per_stream[p_ + 1]
                        for j_ in range(max(len(ca), len(cb))):
                            a_ = ca[j_] if j_ < len(ca) else []
                            b_ = cb[j_] if j_ < len(cb) else []
                            m_ = []
                            for i_ in range(max(len(a_), len(b_))):
                                if i_ < len(a_):
                                    m_.append(a_[i_])
                                if i_ < len(b_):
                                    m_.append(b_[i_])
                            if m_:
                                out.append(m_)
                    return out
                nblk = (scan_steps + 1) // 2 if scan_steps else NBK
                for bi_ in range(min(2, nblk)):
                    for c_ in record_prep(bi_):
                        replay(c_)
                for ms in pre_block(0):
                    for f in ms:
                        f()
                for bi in range(nblk):
                    A = pre_block(bi + 1) if bi + 1 < nblk else []
                    B = [f for pr in (chain_stages(bi, 0) + chain_stages(bi, 1)) for f in pr]
                    C = record_prep(bi + 2) if bi + 2 < nblk else []
                    Af = []
                    for ms in A:
                        flags = [False, True] if len(ms) == 2 else [True, False, False, True]
                        Af += list(zip(ms, flags))
                    nb_per = max(1, (len(B) + max(1, len(Af)) - 1) // max(1, len(Af)))
                    while Af or B or C:
                        safe = True
                        if Af:
                            f, safe = Af.pop(0)
                            f()
                        for _ in range(nb_per if Af else len(B)):
                            if B:
                                B.pop(0)()
                        if C and (safe or not any(op_[0] == "pe" for op_ in C[0])):
                            replay(C.pop(0))
                        if not Af and not B:
                            while C:
                                replay(C.pop(0))
                dbg("yacc%d" % rnd, yacc[:, 0, :], yacc_res[0], [128, SEQ])
                fo = k.op
                fmm = mm
                fctr = [0]

                def FPS():
                    it = psf[fctr[0] % 4]; fctr[0] += 1
                    return it
                def fin_block(hl, hp, b, tmp, ksq, yb, ysqb, bankA, bankB):
                    hc = slice(hp * 128, (hp + 1) * 128)
                    t0 = b * BLK; ts = slice(t0, t0 + BLK); tb = t0 // 512
                    y = yacc[:, hl, ts]; yres = yacc_res[hl][2 * b:2 * b + 2]
                    r_ap, k_ap, v_ap = rkv[:, hp, ts], rkv[:, 4 + hp, ts], rkv[:, 8 + hp, ts]
                    rr, kr_, vr_ = [rkv_res[hp][tb]], [rkv_res[4 + hp][tb]], [rkv_res[8 + hp][tb]]
                    (ysq, ysqr), (mean, meanr), (msq, msqr), (var, varr) = tmp["sw"], tmp["sa"], tmp["cs"], tmp["pin"]
                    (rs, rsr), (yn, ynr), (s0, s0r), (s1, s1r), (bon, bonr) = tmp["pex"], tmp["ege"], tmp["egi"], tmp["kr"], tmp["rn"]
                    fo("dve", lambda e: e.tensor_tensor(out=ysqb[0][:], in0=y, in1=y, op=ALU.mult), reads=yres, writes=[ysqb[1]])
                    fo("act", lambda e: e.activation(out=yb[0][:], in_=y, func=AF.Copy), reads=yres, writes=[yb[1]])
                    pa, par = bankA
                    fmm(pa[:, 0:BLK], onesbd[:], yb[0][:], True, True, [cst, yb[1]], [par])
                    fmm(pa[:, BLK:2 * BLK], onesbd[:], ysqb[0][:], True, True, [cst, ysqb[1]], [par])
                    fo("act", lambda e: e.activation(out=mean[:], in_=pa[:, 0:BLK], func=AF.Copy, scale=1.0 / 64.0), reads=[par], writes=[meanr])
                    fo("dve", lambda e: e.tensor_tensor(out=msq[:], in0=mean[:], in1=mean[:], op=ALU.mult), reads=[meanr], writes=[msqr])
                    fo("dve", lambda e: e.scalar_tensor_tensor(out=var[:], in0=pa[:, BLK:2 * BLK], scalar=1.0 / 64.0, in1=msq[:], op0=ALU.mult, op1=ALU.subtract), reads=[par, msqr], writes=[varr])
                    fo("act", lambda e: e.activation(out=rs[:], in_=var[:], func=AF.Ln, bias=epsgn[:, 0:1], scale=1.0), reads=[varr, cst], writes=[rsr])
                    fo("act", lambda e: e.activation(out=rs[:], in_=rs[:], func=AF.Exp, scale=-0.5), reads=[rsr], writes=[rsr])
                    fo("dve", lambda e: e.tensor_tensor(out=yn[:], in0=y, in1=mean[:], op=ALU.subtract), reads=yres + [meanr], writes=[ynr])
                    fo("dve", lambda e: e.tensor_tensor(out=yn[:], in0=yn[:], in1=rs[:], op=ALU.mult), reads=[ynr, rsr], writes=[ynr])
                    fo("dve", lambda e: e.tensor_scalar(yn[:], yn[:], GNG(hp), GNB(hp), ALU.mult, ALU.add), reads=[ynr, cst], writes=[ynr])
                    pb_, pbr_ = bankA
                    fmm(pb_[:, 0:BLK], a2b[0:64, hc], wag[0:64, 1, ts], True, True, [cst, rkv_res[13][tb]], [pbr_])
                    pb2_, pbr2_ = bankB
                    fmm(pb2_[:, 0:BLK], a2b[64:128, hc], wag[64:128, 1, ts], True, True, [cst, rkv_res[13][tb]], [pbr2_])
                    act_sigmoid(fo, s0[:], pb_[:, 0:BLK], na0T[:, hp:hp + 1], [pbr_], [s0r])
                    act_sigmoid(fo, s1[:], pb2_[:, 0:BLK], na0T[:, 4 + hp:5 + hp], [pbr2_], [s1r])
                    fo("dve", lambda e: e.tensor_tensor(out=s0[:], in0=s0[:], in1=s1[:], op=ALU.add), reads=[s0r, s1r], writes=[s0r])
                    fo("dve", lambda e: e.tensor_scalar(s0[:], s0[:], KA(hp), ppx[:, 4 + hp:5 + hp], ALU.mult, ALU.add), reads=[s0r, cst], writes=[s0r])
                    fo("dve", lambda e: e.tensor_tensor(out=s0[:], in0=s0[:], in1=k_ap, op=ALU.mult), reads=[s0r] + kr_, writes=[s0r])
                    fo("dve", lambda e: e.scalar_tensor_tensor(out=ksq[0][:], in0=r_ap, scalar=RK(hp), in1=s0[:], op0=ALU.mult, op1=ALU.mult), reads=rr + [s0r, cst], writes=[ksq[1]])
                    pc_, pcr_ = bankA
                    fmm(pc_[:, 0:BLK], onesbd[:], ksq[0][:], True, True, [cst, ksq[1]], [pcr_])
                    fmm(pc_[:, BLK:2 * BLK], g2b[:, hc], wag[:, 2, ts], True, True, [cst, rkv_res[14][tb]], [pcr_])
                    fo("dve", lambda e: e.tensor_tensor(out=bon[:], in0=pc_[:, 0:BLK], in1=v_ap, op=ALU.mult), reads=[pcr_] + vr_, writes=[bonr])
                    fo("dve", lambda e: e.tensor_tensor(out=yn[:], in0=yn[:], in1=bon[:], op=ALU.add), reads=[ynr, bonr], writes=[ynr])
                    fo("dve", lambda e: e.tensor_tensor(out=yaT[:, hp, ts], in0=yn[:], in1=pc_[:, BLK:2 * BLK], op=ALU.mult), reads=[ynr, pcr_], writes=[yaT_res[hp][tb]])
                barrier()
                cvf = Carver(64, 128)
                tmp2 = {n_: (cvf.alloc([128, BLK], F32), Res(n_ + "2")) for n_ in tmp}
                ksq2 = (cvf.alloc([128, BLK], BF16), Res("ksq2")); yb2 = (cvf.alloc([128, BLK], BF16), Res("yb2")); ysqb2 = (cvf.alloc([128, BLK], BF16), Res("ysqb2"))
                for b in range(NBLK):
                    recs = []
                    for hl, hp in enumerate(hps):
                        k.defer = []
                        if hl == 0:
                            fin_block(hl, hp, b, tmp, ksq, yb, ysqb, psf[0], psf[1])
                        else:
                            fin_block(hl, hp, b, tmp2, ksq2, yb2, ysqb2, psf[2], psf[3])
                        recs.append([it for it in k.defer if it is not None]); k.defer = None
                    for i_ in range(max(len(r_) for r_ in recs)):
                        for r_ in recs:
                            if i_ < len(r_):
                                k.op(*r_[i_])
                return yacc, yacc_res

        for rnd in range(2):
            yacc, yacc_res = scan_round(rnd)
            if stop == "C0":
                dbg("yaT0", yaT[:, 0, :], yaT_res[0], [128, SEQ])
                finish()
                return nc, dbg_outs
            barrier()
        dbg("yaT0", yaT[:, 0, :], yaT_res[0], [128, SEQ])
        dbg("yaT3", yaT[:, 3, :], yaT_res[3], [128, SEQ])
        if stop == "C":
            finish()
            return nc, dbg_outs

        barrier()
        h0T_res = [Res("h0T%d" % t) for t in range(NT)]
        phase_h0T(h0T, h0T_res, paired=True)
        ybT = view(16, 4 * SEQ * 2, BF16, "p (c t) -> p c t", c=4)
        ybT_res = [Res("ybT%d" % t) for t in range(NT)]
        with ExitStack() as ph:
            Wu = view(96, 8 * 512 * 2, BF16, "p (k c) -> p k c", k=8); Wv = view(104, 8 * 512 * 2, BF16, "p (k c) -> p k c", k=8)
            wr_ = Res("WuWv")
            k.dma("pool", lambda e: e.dma_start(out=Wu, in_=w_in[:, GM0:GM0 + 512].rearrange("(k p) c -> p k c", p=128)), writes=[wr_])
            k.dma("pool", lambda e: e.dma_start(out=Wv, in_=w_in[:, GM0 + 512:GM0 + 1024].rearrange("(k p) c -> p k c", p=128)), writes=[wr_])
            wsT = phase_sb(ph, [128, 8, 128], BF16, "wsT"); bsF = phase_sb(ph, [128, 4, 128], F32, "bsF")
            k.dma("pool", lambda e: e.dma_start(out=wsT[:], in_=wsT_d), writes=[wr_])
            k.dma("sp", lambda e: e.dma_start(out=bsF[:], in_=bsF_d), writes=[wr_])
            gg, ggr = load_bc(ph, gln_g_d, 512, "gg"); gb, gbr = load_bc(ph, gln_b_d, 512, "gb")
            uTs = [(view(112 + 4 * i, 4 * 512 * 2, BF16, "p (c t) -> p c t", c=4), Res("uT")) for i in range(2)]
            vgs = PRing(ph, 2, [128, 512], F32, "vg")
            vnEs = PRing(ph, 2, [128, 512], BF16, "vnE"); vnOs = PRing(ph, 2, [128, 512], BF16, "vnO")
            svs = PRing(ph, 2, [128, 512], F32, "sv")
            small = PRing(ph, 4, [128, 16], F32, "lnsm")
            for (t_, r_) in vnEs.items + vnOs.items:
                k.op("pool", lambda e: e.memset(t_[:], 0.0), writes=[r_])
            g4 = lambda ap, g: ap.rearrange("p (c g d) -> p c g d", g=2, d=64)[:, :, g, :]
            for tb in range(4):
                ts = slice(tb * 512, (tb + 1) * 512)
                uT, uTr = uTs[tb % 2]
                for cu in range(4):
                    pb, pbr = PSF()
                    for kc in range(8):
                        mm(pb[:], Wu[:, kc, cu * 128:(cu + 1) * 128], h0T[:, kc, ts], kc == 0, kc == 7, [wr_] + h0T_res[4 * tb:4 * tb + 4], [pbr])
                    k.op("act", lambda e: e.activation(out=uT[:, cu, :], in_=pb[:], func=AF.Gelu), reads=[pbr], writes=[uTr])
                def tileD(tt, tb=tb, uT=uT, uTr=uTr):
                    t = 4 * tb + tt
                    tsl = slice(t * 128, (t + 1) * 128)
                    pb, pbr = PSF()
                    for kc in range(8):
                        mm(pb[:], h0T[:, kc, tsl], Wv[:, kc, :], kc == 0, kc == 7, [wr_, h0T_res[t]], [pbr])
                    vg, vgr = vgs.get()
                    k.op("act", lambda e: e.activation(out=vg[:], in_=pb[:], func=AF.Gelu), reads=[pbr], writes=[vgr])
                    layer_norm(vg[:], [vgr], gg[:], gb[:], [ggr, gbr], vg[:], [vgr], small, n=512)
                    vnE, vnEr = vnEs.get(); vnO, vnOr = vnOs.get()
                    k.op("act", lambda e: e.activation(out=g4(vnE[:], 0), in_=g4(vg[:], 0), func=AF.Copy), reads=[vgr], writes=[vnEr])
                    k.op("pool", lambda e: e.tensor_copy(g4(vnO[:], 1), g4(vg[:], 1)), reads=[vgr], writes=[vnOr])
                    ps, psr = PSF()
                    for c in range(4):
                        cs_ = slice(c * 128, (c + 1) * 128)
                        mm(ps[:, cs_], vnE[:, cs_], wsT[:, 2 * c, :], True, False, [vnEr, wr_], [psr])
                        mm(ps[:, cs_], vnO[:, cs_], wsT[:, 2 * c + 1, :], False, True, [vnOr, wr_], [psr])
                    sv, svr = svs.get()
                    k.op("dve", lambda e: e.tensor_tensor(out=sv[:], in0=ps[:], in1=bsF[:].rearrange("p c t -> p (c t)"), op=ALU.add), reads=[psr, wr_], writes=[svr])
                    k.op("dve", lambda e: e.tensor_tensor(out=ybT[:, :, tsl], in0=sv[:].rearrange("p (c t) -> p c t", c=4), in1=uT[:, :, tt * 128:(tt + 1) * 128], op=ALU.mult), reads=[svr, uTr], writes=[ybT_res[t]])
                run_pairs(4, tileD)
            barrier()
        dbg("ybT0", ybT[:, 0, :], ybT_res, [128, SEQ])
        if stop == "D":
            finish()
            return nc, dbg_outs

        mgT = view(128, 8 * SEQ * 2, BF16, "p (c t) -> p c t", c=8)
        mg_res = [Res("mg%d" % tb) for tb in range(4)]
        with ExitStack() as ph:
            wbr = view(96, 8 * 1024 * 2, BF16, "p (k d) -> p k d", k=8); wbr_r = Res("wbr")
            for kc in range(8):
                k.dma("pool", lambda e: e.dma_start(out=wbr[:, kc, :], in_=wbr_d[kc * 128:(kc + 1) * 128, :]), writes=[wbr_r])
            wgs = [(view(112 + 2 * i, 8 * 128 * 2, BF16, "p (k c) -> p k c", k=8), Res("wg")) for i in range(4)]
            sigs = PRing(ph, 3, [128, 512], F32, "sig")
            prs = PRing(ph, 2, [128, 512], F32, "pr")
            wi = 0
            for dc in range(8):
                wg2 = []
                for n in range(2):
                    wg, wgr = wgs[wi % 4]; wi += 1
                    c0 = GT0 + n * 1024 + dc * 128
                    k.dma("pool", lambda e: e.dma_start(out=wg, in_=w_in[:, c0:c0 + 128].rearrange("(k p) c -> p k c", p=128)), writes=[wgr])
                    wg2.append((wg, wgr))
                for tb in range(4):
                    ts = slice(tb * 512, (tb + 1) * 512)
                    hr = h0T_res[4 * tb:4 * tb + 4]
                    acc = None
                    for n in range(2):
                        wg, wgr = wg2[n]
                        pg, pgr = PSF()
                        for kc in range(8):
                            mm(pg[:], wg[:, kc, :], h0T[:, kc, ts], kc == 0, kc == 7, [wgr] + hr, [pgr])
                        sg, sgr = sigs.get()
                        k.op("act", lambda e: e.activation(out=sg[:], in_=pg[:], func=AF.Sigmoid), reads=[pgr], writes=[sgr])
                        pbn, pbnr = PSF()
                        yT, yres = (yaT, [yaT_res[c][tb] for c in range(4)]) if n == 0 else (ybT, ybT_res[4 * tb:4 * tb + 4])
                        for c in range(4):
                            mm(pbn[:], wbr[:, n * 4 + c, dc * 128:(dc + 1) * 128], yT[:, c, ts], c == 0, c == 3, [wbr_r] + yres, [pbnr])
                        if n == 0:
                            acc, accr = prs.get()
                            k.op("dve", lambda e: e.tensor_tensor(out=acc[:], in0=sg[:], in1=pbn[:], op=ALU.mult), reads=[sgr, pbnr], writes=[accr])
                        else:
                            k.op("dve", lambda e: e.tensor_tensor(out=sg[:], in0=sg[:], in1=pbn[:], op=ALU.mult), reads=[sgr, pbnr], writes=[sgr])
                            k.op("pool", lambda e: e.tensor_tensor(out=mgT[:, dc, ts], in0=acc[:], in1=sg[:], op=ALU.add), reads=[sgr, accr], writes=[mg_res[tb]])
            barrier()
        dbg("mgT0", mgT[:, 0, :], mg_res, [128, SEQ])
        if stop == "E":
            finish()
            return nc, dbg_outs

        H = view(64, NT * D * 4, F32, "p (t d) -> p t d", t=NT)
        H_res = [Res("H%d" % t) for t in range(NT)]

        def load_w_fm(dst, d_ap, res, q="pool"):
            for kc in range(8):
                k.dma(q, lambda e: e.dma_start(out=dst[:, kc, :], in_=d_ap[kc * 128:(kc + 1) * 128, :]), writes=[res])

        with ExitStack() as ph:
            wmix = view(0, 8 * 1024 * 2, BF16, "p (k d) -> p k d", k=8); wmix_r = Res("wmix")
            load_w_fm(wmix, wmix_d, wmix_r)
            g0, g0r = load_bc(ph, lng["ln_emb_g"], D, "g0"); b0, b0r = load_bc(ph, lng["ln_emb_b"], D, "b0")
            g1, g1r = load_bc(ph, lng["ln1_g"], D, "g1"); b1, b1r = load_bc(ph, lng["ln1_b"], D, "b1")
            xs = PRing(ph, 2, [128, D], F32, "xs")
            small = PRing(ph, 4, [128, 16], F32, "lnsm")
            def tileF(t):
                tsl = slice(t * 128, (t + 1) * 128)
                xt, xr = xs.get()
                k.dma("sp", lambda e: e.dma_start(out=xt[:], in_=x[tsl, :]), writes=[xr])
                layer_norm(xt[:], [xr], g0[:], b0[:], [g0r, b0r], xt[:], [xr], small)

                def half_(half):
                    hs = slice(half * 512, (half + 1) * 512)
                    pm, pmr = PSF()
                    for kc in range(8):
                        mm(pm[:], mgT[:, kc, tsl], wmix[:, kc, hs], kc == 0, kc == 7, [mg_res[t // 4], wmix_r], [pmr])
                    k.op("dve", lambda e: e.scalar_tensor_tensor(out=xt[:, hs], in0=xt[:, hs], scalar=ALPHA, in1=pm[:], op0=ALU.mult, op1=ALU.add), reads=[xr, pmr], writes=[xr])
                half_(0)
                half_(1)
                layer_norm(xt[:], [xr], g1[:], b1[:], [g1r, b1r], H[:, t, :], [H_res[t]], small)
            run_pairs(NT, tileF)
            barrier()
        dbg("h1_t0", H[:, 0, :], [H_res[0]], [128, D])
        dbg("h1_t9", H[:, 9, :], [H_res[9]], [128, D])
        if stop == "F":
            finish()
            return nc, dbg_outs

        with ExitStack() as ph:
            wq = view(0, 8 * 1024 * 2, BF16, "p (k d) -> p k d", k=8); wo = view(16, 8 * 1024 * 2, BF16, "p (k d) -> p k d", k=8)
            wkK = view(32, 8 * 1024 * 2, BF16, "p (k d) -> p k d", k=8); wkV = view(48, 8 * 1024 * 2, BF16, "p (k d) -> p k d", k=8)
            KT = view(128, 8 * 256 * 2, BF16, "p (c m) -> p c m", c=8); Vm = view(132, 2 * 1024 * 2, BF16, "p (m d) -> p m d", m=2)
            memT = view(136, 8 * 256 * 2, BF16, "p (c m) -> p c m", c=8)
            wres = Res("xw"); kvres = Res("kv"); memr = [Res("memT0"), Res("memT1")]
            load_w_fm(wq, wq_d, wres); load_w_fm(wo, wo_d, wres)
            load_w_fm(wkK, wkv_d[:, 0:D], wres); load_w_fm(wkV, wkv_d[:, D:2 * D], wres)
            small = PRing(ph, 4, [128, 16], F32, "lnsm")
            xs = PRing(ph, 2, [128, D], F32, "xs")
            with ExitStack() as ph2:
                gm, gmr = load_bc(ph2, lng["mem_ln_g"], D, "gm"); bm, bmr = load_bc(ph2, lng["mem_ln_b"], D, "bm")
                hbs0 = PRing(ph2, 2, [128, D], BF16, "hb")
                for mt in range(2):
                    xt, xr = xs.get()
                    k.dma("sp", lambda e: e.dma_start(out=xt[:], in_=mem[mt * 128:(mt + 1) * 128, :]), writes=[xr])
                    layer_norm(xt[:], [xr], gm[:], bm[:], [gmr, bmr], xt[:], [xr], small)
                    hb, hbr = hbs0.get()
                    k.op("act", lambda e: e.activation(out=hb[:], in_=xt[:], func=AF.Copy), reads=[xr], writes=[hbr])
                    to_fm(hb, hbr, memT, [memr[mt]], mt)
                barrier()
            g2_, g2r = load_bc(ph, lng["ln2_g"], D, "g2"); b2_, b2r = load_bc(ph, lng["ln2_b"], D, "b2")
            for c in range(8):
                pk, pkr = PSF()
                for kc in range(8):
                    mm(pk[:, 0:256], wkK[:, kc, c * 128:(c + 1) * 128], memT[:, kc, :], kc == 0, kc == 7, [wres] + memr, [pkr])
                k.op("act", lambda e: e.activation(out=KT[:, c, :], in_=pk[:, 0:256], func=AF.Copy), reads=[pkr], writes=[kvres])
            for mt in range(2):
                for half in range(2):
                    hs = slice(half * 512, (half + 1) * 512)
                    pv, pvr = PSF()
                    for kc in range(8):
                        mm(pv[:], memT[:, kc, mt * 128:(mt + 1) * 128], wkV[:, kc, hs], kc == 0, kc == 7, [wres] + memr, [pvr])
                    k.op("act", lambda e: e.activation(out=Vm[:, mt, hs], in_=pv[:], func=AF.Copy), reads=[pvr], writes=[kvres])
            barrier()
            cv = Carver(32, 64)

            class CRing:
                def __init__(self, n, shape, dt, name):
                    self.items = [(cv.alloc(shape, dt), Res(name)) for _ in range(n)]
                    self.i = 0

                def get(self):
                    it = self.items[self.i % len(self.items)]; self.i += 1
                    return it
            hbs = CRing(2, [128, D], BF16, "hb")
            hTs = CRing(2, [128, 8, 128], BF16, "hT")
            qTs = CRing(2, [128, 8, 128], BF16, "qT")
            pexs = CRing(2, [128, 4, 256], BF16, "pex")
            pTs = CRing(2, [128, 8, 128], BF16, "pT")
            oTs = CRing(2, [128, 8, 128], BF16, "oT")
            sm2 = PRing(ph, 2, [128, 16], F32, "sm2")
            SCL = 256.0 ** -0.5
            tab_res = Res("tables")
            stg = [(view(140 + 8 * i, 4 * D * 2, BF16, "p (a f) -> p a f", a=4), Res("stg%d" % i)) for i in range(2)]
            conv_jobs = [(src, dst, ch) for (src, dst) in ((pu_d, Ub_d), (pv_d, Vb_d)) for ch in range(32)]

            def conv_some(n_):
                for _ in range(n_):
                    if not conv_jobs:
                        return
                    src, dst, ch = conv_jobs.pop(0)
                    st_, str_ = stg[ch % 2]
                    k.dma("pool", lambda e: e.dma_start(out=st_, in_=src[ch * 4:(ch + 1) * 4].rearrange("a p f -> p a f")), writes=[str_])
                    k.dma("sp", lambda e: e.dma_start(out=dst[ch * 4:(ch + 1) * 4].rearrange("a p f -> p a f"), in_=st_), reads=[str_], writes=[tab_res])
            for t in range(NT):
                conv_some(4)
                h1 = H[:, t, :]; h1r = H_res[t]
                hb, hbr = hbs.get()
                k.op("act", lambda e: e.activation(out=hb[:], in_=h1, func=AF.Copy), reads=[h1r], writes=[hbr])
                hT, hTr = hTs.get()
                to_fm(hb, hbr, hT, [hTr], 0)
                qT, qTr = qTs.get()
                for g in range(2):
                    pq, pqr = PSF()
                    for cc in range(4):
                        c = g * 4 + cc
                        for kc in range(8):
                            mm(pq[:, cc * 128:(cc + 1) * 128], wq[:, kc, c * 128:(c + 1) * 128], hT[:, kc, :], kc == 0, kc == 7, [wres, hTr], [pqr])
                    k.op("act", lambda e: e.activation(out=qT[:, g * 4:(g + 1) * 4, :], in_=pq[:].rearrange("p (c t) -> p c t", c=4), func=AF.Copy), reads=[pqr], writes=[qTr])
                sm, smr = sm2.get()
                pex, pexr = pexs.get()
                pss = []
                for g in range(2):
                    ps_, psr_ = PSF(); pss.append((ps_, psr_))
                    for hh in range(2):
                        h = g * 2 + hh
                        for j in range(2):
                            mm(ps_[:, hh * 256:(hh + 1) * 256], qT[:, 2 * h + j, :], KT[:, 2 * h + j, :], j == 0, j == 1, [qTr, kvres], [psr_])
                    k.op("dve", lambda e: e.tensor_reduce(out=sm[:, g * 2:(g + 1) * 2], in_=ps_[:].rearrange("p (h m) -> p h m", h=2), axis=AX.X, op=ALU.max), reads=[psr_], writes=[smr])
                k.op("dve", lambda e: e.tensor_scalar(sm[:, 4:8], sm[:, 0:4], -SCL, None, ALU.mult), reads=[smr], writes=[smr])
                for h in range(4):
                    ps_, psr_ = pss[h // 2]
                    k.op("act", lambda e: e.activation(out=pex[:, h, :], in_=ps_[:, (h % 2) * 256:(h % 2 + 1) * 256], func=AF.Exp, bias=sm[:, 4 + h:5 + h], scale=SCL, accum_out=sm[:, 8 + h:9 + h]), reads=[psr_, smr], writes=[pexr, smr])
                k.op("dve", lambda e: e.reciprocal(out=sm[:, 12:16], in_=sm[:, 8:12]), reads=[smr], writes=[smr])
                k.op("dve", lambda e: e.tensor_tensor(out=pex[:], in0=pex[:], in1=sm[:, 12:16].unsqueeze(2).to_broadcast([128, 4, 256]), op=ALU.mult), reads=[pexr, smr], writes=[pexr])
                pt, ptr_ = PSB()
                for h in range(4):
                    for mt in range(2):
                        i = h * 2 + mt
                        k.op("pe", lambda e: e.transpose(pt[:, i * 128:(i + 1) * 128], pex[:, h, mt * 128:(mt + 1) * 128], ident[:]), reads=[pexr, cst], writes=[ptr_])
                pT, pTr = pTs.get()
                k.op("act", lambda e: e.activation(out=pT[:], in_=pt[:].rearrange("p (c t) -> p c t", t=128), func=AF.Copy), reads=[ptr_], writes=[pTr])
                oT, oTr = oTs.get()
                for g in range(2):
                    po, por = PSF()
                    for cc in range(4):
                        c = g * 4 + cc; h = c // 2
                        for mt in range(2):
                            mm(po[:, cc * 128:(cc + 1) * 128], Vm[:, mt, c * 128:(c + 1) * 128], pT[:, h * 2 + mt, :], mt == 0, mt == 1, [kvres, pTr], [por])
                    k.op("dve", lambda e: e.tensor_copy(oT[:, g * 4:(g + 1) * 4, :], po[:].rearrange("p (c t) -> p c t", c=4)), reads=[por], writes=[oTr])
                xt, xr = xs.get()
                for half in range(2):
                    hs = slice(half * 512, (half + 1) * 512)
                    px, pxr = PSF()
                    for c in range(8):
                        mm(px[:], oT[:, c, :], wo[:, c, hs], c == 0, c == 7, [oTr, wres], [pxr])
                    k.op("dve", lambda e: e.scalar_tensor_tensor(out=xt[:, hs], in0=h1[:, hs], scalar=ALPHA, in1=px[:], op0=ALU.mult, op1=ALU.add), reads=[h1r, pxr], writes=[xr])
                layer_norm(xt[:], [xr], g2_[:], b2_[:], [g2r, b2r], H[:, t, :], [H_res[t]], small)
            barrier()
        dbg("h2_t0", H[:, 0, :], [H_res[0]], [128, D])
        dbg("h2_t9", H[:, 9, :], [H_res[9]], [128, D])
        if stop == "H":
            finish()
            return nc, dbg_outs

        with ExitStack() as pho:
            SI1 = phase_sb(pho, [128, NT, 128], F32, "SI1"); SI2 = phase_sb(pho, [128, NT, 128], F32, "SI2"); SG = phase_sb(pho, [128, NT, 128], F32, "SG")
            slot_res = [Res("slot%d" % t) for t in range(NT)]
            with ExitStack() as ph:
                pwq = view(0, 8 * 2048 * 2, BF16, "p (k d) -> p k d", k=8); pw_r = Res("pwq")
                load_w_fm(pwq, pwq_d, pw_r)
                skT = phase_sb(ph, [128, 2, 128], BF16, "skT")
                k.dma("pool", lambda e: e.dma_start(out=skT[:], in_=skT_d), writes=[pw_r])
                sc = view(48, 16 * 128 * 4, F32, "p (c k) -> p c k", c=16); scr = Res("sc")
                cv = Carver(128, 160)

                def CT(shape, dt, name, c=cv):
                    return (c.alloc(shape, dt), Res(name))
                hb, hbr = CT([128, D], BF16, "hb"); hT, hTr = CT([128, 8, 128], BF16, "hT")
                pqT, pqTr = CT([128, 16, 128], BF16, "pqT")
                sc2, sc2r = CT([128, 256], F32, "sc2"); sc2b, sc2br = CT([128, 256], F32, "sc2b")
                ts_c = [Res("ts%d" % c_) for c_ in range(16)]; ti_c = [Res("ti%d" % c_) for c_ in range(16)]
                bs_h = [Res("bs%d" % h_) for h_ in range(8)]; bp_h = [Res("bp%d" % h_) for h_ in range(8)]
                top_s, tsr = CT([128, 256], F32, "top_s"); top_i, tir = CT([128, 256], U32, "top_i"); top_f, tfr = CT([128, 256], F32, "top_f")
                cand, cdr = CT([128, 2048], F32, "cand")
                best_s, bsr = CT([128, 128], F32, "best_s"); best_p, bpr = CT([128, 128], U32, "best_p")
                pf, pfr = CT([128, 128], F32, "pf"); k1f, k1r = CT([128, 128], F32, "k1f"); k2f, k2r = CT([128, 128], F32, "k2f")
                gsum, gsr = CT([128, 16], F32, "gsum")
                eq = cand
                v4 = lambda ap: ap.rearrange("p (h z k) -> p h z k", h=8, z=2)
                v3k = lambda ap: ap.rearrange("p (h k) -> p h k", h=8)
                c4 = lambda ap: ap.rearrange("p (h a b) -> p h a b", h=8, a=16)
                sc_b = [(sc, scr), (view(32, 16 * 128 * 4, F32, "p (c k) -> p c k", c=16), Res("scB"))]
                hb_b = [(hb, hbr), (view(40, D * 2, BF16), Res("hbB"))]
                hT_b = [(hT, hTr), (view(42, 8 * 128 * 2, BF16, "p (c t) -> p c t", c=8), Res("hTB"))]
                pq_b = [(pqT, pqTr), (view(44, 16 * 128 * 2, BF16, "p (c t) -> p c t", c=16), Res("pqTB"))]

                def head(t):
                    hb, hbr = hb_b[t % 2]; hT, hTr = hT_b[t % 2]; pqT, pqTr = pq_b[t % 2]; sc, scr = sc_b[t % 2]
                    h2 = H[:, t, :]; h2r = H_res[t]
                    k.op("act", lambda e: e.activation(out=hb, in_=h2, func=AF.Copy), reads=[h2r], writes=[hbr])
                    to_fm(hb, hbr, hT, [hTr], 0)
                    for g in range(4):
                        pq, pqr = PSF()
                        for cc in range(4):
                            c = g * 4 + cc
                            for kc in range(8):
                                mm(pq[:, cc * 128:(cc + 1) * 128], pwq[:, kc, c * 128:(c + 1) * 128], hT[:, kc, :], kc == 0, kc == 7, [pw_r, hTr], [pqr])
                        k.op("act", lambda e: e.activation(out=pqT[:, g * 4:(g + 1) * 4, :], in_=pq[:].rearrange("p (c t) -> p c t", c=4), func=AF.Copy), reads=[pqr], writes=[pqTr])
                    for g in range(4):
                        ps_, psr_ = PSF()
                        for cc in range(4):
                            c = g * 4 + cc
                            mm(ps_[:, cc * 128:(cc + 1) * 128], pqT[:, c, :], skT[:, c % 2, :], True, True, [pqTr, pw_r], [psr_])
                        k.op("act", lambda e: e.activation(out=sc[:, g * 4:(g + 1) * 4, :], in_=ps_[:].rearrange("p (c k) -> p c k", c=4), func=AF.Copy), reads=[psr_], writes=[scr])

                def tail(t):
                    sc, scr = sc_b[t % 2]
                    def lvl1_ops(c, buf, bufr):
                        lo = slice(c * 16, c * 16 + 8); hi = slice(c * 16 + 8, c * 16 + 16)
                        tr_, ir_ = ts_c[c], ti_c[c]
                        return [
                            lambda: k.op("dve", lambda e: e.max(out=top_s[:, lo], in_=sc[:, c, :]), reads=[scr], writes=[tr_]),
                            lambda: k.op("dve", lambda e: e.max_index(out=top_i[:, lo], in_max=top_s[:, lo], in_values=sc[:, c, :]), reads=[scr, tr_], writes=[ir_]),
                            lambda: k.op("dve", lambda e: e.match_replace(out=buf[:, 0:128], in_to_replace=top_s[:, lo], in_values=sc[:, c, :], imm_value=-1e30), reads=[scr, tr_], writes=[bufr]),
                            lambda: k.op("dve", lambda e: e.max(out=top_s[:, hi], in_=buf[:, 0:128]), reads=[bufr], writes=[tr_]),
                            lambda: k.op("dve", lambda e: e.max_index(out=top_i[:, hi], in_max=top_s[:, hi], in_values=buf[:, 0:128]), reads=[bufr, tr_], writes=[ir_]),
                        ]
                    for c in range(0, 16, 2):
                        oa_ = lvl1_ops(c, sc2, sc2r); ob_ = lvl1_ops(c + 1, sc2b, sc2br)
                        for fa_, fb_ in zip(oa_, ob_):
                            fa_(); fb_()
                    k.op("dve", lambda e: e.tensor_copy(top_f, top_i), reads=ti_c, writes=[tfr])
                    k.op("dve", lambda e: e.tensor_tensor(out=c4(cand), in0=v4(top_s)[:, :, 0, :].unsqueeze(3).to_broadcast([128, 8, 16, 16]),
                                                          in1=v4(top_s)[:, :, 1, :].unsqueeze(2).to_broadcast([128, 8, 16, 16]), op=ALU.add), reads=ts_c, writes=[cdr])
                    candh = cand.rearrange("p (h c) -> p h c", h=8)
                    def lvl2_ops(h, buf, bufr):
                        lo = slice(h * 16, h * 16 + 8); hi = slice(h * 16 + 8, h * 16 + 16)
                        br_, pr_ = bs_h[h], bp_h[h]
                        return [
                            lambda: k.op("dve", lambda e: e.max(out=best_s[:, lo], in_=candh[:, h, :]), reads=[cdr], writes=[br_]),
                            lambda: k.op("dve", lambda e: e.max_index(out=best_p[:, lo], in_max=best_s[:, lo], in_values=candh[:, h, :]), reads=[cdr, br_], writes=[pr_]),
                            lambda: k.op("dve", lambda e: e.match_replace(out=buf, in_to_replace=best_s[:, lo], in_values=candh[:, h, :], imm_value=-1e30), reads=[cdr, br_], writes=[bufr]),
                            lambda: k.op("dve", lambda e: e.max(out=best_s[:, hi], in_=buf), reads=[bufr], writes=[br_]),
                            lambda: k.op("dve", lambda e: e.max_index(out=best_p[:, hi], in_max=best_s[:, hi], in_values=buf), reads=[bufr, br_], writes=[pr_]),
                        ]
                    for h in range(0, 8, 2):
                        oa_ = lvl2_ops(h, sc2, sc2r); ob_ = lvl2_ops(h + 1, sc2b, sc2br)
                        for fa_, fb_ in zip(oa_, ob_):
                            fa_(); fb_()
                    pfu = pf.bitcast(U32)
                    k.op("dve", lambda e: e.tensor_single_scalar(out=pfu, in_=best_p, scalar=4, op=ALU.logical_shift_right), reads=bp_h, writes=[pfr])
                    k.op("dve", lambda e: e.tensor_copy(k1f, pfu), reads=[pfr], writes=[k1r])
                    k.op("dve", lambda e: e.tensor_single_scalar(out=pfu, in_=best_p, scalar=15, op=ALU.bitwise_and), reads=bp_h + [k1r, pfr], writes=[pfr])
                    k.op("dve", lambda e: e.tensor_copy(k2f, pfu), reads=[pfr], writes=[k2r])
                    io4 = iota16[:, :].unsqueeze(1).unsqueeze(1).to_broadcast([128, 8, 16, 16])
                    sr = slot_res[t]
                    for (kf, kr_, z, dst) in ((k1f, k1r, 0, SI1[:, t, :]), (k2f, k2r, 1, SI2[:, t, :])):
                        k.op("dve", lambda e: e.tensor_tensor(out=c4(eq), in0=io4, in1=v3k(kf).unsqueeze(3).to_broadcast([128, 8, 16, 16]), op=ALU.is_equal), reads=[cst, kr_, cdr], writes=[cdr])
                        k.op("dve", lambda e: e.tensor_tensor(out=c4(eq), in0=c4(eq), in1=v4(top_f)[:, :, z, :].unsqueeze(2).to_broadcast([128, 8, 16, 16]), op=ALU.mult), reads=[cdr, tfr], writes=[cdr])
                        k.op("dve", lambda e: e.tensor_reduce(out=dst, in_=eq.rearrange("p (a b) -> p a b", b=16), axis=AX.X, op=ALU.add), reads=[cdr], writes=[sr])
                    gate = SG[:, t, :]
                    k.op("dve", lambda e: e.tensor_tensor(out=v3k(gate), in0=v3k(best_s), in1=v3k(best_s)[:, :, 0:1].to_broadcast([128, 8, 16]), op=ALU.subtract), reads=bs_h, writes=[sr])
                    k.op("act", lambda e: e.activation(out=gate, in_=gate, func=AF.Exp), reads=[sr], writes=[sr])
                    k.op("dve", lambda e: e.tensor_reduce(out=gsum[:, 0:8], in_=v3k(gate), axis=AX.X, op=ALU.add), reads=[sr], writes=[gsr])
                    k.op("dve", lambda e: e.reciprocal(out=gsum[:, 8:16], in_=gsum[:, 0:8]), reads=[gsr], writes=[gsr])
                    k.op("dve", lambda e: e.tensor_tensor(out=v3k(gate), in0=v3k(gate), in1=gsum[:, 8:16].unsqueeze(2).to_broadcast([128, 8, 16]), op=ALU.mult), reads=[sr, gsr], writes=[sr])
                head(0)
                for t in range(NT):
                    if t + 1 < NT:
                        head(t + 1)
                    tail(t)
                barrier()
            dbg("si1", SI1[:, 0, :], [slot_res[0]], [128, 128]); dbg("sg", SG[:, 0, :], [slot_res[0]], [128, 128])
            if stop == "I":
                finish()
                return nc, dbg_outs

            TBK = 256
            with ExitStack() as ph:
                cv = Carver(128, 160)
                NU = 3
                utiles = [(cv.alloc([128, 8, 128], BF16), Res("ut")) for _ in range(NU)]
                vtiles = [(cv.alloc([128, D], BF16), Res("vt")) for _ in range(NU)]
                hTb = cv.alloc([128, 8, TBK], BF16); hTb_res = [Res("hTb0"), Res("hTb1")]
                accs = [(cv.alloc([128, D], F32), Res("acc")) for _ in range(2)]
                hbJ = (cv.alloc([128, D], BF16), Res("hbJ"))
                g3, g3r = load_bc(ph, lng["ln3_g"], D, "g3"); b3, b3r = load_bc(ph, lng["ln3_b"], D, "b3")
                small = PRing(ph, 4, [128, 16], F32, "lnsm")
                iota128 = phase_sb(ph, [128, 128], F32, "iota128"); ior = Res("iota128")
                k.op("pool", lambda e: e.iota(iota128[:], pattern=[[1, 128]], base=0, channel_multiplier=0, allow_small_or_imprecise_dtypes=True), writes=[ior])
                slTs = [(phase_sb(ph, [128, 3, TBK], BF16, "slT"), Res("slT")) for _ in range(2)]
                oh1s = PRing(ph, 8, [128, 128], BF16, "oh1"); oh2s = PRing(ph, 8, [128, 64], BF16, "oh2")

                class VRing:
                    def __init__(self, n, shape, dt, name):
                        self.items = [(cv.alloc(shape, dt), Res(name)) for _ in range(n)]
                        self.i = 0

                    def get(self):
                        it = self.items[self.i % len(self.items)]; self.i += 1
                        return it
                gels = VRing(2, [128, TBK], F32, "gel"); pbs = VRing(2, [128, TBK], BF16, "pb")
                ui = [0]
                NTB = SEQ // TBK
                Gh = [(view(32 * h_, TBK * 64 * 2, BF16, "p (t i) -> p t i", t=TBK), Res("G%d" % h_)) for h_ in range(2)]
                psbf = [(psb[i][0][:].bitcast(F32), psb[i][1]) for i in range(2)]
                gq = [0]

                def prep_block(tb):
                    slT, slTr = slTs[tb % 2]
                    for tt in range(2):
                        pt_, ptr_ = psbf[tt]
                        for a_, arr in enumerate((SI1, SI2, SG)):
                            k.op("pe", lambda e: e.transpose(pt_[:, a_ * 128:(a_ + 1) * 128], arr[:, tb * 2 + tt, :], identf[:]), reads=[slot_res[tb * 2 + tt], cst], writes=[ptr_])
                        k.op("act", lambda e: e.activation(out=slT[:, :, tt * 128:(tt + 1) * 128], in_=pt_[:, 0:384].rearrange("p (a t) -> p a t", a=3), func=AF.Copy), reads=[ptr_], writes=[slTr])

                def g_build_jobs(tb, half):
                    slT, slTr = slTs[tb % 2]
                    Gt, Gtr = Gh[half]
                    jobs = []
                    state = {}
                    pend_pe = []
                    for tk in range(TBK):
                        def job(tk=tk):
                            j = tk % 8
                            if j == 0:
                                state["pg"] = psbf[gq[0] % 2]; gq[0] += 1
                            pg, pgr = state["pg"]
                            o1, o1r = oh1s.get(); o2, o2r = oh2s.get()
                            k.op("dve", lambda e: e.tensor_scalar(o1[:], iota128[:], slT[:, 0, tk:tk + 1], slT[:, 2, tk:tk + 1], ALU.is_equal, ALU.mult), reads=[ior, slTr], writes=[o1r])
                            k.op("dve", lambda e: e.tensor_scalar(o2[:], iota128[:, half * 64:(half + 1) * 64], slT[:, 1, tk:tk + 1], None, ALU.is_equal), reads=[ior, slTr], writes=[o2r])
                            def pe_part():
                                mm(pg[:, j * 64:(j + 1) * 64], o1[:], o2[:], True, True, [o1r, o2r], [pgr])
                                if j == 7:
                                    tq = tk // 8
                                    k.op("act", lambda e: e.activation(out=Gt[:, tq * 8:tq * 8 + 8, :], in_=pg[:].rearrange("p (t i) -> p t i", t=8), func=AF.Copy), reads=[pgr], writes=[Gtr])
                            pend_pe.append(pe_part)
                            while len(pend_pe) > 6:
                                pend_pe.pop(0)()
                        jobs.append(job)

                    def flush():
                        while pend_pe:
                            pend_pe.pop(0)()
                    jobs.append(flush)
                    return jobs

                def issue_act(tb, i2):
                    ut, utr = utiles[ui[0] % NU]; vt, vtr = vtiles[ui[0] % NU]; ui[0] += 1
                    k.dma("sp", lambda e: e.dma_start(out=ut, in_=Ub_d[i2].rearrange("p (k i) -> p k i", k=8)), writes=[utr])
                    k.dma("sp", lambda e: e.dma_start(out=vt, in_=Vb_d[i2]), writes=[vtr])
                    pa, par = psf[4 + i2 % 2]
                    for kc in range(8):
                        mm(pa[:, 0:TBK], ut[:, kc, :], hTb[:, kc, :], kc == 0, kc == 7, [utr] + hTb_res, [par])
                    gel, gelr = gels.get()
                    k.op("act", lambda e: e.activation(out=gel, in_=pa[:, 0:TBK], func=AF.Gelu), reads=[par], writes=[gelr])
                    pb_, pbr_ = pbs.get()
                    Gt, Gtr = Gh[i2 // 64]
                    k.op("dve", lambda e: e.tensor_tensor(out=pb_, in0=gel, in1=Gt[:, :, i2 % 64], op=ALU.mult), reads=[gelr, Gtr], writes=[pbr_])
                    return (pb_, pbr_, vt, vtr)

                def issue_y(i2, st_):
                    pb_, pbr_, vt, vtr = st_
                    for tt in range(2):
                        for half in range(2):
                            py, pyr = psf[tt * 2 + half]
                            mm(py[:], pb_[:, tt * 128:(tt + 1) * 128], vt[:, half * 512:(half + 1) * 512], i2 == 0, i2 == 127, [pbr_, vtr], [pyr])
                prep_block(0)
                for job in g_build_jobs(0, 0):
                    job()
                for tb in range(NTB):
                    t0 = tb * 2
                    for tt in range(2):
                        hb, hbr = hbJ
                        k.op("act", lambda e: e.activation(out=hb, in_=H[:, t0 + tt, :], func=AF.Copy), reads=[H_res[t0 + tt]], writes=[hbr])
                        to_fm(hb, hbr, hTb, [hTb_res[tt]], tt)
                    if tb + 1 < NTB:
                        prep_block(tb + 1)
                    jobs_lo = g_build_jobs(tb, 1)
                    jobs_hi = g_build_jobs(tb + 1, 0) if tb + 1 < NTB else []
                    pend = issue_act(tb, 0)
                    for i2 in range(128):
                        jl = jobs_lo if i2 < 64 else jobs_hi
                        nxt = issue_act(tb, i2 + 1) if i2 + 1 < 128 else None
                        for _ in range(2):
                            if jl:
                                jl.pop(0)()
                        issue_y(i2, pend)
                        pend = nxt
                        for _ in range(2 if i2 % 64 < 62 else 1000):
                            if jl:
                                jl.pop(0)()
                    while jobs_hi:
                        jobs_hi.pop(0)()
                    for tt in range(2):
                        t = t0 + tt
                        acc, accr = accs[tt]
                        for half in range(2):
                            hs = slice(half * 512, (half + 1) * 512)
                            py, pyr = psf[tt * 2 + half]
                            k.op("dve", lambda e: e.scalar_tensor_tensor(out=acc[:, hs], in0=H[:, t, hs], scalar=ALPHA, in1=py[:], op0=ALU.mult, op1=ALU.add), reads=[H_res[t], pyr], writes=[accr])
                        layer_norm(acc, [accr], g3[:], b3[:], [g3r, b3r], acc, [accr], small)
                        k.dma("sp", lambda e: e.dma_start(out=out_d[t * 128:(t + 1) * 128, :], in_=acc), reads=[accr])
                barrier()

        finish()
    return nc, dbg_outs


def _consts():
    c = {}
    c["c_ident"] = np.eye(128, dtype=np.float32)
    ob = np.zeros((128, 128), np.float32); ob[:64, :64] = 1.0; ob[64:, 64:] = 1.0
    c["c_onesbd"] = ob
    s = np.arange(64)
    lt = (s[:, None] < s[None, :]).astype(np.float32)
    le = (s[:, None] <= s[None, :]).astype(np.float32)

    def mk(strict, incl):
        m = np.zeros((128, 512), np.float32)
        bd = np.zeros((128, 128), np.float32); bd[:64, :64] = strict; bd[64:, 64:] = strict
        pl = np.concatenate([incl, incl], axis=0)
        m[:, 0:128] = bd; m[:, 128:192] = pl; m[:, 192:320] = bd; m[:, 320:384] = pl
        bdT = np.zeros((128, 128), np.float32); bdT[:64, :64] = strict.T; bdT[64:, 64:] = strict.T
        m[:, 384:512] = bdT
        return m
    c["c_maskF"] = mk(lt, le)
    c["c_maskB"] = mk(lt.T.copy(), le.T.copy())
    r = np.ones((128, BLK), np.float32); r[:, ::CH] = 0.0
    c["c_rst"] = r
    c["c_iota"] = np.broadcast_to(np.arange(16, dtype=np.float32), (128, 16)).copy()
    return c


def prep_shared(inp):
    f = lambda a: np.ascontiguousarray(np.asarray(a, dtype=np.float32))
    sh = {}
    for n in ("ln_emb_g", "ln_emb_b"):
        sh[n] = f(inp[n]).reshape(1, D)
    for n in ("ln1_g", "ln1_b", "ln2_g", "ln2_b", "ln3_g", "ln3_b", "mem_ln_g", "mem_ln_b"):
        sh[n] = f(inp[n][0]).reshape(1, D)
    sh["w_in"] = f(inp["w_in"][0])
    sh["mu"] = f(inp["rwkv_mu"][0]).reshape(1, RWKV_COLS)
    tr = lambda a: f(np.asarray(a).reshape(-1, 4, 128).transpose(2, 0, 1).reshape(128, -1))
    sh["w0T"] = tr(inp["rwkv_w0"][0]); sh["a0T"] = tr(inp["rwkv_a0"][0])
    sh["w2"] = f(inp["rwkv_w2"][0]).reshape(128, RW); sh["a2"] = f(inp["rwkv_a2"][0]).reshape(128, RW)
    sh["g2"] = f(inp["rwkv_g2"][0])
    cols = [inp["rwkv_k_k"][0], inp["rwkv_k_a"][0], np.asarray(inp["rwkv_r_k"][0]).reshape(-1), inp["rwkv_gn_g"][0], inp["rwkv_gn_b"][0]]
    sh["pp"] = f(np.concatenate([np.asarray(c).reshape(4, 128).T for c in cols], axis=1))
    sh["gln_g"] = f(inp["gmlp_ln_g"][0]).reshape(1, 512); sh["gln_b"] = f(inp["gmlp_ln_b"][0]).reshape(1, 512)
    sh["wsT"] = f(np.asarray(inp["gmlp_w_s"][0]).transpose(2, 0, 1))
    bs = np.repeat(np.asarray(inp["gmlp_b_s"][0]), 64, axis=0)
    sh["bsF"] = f(bs.reshape(4, 128, 128).transpose(1, 0, 2))
    sh["w_branch"] = f(inp["w_branch"][0]).reshape(1024, D)
    sh["w_mix"] = f(inp["w_mix_out"][0])
    sh["wq"] = f(inp["xattn_w_q"][0]); sh["wkv"] = f(inp["xattn_w_kv"][0]); sh["wo"] = f(inp["xattn_w_o"][0])
    sh["pwq"] = f(inp["peer_w_query"][0])
    sh["skT"] = f(np.asarray(inp["peer_sub_keys"][0]).transpose(2, 0, 1))
    sh["puT"] = f(np.asarray(inp["peer_u"][0]).reshape(128, 128, 8, 128).transpose(1, 3, 2, 0).reshape(128, 128, D))
    sh["pvP"] = f(np.asarray(inp["peer_v"][0]).reshape(128, 128, D).transpose(1, 0, 2))
    sh.update(_consts())
    return sh


def make_in_maps(inp, cores):
    sh = prep_shared(inp)
    maps = []
    for b in cores:
        m = dict(sh)
        m["x"] = np.ascontiguousarray(np.asarray(inp["x"][b], dtype=np.float32))
        m["mem"] = np.ascontiguousarray(np.asarray(inp["mem"][b], dtype=np.float32))
        maps.append(m)
    return maps


def kernel(**inputs):
    nc, _ = build_program()
    maps = make_in_maps(inputs, list(range(N_CORES)))
    res = run_bass_kernel_spmd(nc, maps, core_ids=list(range(N_CORES)))
    return np.stack([np.asarray(r["out"], dtype=np.float32) for r in res.results], axis=0)
```
